# Optimizing a Trainium2 kernel written in Bass

```python
import jax, jax.numpy as jnp
from jax import lax
import numpy as np

D_MODEL = 2048
BATCH = 2
SEQ = 4096
DEPTH = 1
DEC_BATCH = 32
DEC_SEQ = 1
PAST_LEN = 16384
PAGE_SIZE = 128

HEAD_DIM = 128
N_ATTN_HEADS = D_MODEL // (2 * HEAD_DIM)
N_KV_HEADS = 2
GQA_GROUP = N_ATTN_HEADS // N_KV_HEADS
ATTN_WIDTH = N_ATTN_HEADS * HEAD_DIM
WINDOW = 128
ATTN_BLOCK = 128
N_RET_HEADS = D_MODEL // 256
RET_QK_DIM = (D_MODEL // 2) // N_RET_HEADS
RET_V_DIM = RET_QK_DIM
RET_WIDTH = N_RET_HEADS * RET_V_DIM
RET_CHUNK = 128
MIX_WIDTH = ATTN_WIDTH + RET_WIDTH
D_FF = -(-8 * D_MODEL // (3 * 256)) * 256
PLE_DIM = 256
ROPE_BASE = 10000.0
NORM_EPS = 1e-6
ATTN_SCALE = HEAD_DIM ** -0.5
RET_K_SCALE = RET_QK_DIM ** -0.5
_SPLIT_SIZES = (ATTN_WIDTH, N_KV_HEADS * HEAD_DIM, N_KV_HEADS * HEAD_DIM,
                N_RET_HEADS * RET_QK_DIM, N_RET_HEADS * RET_QK_DIM, RET_WIDTH, RET_WIDTH)
_SPLIT_POINTS = tuple(int(c) for c in np.cumsum(_SPLIT_SIZES)[:-1])
IN_WIDTH = sum(_SPLIT_SIZES)

kernel_name = 'hybrid_swa_sink_retention_decoder_step'


def _rms_norm(x, g):
    xf = x.astype(jnp.float32)
    y = xf * lax.rsqrt(jnp.mean(xf * xf, axis=-1, keepdims=True) + NORM_EPS)
    return (y * g.astype(jnp.float32)).astype(x.dtype)


def _rotary(x, pos):
    half = x.shape[-1] // 2
    inv = ROPE_BASE ** (-jnp.arange(half, dtype=jnp.float32) / half)
    ang = pos.astype(jnp.float32)[:, None] * inv[None, :]
    cos = jnp.cos(ang)[None, :, None, :]
    sin = jnp.sin(ang)[None, :, None, :]
    xf = x.astype(jnp.float32)
    x1, x2 = xf[..., :half], xf[..., half:]
    return jnp.concatenate([x1 * cos - x2 * sin, x1 * sin + x2 * cos], axis=-1)


def _sink_softmax(s, mask, sink):
    s = jnp.where(mask, s, -jnp.inf)
    m = jnp.maximum(jnp.max(s, axis=-1), sink)
    e = jnp.exp(s - m[..., None])
    denom = jnp.sum(e, axis=-1) + jnp.exp(sink - m)
    return e / denom[..., None]


def _swa_prompt(q, k, v, sinks):
    b, t = q.shape[0], q.shape[1]
    nb = t // ATTN_BLOCK
    qb = q.reshape(b, nb, ATTN_BLOCK, N_KV_HEADS, GQA_GROUP, HEAD_DIM)

    def band(a):
        ap = jnp.pad(a, ((0, 0), (ATTN_BLOCK, 0), (0, 0), (0, 0)))
        ap = ap.reshape(b, nb + 1, ATTN_BLOCK, N_KV_HEADS, HEAD_DIM)
        return jnp.concatenate([ap[:, :-1], ap[:, 1:]], axis=2)

    kb, vb = band(k), band(v)
    blk = jnp.arange(nb)[:, None]
    qpos = blk * ATTN_BLOCK + jnp.arange(ATTN_BLOCK)[None, :]
    kpos = (blk - 1) * ATTN_BLOCK + jnp.arange(2 * ATTN_BLOCK)[None, :]
    dist = qpos[:, :, None] - kpos[:, None, :]
    mask = (dist >= 0) & (dist <= WINDOW) & (kpos[:, None, :] >= 0)
    s = jnp.einsum('bnqhgd,bnkhd->bnhgqk', qb, kb, preferred_element_type=jnp.float32) * ATTN_SCALE
    sink = sinks.astype(jnp.float32).reshape(N_KV_HEADS, GQA_GROUP)[None, None, :, :, None]
    p = _sink_softmax(s, mask[None, :, None, None], sink)
    o = jnp.einsum('bnhgqk,bnkhd->bnqhgd', p.astype(vb.dtype), vb)
    return o.reshape(b, t, ATTN_WIDTH)


def _swa_sample(q, k, v, k_buf, v_buf, sinks):
    b, t = q.shape[0], q.shape[1]
    w = k_buf.shape[1]
    kk = jnp.concatenate([k_buf.astype(k.dtype), k], axis=1)
    vv = jnp.concatenate([v_buf.astype(v.dtype), v], axis=1)
    qpos = PAST_LEN + jnp.arange(t)
    kpos = PAST_LEN - w + jnp.arange(w + t)
    dist = qpos[:, None] - kpos[None, :]
    mask = (dist >= 0) & (dist <= WINDOW)
    qg = q.reshape(b, t, N_KV_HEADS, GQA_GROUP, HEAD_DIM)
    s = jnp.einsum('bqhgd,bkhd->bhgqk', qg, kk, preferred_element_type=jnp.float32) * ATTN_SCALE
    sink = sinks.astype(jnp.float32).reshape(N_KV_HEADS, GQA_GROUP)[None, :, :, None]
    p = _sink_softmax(s, mask, sink)
    o = jnp.einsum('bhgqk,bkhd->bqhgd', p.astype(vv.dtype), vv).reshape(b, t, ATTN_WIDTH)
    return o, kk[:, -w:], vv[:, -w:]


def _ret_log_decay():
    return jnp.log1p(-jnp.exp2(-5.0 - jnp.arange(N_RET_HEADS, dtype=jnp.float32)))


def _retention(q, k, v, s0):
    b, t, h = q.shape[0], q.shape[1], q.shape[2]
    c = RET_CHUNK if t % RET_CHUNK == 0 else t
    nc = t // c
    lg = _ret_log_decay()
    idx = jnp.arange(c, dtype=jnp.float32)
    diff = idx[:, None] - idx[None, :]
    intra = jnp.where(diff >= 0, jnp.exp(lg[:, None, None] * jnp.maximum(diff, 0.0)), 0.0)
    q_dec = jnp.exp(lg[:, None] * (idx[None, :] + 1.0))[None, :, :, None]
    k_dec = jnp.exp(lg[:, None] * (c - 1.0 - idx[None, :]))[None, :, :, None]
    chunk_dec = jnp.exp(lg * c)[None, :, None, None]

    def to_chunks(a):
        return a.reshape(b, nc, c, h, a.shape[-1]).transpose(1, 0, 3, 2, 4)

    def step(s, inp):
        qc, kc, vc = inp
        sc = jnp.einsum('bhcd,bhed->bhce', qc, kc) * intra
        o = jnp.einsum('bhce,bhev->bhcv', sc, vc) + jnp.einsum('bhcd,bhdv->bhcv', qc * q_dec, s)
        s = s * chunk_dec + jnp.einsum('bhcd,bhcv->bhdv', kc * k_dec, vc)
        return s, o

    s_final, o = lax.scan(step, s0, (to_chunks(q), to_chunks(k), to_chunks(v)))
    o = o.transpose(1, 0, 3, 2, 4).reshape(b, t, h, v.shape[-1])
    return o, s_final


def _layer(h, pe, k_buf, v_buf, s0, pos, wts):
    (an, w_in, qn, kn, sinks, rg, w_out, fn, w_gate, w_up, w_down, pn, w_ple, w_pg) = wts
    b, t = h.shape[0], h.shape[1]
    a = _rms_norm(h, an)
    z = a @ w_in
    aq, ak, av, rq, rk, rv, rgate = jnp.split(z, _SPLIT_POINTS, axis=-1)
    aq = _rms_norm(aq.reshape(b, t, N_ATTN_HEADS, HEAD_DIM), qn)
    ak = _rms_norm(ak.reshape(b, t, N_KV_HEADS, HEAD_DIM), kn)
    av = av.reshape(b, t, N_KV_HEADS, HEAD_DIM)
    if k_buf is None:
        o_attn = _swa_prompt(aq, ak, av, sinks)
        w = min(WINDOW, t)
        nk, nv = ak[:, -w:], av[:, -w:]
    else:
        o_attn, nk, nv = _swa_sample(aq, ak, av, k_buf, v_buf, sinks)
    rq = _rotary(rq.reshape(b, t, N_RET_HEADS, RET_QK_DIM), pos)
    rk = _rotary(rk.reshape(b, t, N_RET_HEADS, RET_QK_DIM), pos) * RET_K_SCALE
    rv = rv.reshape(b, t, N_RET_HEADS, RET_V_DIM).astype(jnp.float32)
    o_ret, s_new = _retention(rq, rk, rv, s0.astype(jnp.float32))
    o_ret = o_ret * lax.rsqrt(jnp.mean(o_ret * o_ret, axis=-1, keepdims=True) + NORM_EPS)
    o_ret = o_ret.reshape(b, t, RET_WIDTH) * rg.astype(jnp.float32) * jax.nn.silu(rgate.astype(jnp.float32))
    mix = jnp.concatenate([o_attn, o_ret.astype(h.dtype)], axis=-1)
    h = h + mix @ w_out
    f = _rms_norm(h, fn)
    h = h + (jax.nn.silu(f @ w_gate) * (f @ w_up)) @ w_down
    h = h + (pe @ w_ple) * jax.nn.sigmoid(_rms_norm(h, pn) @ w_pg)
    return h, nk, nv, s_new.astype(h.dtype)


def setup_inputs(seed: int = 0) -> dict:
    key = jax.random.key(seed)
    ks = jax.random.split(key, 24)
    f32 = jnp.float32
    n = lambda i, shape: jax.random.normal(ks[i], shape, f32)
    cache_win = min(WINDOW, PAST_LEN)
    return {
        'x_prompt': n(0, (BATCH, SEQ, D_MODEL)),
        'x_sample': n(1, (DEC_BATCH, DEC_SEQ, D_MODEL)),
        'cache_k_win': n(2, (DEPTH, DEC_BATCH, cache_win, N_KV_HEADS, HEAD_DIM)),
        'cache_v_win': n(3, (DEPTH, DEC_BATCH, cache_win, N_KV_HEADS, HEAD_DIM)),
        'state_ret': n(4, (DEPTH, DEC_BATCH, N_RET_HEADS, RET_QK_DIM, RET_V_DIM)),
        'p_prompt': n(5, (DEPTH, BATCH, SEQ, PLE_DIM)),
        'p_sample': n(6, (DEPTH, DEC_BATCH, DEC_SEQ, PLE_DIM)),
        'attn_norm_g': 1.0 + 0.05 * n(7, (DEPTH, D_MODEL)),
        'w_in': n(8, (DEPTH, D_MODEL, IN_WIDTH)) * D_MODEL ** -0.5,
        'q_norm_g': 1.0 + 0.05 * n(9, (DEPTH, HEAD_DIM)),
        'k_norm_g': 1.0 + 0.05 * n(10, (DEPTH, HEAD_DIM)),
        'attn_sinks': 0.5 * n(11, (DEPTH, N_ATTN_HEADS)),
        'ret_out_g': 1.0 + 0.05 * n(12, (DEPTH, RET_WIDTH)),
        'w_out': n(13, (DEPTH, MIX_WIDTH, D_MODEL)) * MIX_WIDTH ** -0.5,
        'ffn_norm_g': 1.0 + 0.05 * n(14, (DEPTH, D_MODEL)),
        'w_gate': n(15, (DEPTH, D_MODEL, D_FF)) * D_MODEL ** -0.5,
        'w_up': n(16, (DEPTH, D_MODEL, D_FF)) * D_MODEL ** -0.5,
        'w_down': n(17, (DEPTH, D_FF, D_MODEL)) * D_FF ** -0.5,
        'ple_norm_g': 1.0 + 0.05 * n(18, (DEPTH, D_MODEL)),
        'w_ple': n(19, (DEPTH, PLE_DIM, D_MODEL)) * PLE_DIM ** -0.5,
        'w_ple_gate': n(20, (DEPTH, D_MODEL, D_MODEL)) * D_MODEL ** -0.5,
    }


def reference(x_prompt, x_sample, cache_k_win, cache_v_win, state_ret, p_prompt, p_sample,
              attn_norm_g, w_in, q_norm_g, k_norm_g, attn_sinks, ret_out_g, w_out,
              ffn_norm_g, w_gate, w_up, w_down, ple_norm_g, w_ple, w_ple_gate):
    y_p, y_s = x_prompt, x_sample
    pos_p = jnp.arange(x_prompt.shape[1])
    pos_s = PAST_LEN + jnp.arange(x_sample.shape[1])
    kp_l, vp_l, sp_l, ks_l, vs_l, ss_l = [], [], [], [], [], []
    for l in range(DEPTH):
        wts = (attn_norm_g[l], w_in[l], q_norm_g[l], k_norm_g[l], attn_sinks[l], ret_out_g[l], w_out[l],
               ffn_norm_g[l], w_gate[l], w_up[l], w_down[l], ple_norm_g[l], w_ple[l], w_ple_gate[l])
        s0 = jnp.zeros((x_prompt.shape[0], N_RET_HEADS, RET_QK_DIM, RET_V_DIM), jnp.float32)
        y_p, kp, vp, sp = _layer(y_p, p_prompt[l], None, None, s0, pos_p, wts)
        y_s, kd, vd, sd = _layer(y_s, p_sample[l], cache_k_win[l], cache_v_win[l], state_ret[l], pos_s, wts)
        kp_l.append(kp); vp_l.append(vp); sp_l.append(sp)
        ks_l.append(kd); vs_l.append(vd); ss_l.append(sd)
    k_win_prompt = jnp.stack(kp_l)
    v_win_prompt = jnp.stack(vp_l)
    ret_state_prompt = jnp.stack(sp_l)
    k_win_sample = jnp.stack(ks_l)
    v_win_sample = jnp.stack(vs_l)
    ret_state_sample = jnp.stack(ss_l)
    return (y_p, y_s, k_win_prompt, v_win_prompt, ret_state_prompt, k_win_sample, v_win_sample, ret_state_sample)
```

```python
import math
from contextlib import ExitStack

import numpy as np
import ml_dtypes

import concourse.bass as bass
import concourse.mybir as mybir
from concourse.bass_utils import run_bass_kernel_spmd

F32 = mybir.dt.float32
BF16 = mybir.dt.bfloat16
U8 = mybir.dt.uint8
AF = mybir.ActivationFunctionType
ALU = mybir.AluOpType
AX = mybir.AxisListType

D = 2048
NP_ = 1024
NS = 4
NT = NP_ + NS
NHALO = 128
NA = NT + NHALO
KC = 16
DFF = 5632
EPS = 1e-6
ATTN_SCALE = 128 ** -0.5
RET_K_SCALE = 128 ** -0.5
PAST_LEN = 16384
TT = [(0, 344), (344, 344), (688, 340)]
TT_H = TT + [(NT, NHALO)]
NSLOT = 4
SLOT_BYTES = 8192
GAMMA = [1.0 - 2.0 ** (-5 - h) for h in range(8)]


class Op:
    __slots__ = ("eng", "fn", "dma", "idx", "waits", "sig", "val", "fence")

    def __init__(self, eng, fn, dma, idx):
        self.eng = eng
        self.fn = fn
        self.dma = dma
        self.idx = idx
        self.waits = []
        self.sig = dma is not None
        self.val = None
        self.fence = None


class Prog:
    ENGS = ("sp", "act", "dve", "pool", "pe")

    def __init__(self):
        self.ops = []
        self.last_w = {}
        self.readers = {}
        self.last_on = {}
        self.dma_ops = {}
        self.pending_fence = {}

    def add(self, eng, fn, r=(), w=(), dma=None, nofence=False):
        op = Op(eng, fn, dma, len(self.ops))
        psr = [x for x in r if isinstance(x, tuple) and x[0] == "ps"]
        if psr:
            r = [x for x in r if not (isinstance(x, tuple) and x[0] == "ps")]
            w = list(w) + psr
        deps = {}
        for res in r:
            lw = self.last_w.get(res)
            if lw is not None:
                deps.setdefault(lw, set()).add("raw")
        for res in w:
            lw = self.last_w.get(res)
            if lw is not None:
                deps.setdefault(lw, set()).add("waw")
            for rd in self.readers.get(res, ()):
                deps.setdefault(rd, set()).add("war")
        best = {}
        for d, kinds in deps.items():
            if d is op:
                continue
            if d.dma is not None:
                op.waits.append(d)
                continue
            if op.dma is None and d.eng == eng:
                if eng == "pe":
                    continue
                if "raw" not in kinds:
                    continue
            b = best.get(d.eng)
            if b is None or d.idx > b.idx:
                best[d.eng] = d
        for d in best.values():
            d.sig = True
            op.waits.append(d)
        for res in r:
            self.readers.setdefault(res, []).append(op)
        for res in w:
            self.last_w[res] = op
            self.readers[res] = []
        if eng in self.pending_fence and not nofence:
            op.fence = self.pending_fence.pop(eng)
        self.ops.append(op)
        if dma is None:
            self.last_on[eng] = op
        else:
            self.dma_ops.setdefault(dma, []).append(op)
        return op

    def fence(self):
        st = {"comp": dict(self.last_on), "dma": {k: v[-1] for k, v in self.dma_ops.items()}}
        for o in st["comp"].values():
            o.sig = True
        for e in self.ENGS:
            self.pending_fence[e] = st

    def emit(self, nc, block, sems, dma_sems, final_waits):
        cnt = {e: 0 for e in self.ENGS}
        dcnt = {}
        for op in self.ops:
            if op.dma is not None:
                dcnt[op.dma] = dcnt.get(op.dma, 0) + (1 if op.dma == "cc" else 16)
                op.val = dcnt[op.dma]
            elif op.sig:
                cnt[op.eng] += 1
                op.val = cnt[op.eng]
        self.final_counts = (cnt, dcnt)

        def semof(op):
            return dma_sems[op.dma] if op.dma is not None else sems[op.eng]

        def run(engname):
            def body(eng):
                waited = {}

                def wait(sem_key, sem, val):
                    if waited.get(sem_key, 0) >= val:
                        return
                    waited[sem_key] = val
                    eng.wait_ge(sem, val)

                for op in self.ops:
                    if op.eng != engname:
                        continue
                    if op.fence is not None:
                        for o in op.fence["comp"].values():
                            if o.eng != engname or engname != "pe":
                                wait(("c", o.eng), sems[o.eng], o.val)
                        for k, o in op.fence["dma"].items():
                            wait(("d", k), dma_sems[k], o.val)
                    for d in op.waits:
                        key = ("d", d.dma) if d.dma is not None else ("c", d.eng)
                        wait(key, semof(d), d.val)
                    ins = op.fn(eng)
                    if op.dma is not None:
                        ins.then_inc(dma_sems[op.dma], 1 if op.dma == "cc" else 16)
                    elif op.sig:
                        ins.then_inc(sems[op.eng], 1)
                if engname == "sp":
                    for k in final_waits:
                        if k in dcnt:
                            eng.wait_ge(dma_sems[k], dcnt[k])
            return body

        block.sync(run("sp"))
        block.scalar(run("act"))
        block.vector(run("dve"))
        block.gpsimd(run("pool"))
        block.tensor(run("pe"))


CF = {}
_off = 0
for _n, _w in [("ident", 128), ("ones_row", 128), ("fn_g", 16), ("pn_g", 16), ("rg", 8), ("qg", 1), ("kg", 1),
               ("sinks", 8), ("intraT", 1024), ("qdec", 1024), ("kdec", 8), ("kdlong", 64), ("coef", 32),
               ("onehot", 4), ("cos", NT), ("sin", NT)]:
    CF[_n] = (_off, _w)
    _off += _w
NCF = _off
CB = {}
_off = 0
for _n, _w in [("ident", 128), ("ones", 128), ("onesD", 128), ("ones128", 128), ("rrot", 128), ("mask", 384)]:
    CB[_n] = (_off, _w)
    _off += _w
NCB = _off


def host_consts(core, inp):
    m = core % 4
    cf = np.zeros((128, NCF), np.float32)

    def put(name, arr):
        o, w = CF[name]
        cf[:, o:o + w] = np.asarray(arr, np.float32).reshape(128, w) if np.ndim(arr) == 2 else np.broadcast_to(
            np.asarray(arr, np.float32).reshape(1, w), (128, w))

    put("ident", np.eye(128, dtype=np.float32))
    put("ones_row", np.ones((128, 128), np.float32))
    put("fn_g", inp["ffn_norm_g"][0].reshape(16, 128).T)
    put("pn_g", inp["ple_norm_g"][0].reshape(16, 128).T)
    put("rg", inp["ret_out_g"][0].reshape(8, 128).T)
    put("qg", inp["q_norm_g"][0].reshape(128, 1))
    put("kg", inp["k_norm_g"][0].reshape(128, 1))
    put("sinks", inp["attn_sinks"][0].reshape(8))
    g = np.array(GAMMA, np.float64)
    j = np.arange(128)
    diff = j[None, :] - j[:, None]
    intraT = np.where(diff[:, None, :] >= 0, g[None, :, None] ** np.maximum(diff, 0)[:, None, :], 0.0) * RET_K_SCALE
    put("intraT", intraT.reshape(128, 1024))
    qdec = g[:, None] ** (j[None, :] + 1.0)
    put("qdec", qdec.reshape(1024))
    put("kdec", (g[None, :] ** (127.0 - j[:, None])) * RET_K_SCALE)
    c = np.arange(8)
    kdl = g[None, :, None] ** (1023.0 - (128.0 * c[None, None, :] + j[:, None, None])) * RET_K_SCALE
    put("kdlong", kdl.reshape(128, 64))
    coef = np.zeros((4, 8))
    for r in range(4):
        if r < m:
            coef[r] = g ** (1024.0 * (m - r - 1))
    put("coef", coef.reshape(32))
    oh = np.zeros((128, 4), np.float32)
    oh[:4] = np.eye(4)
    put("onehot", oh)
    pos = np.concatenate([m * 1024 + np.arange(1024), np.full(4, PAST_LEN)]).astype(np.float32)
    inv = (np.float32(10000.0) ** (-np.arange(64, dtype=np.float32) / np.float32(64))).astype(np.float32)
    ang = (pos[None, :] * inv[:, None]).astype(np.float32).astype(np.float64)
    cos = np.cos(ang)
    sin = np.sin(ang)
    put("cos", np.concatenate([cos, cos], 0))
    put("sin", np.concatenate([-sin, sin], 0))

    cb = np.zeros((128, NCB), np.float32)

    def putb(name, arr):
        o, w = CB[name]
        cb[:, o:o + w] = arr

    putb("ident", np.eye(128))
    putb("ones", np.ones((128, 128)))
    putb("onesD", np.full((128, 128), 1.0 / D))
    putb("ones128", np.full((128, 128), 1.0 / 128))
    rr = np.zeros((128, 128))
    for p in range(128):
        rr[(p + 64) % 128, p] = 1.0
    putb("rrot", rr)
    NEG = -30000.0
    own = np.where(j[:, None] <= j[None, :], 0.0, NEG).astype(np.float32)
    prev = np.where(j[:, None] >= j[None, :], 0.0, NEG).astype(np.float32)
    putb("mask", np.concatenate([own, prev, prev if m != 0 else np.full((128, 128), NEG, np.float32)], 1))
    return cf, cb.astype(ml_dtypes.bfloat16)


class _Stop(Exception):
    pass


def build_program(dbg=None, stop=99, groups=None):
    nc = bass.Bass("TRN2", target_bir_lowering=False)
    P = Prog()
    dbg = dbg or {}

    stopped = [False]

    def chk(k):
        if stop == k:
            stopped[0] = True

    def din(name, shape, dt=F32):
        return nc.dram_tensor(name, list(shape), dt, kind="ExternalInput").ap()

    def dout(name, shape, dt=F32):
        return nc.dram_tensor(name, list(shape), dt, kind="ExternalOutput").ap()

    x_main = din("x_main", [NP_, D]); x_halo = din("x_halo", [NHALO, D]); x_smp = din("x_smp", [NS, D])
    p_main = din("p_main", [NP_, 256]); p_smp = din("p_smp", [NS, 256])
    ck = din("ck", [NS, 128, 2, 128]); cv = din("cv", [NS, 128, 2, 128]); st = din("st", [NS, 8, 128, 128])
    an_g = din("an_g", [1, D])
    w_in = din("w_in", [D, DFF]); w_out = din("w_out", [D, D]); w_gate = din("w_gate", [D, DFF])
    w_up = din("w_up", [D, DFF]); w_down = din("w_down", [DFF, D]); w_ple = din("w_ple", [256, D])
    w_pg = din("w_pg", [D, D])
    cf_d = din("cf", [128, NCF]); cb_d = din("cb", [128, NCB], BF16)

    y_main = dout("y_main", [NP_, D]); y_smp = dout("y_smp", [NS, D])
    kwin = dout("kwin", [128, 2, 128]); vwin = dout("vwin", [128, 2, 128]); rstate = dout("rstate", [8, 128, 128])
    ks_out = dout("ks_out", [NS, 128, 2, 128]); vs_out = dout("vs_out", [NS, 128, 2, 128])
    ss_out = dout("ss_out", [NS, 8, 128, 128])
    krs = nc.dram_tensor("krs", [8, 128, NT], BF16)
    vts = nc.dram_tensor("vts", [8, 128, NT], BF16)
    ag_in = nc.dram_tensor("ag_in", [8 * 128, 128], F32)
    ag_out = nc.dram_tensor("ag_out", [4 * 8 * 128, 128], F32)
    dbg_out = {}
    for name, shape in dbg.items():
        dbg_out[name] = dout("dbg_" + name, shape)

    def dump(name, ap, r=()):
        if name in dbg_out:
            P.add("pool", lambda e: e.dma_start(out=dbg_out[name], in_=ap), r, [("dbg", name)], dma="o_dbg")

    es = ExitStack()
    total = (nc.sbuf_bytes_remaining - 64) // 64 * 64
    arena = es.enter_context(nc.sbuf_tensor("arena", [128, total], U8))
    ps = [es.enter_context(nc.psum_tensor("ps%d" % i, [128, 512], F32)) for i in range(8)]

    class Alloc:
        def __init__(self, base, limit):
            self.p = base
            self.limit = limit

        def __call__(self, shape, dt):
            esz = 4 if dt == F32 else 2
            n = int(np.prod(shape)) * esz
            off = (self.p + 31) // 32 * 32
            self.p = off + n
            assert self.p <= self.limit, (self.p, self.limit)
            v = arena[:, off:off + n].bitcast(dt)
            if len(shape) == 2:
                return v.rearrange("p (a b) -> p a b", a=shape[0])
            if len(shape) == 3:
                return v.rearrange("p (a b c) -> p a b c", a=shape[0], b=shape[1])
            return v

    A = Alloc(0, total)
    cf = A([NCF], F32)
    cb = A([NCB], BF16)
    negc = A([1], F32); esink = A([8], F32); S0 = A([8, 128], F32)
    misc = A([64], F32)
    wslots = [A([SLOT_BYTES // 2], BF16) for _ in range(NSLOT)]
    aT = A([KC, NA], BF16)
    mixT = A([KC, NT], BF16)
    T2base = A.p
    T2 = Alloc(T2base, T2base + 12 * 1024)
    A.p = T2base + 12 * 1024
    HB = (A.p + 31) // 32 * 32
    hT = A([KC, NT], F32)
    HEND = A.p
    print("sbuf used", A.p, "of", total)

    def cfv(name, lo=0, hi=None):
        o, w = CF[name]
        return cf[:, o + lo:o + (w if hi is None else hi)]

    def cbv(name, lo=0, hi=None):
        o, w = CB[name]
        return cb[:, o + lo:o + (w if hi is None else hi)]

    ident_f = cfv("ident"); ident_b = cbv("ident")

    def psb(i, n=1024):
        return ps[i][:, :].bitcast(BF16)[:, 0:n]

    def pe(fn, r=(), w=()): return P.add("pe", fn, r, w)
    def act(fn, r=(), w=()): return P.add("act", fn, r, w)
    def dve(fn, r=(), w=()): return P.add("dve", fn, r, w)
    def pool(fn, r=(), w=()): return P.add("pool", fn, r, w)
    def dma(q, sem, out, in_, r=(), w=(), nofence=False, slow=False):
        if slow:
            return P.add(q, lambda e: e.dma_start(out=out, in_=in_, allow_slow_non_contiguous=True), r, w, dma=sem)
        return P.add(q, lambda e: e.dma_start(out=out, in_=in_), r, w, dma=sem, nofence=nofence)

    dma_sem_names = set()
    _orig_add = P.add

    def add_track(eng, fn, r=(), w=(), dma=None, nofence=False):
        if stopped[0]:
            return None
        if dma is not None:
            dma_sem_names.add(dma)
        return _orig_add(eng, fn, r, w, dma, nofence)
    P.add = add_track

    rot = {"d": 0, "m": 0}

    def bank_d():
        b = rot["d"] % 4
        rot["d"] += 1
        return b

    def bank_m():
        b = 4 + rot["m"] % 4
        rot["m"] += 1
        return b

    wq = []
    wstate = {"issued": 0, "used": 0, "done": 0}
    w_extra = []

    def w_issue_upto(n):
        while wstate["issued"] < min(n, len(wq)):
            i = wstate["issued"]
            src, kc, ncols = wq[i]
            s = i % NSLOT
            dst = wslots[s][:, 0:kc * ncols].rearrange("p (k n) -> p k n", k=kc)
            dma("pool", "w%d" % s, dst, src.rearrange("(k p) n -> p k n", p=128), r=list(w_extra), w=[("w", s)], nofence=True)
            wstate["issued"] += 1

    def w_done(n=1):
        wstate["done"] += n
        w_issue_upto(wstate["done"] + NSLOT)

    def w_next():
        i = wstate["used"]
        assert i < wstate["done"] + NSLOT
        w_issue_upto(i + 1)
        src, kc, ncols = wq[i]
        s = i % NSLOT
        wstate["used"] += 1
        return wslots[s][:, 0:kc * ncols].rearrange("p (k n) -> p k n", k=kc), ("w", s)

    def wblk(wap, k0, k1, c0, ncols):
        return (wap[k0 * 128:k1 * 128, c0:c0 + ncols], k1 - k0, ncols)

    C_AQ, C_AK, C_AV, C_RQ, C_RK, C_RV, C_RG = 0, 1024, 1280, 1536, 2560, 3584, 4608
    for j in range(4):
        wq.append(wblk(w_in, 0, 16, C_RK + 256 * j, 256))
        wq.append(wblk(w_in, 0, 16, C_RV + 256 * j, 256))
    wq.append(wblk(w_in, 0, 16, C_AK, 256))
    wq.append(wblk(w_in, 0, 16, C_AV, 256))
    for j in range(4):
        wq.append(wblk(w_in, 0, 16, C_AQ + 256 * j, 256))
    for j in range(4):
        for cbase in (C_RQ, C_RG):
            wq.append(wblk(w_in, 0, 16, cbase + 256 * j, 256))
    for j in range(8):
        wq.append(wblk(w_out, 0, 16, 256 * j, 256))
    QUART = [(0, 6), (6, 6), (12, 5), (17, 5)]
    for (b0, nb) in QUART:
        for b in range(b0, b0 + nb):
            wq.append(wblk(w_gate, 0, 16, 256 * b, 256))
            wq.append(wblk(w_up, 0, 16, 256 * b, 256))
        for j in range(8):
            wq.append(wblk(w_down, 2 * b0, 2 * (b0 + nb), 256 * j, 256))
    for j in range(8):
        wq.append(wblk(w_pg, 0, 16, 256 * j, 256))

    def dense_block(kc, rhs_fn, rhs_res, evac, tiles=TT, nm=2):
        wv, wres = w_next()
        for ml in range(nm):
            for ti, (c0, n) in enumerate(tiles):
                b = bank_d()
                for k in range(kc):
                    pe(lambda e, b=b, k=k, ml=ml, c0=c0, n=n, wv=wv: e.matmul(
                        ps[b][:, 0:n], lhsT=wv[:, k, ml * 128:(ml + 1) * 128], rhs=rhs_fn(k, c0, n),
                        start=(k == 0), stop=(k == kc - 1)),
                       r=[wres] + list(rhs_res), w=[("ps", b)])
                evac(ml, ti, c0, n, b)
        w_done()

    dma("sp", "c_cf", cf, cf_d, w=["cf"])
    dma("sp", "c_cb", cb, cb_d, w=["cb"])

    Z = Alloc(HB, total)
    NXT = 4
    xt = [Z([D], F32) for _ in range(NXT)]
    junk = Z([D], BF16)
    xn = [Z([D], BF16) for _ in range(2)]
    gbc = Z([D], F32)
    ss = misc[:, 0:10]; rstd1 = misc[:, 10:20]; tmpa = misc[:, 20:30]
    gpa = misc[:, 30:31]; mx = misc[:, 31:32]; negc1 = misc[:, 32:33]; epsv = misc[:, 34:35]
    dve(lambda e: e.memset(epsv, EPS), w=["epsv"])

    dma("sp", "c_gbc", gbc, an_g.partition_broadcast(128), w=["gbc"])

    dve(lambda e: e.tensor_tensor(out=gpa, in0=cfv("qg"), in1=cfv("kg"), op=ALU.mult), r=["cf"], w=["gpa"])
    gpa2 = misc[:, 33:34]
    dve(lambda e: e.tensor_tensor(out=gpa2, in0=gpa, in1=gpa, op=ALU.mult), r=["gpa"], w=["gpa2"])
    b = bank_m()
    pe(lambda e, b=b: e.transpose(ps[b][0:1, 0:128], gpa2, ident_f), r=["gpa2", "cf"], w=[("ps", b)])
    dve(lambda e, b=b: e.tensor_reduce(out=mx[0:1, :], in_=ps[b][0:1, 0:128], axis=AX.X, op=ALU.max),
        r=[("ps", b)], w=["mx"])
    act(lambda e: e.activation(out=mx[0:1, :], in_=mx[0:1, :], func=AF.Sqrt), r=["mx"], w=["mx"])
    dve(lambda e: e.tensor_scalar(out=negc1[0:1, :], in0=mx[0:1, :], scalar1=-(ATTN_SCALE * 128.0), scalar2=None,
                                  op0=ALU.mult), r=["mx"], w=["negc1"])
    b = bank_m()
    pe(lambda e, b=b: e.matmul(ps[b][:, 0:1], lhsT=cfv("ones_row")[0:1, :], rhs=negc1[0:1, :], start=True, stop=True),
       r=["negc1", "cf"], w=[("ps", b)])
    dve(lambda e, b=b: e.tensor_copy(out=negc, in_=ps[b][:, 0:1]), r=[("ps", b)], w=["negc"])
    act(lambda e: e.activation(out=esink, in_=cfv("sinks"), func=AF.Exp, bias=negc, scale=1.0),
        r=["negc", "cf"], w=["esink"])

    tiles1 = [(x_main[i * 128:(i + 1) * 128, :], 128, i * 128) for i in range(8)]
    tiles1.append((x_halo, 128, NT))
    tiles1.append((x_smp, NS, NP_))
    def p1_load(i, src, rows, c0):
        s4 = i % NXT
        dma("sp", "x%d" % s4, xt[s4][0:rows, :], src, w=[("xt", s4)])

    def p1_stage1(i, src, rows, c0):
        s = i % 2
        s4 = i % NXT
        act(lambda e, s4=s4, rows=rows, i=i: e.activation(out=junk[0:rows, :], in_=xt[s4][0:rows, :], func=AF.Square,
                                                         accum_out=ss[0:rows, i:i + 1]),
            r=[("xt", s4)], w=["junk", ("ss", i)])
        act(lambda e, rows=rows, i=i: e.activation(out=tmpa[0:rows, i:i + 1], in_=ss[0:rows, i:i + 1], func=AF.Sqrt,
                                                   bias=epsv[0:rows, :], scale=1.0 / D),
            r=[("ss", i), "epsv"], w=[("tmpa", i)])
        dve(lambda e, rows=rows, i=i: e.reciprocal(out=rstd1[0:rows, i:i + 1], in_=tmpa[0:rows, i:i + 1]),
            r=[("tmpa", i)], w=[("rstd1", i)])
        dve(lambda e, s=s, s4=s4, rows=rows, i=i: e.scalar_tensor_tensor(
            out=xn[s][0:rows, :], in0=xt[s4][0:rows, :], scalar=rstd1[0:rows, i:i + 1], in1=gbc[0:rows, :],
            op0=ALU.mult, op1=ALU.mult), r=[("xt", s4), ("rstd1", i), "gbc"], w=[("xn", s)])

    def p1_stage2(i, src, rows, c0):
        s = i % 2
        for half in range(2):
            b = bank_m()
            for kk in range(8):
                k = half * 8 + kk
                pe(lambda e, b=b, kk=kk, k=k, s=s, rows=rows: e.transpose(
                    psb(b)[:, kk * 128:kk * 128 + rows], xn[s][0:rows, k * 128:(k + 1) * 128],
                    ident_b[0:rows, 0:rows]), r=[("xn", s), "cb"], w=[("ps", b)])
            src_ps = lambda b=b, rows=rows: psb(b).rearrange("p (a c) -> p a c", a=8)[:, :, 0:rows]
            dst = aT[:, half * 8:(half + 1) * 8, c0:c0 + rows]
            if half == 0:
                act(lambda e, dst=dst, src_ps=src_ps: e.copy(out=dst, in_=src_ps()), r=[("ps", b)], w=[("aT", c0, half)])
            else:
                dve(lambda e, dst=dst, src_ps=src_ps: e.tensor_copy(out=dst, in_=src_ps()), r=[("ps", b)], w=[("aT", c0, half)])

    for i in range(NXT):
        p1_load(i, *tiles1[i])
    p1_stage1(0, *tiles1[0])
    for i in range(len(tiles1)):
        if i + 1 < len(tiles1):
            p1_stage1(i + 1, *tiles1[i + 1])
        if i + NXT < len(tiles1):
            p1_load(i + NXT, *tiles1[i + NXT])
        if i in (3, 5, 7, 8):
            w_extra[:] = [("xt", (i + 1) % NXT)]
            w_issue_upto(wstate["issued"] + 1)
            w_extra[:] = []
        p1_stage2(i, *tiles1[i])
    w_issue_upto(NSLOT)
    P.fence()
    dump("aT", aT, ["aT"])
    chk(1)

    aT_rhs = lambda k, c0, n: aT[:, k, c0:c0 + n]
    cosv = cfv("cos"); sinv = cfv("sin")

    def rotary_ops(tag, b, c0, n, xb, t1, t2, outb):
        act(lambda e: e.copy(out=xb[:, c0:c0 + n], in_=ps[b][:, 0:n]), r=[("ps", b)], w=[(tag, "xb", c0)])
        dve(lambda e: e.tensor_tensor(out=t1[:, c0:c0 + n], in0=ps[b][:, 0:n], in1=cosv[:, c0:c0 + n], op=ALU.mult),
            r=[("ps", b), "cf"], w=[(tag, "t1", c0)])
        b2 = bank_m()
        pe(lambda e: e.matmul(ps[b2][:, 0:n], lhsT=cbv("rrot"), rhs=xb[:, c0:c0 + n], start=True, stop=True),
           r=[(tag, "xb", c0), "cb"], w=[("ps", b2)])
        dve(lambda e: e.tensor_tensor(out=t2[:, c0:c0 + n], in0=ps[b2][:, 0:n], in1=sinv[:, c0:c0 + n], op=ALU.mult),
            r=[("ps", b2), "cf"], w=[(tag, "t2", c0)])
        dve(lambda e: e.tensor_tensor(out=outb[:, c0:c0 + n], in0=t1[:, c0:c0 + n], in1=t2[:, c0:c0 + n], op=ALU.add),
            r=[(tag, "t1", c0), (tag, "t2", c0)], w=[(tag, "rot", c0)])

    Z = Alloc(HB, total)
    p1 = []
    for hp in range(2):
        p1.append(dict(xb=Z([NT], BF16), t1=Z([NT], F32), t2=Z([NT], F32), kr=Z([NT], BF16),
                       kD=Z([8, 128], BF16), vT=Z([NT], BF16), vtok=Z([8, 128], BF16), sloc=Z([128], F32)))
    kdl = cfv("kdlong").rearrange("p (h c) -> p h c", h=8)

    for j in range(4):
        hs = (2 * j, 2 * j + 1)

        def evac_k(ml, ti, c0, n, b, hs=hs):
            bf = p1[ml]
            rotary_ops(("p1", ml), b, c0, n, bf["xb"], bf["t1"], bf["t2"], bf["kr"])

        def evac_v(ml, ti, c0, n, b, hs=hs):
            bf = p1[ml]
            act(lambda e: e.copy(out=bf["vT"][:, c0:c0 + n], in_=ps[b][:, 0:n]), r=[("ps", b)],
                w=[("p1", ml, "vT", c0)])
        dense_block(16, aT_rhs, ["aT"], evac_k)
        chk(20)
        dense_block(16, aT_rhs, ["aT"], evac_v)
        chk(21)
        for ml in range(2):
            h = hs[ml]
            bf = p1[ml]
            rk_res = [(("p1", ml), "rot", c0) for (c0, n) in TT]
            rv_res = [("p1", ml, "vT", c0) for (c0, n) in TT]
            dma("sp", "spk%d" % ml, krs.ap()[h], bf["kr"], r=rk_res, w=[("krs", h)])
            dma("sp", "spv%d" % ml, vts.ap()[h], bf["vT"], r=rv_res, w=[("vts", h)])
            for half in range(2):
                bk = bank_m()
                for cc in range(4):
                    c = half * 4 + cc
                    pe(lambda e, bk=bk, cc=cc, c=c, bf=bf: e.transpose(
                        psb(bk)[:, cc * 128:(cc + 1) * 128], bf["kr"][:, c * 128:(c + 1) * 128], ident_b),
                       r=rk_res + ["cb"], w=[("ps", bk)])
                for cc in range(4):
                    c = half * 4 + cc
                    dve(lambda e, bk=bk, cc=cc, c=c, bf=bf, h=h: e.tensor_scalar(
                        out=bf["kD"][:, c, :], in0=psb(bk)[:, cc * 128:(cc + 1) * 128], scalar1=kdl[:, h, c:c + 1],
                        scalar2=None, op0=ALU.mult), r=[("ps", bk), "cf"], w=[("p1", ml, "kD")])
                bv = bank_m()
                for cc in range(4):
                    c = half * 4 + cc
                    pe(lambda e, bv=bv, cc=cc, c=c, bf=bf: e.transpose(
                        psb(bv)[:, cc * 128:(cc + 1) * 128], bf["vT"][:, c * 128:(c + 1) * 128], ident_b),
                       r=rv_res + ["cb"], w=[("ps", bv)])
                act(lambda e, bv=bv, half=half, bf=bf: e.copy(
                    out=bf["vtok"][:, half * 4:(half + 1) * 4, :],
                    in_=psb(bv)[:, 0:512].rearrange("p (a c) -> p a c", a=4)), r=[("ps", bv)], w=[("p1", ml, "vtok")])
            chk(22)
            bs = bank_m()
            for c in range(8):
                pe(lambda e, bs=bs, c=c, bf=bf: e.matmul(ps[bs][:, 0:128], lhsT=bf["kD"][:, c, :], rhs=bf["vtok"][:, c, :],
                                                        start=(c == 0), stop=(c == 7)),
                   r=[("p1", ml, "kD"), ("p1", ml, "vtok")], w=[("ps", bs)])
            dve(lambda e, bs=bs, bf=bf: e.tensor_copy(out=bf["sloc"], in_=ps[bs][:, 0:128]), r=[("ps", bs)],
                w=[("p1", ml, "sloc")])
            chk(23)
            dma("sp", "agi", ag_in.ap()[h * 128:(h + 1) * 128, :], bf["sloc"], r=[("p1", ml, "sloc")], w=["ag_in"])
            chk(24)

    dump("ag_in", ag_in.ap(), ["ag_in"])
    chk(2)
    P.fence()
    P.add("pool", lambda e: e.collective_compute("AllGather", ALU.bypass, replica_groups=groups or [[0, 1, 2, 3], [4, 5, 6, 7]],
                                                 ins=[ag_in.ap().opt()], outs=[ag_out.ap().opt()]),
          r=["ag_in"], w=["ag_out"], dma="cc")
    chk(3)

    Z = Alloc(HB, total)
    zf = [Z([NA], F32) for _ in range(2)]
    sq = [Z([NA], BF16) for _ in range(2)]
    rstdb = [Z([NA], F32) for _ in range(2)]
    knT = Z([2, NA], BF16)
    kn32 = Z([2, 132], F32)
    v32 = Z([2, 132], F32)
    vTb = Z([2, NA], BF16)
    vtokA = Z([2, 9, 128], BF16)
    qnT = Z([4, NT], BF16)
    PT = [Z([2, 512], BF16) for _ in range(2)]
    PTm = [Z([2, 512], BF16) for _ in range(2)]
    rec = [Z([512], F32) for _ in range(2)]
    win_t = Z([2, 128], F32)
    vstok = Z([2, 128], BF16)
    kc_b = [Z([128], BF16) for _ in range(NS)]
    vc_b = [Z([128], BF16) for _ in range(NS)]
    kcT = Z([4, 128], BF16)
    PTc = Z([16], BF16)
    Pn = Z([16], BF16)
    recs = Z([16], F32)
    cnt = {"qk": 0}

    def qknorm(b, c0, n, gname, outbf, tagres, out32=None):
        s = cnt["qk"] % 2
        cnt["qk"] += 1
        act(lambda e: e.activation(out=sq[s][:, 0:n], in_=ps[b][:, 0:n], func=AF.Square), r=[("ps", b)], w=[("sq", s)])
        dve(lambda e: e.tensor_copy(out=zf[s][:, 0:n], in_=ps[b][:, 0:n]), r=[("ps", b)], w=[("zf", s)])
        b2 = bank_m()
        pe(lambda e: e.matmul(ps[b2][:, 0:n], lhsT=cbv("ones128"), rhs=sq[s][:, 0:n], start=True, stop=True),
           r=[("sq", s), "cb"], w=[("ps", b2)])
        act(lambda e: e.activation(out=rstdb[s][:, 0:n], in_=ps[b2][:, 0:n], func=AF.Ln, bias=epsv, scale=1.0),
            r=[("ps", b2)], w=[("rstdb", s)])
        act(lambda e: e.activation(out=rstdb[s][:, 0:n], in_=rstdb[s][:, 0:n], func=AF.Exp, scale=-0.5),
            r=[("rstdb", s)], w=[("rstdb", s)])
        dve(lambda e: e.scalar_tensor_tensor(out=outbf, in0=zf[s][:, 0:n], scalar=cfv(gname), in1=rstdb[s][:, 0:n],
                                             op0=ALU.mult, op1=ALU.mult),
            r=[("zf", s), ("rstdb", s), "cf"], w=[tagres])
        if out32 is not None:
            lo, hi, dst = out32
            dve(lambda e: e.scalar_tensor_tensor(out=dst, in0=zf[s][:, lo:hi], scalar=cfv(gname),
                                                 in1=rstdb[s][:, lo:hi], op0=ALU.mult, op1=ALU.mult),
                r=[("zf", s), ("rstdb", s), "cf"], w=[("kn32", tagres)])

    def evac_ak(ml, ti, c0, n, b):
        o32 = None
        if ti == 2:
            o32 = (208, 340, kn32[:, ml, :])
        qknorm(b, c0, n, "kg", knT[:, ml, c0:c0 + n], ("knT", ml, c0), o32)

    def evac_av(ml, ti, c0, n, b):
        act(lambda e: e.copy(out=vTb[:, ml, c0:c0 + n], in_=ps[b][:, 0:n]), r=[("ps", b)], w=[("vTb", ml, c0)])
        if ti == 2:
            dve(lambda e: e.tensor_copy(out=v32[:, ml, :], in_=ps[b][:, 208:340]), r=[("ps", b)], w=[("v32", ml)])

    dense_block(16, aT_rhs, ["aT"], evac_ak, tiles=TT_H)
    dense_block(16, aT_rhs, ["aT"], evac_av, tiles=TT_H)
    vres = lambda g: [("vTb", g, c0) for (c0, n) in TT_H]
    kres = lambda g: [("knT", g, c0) for (c0, n) in TT_H]
    for g in range(2):
        for grp in range(3):
            blks = [0, 1, 2, 3] if grp == 0 else ([4, 5, 6, 7] if grp == 1 else [8])
            bv = bank_m()
            for ii, blk in enumerate(blks):
                col = NT if blk == 0 else (blk - 1) * 128
                pe(lambda e, bv=bv, ii=ii, col=col, g=g: e.transpose(
                    psb(bv)[:, ii * 128:(ii + 1) * 128], vTb[:, g, col:col + 128], ident_b),
                   r=vres(g) + ["cb"], w=[("ps", bv)])
            nb = len(blks)
            act(lambda e, bv=bv, g=g, blks=blks, nb=nb: e.copy(
                out=vtokA[:, g, blks[0]:blks[0] + nb, :],
                in_=psb(bv)[:, 0:nb * 128].rearrange("p (a c) -> p a c", a=nb)), r=[("ps", bv)], w=[("vtokA", g)])
        bv = bank_m()
        pe(lambda e, bv=bv, g=g: e.transpose(psb(bv)[0:NS, 0:128], vTb[:, g, NP_:NP_ + NS], ident_b),
           r=vres(g) + ["cb"], w=[("ps", bv)])
        act(lambda e, bv=bv, g=g: e.copy(out=vstok[0:NS, g, :], in_=psb(bv)[0:NS, 0:128]), r=[("ps", bv)],
            w=[("vstok", g)])
    for (src32, dst, nm) in ((kn32, kwin, "kw"), (v32, vwin, "vw")):
        for g in range(2):
            bw = bank_m()
            pe(lambda e, bw=bw, g=g, src32=src32: e.transpose(ps[bw][:, 0:128], src32[:, g, 0:128], ident_f),
               r=[("kn32", ("knT", g, 688)), ("v32", g), "cf"], w=[("ps", bw)])
            dve(lambda e, bw=bw, g=g: e.tensor_copy(out=win_t[:, g, :], in_=ps[bw][:, 0:128]), r=[("ps", bw)],
                w=[("win_t", g)])
        dma("sp", "o_" + nm, dst, win_t, r=[("win_t", 0), ("win_t", 1)], w=["out_" + nm])
    dma("sp", "o_ks", ks_out[:, 0:127, :, :], ck[:, 1:128, :, :], w=["ks_out_a"])
    dma("sp", "o_vs", vs_out[:, 0:127, :, :], cv[:, 1:128, :, :], w=["vs_out_a"])
    for g in range(2):
        for s in range(NS):
            dma("sp", "o_ks", ks_out[s, 127, g, :].rearrange("(d o) -> d o", o=1), kn32[:, g, 128 + s:129 + s],
                r=[("kn32", ("knT", g, 688))], w=[("ks_out_b", g, s)], slow=True)
            dma("sp", "o_vs", vs_out[s, 127, g, :].rearrange("(d o) -> d o", o=1), v32[:, g, 128 + s:129 + s],
                r=[("v32", g)], w=[("vs_out_b", g, s)], slow=True)

    mask = cbv("mask").rearrange("p (a c) -> p a c", a=3)
    v3 = lambda ap: ap[:, 0:1024].rearrange("p (a c) -> p a c", a=8)
    esr_f = v3(zf[0]); esr_d = v3(zf[1]); esr_hi = v3(sq[0]); esr_lo = v3(sq[1])
    esr2 = Z([8, 128], BF16)
    oh = cfv("onehot")
    dve(lambda e: e.tensor_copy(out=esr_f[0:2], in_=esink[0:2, :].unsqueeze(2).broadcast_to([2, 8, 128])),
        r=["esink"], w=[("zf", 0)])
    dve(lambda e: e.tensor_copy(out=esr_hi[0:2], in_=esr_f[0:2]), r=[("zf", 0)], w=[("sq", 0)])
    dve(lambda e: e.tensor_tensor(out=esr_d[0:2], in0=esr_f[0:2], in1=esr_hi[0:2], op=ALU.subtract),
        r=[("zf", 0), ("sq", 0)], w=[("zf", 1)])
    dve(lambda e: e.tensor_copy(out=esr_lo[0:2], in_=esr_d[0:2]), r=[("zf", 1)], w=[("sq", 1)])
    dve(lambda e: e.tensor_scalar(out=esr2[0:2], in0=esr_hi[0:2], scalar1=oh[0:2, 0:1], scalar2=None, op0=ALU.mult),
        r=[("sq", 0), "cf"], w=["esr2a"])
    dve(lambda e: e.scalar_tensor_tensor(out=esr2[0:2], in0=esr_lo[0:2], scalar=oh[0:2, 1:2], in1=esr2[0:2],
                                         op0=ALU.mult, op1=ALU.add), r=[("sq", 1), "esr2a", "cf"], w=["esr2"])
    for g in range(2):
        for jj in range(2):
            def evac_q(ml, ti, c0, n, b, jj=jj):
                hh = jj * 2 + ml
                qknorm(b, c0, n, "qg", qnT[:, hh, c0:c0 + n], ("qnT", hh, c0))
            dense_block(16, aT_rhs, ["aT"], evac_q)
        qres = [("qnT", hh, c0) for hh in range(4) for (c0, n) in TT]
        def att_S(blk, g=g):
            s = blk % 2
            q_rhs = qnT[:, :, blk * 128:(blk + 1) * 128]
            k_own = knT[:, g, blk * 128:(blk + 1) * 128]
            k_prev = knT[:, g, NT:NT + 128] if blk == 0 else knT[:, g, (blk - 1) * 128:blk * 128]
            bo = bank_m(); bp = bank_m()
            mprev = 2 if blk == 0 else 1
            for (bb_, kk_, mi) in ((bo, k_own, 0), (bp, k_prev, mprev)):
                pe(lambda e, bb_=bb_, kk_=kk_, q_rhs=q_rhs: e.matmul(ps[bb_][:, :], lhsT=kk_, rhs=q_rhs, start=True, stop=False),
                   r=qres + kres(g), w=[("ps", bb_)])
                pe(lambda e, bb_=bb_, mi=mi: e.matmul(ps[bb_][:, :], lhsT=ident_b,
                                                     rhs=mask[:, mi:mi + 1, :].broadcast_to([128, 4, 128]),
                                                     start=False, stop=True), r=["cb"], w=[("ps", bb_)])
            act(lambda e, bo=bo, s=s: e.activation(out=PTm[s][:, 0, :], in_=ps[bo][:, :], func=AF.Exp, bias=negc,
                                                   scale=ATTN_SCALE), r=[("ps", bo), "negc"], w=[("PTm", s, 0)])
            act(lambda e, bp=bp, s=s: e.activation(out=PTm[s][:, 1, :], in_=ps[bp][:, :], func=AF.Exp, bias=negc,
                                                   scale=ATTN_SCALE), r=[("ps", bp), "negc"], w=[("PTm", s, 1)])

        def att_PV(blk, g=g):
            s = blk % 2
            bO = bank_m(); bD = bank_m()
            for t, vb in ((0, blk + 1), (1, blk)):
                pe(lambda e, bO=bO, t=t, vb=vb, s=s, g=g: e.matmul(ps[bO][:, :], lhsT=vtokA[:, g, vb, :], rhs=PTm[s][:, t, :],
                                                                  start=(t == 0), stop=(t == 1)),
                   r=[("PTm", s, t), ("vtokA", g)], w=[("ps", bO)])
            for t in range(2):
                pe(lambda e, bD=bD, t=t, s=s: e.matmul(ps[bD][:, :], lhsT=cbv("ones"), rhs=PTm[s][:, t, :],
                                                      start=(t == 0), stop=False),
                   r=[("PTm", s, t), "cb"], w=[("ps", bD)])
            pe(lambda e, bD=bD, g=g: e.matmul(ps[bD][:, :], lhsT=cbv("ones")[0:2, :], rhs=esr2[0:2, 4 * g:4 * g + 4, :],
                                             start=False, stop=True), r=["esr2", "cb"], w=[("ps", bD)])
            if blk % 2 == 0:
                act(lambda e, bD=bD, s=s: e.activation(out=rec[s], in_=ps[bD][:, :], func=AF.Ln),
                    r=[("ps", bD)], w=[("rec", s)])
                act(lambda e, s=s: e.activation(out=rec[s], in_=rec[s], func=AF.Exp, scale=-1.0), r=[("rec", s)],
                    w=[("rec", s)])
            else:
                dve(lambda e, bD=bD, s=s: e.reciprocal(out=rec[s], in_=ps[bD][:, :]), r=[("ps", bD)], w=[("rec", s)])
            dve(lambda e, bO=bO, s=s, g=g, blk=blk: e.tensor_tensor(
                out=mixT[:, 4 * g:4 * g + 4, blk * 128:(blk + 1) * 128],
                in0=ps[bO][:, :].rearrange("p (a c) -> p a c", a=4),
                in1=rec[s][:, :].rearrange("p (a c) -> p a c", a=4), op=ALU.mult),
                r=[("ps", bO), ("rec", s)], w=["mixT"])

        def pe_warm(nmm):
            for _ in range(nmm):
                pe(lambda e: e.matmul(ps[0][:, :], lhsT=ident_b, rhs=mask[:, 0:1, :].broadcast_to([128, 4, 128]),
                                      start=True, stop=True), r=["cb"], w=[("ps", 0)])
        pe_warm(14)
        att_S(0)
        for blk in range(8):
            if blk + 1 < 8:
                att_S(blk + 1)
            pe_warm(3)
            att_PV(blk)
        for sm in range(NS):
            dma("pool", "kc%d" % sm, kc_b[sm], ck[sm, :, g, :], w=[("kc_b", sm)])
            dma("pool", "vc%d" % sm, vc_b[sm], cv[sm, :, g, :], w=[("vc_b", sm)])
        bt = bank_m()
        for sm in range(NS):
            pe(lambda e, bt=bt, sm=sm: e.transpose(psb(bt)[:, sm * 128:(sm + 1) * 128], kc_b[sm], ident_b),
               r=[("kc_b", sm), "cb"], w=[("ps", bt)])
        act(lambda e, bt=bt: e.copy(out=kcT, in_=psb(bt)[:, 0:512].rearrange("p (a c) -> p a c", a=4)),
            r=[("ps", bt)], w=["kcT"])
        bS = bank_m()
        for sm in range(NS):
            pe(lambda e, bS=bS, sm=sm: e.matmul(ps[bS][:, sm * 4:(sm + 1) * 4], lhsT=kcT[:, sm, :], rhs=qnT[:, :, NP_ + sm],
                                               start=True, stop=True, skip_group_check=True),
               r=["kcT"] + qres, w=[("ps", bS)])
        for sm in range(NS):
            pe(lambda e, bS=bS, sm=sm, g=g: e.matmul(ps[bS][0:NS, 32 + sm * 4:32 + (sm + 1) * 4],
                                                    lhsT=knT[:, g, NP_:NP_ + NS], rhs=qnT[:, :, NP_ + sm],
                                                    start=True, stop=True, skip_group_check=True),
               r=kres(g) + qres, w=[("ps", bS)])
        act(lambda e, bS=bS: e.activation(out=PTc, in_=ps[bS][:, 0:16], func=AF.Exp, bias=negc, scale=ATTN_SCALE),
            r=[("ps", bS), "negc"], w=["PTc"])
        act(lambda e, bS=bS: e.activation(out=Pn[0:NS, :], in_=ps[bS][0:NS, 32:48], func=AF.Exp,
                                          bias=negc[0:NS, :], scale=ATTN_SCALE), r=[("ps", bS), "negc"], w=["Pn"])
        dve(lambda e: e.tensor_tensor(out=Pn[0:NS, :].rearrange("p (s h) -> p s h", s=4),
                                      in0=Pn[0:NS, :].rearrange("p (s h) -> p s h", s=4),
                                      in1=oh[0:NS, 0:4].unsqueeze(2).broadcast_to([NS, 4, 4]), op=ALU.mult),
            r=["Pn", "cf"], w=["Pn"])
        bO = bank_m(); bD = bank_m()
        for sm in range(NS):
            pe(lambda e, bO=bO, sm=sm: e.matmul(ps[bO][:, sm * 4:(sm + 1) * 4], lhsT=vc_b[sm], rhs=PTc[:, sm * 4:(sm + 1) * 4],
                                               start=(sm == 0), stop=False, skip_group_check=True),
               r=[("vc_b", sm), "PTc"], w=[("ps", bO)])
            pe(lambda e, bO=bO, sm=sm, g=g: e.matmul(ps[bO][:, sm * 4:(sm + 1) * 4], lhsT=vstok[0:NS, g, :],
                                                    rhs=Pn[0:NS, sm * 4:(sm + 1) * 4], start=False, stop=True,
                                                    skip_group_check=True),
               r=[("vstok", g), "Pn"], w=[("ps", bO)])
        for sm in range(NS):
            pe(lambda e, bD=bD, sm=sm: e.matmul(ps[bD][:, sm * 4:(sm + 1) * 4], lhsT=cbv("ones"), rhs=PTc[:, sm * 4:(sm + 1) * 4],
                                               start=(sm == 0), stop=False, skip_group_check=True),
               r=["PTc", "cb"], w=[("ps", bD)])
            pe(lambda e, bD=bD, sm=sm: e.matmul(ps[bD][:, sm * 4:(sm + 1) * 4], lhsT=cbv("ones")[0:NS, :],
                                               rhs=Pn[0:NS, sm * 4:(sm + 1) * 4], start=False, stop=False,
                                               skip_group_check=True), r=["Pn", "cb"], w=[("ps", bD)])
        pe(lambda e, bD=bD, g=g: e.matmul(ps[bD][:, 0:16], lhsT=cbv("ones")[0:2, :],
                                         rhs=esr2[0:2, 4 * g:4 * g + 4, 0].unsqueeze(1).broadcast_to([2, 4, 4]),
                                         start=False, stop=True, skip_group_check=True),
           r=["esr2", "cb"], w=[("ps", bD)])
        act(lambda e, bD=bD: e.activation(out=recs, in_=ps[bD][:, 0:16], func=AF.Ln), r=[("ps", bD)], w=["recs"])
        act(lambda e: e.activation(out=recs, in_=recs, func=AF.Exp, scale=-1.0), r=["recs"], w=["recs"])
        dve(lambda e, bO=bO, g=g: e.tensor_tensor(
            out=mixT[:, 4 * g:4 * g + 4, NP_:NP_ + NS], in0=ps[bO][:, 0:16].rearrange("p (s h) -> p h s", s=4),
            in1=recs[:, :].rearrange("p (s h) -> p h s", s=4), op=ALU.mult),
            r=[("ps", bO), "recs"], w=["mixT"])
    P.fence()
    dump("mixA", mixT[:, 0:8, :], ["mixT"])
    chk(4)

    Z = Alloc(HB, total)
    agl = [Z([8, 128], F32) for _ in range(2)]
    coef = cfv("coef")
    ago = ag_out.ap()
    for r_ in range(4):
        s = r_ % 2
        dma("sp", "agl%d" % s, agl[s], ago[r_ * 1024:(r_ + 1) * 1024, :].rearrange("(h p) n -> p h n", p=128),
            r=["ag_out"], w=[("agl", s)])
        for h in range(8):
            if r_ == 0:
                dve(lambda e, s=s, h=h, r_=r_: e.tensor_scalar(out=S0[:, h, :], in0=agl[s][:, h, :],
                                                              scalar1=coef[:, r_ * 8 + h:r_ * 8 + h + 1], scalar2=None,
                                                              op0=ALU.mult), r=[("agl", s), "cf"], w=[("S0", h)])
            else:
                dve(lambda e, s=s, h=h, r_=r_: e.scalar_tensor_tensor(
                    out=S0[:, h, :], in0=agl[s][:, h, :], scalar=coef[:, r_ * 8 + h:r_ * 8 + h + 1], in1=S0[:, h, :],
                    op0=ALU.mult, op1=ALU.add), r=[("agl", s), "cf", ("S0", h)], w=[("S0", h)])

    dump("S0", S0, [("S0", h) for h in range(8)])
    chk(5)
    G1 = [dict(qr=Z([NT], BF16), kr=Z([NT], BF16), vT=Z([NT], BF16), sg=Z([NT], F32)) for _ in range(2)]
    RT = dict(xb=[Z([344], BF16) for _ in range(3)], t1=[Z([344], F32) for _ in range(3)],
              t2=[Z([344], F32) for _ in range(3)])
    R = dict(qd=Z([8, 128], BF16), kd=Z([8, 128], BF16), vtok=Z([8, 128], BF16), scm=Z([8, 128], BF16),
             Sb=Z([8, 128], BF16), Srun=Z([2, 128], F32), o_sb=Z([NT], F32), sq=Z([NT], BF16), rstd=Z([NT], F32),
             tt=Z([NT], F32), ktok_s=Z([128], BF16), vtok_s=Z([128], BF16))
    vm = [Z([128], BF16) for _ in range(NS)]
    Sst = [Z([128], F32) for _ in range(NS)]
    Snew = [Z([128], F32) for _ in range(NS)]
    Sbs = [Z([128], BF16) for _ in range(NS)]
    intraT = cfv("intraT").rearrange("p (h c) -> p h c", h=8)
    qdecv = cfv("qdec").rearrange("p (h c) -> p h c", h=8)
    kdecv = cfv("kdec")
    rgv = cfv("rg")
    tg = "p2"
    p2c = {"d": 0, "r": 0, "t": 0}

    def bank_d3():
        b = p2c["d"] % 3
        p2c["d"] += 1
        return b

    def bank_r():
        b = 3 + p2c["r"] % 2
        p2c["r"] += 1
        return b
    M0, M1, M2 = 5, 6, 7
    p2blocks = {}

    def rot_a(b, c0, n):
        sl = p2c["t"] % 3
        p2c["t"] += 1
        xb, t1 = RT["xb"][sl], RT["t1"][sl]
        act(lambda e: e.copy(out=xb[:, 0:n], in_=ps[b][:, 0:n]), r=[("ps", b)], w=[("rt_xb", sl)])
        dve(lambda e: e.tensor_tensor(out=t1[:, 0:n], in0=ps[b][:, 0:n], in1=cosv[:, c0:c0 + n], op=ALU.mult),
            r=[("ps", b), "cf"], w=[("rt_t1", sl)])
        return sl

    def rot_b(sl, c0, n, outb, outres):
        xb, t1, t2 = RT["xb"][sl], RT["t1"][sl], RT["t2"][sl]
        b2 = bank_r()
        pe(lambda e: e.matmul(ps[b2][:, 0:n], lhsT=cbv("rrot"), rhs=xb[:, 0:n], start=True, stop=True),
           r=[("rt_xb", sl), "cb"], w=[("ps", b2)])
        dve(lambda e: e.tensor_tensor(out=t2[:, 0:n], in0=ps[b2][:, 0:n], in1=sinv[:, c0:c0 + n], op=ALU.mult),
            r=[("ps", b2), "cf"], w=[("rt_t2", sl)])
        dve(lambda e: e.tensor_tensor(out=outb[:, c0:c0 + n], in0=t1[:, 0:n], in1=t2[:, 0:n], op=ALU.add),
            r=[("rt_t1", sl), ("rt_t2", sl)], w=[outres])

    def stageA(h):
        j, ml = h // 2, h % 2
        if ml == 0:
            p2blocks[j] = [w_next() for _ in range(2)]
        blocks = p2blocks[j]
        g1 = G1[h % 2]
        gp = h % 2
        dma("sp", "ldk%d" % gp, g1["kr"], krs.ap()[h], r=[("krs", h)], w=[(tg, "kr", gp, c0) for (c0, n) in TT])
        dma("sp", "ldv%d" % gp, g1["vT"], vts.ap()[h], r=[("vts", h)], w=[(tg, "vT", gp, c0) for (c0, n) in TT])
        for bi in range(2):
            wv, wres = blocks[bi]
            pending = None
            for ti, (c0, n) in enumerate(TT):
                b = bank_d3()
                for k in range(16):
                    pe(lambda e, b=b, k=k, c0=c0, n=n, wv=wv, ml=ml: e.matmul(
                        ps[b][:, 0:n], lhsT=wv[:, k, ml * 128:(ml + 1) * 128], rhs=aT[:, k, c0:c0 + n],
                        start=(k == 0), stop=(k == 15)), r=[wres, "aT"], w=[("ps", b)])
                if bi == 0:
                    sl = rot_a(b, c0, n)
                    if pending is not None:
                        rot_b(*pending)
                    pending = (sl, c0, n, g1["qr"], (tg, "qr", gp, c0))
                else:
                    act(lambda e, b=b, c0=c0, n=n, g1=g1: e.activation(out=g1["sg"][:, c0:c0 + n], in_=ps[b][:, 0:n],
                                                                       func=AF.Silu), r=[("ps", b)], w=[(tg, "sg", gp, c0)])
                if ti == 2:
                    if pending is not None:
                        rot_b(*pending)
                    if ml == 1:
                        w_done()
                yield

    def stageB(h):
        g1 = G1[h % 2]
        gp = h % 2
        q_res = [(tg, "qr", gp, c0) for (c0, n) in TT]
        k_res = [(tg, "kr", gp, c0) for (c0, n) in TT]
        v_res = [(tg, "vT", gp, c0) for (c0, n) in TT]
        g_res = [(tg, "sg", gp, c0) for (c0, n) in TT]
        dve(lambda e: e.tensor_tensor(out=R["qd"], in0=g1["qr"][:, 0:NP_].rearrange("p (a c) -> p a c", a=8),
                                      in1=qdecv[:, h:h + 1, :].broadcast_to([128, 8, 128]), op=ALU.mult),
            r=q_res + ["cf"], w=[(tg, "qd")])
        for half in range(2):
            for cc in range(4):
                c = half * 4 + cc
                pe(lambda e, cc=cc, c=c: e.transpose(psb(M0)[:, cc * 128:(cc + 1) * 128],
                                                    g1["kr"][:, c * 128:(c + 1) * 128], ident_b),
                   r=k_res + ["cb"], w=[("ps", M0)])
            dve(lambda e, half=half: e.tensor_scalar(
                out=R["kd"][:, half * 4:(half + 1) * 4, :], in0=psb(M0)[:, 0:512].rearrange("p (a c) -> p a c", a=4),
                scalar1=kdecv[:, h:h + 1], scalar2=None, op0=ALU.mult), r=[("ps", M0), "cf"], w=[(tg, "kd")])
            for cc in range(4):
                c = half * 4 + cc
                pe(lambda e, cc=cc, c=c: e.transpose(psb(M1)[:, cc * 128:(cc + 1) * 128],
                                                    g1["vT"][:, c * 128:(c + 1) * 128], ident_b),
                   r=v_res + ["cb"], w=[("ps", M1)])
            act(lambda e, half=half: e.copy(out=R["vtok"][:, half * 4:(half + 1) * 4, :],
                                            in_=psb(M1)[:, 0:512].rearrange("p (a c) -> p a c", a=4)),
                r=[("ps", M1)], w=[(tg, "vtok")])
        act(lambda e: e.copy(out=R["Sb"][:, 0, :], in_=S0[:, h, :]), r=[("S0", h)], w=[(tg, "Sb", 0)])
        yield
        g128 = float(GAMMA[h] ** 128)
        for half in range(2):
            for cc in range(4):
                c = half * 4 + cc
                pe(lambda e, cc=cc, c=c: e.matmul(ps[M2][:, cc * 128:(cc + 1) * 128], lhsT=R["kd"][:, c, :],
                                                 rhs=R["vtok"][:, c, :], start=True, stop=True),
                   r=[(tg, "kd"), (tg, "vtok")], w=[("ps", M2)])
            for cc in range(4):
                c = half * 4 + cc
                prev = S0[:, h, :] if c == 0 else R["Srun"][:, (c - 1) % 2, :]
                prev_res = ("S0", h) if c == 0 else (tg, "Srun", (c - 1) % 2)
                dve(lambda e, cc=cc, c=c, prev=prev: e.scalar_tensor_tensor(
                    out=R["Srun"][:, c % 2, :], in0=prev, scalar=g128, in1=ps[M2][:, cc * 128:(cc + 1) * 128],
                    op0=ALU.mult, op1=ALU.add), r=[prev_res, ("ps", M2)], w=[(tg, "Srun", c % 2)])
                if c < 7:
                    act(lambda e, c=c: e.copy(out=R["Sb"][:, c + 1, :], in_=R["Srun"][:, c % 2, :]),
                        r=[(tg, "Srun", c % 2)], w=[(tg, "Sb", c + 1)])
                else:
                    dma("sp", "o_rs", rstate[h], R["Srun"][:, c % 2, :], r=[(tg, "Srun", c % 2)], w=[("rstate", h)])
        yield
        for half, bsc in ((0, M0), (1, M1)):
            for cc in range(4):
                c = half * 4 + cc
                pe(lambda e, bsc=bsc, cc=cc, c=c: e.matmul(ps[bsc][:, cc * 128:(cc + 1) * 128],
                                                          lhsT=g1["kr"][:, c * 128:(c + 1) * 128],
                                                          rhs=g1["qr"][:, c * 128:(c + 1) * 128], start=True, stop=True),
                   r=k_res + q_res, w=[("ps", bsc)])
            dve(lambda e, bsc=bsc, half=half: e.tensor_tensor(
                out=R["scm"][:, half * 4:(half + 1) * 4, :], in0=ps[bsc][:, :].rearrange("p (a c) -> p a c", a=4),
                in1=intraT[:, h:h + 1, :].broadcast_to([128, 4, 128]), op=ALU.mult),
                r=[("ps", bsc), "cf"], w=[(tg, "scm", half)])
        yield
        obanks = [M0, M1]
        for half in range(2):
            bo = obanks[half]
            for cc in range(4):
                c = half * 4 + cc
                pe(lambda e, bo=bo, cc=cc, c=c: e.matmul(ps[bo][:, cc * 128:(cc + 1) * 128], lhsT=R["vtok"][:, c, :],
                                                        rhs=R["scm"][:, c, :], start=(cc == 0), stop=False,
                                                        skip_group_check=True),
                   r=[(tg, "vtok"), (tg, "scm", half)], w=[("ps", bo)])
        pe(lambda e: e.transpose(psb(M2)[0:NS, 0:128], g1["kr"][:, NP_:NT], ident_b), r=k_res + ["cb"], w=[("ps", M2)])
        pe(lambda e: e.transpose(psb(M2)[0:NS, 128:256], g1["vT"][:, NP_:NT], ident_b), r=v_res + ["cb"], w=[("ps", M2)])
        act(lambda e: e.mul(out=R["ktok_s"][0:NS, :], in_=psb(M2)[0:NS, 0:128], mul=RET_K_SCALE),
            r=[("ps", M2)], w=[(tg, "ktok_s")])
        act(lambda e: e.copy(out=R["vtok_s"][0:NS, :], in_=psb(M2)[0:NS, 128:256]), r=[("ps", M2)], w=[(tg, "vtok_s")])
        for sm in range(NS):
            dma("sp", "st%d" % sm, Sst[sm], st[sm, h], w=[("Sst", sm)])
            dve(lambda e, sm=sm: e.tensor_scalar(out=vm[sm][0:NS, :], in0=R["vtok_s"][0:NS, :],
                                                 scalar1=cfv("onehot")[0:NS, sm:sm + 1], scalar2=None, op0=ALU.mult),
                r=[(tg, "vtok_s"), "cf"], w=[("vm", sm)])
        yield
        for half in range(2):
            bo = obanks[half]
            for cc in range(4):
                c = half * 4 + cc
                pe(lambda e, bo=bo, cc=cc, c=c: e.matmul(ps[bo][:, cc * 128:(cc + 1) * 128], lhsT=R["Sb"][:, c, :],
                                                        rhs=R["qd"][:, c, :], start=False, stop=True,
                                                        skip_group_check=True),
                   r=[(tg, "Sb", c), (tg, "qd")], w=[("ps", bo)])

        def o_read(bo, c0, n):
            act(lambda e: e.activation(out=R["sq"][:, c0:c0 + n], in_=ps[bo][:, 0:n], func=AF.Square),
                r=[("ps", bo)], w=[(tg, "sq", c0)])
            dve(lambda e: e.tensor_copy(out=R["o_sb"][:, c0:c0 + n], in_=ps[bo][:, 0:n]),
                r=[("ps", bo)], w=[(tg, "o_sb", c0)])
        o_read(M0, 0, 512)
        o_read(M1, 512, 512)
        yield
        for sm in range(NS):
            bu = M0 if sm % 2 == 0 else M1
            pe(lambda e, bu=bu, sm=sm: e.matmul(ps[bu][:, 0:128], lhsT=R["ktok_s"][0:NS, :], rhs=vm[sm][0:NS, :],
                                               start=True, stop=True), r=[(tg, "ktok_s"), ("vm", sm)], w=[("ps", bu)])
            dve(lambda e, bu=bu, sm=sm: e.scalar_tensor_tensor(out=Snew[sm], in0=Sst[sm], scalar=float(GAMMA[h]),
                                                               in1=ps[bu][:, 0:128], op0=ALU.mult, op1=ALU.add),
                r=[("Sst", sm), ("ps", bu)], w=[("Snew", sm)])
            dma("sp", "o_ss%d" % sm, ss_out[sm, h], Snew[sm], r=[("Snew", sm)], w=[("ss_out", sm, h)])
            act(lambda e, sm=sm: e.copy(out=Sbs[sm], in_=Snew[sm]), r=[("Snew", sm)], w=[("Sbs", sm)])
        yield
        for sm in range(NS):
            pe(lambda e, sm=sm: e.matmul(ps[M2][:, sm:sm + 1], lhsT=Sbs[sm], rhs=g1["qr"][:, NP_ + sm:NP_ + sm + 1],
                                        start=True, stop=True), r=[("Sbs", sm)] + q_res, w=[("ps", M2)])
        o_read(M2, NP_, NS)
        yield
        for pi, (c0, n) in enumerate(((0, 512), (512, 512), (NP_, NS))):
            bm_ = M0 if pi % 2 == 0 else M1
            pe(lambda e, bm_=bm_, c0=c0, n=n: e.matmul(ps[bm_][:, 0:n], lhsT=cbv("ones128"), rhs=R["sq"][:, c0:c0 + n],
                                                      start=True, stop=True), r=[(tg, "sq", c0), "cb"], w=[("ps", bm_)])
            act(lambda e, bm_=bm_, c0=c0, n=n: e.activation(out=R["rstd"][:, c0:c0 + n], in_=ps[bm_][:, 0:n], func=AF.Ln,
                                                            bias=epsv, scale=1.0), r=[("ps", bm_)], w=[(tg, "rstd", c0)])
            act(lambda e, c0=c0, n=n: e.activation(out=R["rstd"][:, c0:c0 + n], in_=R["rstd"][:, c0:c0 + n], func=AF.Exp,
                                                   scale=-0.5), r=[(tg, "rstd", c0)], w=[(tg, "rstd", c0)])
            dve(lambda e, c0=c0, n=n: e.scalar_tensor_tensor(
                out=R["tt"][:, c0:c0 + n], in0=R["o_sb"][:, c0:c0 + n], scalar=rgv[:, h:h + 1],
                in1=R["rstd"][:, c0:c0 + n], op0=ALU.mult, op1=ALU.mult),
                r=[(tg, "o_sb", c0), (tg, "rstd", c0), "cf"], w=[(tg, "tt", c0)])
            pool(lambda e, c0=c0, n=n: e.tensor_tensor(out=mixT[:, 8 + h, c0:c0 + n], in0=R["tt"][:, c0:c0 + n],
                                                       in1=g1["sg"][:, c0:c0 + n], op=ALU.mult),
                 r=[(tg, "tt", c0)] + g_res, w=["mixT"])
        yield

    def drive(gens):
        gens = [g for g in gens if g is not None]
        while gens:
            for g in list(gens):
                try:
                    next(g)
                except StopIteration:
                    gens.remove(g)

    def drive2(gb, ga):
        pat = [1, 1, 1, 0, 1, 0, 1, 1, 9, 9]
        bi = 0
        b_alive, a_alive = True, ga is not None
        while b_alive or a_alive:
            if b_alive:
                try:
                    next(gb)
                except StopIteration:
                    b_alive = False
            na = pat[min(bi, len(pat) - 1)] if b_alive else 99
            bi += 1
            for _ in range(na):
                if not a_alive:
                    break
                try:
                    next(ga)
                except StopIteration:
                    a_alive = False

    drive([stageA(0)])
    for h in range(8):
        drive2(stageB(h), stageA(h + 1) if h + 1 < 8 else None)
    P.fence()
    dump("mixT", mixT, ["mixT"])
    chk(6)

    Zt = Alloc(T2base, T2base + 12 * 1024)
    Zx = Alloc(0, 0)
    xs_off = None
    xstage = [aT[:, 0:8, :].rearrange("p a c -> p (a c)").bitcast(F32)[:, 0:D],
              aT[:, 8:16, :].rearrange("p a c -> p (a c)").bitcast(F32)[:, 0:D]]
    tiles_x = [(x_main[i * 128:(i + 1) * 128, :], 128, i * 128) for i in range(8)] + [(x_smp, NS, NP_)]
    for i, (src, rows, c0) in enumerate(tiles_x):
        s = i % 2
        dma("sp", "x%d" % s, xstage[s][0:rows, :], src, w=[("xstage", s)])
        for q4 in range(4):
            b = bank_m()
            for kk in range(4):
                k = q4 * 4 + kk
                pe(lambda e, b=b, kk=kk, k=k, s=s, rows=rows: e.transpose(
                    ps[b][:, kk * 128:kk * 128 + rows], xstage[s][0:rows, k * 128:(k + 1) * 128],
                    ident_f[0:rows, 0:rows]), r=[("xstage", s), "cf"], w=[("ps", b)])
            src_ps = lambda b=b, rows=rows: ps[b][:, :].rearrange("p (a c) -> p a c", a=4)[:, :, 0:rows]
            dst = hT[:, q4 * 4:(q4 + 1) * 4, c0:c0 + rows]
            if q4 % 2 == 0:
                act(lambda e, dst=dst, src_ps=src_ps: e.copy(out=dst, in_=src_ps()), r=[("ps", b)], w=[("hT", q4)])
            else:
                dve(lambda e, dst=dst, src_ps=src_ps: e.tensor_copy(out=dst, in_=src_ps()), r=[("ps", b)], w=[("hT", q4)])
    P.fence()

    sqs = [Zt([344], BF16) for _ in range(4)]
    rstdn = Zt([NT], F32)
    fT = aT[:, :, 0:NT]

    class NormState:
        pass

    def norm_begin(tag):
        st_ = NormState()
        st_.tag = tag
        st_.banks = [bank_m() for _ in TT]
        st_.pending = None
        st_.cnt = 0
        return st_

    def norm_flush(st_):
        if st_.pending is not None:
            slot, m, ti, n = st_.pending
            pe(lambda e: e.matmul(ps[st_.banks[ti]][:, 0:n], lhsT=cbv("onesD"), rhs=sqs[slot][:, 0:n],
                                  start=(m == 0), stop=(m == 15)),
               r=[("sqs", slot), "cb"], w=[("ps", st_.banks[ti])])
            st_.pending = None

    def norm_tile(st_, m, ti, c0, n):
        norm_flush(st_)
        slot = st_.cnt % 4
        st_.cnt += 1
        act(lambda e: e.activation(out=sqs[slot][:, 0:n], in_=hT[:, m, c0:c0 + n], func=AF.Square),
            r=[("hTm", m, c0)], w=[("sqs", slot)])
        st_.pending = (slot, m, ti, n)

    def norm_finish(st_, gname):
        tag = st_.tag
        norm_flush(st_)
        for ti, (c0, n) in enumerate(TT):
            act(lambda e, ti=ti, c0=c0, n=n: e.activation(out=rstdn[:, c0:c0 + n], in_=ps[st_.banks[ti]][:, 0:n], func=AF.Ln,
                                                          bias=epsv, scale=1.0),
                r=[("ps", st_.banks[ti])], w=[(tag, "rstdn0", ti)])
            act(lambda e, ti=ti, c0=c0, n=n: e.activation(out=rstdn[:, c0:c0 + n], in_=rstdn[:, c0:c0 + n], func=AF.Exp,
                                                          scale=-0.5),
                r=[(tag, "rstdn0", ti)], w=[(tag, "rstdn")])
        gv = cfv(gname)
        for k in range(16):
            dve(lambda e, k=k: e.scalar_tensor_tensor(out=fT[:, k, :], in0=hT[:, k, :], scalar=gv[:, k:k + 1], in1=rstdn,
                                                      op0=ALU.mult, op1=ALU.mult),
                r=[(tag, "rstdn"), "cf"] + [("hTm", k, c0) for (c0, n) in TT], w=[("fT", k)])

    mix_rhs = lambda k, c0, n: mixT[:, k, c0:c0 + n]
    n2 = norm_begin("n2")
    for j in range(8):
        def evac_out(ml, ti, c0, n, b, j=j):
            m = 2 * j + ml
            dve(lambda e: e.tensor_tensor(out=hT[:, m, c0:c0 + n], in0=ps[b][:, 0:n], in1=hT[:, m, c0:c0 + n], op=ALU.add),
                r=[("ps", b), ("hTm", m, c0)], w=[("hTm", m, c0)])
            norm_tile(n2, m, ti, c0, n)
        dense_block(16, mix_rhs, ["mixT"], evac_out)
    P.fence()
    dump("h1", hT, [])
    chk(7)

    norm_finish(n2, "fn_g")
    P.fence()
    dump("fT", fT, [])
    chk(8)

    uT = mixT
    sgt = [Zt([344], F32) for _ in range(2)]
    f_rhs = lambda k, c0, n: fT[:, k, c0:c0 + n]
    cnt_s = {"i": 0}
    for qi, (b0, nb) in enumerate(QUART):
        for bb in range(nb):
            wg, wgres = w_next()
            wu, wures = w_next()
            for ml in range(2):
                cl = 2 * bb + ml
                for ti, (c0, n) in enumerate(TT):
                    bg = bank_d(); bu = bank_d()
                    for (bk_, wv, wres) in ((bg, wg, wgres), (bu, wu, wures)):
                        for k in range(16):
                            pe(lambda e, bk_=bk_, wv=wv, k=k, ml=ml, c0=c0, n=n: e.matmul(
                                ps[bk_][:, 0:n], lhsT=wv[:, k, ml * 128:(ml + 1) * 128], rhs=fT[:, k, c0:c0 + n],
                                start=(k == 0), stop=(k == 15)), r=[wres], w=[("ps", bk_)])
                    s = cnt_s["i"] % 2
                    cnt_s["i"] += 1
                    act(lambda e, bg=bg, s=s, n=n: e.activation(out=sgt[s][:, 0:n], in_=ps[bg][:, 0:n], func=AF.Silu),
                        r=[("ps", bg)], w=[("sgt", s)])
                    dve(lambda e, bu=bu, s=s, n=n, cl=cl, c0=c0: e.tensor_tensor(out=uT[:, cl, c0:c0 + n], in0=ps[bu][:, 0:n],
                                                                               in1=sgt[s][:, 0:n], op=ALU.mult),
                        r=[("ps", bu), ("sgt", s)], w=[("uT", qi)])
            w_done(2)
        kq = 2 * nb
        u_rhs = lambda k, c0, n: uT[:, k, c0:c0 + n]
        if qi == 3:
            n3 = norm_begin("n3")
        for j in range(8):
            def evac_dn(ml, ti, c0, n, b, j=j):
                m = 2 * j + ml
                dve(lambda e: e.tensor_tensor(out=hT[:, m, c0:c0 + n], in0=ps[b][:, 0:n], in1=hT[:, m, c0:c0 + n],
                                              op=ALU.add), r=[("ps", b), ("hTm", m, c0)], w=[("hTm", m, c0)])
                if qi == 3:
                    norm_tile(n3, m, ti, c0, n)
            dense_block(kq, u_rhs, [("uT", qi)], evac_dn)
    P.fence()
    dump("h2", hT, [])
    chk(9)

    norm_finish(n3, "pn_g")
    pst = [uT[:, 0:1, :].rearrange("p a c -> p (a c)").bitcast(F32)[:, 0:256],
           uT[:, 1:2, :].rearrange("p a c -> p (a c)").bitcast(F32)[:, 0:256]]
    peT = uT[:, 4:6, :]
    tiles_p = [(p_main[i * 128:(i + 1) * 128, :], 128, i * 128) for i in range(8)] + [(p_smp, NS, NP_)]
    for i, (src, rows, c0) in enumerate(tiles_p):
        s = i % 2
        dma("sp", "x%d" % s, pst[s][0:rows, :], src, w=[("pst", s)])
        b = bank_m()
        for kk in range(2):
            pe(lambda e, b=b, kk=kk, s=s, rows=rows: e.transpose(ps[b][:, kk * 128:kk * 128 + rows],
                                                                pst[s][0:rows, kk * 128:(kk + 1) * 128],
                                                                ident_f[0:rows, 0:rows]), r=[("pst", s), "cf"], w=[("ps", b)])
        act(lambda e, b=b, rows=rows, c0=c0: e.copy(out=peT[:, :, c0:c0 + rows],
                                                   in_=ps[b][:, 0:256].rearrange("p (a c) -> p a c", a=2)[:, :, 0:rows]),
            r=[("ps", b)], w=["peT"])
    P.fence()
    wple = uT[:, 8:12, :].rearrange("p a c -> p (a c)")[:, 0:4096].rearrange("p (k n) -> p k n", k=2)
    wpleres = "wple"
    dma("pool", "wple", wple, w_ple.rearrange("(k p) n -> p k n", p=128), w=["wple"])
    sB = [uT[:, 12 + i, :].bitcast(F32)[:, 0:344] for i in range(2)]
    tA = [uT[:, 14 + i, :].bitcast(F32)[:, 0:344] for i in range(2)]
    for j in range(8):
        wv, wres = w_next()
        for ml in range(2):
            m = 2 * j + ml
            for ti, (c0, n) in enumerate(TT):
                bA = 2 * (cnt_s["i"] % 4); bB = bA + 1
                for k in range(2):
                    pe(lambda e, bA=bA, k=k, m=m, c0=c0, n=n: e.matmul(ps[bA][:, 0:n], lhsT=wple[:, k, m * 128:(m + 1) * 128],
                                                                      rhs=peT[:, k, c0:c0 + n], start=(k == 0), stop=(k == 1)),
                       r=[wpleres, "peT"], w=[("ps", bA)])
                for k in range(16):
                    pe(lambda e, bB=bB, k=k, ml=ml, c0=c0, n=n, wv=wv: e.matmul(
                        ps[bB][:, 0:n], lhsT=wv[:, k, ml * 128:(ml + 1) * 128], rhs=fT[:, k, c0:c0 + n],
                        start=(k == 0), stop=(k == 15)), r=[wres], w=[("ps", bB)])
                s = cnt_s["i"] % 2
                cnt_s["i"] += 1
                act(lambda e, bB=bB, s=s, n=n: e.activation(out=sB[s][:, 0:n], in_=ps[bB][:, 0:n], func=AF.Sigmoid),
                    r=[("ps", bB)], w=[("sB", s)])
                dve(lambda e, bA=bA, s=s, n=n: e.tensor_tensor(out=tA[s][:, 0:n], in0=ps[bA][:, 0:n], in1=sB[s][:, 0:n],
                                                              op=ALU.mult), r=[("ps", bA), ("sB", s)], w=[("tA", s)])
                dve(lambda e, s=s, n=n, m=m, c0=c0: e.tensor_tensor(out=hT[:, m, c0:c0 + n], in0=tA[s][:, 0:n],
                                                                   in1=hT[:, m, c0:c0 + n], op=ALU.add),
                    r=[("tA", s), ("hTm", m, c0)], w=[("hTm", m, c0)])
        w_done()
    P.fence()
    dump("h3", hT, [])
    chk(10)

    ystage = [aT[:, 0:8, :].rearrange("p a c -> p (a c)").bitcast(F32)[:, 0:D],
              aT[:, 8:16, :].rearrange("p a c -> p (a c)").bitcast(F32)[:, 0:D]]
    tiles_y = [(y_main[i * 128:(i + 1) * 128, :], 128, i * 128) for i in range(8)] + [(y_smp, NS, NP_)]
    for i, (dst, rows, c0) in enumerate(tiles_y):
        s = i % 2
        for q4 in range(4):
            b = bank_m()
            for kk in range(4):
                k = q4 * 4 + kk
                pe(lambda e, b=b, kk=kk, k=k, rows=rows, c0=c0: e.transpose(ps[b][0:rows, kk * 128:(kk + 1) * 128],
                                                                           hT[:, k, c0:c0 + rows], ident_f),
                   r=["cf"], w=[("ps", b)])
            if q4 % 2 == 0:
                act(lambda e, b=b, s=s, rows=rows, q4=q4: e.copy(out=ystage[s][0:rows, q4 * 512:(q4 + 1) * 512],
                                                                in_=ps[b][0:rows, :]), r=[("ps", b)], w=[("ystage", s, q4)])
            else:
                dve(lambda e, b=b, s=s, rows=rows, q4=q4: e.tensor_copy(out=ystage[s][0:rows, q4 * 512:(q4 + 1) * 512],
                                                                       in_=ps[b][0:rows, :]), r=[("ps", b)],
                    w=[("ystage", s, q4)])
        dma("sp", "y%d" % s, dst, ystage[s][0:rows, :], r=[("ystage", s, q4) for q4 in range(4)], w=[("y", i)])

    assert stop != 99 or wstate["used"] == len(wq), (wstate, len(wq))

    sems = {e: es.enter_context(nc.semaphore("s_" + e)) for e in Prog.ENGS}
    dma_sems = {k: es.enter_context(nc.semaphore("d_" + k)) for k in sorted(dma_sem_names)}
    block = es.enter_context(nc.Block())
    finals = sorted(dma_sem_names)
    P.emit(nc, block, sems, dma_sems, finals)
    es.close()
    print("ops", len(P.ops), "counts", P.final_counts[0])
    return nc


_CACHE = {}


def kernel(**inputs):
    inp = {k: np.asarray(v) for k, v in inputs.items()}
    if "nc" not in _CACHE:
        _CACHE["nc"] = build_program()
    nc = _CACHE["nc"]
    xp = inp["x_prompt"]; xs = inp["x_sample"]
    in_maps = []
    zeros_halo = np.zeros((NHALO, D), np.float32)
    shared = dict(
        an_g=np.ascontiguousarray(inp["attn_norm_g"].reshape(1, D)),
        w_in=np.ascontiguousarray(inp["w_in"][0]), w_out=np.ascontiguousarray(inp["w_out"][0]),
        w_gate=np.ascontiguousarray(inp["w_gate"][0]), w_up=np.ascontiguousarray(inp["w_up"][0]),
        w_down=np.ascontiguousarray(inp["w_down"][0]), w_ple=np.ascontiguousarray(inp["w_ple"][0]),
        w_pg=np.ascontiguousarray(inp["w_ple_gate"][0]),
    )
    for c in range(8):
        b, m = c // 4, c % 4
        t0 = m * 1024
        cf, cb = host_consts(c, inp)
        d = dict(shared)
        d.update(
            x_main=np.ascontiguousarray(xp[b, t0:t0 + 1024]),
            x_halo=np.ascontiguousarray(xp[b, t0 - 128:t0]) if m > 0 else zeros_halo,
            x_smp=np.ascontiguousarray(xs[4 * c:4 * c + 4, 0]),
            p_main=np.ascontiguousarray(inp["p_prompt"][0, b, t0:t0 + 1024]),
            p_smp=np.ascontiguousarray(inp["p_sample"][0, 4 * c:4 * c + 4, 0]),
            ck=np.ascontiguousarray(inp["cache_k_win"][0, 4 * c:4 * c + 4]),
            cv=np.ascontiguousarray(inp["cache_v_win"][0, 4 * c:4 * c + 4]),
            st=np.ascontiguousarray(inp["state_ret"][0, 4 * c:4 * c + 4]),
            cf=cf, cb=cb,
        )
        in_maps.append(d)
    res = run_bass_kernel_spmd(nc, in_maps, core_ids=list(range(8)))
    R = res.results
    _CACHE["last"] = R
    y_p = np.stack([np.concatenate([R[4 * b + m]["y_main"] for m in range(4)], 0) for b in range(2)], 0)
    y_s = np.concatenate([R[c]["y_smp"] for c in range(8)], 0)[:, None, :]
    kwp = np.stack([R[4 * b + 3]["kwin"] for b in range(2)], 0)[None]
    vwp = np.stack([R[4 * b + 3]["vwin"] for b in range(2)], 0)[None]
    rsp = np.stack([R[4 * b + 3]["rstate"] for b in range(2)], 0)[None]
    kws = np.concatenate([R[c]["ks_out"] for c in range(8)], 0)[None]
    vws = np.concatenate([R[c]["vs_out"] for c in range(8)], 0)[None]
    rss = np.concatenate([R[c]["ss_out"] for c in range(8)], 0)[None]
    return (y_p.astype(np.float32), y_s.astype(np.float32), kwp.astype(np.float32), vwp.astype(np.float32),
            rsp.astype(np.float32), kws.astype(np.float32), vws.astype(np.float32), rss.astype(np.float32))
```

```python
import math
from contextlib import ExitStack

import numpy as np
import ml_dtypes

import concourse.bass as bass
import concourse.mybir as mybir
from concourse.bass_utils import run_bass_kernel_spmd

F32 = mybir.dt.float32
BF16 = mybir.dt.bfloat16
U8 = mybir.dt.uint8
AF = mybir.ActivationFunctionType
ALU = mybir.AluOpType
AX = mybir.AxisListType

D = 2048
NP_ = 1024
NS = 4
NT = NP_ + NS
NHALO = 128
NA = NT + NHALO
KC = 16
DFF = 5632
EPS = 1e-6
ATTN_SCALE = 128 ** -0.5
RET_K_SCALE = 128 ** -0.5
PAST_LEN = 16384
TT = [(0, 344), (344, 344), (688, 340)]
TT_H = TT + [(NT, NHALO)]
NSLOT = 4
SLOT_BYTES = 8192
GAMMA = [1.0 - 2.0 ** (-5 - h) for h in range(8)]


class Op:
    __slots__ = ("eng", "fn", "dma", "idx", "waits", "sig", "val", "fence")

    def __init__(self, eng, fn, dma, idx):
        self.eng = eng
        self.fn = fn
        self.dma = dma
        self.idx = idx
        self.waits = []
        self.sig = dma is not None
        self.val = None
        self.fence = None


class Prog:
    ENGS = ("sp", "act", "dve", "pool", "pe")

    def __init__(self):
        self.ops = []
        self.last_w = {}
        self.readers = {}
        self.last_on = {}
        self.dma_ops = {}
        self.pending_fence = {}

    def add(self, eng, fn, r=(), w=(), dma=None, nofence=False):
        op = Op(eng, fn, dma, len(self.ops))
        psr = [x for x in r if isinstance(x, tuple) and x[0] == "ps"]
        if psr:
            r = [x for x in r if not (isinstance(x, tuple) and x[0] == "ps")]
            w = list(w) + psr
        deps = {}
        for res in r:
            lw = self.last_w.get(res)
            if lw is not None:
                deps.setdefault(lw, set()).add("raw")
        for res in w:
            lw = self.last_w.get(res)
            if lw is not None:
                deps.setdefault(lw, set()).add("waw")
            for rd in self.readers.get(res, ()):
                deps.setdefault(rd, set()).add("war")
        best = {}
        for d, kinds in deps.items():
            if d is op:
                continue
            if d.dma is not None:
                op.waits.append(d)
                continue
            if op.dma is None and d.eng == eng:
                if eng == "pe":
                    continue
            b = best.get(d.eng)
            if b is None or d.idx > b.idx:
                best[d.eng] = d
        for d in best.values():
            d.sig = True
            op.waits.append(d)
        for res in r:
            self.readers.setdefault(res, []).append(op)
        for res in w:
            self.last_w[res] = op
            self.readers[res] = []
        if eng in self.pending_fence and not nofence:
            op.fence = self.pending_fence.pop(eng)
        self.ops.append(op)
        if dma is None:
            self.last_on[eng] = op
        else:
            self.dma_ops.setdefault(dma, []).append(op)
        return op

    def fence(self):
        st = {"comp": dict(self.last_on), "dma": {k: v[-1] for k, v in self.dma_ops.items()}}
        for o in st["comp"].values():
            o.sig = True
        for e in self.ENGS:
            self.pending_fence[e] = st

    def emit(self, nc, block, sems, dma_sems, final_waits):
        cnt = {e: 0 for e in self.ENGS}
        dcnt = {}
        for op in self.ops:
            if op.dma is not None:
                dcnt[op.dma] = dcnt.get(op.dma, 0) + (1 if op.dma == "cc" else 16)
                op.val = dcnt[op.dma]
            elif op.sig:
                cnt[op.eng] += 1
                op.val = cnt[op.eng]
        self.final_counts = (cnt, dcnt)

        def semof(op):
            return dma_sems[op.dma] if op.dma is not None else sems[op.eng]

        def run(engname):
            def body(eng):
                waited = {}

                def wait(sem_key, sem, val):
                    if waited.get(sem_key, 0) >= val:
                        return
                    waited[sem_key] = val
                    eng.wait_ge(sem, val)

                for op in self.ops:
                    if op.eng != engname:
                        continue
                    if op.fence is not None:
                        for o in op.fence["comp"].values():
                            if o.eng != engname or engname != "pe":
                                wait(("c", o.eng), sems[o.eng], o.val)
                        for k, o in op.fence["dma"].items():
                            wait(("d", k), dma_sems[k], o.val)
                    for d in op.waits:
                        key = ("d", d.dma) if d.dma is not None else ("c", d.eng)
                        wait(key, semof(d), d.val)
                    ins = op.fn(eng)
                    if op.dma is not None:
                        ins.then_inc(dma_sems[op.dma], 1 if op.dma == "cc" else 16)
                    elif op.sig:
                        ins.then_inc(sems[op.eng], 1)
                if engname == "sp":
                    for k in final_waits:
                        if k in dcnt:
                            eng.wait_ge(dma_sems[k], dcnt[k])
            return body

        block.sync(run("sp"))
        block.scalar(run("act"))
        block.vector(run("dve"))
        block.gpsimd(run("pool"))
        block.tensor(run("pe"))


CF = {}
_off = 0
for _n, _w in [("ident", 128), ("ones_row", 128), ("fn_g", 16), ("pn_g", 16), ("rg", 8), ("qg", 1), ("kg", 1),
               ("sinks", 8), ("intraT", 1024), ("qdec", 1024), ("kdec", 8), ("kdlong", 64), ("coef", 32),
               ("onehot", 4), ("cos", NT), ("sin", NT)]:
    CF[_n] = (_off, _w)
    _off += _w
NCF = _off
CB = {}
_off = 0
for _n, _w in [("ident", 128), ("ones", 128), ("onesD", 128), ("ones128", 128), ("rrot", 128), ("mask", 384)]:
    CB[_n] = (_off, _w)
    _off += _w
NCB = _off


def host_consts(core, inp):
    m = core % 4
    cf = np.zeros((128, NCF), np.float32)

    def put(name, arr):
        o, w = CF[name]
        cf[:, o:o + w] = np.asarray(arr, np.float32).reshape(128, w) if np.ndim(arr) == 2 else np.broadcast_to(
            np.asarray(arr, np.float32).reshape(1, w), (128, w))

    put("ident", np.eye(128, dtype=np.float32))
    put("ones_row", np.ones((128, 128), np.float32))
    put("fn_g", inp["ffn_norm_g"][0].reshape(16, 128).T)
    put("pn_g", inp["ple_norm_g"][0].reshape(16, 128).T)
    put("rg", inp["ret_out_g"][0].reshape(8, 128).T)
    put("qg", inp["q_norm_g"][0].reshape(128, 1))
    put("kg", inp["k_norm_g"][0].reshape(128, 1))
    put("sinks", inp["attn_sinks"][0].reshape(8))
    g = np.array(GAMMA, np.float64)
    j = np.arange(128)
    diff = j[None, :] - j[:, None]
    intraT = np.where(diff[:, None, :] >= 0, g[None, :, None] ** np.maximum(diff, 0)[:, None, :], 0.0) * RET_K_SCALE
    put("intraT", intraT.reshape(128, 1024))
    qdec = g[:, None] ** (j[None, :] + 1.0)
    put("qdec", qdec.reshape(1024))
    put("kdec", (g[None, :] ** (127.0 - j[:, None])) * RET_K_SCALE)
    c = np.arange(8)
    kdl = g[None, :, None] ** (1023.0 - (128.0 * c[None, None, :] + j[:, None, None])) * RET_K_SCALE
    put("kdlong", kdl.reshape(128, 64))
    coef = np.zeros((4, 8))
    for r in range(4):
        if r < m:
            coef[r] = g ** (1024.0 * (m - r - 1))
    put("coef", coef.reshape(32))
    oh = np.zeros((128, 4), np.float32)
    oh[:4] = np.eye(4)
    put("onehot", oh)
    pos = np.concatenate([m * 1024 + np.arange(1024), np.full(4, PAST_LEN)]).astype(np.float32)
    inv = (np.float32(10000.0) ** (-np.arange(64, dtype=np.float32) / np.float32(64))).astype(np.float32)
    ang = (pos[None, :] * inv[:, None]).astype(np.float32).astype(np.float64)
    cos = np.cos(ang)
    sin = np.sin(ang)
    put("cos", np.concatenate([cos, cos], 0))
    put("sin", np.concatenate([-sin, sin], 0))

    cb = np.zeros((128, NCB), np.float32)

    def putb(name, arr):
        o, w = CB[name]
        cb[:, o:o + w] = arr

    putb("ident", np.eye(128))
    putb("ones", np.ones((128, 128)))
    putb("onesD", np.full((128, 128), 1.0 / D))
    putb("ones128", np.full((128, 128), 1.0 / 128))
    rr = np.zeros((128, 128))
    for p in range(128):
        rr[(p + 64) % 128, p] = 1.0
    putb("rrot", rr)
    NEG = -30000.0
    own = np.where(j[:, None] <= j[None, :], 0.0, NEG).astype(np.float32)
    prev = np.where(j[:, None] >= j[None, :], 0.0, NEG).astype(np.float32)
    putb("mask", np.concatenate([own, prev, prev if m != 0 else np.full((128, 128), NEG, np.float32)], 1))
    return cf, cb.astype(ml_dtypes.bfloat16)


class _Stop(Exception):
    pass


def build_program(dbg=None, stop=99, groups=None):
    nc = bass.Bass("TRN2", target_bir_lowering=False)
    P = Prog()
    dbg = dbg or {}

    stopped = [False]

    def chk(k):
        if stop == k:
            stopped[0] = True

    def din(name, shape, dt=F32):
        return nc.dram_tensor(name, list(shape), dt, kind="ExternalInput").ap()

    def dout(name, shape, dt=F32):
        return nc.dram_tensor(name, list(shape), dt, kind="ExternalOutput").ap()

    x_main = din("x_main", [NP_, D]); x_halo = din("x_halo", [NHALO, D]); x_smp = din("x_smp", [NS, D])
    p_main = din("p_main", [NP_, 256]); p_smp = din("p_smp", [NS, 256])
    ck = din("ck", [NS, 128, 2, 128]); cv = din("cv", [NS, 128, 2, 128]); st = din("st", [NS, 8, 128, 128])
    an_g = din("an_g", [1, D])
    w_in = din("w_in", [D, DFF]); w_out = din("w_out", [D, D]); w_gate = din("w_gate", [D, DFF])
    w_up = din("w_up", [D, DFF]); w_down = din("w_down", [DFF, D]); w_ple = din("w_ple", [256, D])
    w_pg = din("w_pg", [D, D])
    cf_d = din("cf", [128, NCF]); cb_d = din("cb", [128, NCB], BF16)

    y_main = dout("y_main", [NP_, D]); y_smp = dout("y_smp", [NS, D])
    kwin = dout("kwin", [128, 2, 128]); vwin = dout("vwin", [128, 2, 128]); rstate = dout("rstate", [8, 128, 128])
    ks_out = dout("ks_out", [NS, 128, 2, 128]); vs_out = dout("vs_out", [NS, 128, 2, 128])
    ss_out = dout("ss_out", [NS, 8, 128, 128])
    krs = nc.dram_tensor("krs", [8, 128, NT], BF16)
    vts = nc.dram_tensor("vts", [8, 128, NT], BF16)
    ag_in = nc.dram_tensor("ag_in", [8 * 128, 128], F32)
    ag_out = nc.dram_tensor("ag_out", [4 * 8 * 128, 128], F32)
    dbg_out = {}
    for name, shape in dbg.items():
        dbg_out[name] = dout("dbg_" + name, shape)

    def dump(name, ap, r=()):
        if name in dbg_out:
            P.add("pool", lambda e: e.dma_start(out=dbg_out[name], in_=ap), r, [("dbg", name)], dma="o_dbg")

    es = ExitStack()
    total = (nc.sbuf_bytes_remaining - 64) // 64 * 64
    arena = es.enter_context(nc.sbuf_tensor("arena", [128, total], U8))
    ps = [es.enter_context(nc.psum_tensor("ps%d" % i, [128, 512], F32)) for i in range(8)]

    class Alloc:
        def __init__(self, base, limit):
            self.p = base
            self.limit = limit

        def __call__(self, shape, dt):
            esz = 4 if dt == F32 else 2
            n = int(np.prod(shape)) * esz
            off = (self.p + 31) // 32 * 32
            self.p = off + n
            assert self.p <= self.limit, (self.p, self.limit)
            v = arena[:, off:off + n].bitcast(dt)
            if len(shape) == 2:
                return v.rearrange("p (a b) -> p a b", a=shape[0])
            if len(shape) == 3:
                return v.rearrange("p (a b c) -> p a b c", a=shape[0], b=shape[1])
            return v

    A = Alloc(0, total)
    cf = A([NCF], F32)
    cb = A([NCB], BF16)
    negc = A([1], F32); esink = A([8], F32); S0 = A([8, 128], F32)
    misc = A([64], F32)
    wslots = [A([SLOT_BYTES // 2], BF16) for _ in range(NSLOT)]
    aT = A([KC, NA], BF16)
    mixT = A([KC, NT], BF16)
    T2base = A.p
    T2 = Alloc(T2base, T2base + 12 * 1024)
    A.p = T2base + 12 * 1024
    HB = (A.p + 31) // 32 * 32
    hT = A([KC, NT], F32)
    HEND = A.p
    print("sbuf used", A.p, "of", total)

    def cfv(name, lo=0, hi=None):
        o, w = CF[name]
        return cf[:, o + lo:o + (w if hi is None else hi)]

    def cbv(name, lo=0, hi=None):
        o, w = CB[name]
        return cb[:, o + lo:o + (w if hi is None else hi)]

    ident_f = cfv("ident"); ident_b = cbv("ident")

    def psb(i, n=1024):
        return ps[i][:, :].bitcast(BF16)[:, 0:n]

    def pe(fn, r=(), w=()): return P.add("pe", fn, r, w)
    def act(fn, r=(), w=()): return P.add("act", fn, r, w)
    def dve(fn, r=(), w=()): return P.add("dve", fn, r, w)
    def pool(fn, r=(), w=()): return P.add("pool", fn, r, w)
    def dma(q, sem, out, in_, r=(), w=(), nofence=False, slow=False):
        if slow:
            return P.add(q, lambda e: e.dma_start(out=out, in_=in_, allow_slow_non_contiguous=True), r, w, dma=sem)
        return P.add(q, lambda e: e.dma_start(out=out, in_=in_), r, w, dma=sem, nofence=nofence)

    dma_sem_names = set()
    _orig_add = P.add

    def add_track(eng, fn, r=(), w=(), dma=None, nofence=False):
        if stopped[0]:
            return None
        if dma is not None:
            dma_sem_names.add(dma)
        return _orig_add(eng, fn, r, w, dma, nofence)
    P.add = add_track

    rot = {"d": 0, "m": 0}

    att_dense3 = [False]

    def bank_d():
        b = rot["d"] % (3 if att_dense3[0] else 4)
        rot["d"] += 1
        return b

    def bank_m():
        b = 4 + rot["m"] % 4
        rot["m"] += 1
        return b

    wq = []
    wstate = {"issued": 0, "used": 0, "done": 0}
    w_extra = []

    def w_issue_upto(n):
        while wstate["issued"] < min(n, len(wq)):
            i = wstate["issued"]
            src, kc, ncols = wq[i]
            s = i % NSLOT
            dst = wslots[s][:, 0:kc * ncols].rearrange("p (k n) -> p k n", k=kc)
            dma("pool", "w%d" % s, dst, src.rearrange("(k p) n -> p k n", p=128), r=list(w_extra), w=[("w", s)], nofence=True)
            wstate["issued"] += 1

    def w_done(n=1):
        wstate["done"] += n
        w_issue_upto(wstate["done"] + NSLOT)

    def w_next():
        i = wstate["used"]
        assert i < wstate["done"] + NSLOT
        w_issue_upto(i + 1)
        src, kc, ncols = wq[i]
        s = i % NSLOT
        wstate["used"] += 1
        return wslots[s][:, 0:kc * ncols].rearrange("p (k n) -> p k n", k=kc), ("w", s)

    def wblk(wap, k0, k1, c0, ncols):
        return (wap[k0 * 128:k1 * 128, c0:c0 + ncols], k1 - k0, ncols)

    C_AQ, C_AK, C_AV, C_RQ, C_RK, C_RV, C_RG = 0, 1024, 1280, 1536, 2560, 3584, 4608
    for j in range(4):
        wq.append(wblk(w_in, 0, 16, C_RK + 256 * j, 256))
        wq.append(wblk(w_in, 0, 16, C_RV + 256 * j, 256))
    wq.append(wblk(w_in, 0, 16, C_AK, 256))
    wq.append(wblk(w_in, 0, 16, C_AV, 256))
    for j in range(4):
        wq.append(wblk(w_in, 0, 16, C_AQ + 256 * j, 256))
    for j in range(4):
        for cbase in (C_RQ, C_RG):
            wq.append(wblk(w_in, 0, 16, cbase + 256 * j, 256))
    for j in range(8):
        wq.append(wblk(w_out, 0, 16, 256 * j, 256))
    QUART = [(0, 6), (6, 6), (12, 5), (17, 5)]
    for (b0, nb) in QUART:
        for b in range(b0, b0 + nb):
            wq.append(wblk(w_gate, 0, 16, 256 * b, 256))
            wq.append(wblk(w_up, 0, 16, 256 * b, 256))
        for j in range(8):
            wq.append(wblk(w_down, 2 * b0, 2 * (b0 + nb), 256 * j, 256))
    for j in range(8):
        wq.append(wblk(w_pg, 0, 16, 256 * j, 256))

    def dense_block(kc, rhs_fn, rhs_res, evac, tiles=TT, nm=2):
        wv, wres = w_next()
        for ml in range(nm):
            for ti, (c0, n) in enumerate(tiles):
                b = bank_d()
                for k in range(kc):
                    pe(lambda e, b=b, k=k, ml=ml, c0=c0, n=n, wv=wv: e.matmul(
                        ps[b][:, 0:n], lhsT=wv[:, k, ml * 128:(ml + 1) * 128], rhs=rhs_fn(k, c0, n),
                        start=(k == 0), stop=(k == kc - 1)),
                       r=[wres] + list(rhs_res), w=[("ps", b)])
                evac(ml, ti, c0, n, b)
        w_done()

    dma("sp", "c_cf", cf, cf_d, w=["cf"])
    dma("sp", "c_cb", cb, cb_d, w=["cb"])

    Z = Alloc(HB, total)
    NXT = 4
    xt = [Z([D], F32) for _ in range(NXT)]
    junk = Z([D], BF16)
    xn = [Z([D], BF16) for _ in range(2)]
    gbc = Z([D], F32)
    ss = misc[:, 0:10]; rstd1 = misc[:, 10:20]; tmpa = misc[:, 20:30]
    gpa = misc[:, 30:31]; mx = misc[:, 31:32]; negc1 = misc[:, 32:33]; epsv = misc[:, 34:35]
    dve(lambda e: e.memset(epsv, EPS), w=["epsv"])

    dma("sp", "c_gbc", gbc, an_g.partition_broadcast(128), w=["gbc"])

    dve(lambda e: e.tensor_tensor(out=gpa, in0=cfv("qg"), in1=cfv("kg"), op=ALU.mult), r=["cf"], w=["gpa"])
    gpa2 = misc[:, 33:34]
    dve(lambda e: e.tensor_tensor(out=gpa2, in0=gpa, in1=gpa, op=ALU.mult), r=["gpa"], w=["gpa2"])
    b = bank_m()
    pe(lambda e, b=b: e.transpose(ps[b][0:1, 0:128], gpa2, ident_f), r=["gpa2", "cf"], w=[("ps", b)])
    dve(lambda e, b=b: e.tensor_reduce(out=mx[0:1, :], in_=ps[b][0:1, 0:128], axis=AX.X, op=ALU.max),
        r=[("ps", b)], w=["mx"])
    act(lambda e: e.activation(out=mx[0:1, :], in_=mx[0:1, :], func=AF.Sqrt), r=["mx"], w=["mx"])
    dve(lambda e: e.tensor_scalar(out=negc1[0:1, :], in0=mx[0:1, :], scalar1=-(ATTN_SCALE * 128.0), scalar2=None,
                                  op0=ALU.mult), r=["mx"], w=["negc1"])
    b = bank_m()
    pe(lambda e, b=b: e.matmul(ps[b][:, 0:1], lhsT=cfv("ones_row")[0:1, :], rhs=negc1[0:1, :], start=True, stop=True),
       r=["negc1", "cf"], w=[("ps", b)])
    dve(lambda e, b=b: e.tensor_copy(out=negc, in_=ps[b][:, 0:1]), r=[("ps", b)], w=["negc"])
    act(lambda e: e.activation(out=esink, in_=cfv("sinks"), func=AF.Exp, bias=negc, scale=1.0),
        r=["negc", "cf"], w=["esink"])

    tiles1 = [(x_main[i * 128:(i + 1) * 128, :], 128, i * 128) for i in range(8)]
    tiles1.append((x_halo, 128, NT))
    tiles1.append((x_smp, NS, NP_))
    def p1_load(i, src, rows, c0):
        s4 = i % NXT
        dma("sp", "x%d" % s4, xt[s4][0:rows, :], src, w=[("xt", s4)])

    def p1_stage1(i, src, rows, c0):
        s = i % 2
        s4 = i % NXT
        act(lambda e, s4=s4, rows=rows, i=i: e.activation(out=junk[0:rows, :], in_=xt[s4][0:rows, :], func=AF.Square,
                                                         accum_out=ss[0:rows, i:i + 1]),
            r=[("xt", s4)], w=["junk", ("ss", i)])
        act(lambda e, rows=rows, i=i: e.activation(out=tmpa[0:rows, i:i + 1], in_=ss[0:rows, i:i + 1], func=AF.Sqrt,
                                                   bias=epsv[0:rows, :], scale=1.0 / D),
            r=[("ss", i), "epsv"], w=[("tmpa", i)])
        dve(lambda e, rows=rows, i=i: e.reciprocal(out=rstd1[0:rows, i:i + 1], in_=tmpa[0:rows, i:i + 1]),
            r=[("tmpa", i)], w=[("rstd1", i)])
        dve(lambda e, s=s, s4=s4, rows=rows, i=i: e.scalar_tensor_tensor(
            out=xn[s][0:rows, :], in0=xt[s4][0:rows, :], scalar=rstd1[0:rows, i:i + 1], in1=gbc[0:rows, :],
            op0=ALU.mult, op1=ALU.mult), r=[("xt", s4), ("rstd1", i), "gbc"], w=[("xn", s)])

    def p1_stage2(i, src, rows, c0):
        s = i % 2
        for half in range(2):
            b = bank_m()
            for kk in range(8):
                k = half * 8 + kk
                pe(lambda e, b=b, kk=kk, k=k, s=s, rows=rows: e.transpose(
                    psb(b)[:, kk * 128:kk * 128 + rows], xn[s][0:rows, k * 128:(k + 1) * 128],
                    ident_b[0:rows, 0:rows]), r=[("xn", s), "cb"], w=[("ps", b)])
            src_ps = lambda b=b, rows=rows: psb(b).rearrange("p (a c) -> p a c", a=8)[:, :, 0:rows]
            dst = aT[:, half * 8:(half + 1) * 8, c0:c0 + rows]
            if half == 0:
                act(lambda e, dst=dst, src_ps=src_ps: e.copy(out=dst, in_=src_ps()), r=[("ps", b)], w=[("aT", c0, half)])
            else:
                dve(lambda e, dst=dst, src_ps=src_ps: e.tensor_copy(out=dst, in_=src_ps()), r=[("ps", b)], w=[("aT", c0, half)])

    for i in range(NXT):
        p1_load(i, *tiles1[i])
    p1_stage1(0, *tiles1[0])
    for i in range(len(tiles1)):
        if i + 1 < len(tiles1):
            p1_stage1(i + 1, *tiles1[i + 1])
        if i + NXT < len(tiles1):
            p1_load(i + NXT, *tiles1[i + NXT])
        if i in (3, 5, 7, 8):
            w_extra[:] = [("xt", (i + 1) % NXT)]
            w_issue_upto(wstate["issued"] + 1)
            w_extra[:] = []
        p1_stage2(i, *tiles1[i])
    w_issue_upto(NSLOT)
    P.fence()
    dump("aT", aT, ["aT"])
    chk(1)

    aT_rhs = lambda k, c0, n: aT[:, k, c0:c0 + n]
    cosv = cfv("cos"); sinv = cfv("sin")

    def rotary_ops(tag, b, c0, n, xb, t1, t2, outb):
        act(lambda e: e.copy(out=xb[:, c0:c0 + n], in_=ps[b][:, 0:n]), r=[("ps", b)], w=[(tag, "xb", c0)])
        dve(lambda e: e.tensor_tensor(out=t1[:, c0:c0 + n], in0=ps[b][:, 0:n], in1=cosv[:, c0:c0 + n], op=ALU.mult),
            r=[("ps", b), "cf"], w=[(tag, "t1", c0)])
        b2 = bank_m()
        pe(lambda e: e.matmul(ps[b2][:, 0:n], lhsT=cbv("rrot"), rhs=xb[:, c0:c0 + n], start=True, stop=True),
           r=[(tag, "xb", c0), "cb"], w=[("ps", b2)])
        dve(lambda e: e.tensor_tensor(out=t2[:, c0:c0 + n], in0=ps[b2][:, 0:n], in1=sinv[:, c0:c0 + n], op=ALU.mult),
            r=[("ps", b2), "cf"], w=[(tag, "t2", c0)])
        dve(lambda e: e.tensor_tensor(out=outb[:, c0:c0 + n], in0=t1[:, c0:c0 + n], in1=t2[:, c0:c0 + n], op=ALU.add),
            r=[(tag, "t1", c0), (tag, "t2", c0)], w=[(tag, "rot", c0)])

    Z = Alloc(HB, total)
    p1 = []
    for hp in range(2):
        p1.append(dict(xb=Z([NT], BF16), t1=Z([NT], F32), t2=Z([NT], F32), kr=Z([NT], BF16),
                       kD=Z([8, 128], BF16), vT=Z([NT], BF16), vtok=Z([8, 128], BF16), sloc=Z([128], F32)))
    kdl = cfv("kdlong").rearrange("p (h c) -> p h c", h=8)

    for j in range(4):
        hs = (2 * j, 2 * j + 1)

        def evac_k(ml, ti, c0, n, b, hs=hs):
            bf = p1[ml]
            rotary_ops(("p1", ml), b, c0, n, bf["xb"], bf["t1"], bf["t2"], bf["kr"])

        def evac_v(ml, ti, c0, n, b, hs=hs):
            bf = p1[ml]
            act(lambda e: e.copy(out=bf["vT"][:, c0:c0 + n], in_=ps[b][:, 0:n]), r=[("ps", b)],
                w=[("p1", ml, "vT", c0)])
        dense_block(16, aT_rhs, ["aT"], evac_k)
        chk(20)
        dense_block(16, aT_rhs, ["aT"], evac_v)
        chk(21)
        for ml in range(2):
            h = hs[ml]
            bf = p1[ml]
            rk_res = [(("p1", ml), "rot", c0) for (c0, n) in TT]
            rv_res = [("p1", ml, "vT", c0) for (c0, n) in TT]
            dma("sp", "spk%d" % ml, krs.ap()[h], bf["kr"], r=rk_res, w=[("krs", h)])
            dma("sp", "spv%d" % ml, vts.ap()[h], bf["vT"], r=rv_res, w=[("vts", h)])
            for half in range(2):
                bk = bank_m()
                for cc in range(4):
                    c = half * 4 + cc
                    pe(lambda e, bk=bk, cc=cc, c=c, bf=bf: e.transpose(
                        psb(bk)[:, cc * 128:(cc + 1) * 128], bf["kr"][:, c * 128:(c + 1) * 128], ident_b),
                       r=rk_res + ["cb"], w=[("ps", bk)])
                for cc in range(4):
                    c = half * 4 + cc
                    dve(lambda e, bk=bk, cc=cc, c=c, bf=bf, h=h: e.tensor_scalar(
                        out=bf["kD"][:, c, :], in0=psb(bk)[:, cc * 128:(cc + 1) * 128], scalar1=kdl[:, h, c:c + 1],
                        scalar2=None, op0=ALU.mult), r=[("ps", bk), "cf"], w=[("p1", ml, "kD")])
                bv = bank_m()
                for cc in range(4):
                    c = half * 4 + cc
                    pe(lambda e, bv=bv, cc=cc, c=c, bf=bf: e.transpose(
                        psb(bv)[:, cc * 128:(cc + 1) * 128], bf["vT"][:, c * 128:(c + 1) * 128], ident_b),
                       r=rv_res + ["cb"], w=[("ps", bv)])
                act(lambda e, bv=bv, half=half, bf=bf: e.copy(
                    out=bf["vtok"][:, half * 4:(half + 1) * 4, :],
                    in_=psb(bv)[:, 0:512].rearrange("p (a c) -> p a c", a=4)), r=[("ps", bv)], w=[("p1", ml, "vtok")])
            chk(22)
            bs = bank_m()
            for c in range(8):
                pe(lambda e, bs=bs, c=c, bf=bf: e.matmul(ps[bs][:, 0:128], lhsT=bf["kD"][:, c, :], rhs=bf["vtok"][:, c, :],
                                                        start=(c == 0), stop=(c == 7)),
                   r=[("p1", ml, "kD"), ("p1", ml, "vtok")], w=[("ps", bs)])
            dve(lambda e, bs=bs, bf=bf: e.tensor_copy(out=bf["sloc"], in_=ps[bs][:, 0:128]), r=[("ps", bs)],
                w=[("p1", ml, "sloc")])
            chk(23)
            dma("sp", "agi", ag_in.ap()[h * 128:(h + 1) * 128, :], bf["sloc"], r=[("p1", ml, "sloc")], w=["ag_in"])
            chk(24)

    dump("ag_in", ag_in.ap(), ["ag_in"])
    chk(2)
    P.fence()
    P.add("pool", lambda e: e.collective_compute("AllGather", ALU.bypass, replica_groups=groups or [[0, 1, 2, 3], [4, 5, 6, 7]],
                                                 ins=[ag_in.ap().opt()], outs=[ag_out.ap().opt()]),
          r=["ag_in"], w=["ag_out"], dma="cc")
    chk(3)

    Z = Alloc(HB, total)
    zf = [Z([NA], F32) for _ in range(2)]
    sq = [Z([NA], BF16) for _ in range(2)]
    rstdb = [Z([NA], F32)] * 2
    knT = Z([2, NA], BF16)
    kn32 = Z([2, 132], F32)
    v32 = Z([2, 132], F32)
    vTb = Z([2, NA], BF16)
    vtokA = Z([2, 9, 128], BF16)
    qnT = Z([4, NT], BF16)
    qnT2 = Z([4, NT], BF16)
    PTm = [Z([2, 512], BF16) for _ in range(2)]
    rec = [Z([512], F32) for _ in range(2)]
    win_t = Z([2, 128], F32)
    vstok = Z([2, 128], BF16)
    kc_b = [Z([128], BF16) for _ in range(NS)]
    vc_b = [Z([128], BF16) for _ in range(NS)]
    kcT = Z([4, 128], BF16)
    PTc = Z([16], BF16)
    Pn = Z([16], BF16)
    recs = Z([16], F32)
    cnt = {"qk": 0}

    qk_pending = [None]

    def qk_flush():
        if qk_pending[0] is not None:
            f = qk_pending[0]
            qk_pending[0] = None
            f()

    def qknorm(b, c0, n, gname, outbf, tagres, out32=None):
        s = cnt["qk"] % 2
        cnt["qk"] += 1
        act(lambda e: e.activation(out=sq[s][:, 0:n], in_=ps[b][:, 0:n], func=AF.Square), r=[("ps", b)], w=[("sq", s)])
        dve(lambda e: e.tensor_copy(out=zf[s][:, 0:n], in_=ps[b][:, 0:n]), r=[("ps", b)], w=[("zf", s)])
        qk_flush()

        def part_b():
            b2 = 3
            pe(lambda e: e.matmul(ps[b2][:, 0:n], lhsT=cbv("ones128"), rhs=sq[s][:, 0:n], start=True, stop=True),
               r=[("sq", s), "cb"], w=[("ps", b2)])
            act(lambda e: e.activation(out=rstdb[0][:, 0:n], in_=ps[b2][:, 0:n], func=AF.Ln, bias=epsv, scale=1.0),
                r=[("ps", b2)], w=["rstdb"])
            act(lambda e: e.activation(out=rstdb[0][:, 0:n], in_=rstdb[0][:, 0:n], func=AF.Exp, scale=-0.5),
                r=["rstdb"], w=["rstdb"])
            dve(lambda e: e.scalar_tensor_tensor(out=outbf, in0=zf[s][:, 0:n], scalar=cfv(gname), in1=rstdb[0][:, 0:n],
                                                 op0=ALU.mult, op1=ALU.mult),
                r=[("zf", s), "rstdb", "cf"], w=[tagres])
            if out32 is not None:
                lo, hi, dst = out32
                dve(lambda e: e.scalar_tensor_tensor(out=dst, in0=zf[s][:, lo:hi], scalar=cfv(gname),
                                                     in1=rstdb[0][:, lo:hi], op0=ALU.mult, op1=ALU.mult),
                    r=[("zf", s), "rstdb", "cf"], w=[("kn32", tagres)])
        qk_pending[0] = part_b

    def evac_ak(ml, ti, c0, n, b):
        o32 = None
        if ti == 2:
            o32 = (208, 340, kn32[:, ml, :])
        qknorm(b, c0, n, "kg", knT[:, ml, c0:c0 + n], ("knT", ml, c0), o32)

    def evac_av(ml, ti, c0, n, b):
        act(lambda e: e.copy(out=vTb[:, ml, c0:c0 + n], in_=ps[b][:, 0:n]), r=[("ps", b)], w=[("vTb", ml, c0)])
        if ti == 2:
            dve(lambda e: e.tensor_copy(out=v32[:, ml, :], in_=ps[b][:, 208:340]), r=[("ps", b)], w=[("v32", ml)])

    att_dense3 = [True]
    dense_block(16, aT_rhs, ["aT"], evac_ak, tiles=TT_H)
    qk_flush()
    dense_block(16, aT_rhs, ["aT"], evac_av, tiles=TT_H)
    vres = lambda g: [("vTb", g, c0) for (c0, n) in TT_H]
    kres = lambda g: [("knT", g, c0) for (c0, n) in TT_H]
    for g in range(2):
        for grp in range(3):
            blks = [0, 1, 2, 3] if grp == 0 else ([4, 5, 6, 7] if grp == 1 else [8])
            bv = bank_m()
            for ii, blk in enumerate(blks):
                col = NT if blk == 0 else (blk - 1) * 128
                pe(lambda e, bv=bv, ii=ii, col=col, g=g: e.transpose(
                    psb(bv)[:, ii * 128:(ii + 1) * 128], vTb[:, g, col:col + 128], ident_b),
                   r=vres(g) + ["cb"], w=[("ps", bv)])
            nb = len(blks)
            act(lambda e, bv=bv, g=g, blks=blks, nb=nb: e.copy(
                out=vtokA[:, g, blks[0]:blks[0] + nb, :],
                in_=psb(bv)[:, 0:nb * 128].rearrange("p (a c) -> p a c", a=nb)), r=[("ps", bv)], w=[("vtokA", g)])
        bv = bank_m()
        pe(lambda e, bv=bv, g=g: e.transpose(psb(bv)[0:NS, 0:128], vTb[:, g, NP_:NP_ + NS], ident_b),
           r=vres(g) + ["cb"], w=[("ps", bv)])
        act(lambda e, bv=bv, g=g: e.copy(out=vstok[0:NS, g, :], in_=psb(bv)[0:NS, 0:128]), r=[("ps", bv)],
            w=[("vstok", g)])
    for (src32, dst, nm) in ((kn32, kwin, "kw"), (v32, vwin, "vw")):
        for g in range(2):
            bw = bank_m()
            pe(lambda e, bw=bw, g=g, src32=src32: e.transpose(ps[bw][:, 0:128], src32[:, g, 0:128], ident_f),
               r=[("kn32", ("knT", g, 688)), ("v32", g), "cf"], w=[("ps", bw)])
            dve(lambda e, bw=bw, g=g: e.tensor_copy(out=win_t[:, g, :], in_=ps[bw][:, 0:128]), r=[("ps", bw)],
                w=[("win_t", g)])
        dma("sp", "o_" + nm, dst, win_t, r=[("win_t", 0), ("win_t", 1)], w=["out_" + nm])
    dma("sp", "o_ks", ks_out[:, 0:127, :, :], ck[:, 1:128, :, :], w=["ks_out_a"])
    dma("sp", "o_vs", vs_out[:, 0:127, :, :], cv[:, 1:128, :, :], w=["vs_out_a"])
    for g in range(2):
        for s in range(NS):
            dma("sp", "o_ks", ks_out[s, 127, g, :].rearrange("(d o) -> d o", o=1), kn32[:, g, 128 + s:129 + s],
                r=[("kn32", ("knT", g, 688))], w=[("ks_out_b", g, s)], slow=True)
            dma("sp", "o_vs", vs_out[s, 127, g, :].rearrange("(d o) -> d o", o=1), v32[:, g, 128 + s:129 + s],
                r=[("v32", g)], w=[("vs_out_b", g, s)], slow=True)

    mask = cbv("mask").rearrange("p (a c) -> p a c", a=3)
    v3 = lambda ap: ap[:, 0:1024].rearrange("p (a c) -> p a c", a=8)
    esr_f = v3(zf[0]); esr_d = v3(zf[1]); esr_hi = v3(sq[0]); esr_lo = v3(sq[1])
    esr2 = Z([8, 128], BF16)
    oh = cfv("onehot")
    dve(lambda e: e.tensor_copy(out=esr_f[0:2], in_=esink[0:2, :].unsqueeze(2).broadcast_to([2, 8, 128])),
        r=["esink"], w=[("zf", 0)])
    dve(lambda e: e.tensor_copy(out=esr_hi[0:2], in_=esr_f[0:2]), r=[("zf", 0)], w=[("sq", 0)])
    dve(lambda e: e.tensor_tensor(out=esr_d[0:2], in0=esr_f[0:2], in1=esr_hi[0:2], op=ALU.subtract),
        r=[("zf", 0), ("sq", 0)], w=[("zf", 1)])
    dve(lambda e: e.tensor_copy(out=esr_lo[0:2], in_=esr_d[0:2]), r=[("zf", 1)], w=[("sq", 1)])
    dve(lambda e: e.tensor_scalar(out=esr2[0:2], in0=esr_hi[0:2], scalar1=oh[0:2, 0:1], scalar2=None, op0=ALU.mult),
        r=[("sq", 0), "cf"], w=["esr2a"])
    dve(lambda e: e.scalar_tensor_tensor(out=esr2[0:2], in0=esr_lo[0:2], scalar=oh[0:2, 1:2], in1=esr2[0:2],
                                         op0=ALU.mult, op1=ALU.add), r=[("sq", 1), "esr2a", "cf"], w=["esr2"])
    def make_att(g, qnT, qres):
        def att_S(blk, g=g):
            s = blk % 2
            q_rhs = qnT[:, :, blk * 128:(blk + 1) * 128]
            k_own = knT[:, g, blk * 128:(blk + 1) * 128]
            k_prev = knT[:, g, NT:NT + 128] if blk == 0 else knT[:, g, (blk - 1) * 128:blk * 128]
            bo = bank_m(); bp = bank_m()
            mprev = 2 if blk == 0 else 1
            for (bb_, kk_, mi) in ((bo, k_own, 0), (bp, k_prev, mprev)):
                pe(lambda e, bb_=bb_, kk_=kk_, q_rhs=q_rhs: e.matmul(ps[bb_][:, :], lhsT=kk_, rhs=q_rhs, start=True, stop=False),
                   r=qres + kres(g), w=[("ps", bb_)])
                pe(lambda e, bb_=bb_, mi=mi: e.matmul(ps[bb_][:, :], lhsT=ident_b,
                                                     rhs=mask[:, mi:mi + 1, :].broadcast_to([128, 4, 128]),
                                                     start=False, stop=True), r=["cb"], w=[("ps", bb_)])
            act(lambda e, bo=bo, s=s: e.activation(out=PTm[s][:, 0, :], in_=ps[bo][:, :], func=AF.Exp, bias=negc,
                                                   scale=ATTN_SCALE), r=[("ps", bo), "negc"], w=[("PTm", s, 0)])
            act(lambda e, bp=bp, s=s: e.activation(out=PTm[s][:, 1, :], in_=ps[bp][:, :], func=AF.Exp, bias=negc,
                                                   scale=ATTN_SCALE), r=[("ps", bp), "negc"], w=[("PTm", s, 1)])

        def att_PV(blk, g=g):
            s = blk % 2
            bO = bank_m(); bD = bank_m()
            for t, vb in ((0, blk + 1), (1, blk)):
                pe(lambda e, bO=bO, t=t, vb=vb, s=s, g=g: e.matmul(ps[bO][:, :], lhsT=vtokA[:, g, vb, :], rhs=PTm[s][:, t, :],
                                                                  start=(t == 0), stop=(t == 1)),
                   r=[("PTm", s, t), ("vtokA", g)], w=[("ps", bO)])
            for t in range(2):
                pe(lambda e, bD=bD, t=t, s=s: e.matmul(ps[bD][:, :], lhsT=cbv("ones"), rhs=PTm[s][:, t, :],
                                                      start=(t == 0), stop=False),
                   r=[("PTm", s, t), "cb"], w=[("ps", bD)])
            pe(lambda e, bD=bD, g=g: e.matmul(ps[bD][:, :], lhsT=cbv("ones")[0:2, :], rhs=esr2[0:2, 4 * g:4 * g + 4, :],
                                             start=False, stop=True), r=["esr2", "cb"], w=[("ps", bD)])
            if blk % 2 == 0:
                act(lambda e, bD=bD, s=s: e.activation(out=rec[s], in_=ps[bD][:, :], func=AF.Ln),
                    r=[("ps", bD)], w=[("rec", s)])
                act(lambda e, s=s: e.activation(out=rec[s], in_=rec[s], func=AF.Exp, scale=-1.0), r=[("rec", s)],
                    w=[("rec", s)])
            else:
                dve(lambda e, bD=bD, s=s: e.reciprocal(out=rec[s], in_=ps[bD][:, :]), r=[("ps", bD)], w=[("rec", s)])
            dve(lambda e, bO=bO, s=s, g=g, blk=blk: e.tensor_tensor(
                out=mixT[:, 4 * g:4 * g + 4, blk * 128:(blk + 1) * 128],
                in0=ps[bO][:, :].rearrange("p (a c) -> p a c", a=4),
                in1=rec[s][:, :].rearrange("p (a c) -> p a c", a=4), op=ALU.mult),
                r=[("ps", bO), ("rec", s)], w=["mixT"])


        def att_gen():
            att_S(0)
            yield
            for blk in range(8):
                if blk + 1 < 8:
                    att_S(blk + 1)
                    yield
                att_PV(blk)
                yield

        def att_sample():
            for sm in range(NS):
                dma("pool", "kc%d" % sm, kc_b[sm], ck[sm, :, g, :], w=[("kc_b", sm)])
                dma("pool", "vc%d" % sm, vc_b[sm], cv[sm, :, g, :], w=[("vc_b", sm)])
            bt = bank_m()
            for sm in range(NS):
                pe(lambda e, bt=bt, sm=sm: e.transpose(psb(bt)[:, sm * 128:(sm + 1) * 128], kc_b[sm], ident_b),
                   r=[("kc_b", sm), "cb"], w=[("ps", bt)])
            act(lambda e, bt=bt: e.copy(out=kcT, in_=psb(bt)[:, 0:512].rearrange("p (a c) -> p a c", a=4)),
                r=[("ps", bt)], w=["kcT"])
            bS = bank_m()
            for sm in range(NS):
                pe(lambda e, bS=bS, sm=sm: e.matmul(ps[bS][:, sm * 4:(sm + 1) * 4], lhsT=kcT[:, sm, :], rhs=qnT[:, :, NP_ + sm],
                                                   start=True, stop=True, skip_group_check=True),
                   r=["kcT"] + qres, w=[("ps", bS)])
            for sm in range(NS):
                pe(lambda e, bS=bS, sm=sm, g=g: e.matmul(ps[bS][0:NS, 32 + sm * 4:32 + (sm + 1) * 4],
                                                        lhsT=knT[:, g, NP_:NP_ + NS], rhs=qnT[:, :, NP_ + sm],
                                                        start=True, stop=True, skip_group_check=True),
                   r=kres(g) + qres, w=[("ps", bS)])
            act(lambda e, bS=bS: e.activation(out=PTc, in_=ps[bS][:, 0:16], func=AF.Exp, bias=negc, scale=ATTN_SCALE),
                r=[("ps", bS), "negc"], w=["PTc"])
            act(lambda e, bS=bS: e.activation(out=Pn[0:NS, :], in_=ps[bS][0:NS, 32:48], func=AF.Exp,
                                              bias=negc[0:NS, :], scale=ATTN_SCALE), r=[("ps", bS), "negc"], w=["Pn"])
            dve(lambda e: e.tensor_tensor(out=Pn[0:NS, :].rearrange("p (s h) -> p s h", s=4),
                                          in0=Pn[0:NS, :].rearrange("p (s h) -> p s h", s=4),
                                          in1=oh[0:NS, 0:4].unsqueeze(2).broadcast_to([NS, 4, 4]), op=ALU.mult),
                r=["Pn", "cf"], w=["Pn"])
            bO = bank_m(); bD = bank_m()
            for sm in range(NS):
                pe(lambda e, bO=bO, sm=sm: e.matmul(ps[bO][:, sm * 4:(sm + 1) * 4], lhsT=vc_b[sm], rhs=PTc[:, sm * 4:(sm + 1) * 4],
                                                   start=(sm == 0), stop=False, skip_group_check=True),
                   r=[("vc_b", sm), "PTc"], w=[("ps", bO)])
                pe(lambda e, bO=bO, sm=sm, g=g: e.matmul(ps[bO][:, sm * 4:(sm + 1) * 4], lhsT=vstok[0:NS, g, :],
                                                        rhs=Pn[0:NS, sm * 4:(sm + 1) * 4], start=False, stop=True,
                                                        skip_group_check=True),
                   r=[("vstok", g), "Pn"], w=[("ps", bO)])
            for sm in range(NS):
                pe(lambda e, bD=bD, sm=sm: e.matmul(ps[bD][:, sm * 4:(sm + 1) * 4], lhsT=cbv("ones"), rhs=PTc[:, sm * 4:(sm + 1) * 4],
                                                   start=(sm == 0), stop=False, skip_group_check=True),
                   r=["PTc", "cb"], w=[("ps", bD)])
                pe(lambda e, bD=bD, sm=sm: e.matmul(ps[bD][:, sm * 4:(sm + 1) * 4], lhsT=cbv("ones")[0:NS, :],
                                                   rhs=Pn[0:NS, sm * 4:(sm + 1) * 4], start=False, stop=False,
                                                   skip_group_check=True), r=["Pn", "cb"], w=[("ps", bD)])
            pe(lambda e, bD=bD, g=g: e.matmul(ps[bD][:, 0:16], lhsT=cbv("ones")[0:2, :],
                                             rhs=esr2[0:2, 4 * g:4 * g + 4, 0].unsqueeze(1).broadcast_to([2, 4, 4]),
                                             start=False, stop=True, skip_group_check=True),
               r=["esr2", "cb"], w=[("ps", bD)])
            act(lambda e, bD=bD: e.activation(out=recs, in_=ps[bD][:, 0:16], func=AF.Ln), r=[("ps", bD)], w=["recs"])
            act(lambda e: e.activation(out=recs, in_=recs, func=AF.Exp, scale=-1.0), r=["recs"], w=["recs"])
            dve(lambda e, bO=bO, g=g: e.tensor_tensor(
                out=mixT[:, 4 * g:4 * g + 4, NP_:NP_ + NS], in0=ps[bO][:, 0:16].rearrange("p (s h) -> p h s", s=4),
                in1=recs[:, :].rearrange("p (s h) -> p h s", s=4), op=ALU.mult),
                r=[("ps", bO), "recs"], w=["mixT"])

        return att_gen, att_sample

    qbufs = [qnT, qnT2]
    qresf = lambda gi: [("qnT", gi, hh, c0) for hh in range(4) for (c0, n) in TT]

    def qproj_gen(gi):
        qb = qbufs[gi]
        for jj in range(2):
            wv, wres = w_next()
            for ml in range(2):
                hh = jj * 2 + ml
                for ti, (c0, n) in enumerate(TT):
                    b = bank_d()
                    for k in range(16):
                        pe(lambda e, b=b, k=k, ml=ml, c0=c0, n=n, wv=wv: e.matmul(
                            ps[b][:, 0:n], lhsT=wv[:, k, ml * 128:(ml + 1) * 128], rhs=aT[:, k, c0:c0 + n],
                            start=(k == 0), stop=(k == 15)), r=[wres, "aT"], w=[("ps", b)])
                    qknorm(b, c0, n, "qg", qb[:, hh, c0:c0 + n], ("qnT", gi, hh, c0))
                    yield
            w_done()
        qk_flush()

    def drive_mix(ga, gb, pat):
        a_alive, b_alive = True, gb is not None
        while a_alive or b_alive:
            if a_alive:
                try:
                    next(ga)
                except StopIteration:
                    a_alive = False
            for _ in range(pat if a_alive else 99):
                if not b_alive:
                    break
                try:
                    next(gb)
                except StopIteration:
                    b_alive = False

    for _ in qproj_gen(0):
        pass
    attg0, atts0 = make_att(0, qbufs[0], qresf(0))
    attg1, atts1 = make_att(1, qbufs[1], qresf(1))
    drive_mix(attg0(), qproj_gen(1), 1)
    atts0()
    for _ in attg1():
        pass
    atts1()
    att_dense3[0] = False
    P.fence()
    dump("mixA", mixT[:, 0:8, :], ["mixT"])
    chk(4)

    Z = Alloc(HB, total)
    agl = [Z([8, 128], F32) for _ in range(2)]
    coef = cfv("coef")
    ago = ag_out.ap()
    for r_ in range(4):
        s = r_ % 2
        dma("sp", "agl%d" % s, agl[s], ago[r_ * 1024:(r_ + 1) * 1024, :].rearrange("(h p) n -> p h n", p=128),
            r=["ag_out"], w=[("agl", s)])
        for h in range(8):
            if r_ == 0:
                dve(lambda e, s=s, h=h, r_=r_: e.tensor_scalar(out=S0[:, h, :], in0=agl[s][:, h, :],
                                                              scalar1=coef[:, r_ * 8 + h:r_ * 8 + h + 1], scalar2=None,
                                                              op0=ALU.mult), r=[("agl", s), "cf"], w=[("S0", h)])
            else:
                dve(lambda e, s=s, h=h, r_=r_: e.scalar_tensor_tensor(
                    out=S0[:, h, :], in0=agl[s][:, h, :], scalar=coef[:, r_ * 8 + h:r_ * 8 + h + 1], in1=S0[:, h, :],
                    op0=ALU.mult, op1=ALU.add), r=[("agl", s), "cf", ("S0", h)], w=[("S0", h)])

    dump("S0", S0, [("S0", h) for h in range(8)])
    chk(5)
    G1 = [dict(qr=Z([NT], BF16), kr=Z([NT], BF16), vT=Z([NT], BF16), sg=Z([NT], F32)) for _ in range(2)]
    RT = dict(xb=[Z([344], BF16) for _ in range(3)], t1=[Z([344], F32) for _ in range(3)],
              t2=[Z([344], F32) for _ in range(3)])
    R = dict(qd=Z([8, 128], BF16), kd=Z([8, 128], BF16), vtok=Z([8, 128], BF16), scm=Z([8, 128], BF16),
             Sb=Z([8, 128], BF16), Srun=Z([2, 128], F32), o_sb=Z([NT], F32), sq=Z([NT], BF16), rstd=Z([NT], F32),
             tt=Z([NT], F32), ktok_s=Z([128], BF16), vtok_s=Z([128], BF16))
    vm = [Z([128], BF16) for _ in range(NS)]
    Sst = [Z([128], F32) for _ in range(NS)]
    Snew = [Z([128], F32) for _ in range(NS)]
    Sbs = [Z([128], BF16) for _ in range(NS)]
    intraT = cfv("intraT").rearrange("p (h c) -> p h c", h=8)
    qdecv = cfv("qdec").rearrange("p (h c) -> p h c", h=8)
    kdecv = cfv("kdec")
    rgv = cfv("rg")
    tg = "p2"
    p2c = {"d": 0, "r": 0, "t": 0}

    def bank_d3():
        b = p2c["d"] % 3
        p2c["d"] += 1
        return b

    def bank_r():
        b = 3 + p2c["r"] % 2
        p2c["r"] += 1
        return b
    M0, M1, M2 = 5, 6, 7
    p2blocks = {}

    def rot_a(b, c0, n):
        sl = p2c["t"] % 3
        p2c["t"] += 1
        xb, t1 = RT["xb"][sl], RT["t1"][sl]
        act(lambda e: e.copy(out=xb[:, 0:n], in_=ps[b][:, 0:n]), r=[("ps", b)], w=[("rt_xb", sl)])
        dve(lambda e: e.tensor_tensor(out=t1[:, 0:n], in0=ps[b][:, 0:n], in1=cosv[:, c0:c0 + n], op=ALU.mult),
            r=[("ps", b), "cf"], w=[("rt_t1", sl)])
        return sl

    def rot_b(sl, c0, n, outb, outres):
        xb, t1, t2 = RT["xb"][sl], RT["t1"][sl], RT["t2"][sl]
        b2 = bank_r()
        pe(lambda e: e.matmul(ps[b2][:, 0:n], lhsT=cbv("rrot"), rhs=xb[:, 0:n], start=True, stop=True),
           r=[("rt_xb", sl), "cb"], w=[("ps", b2)])
        dve(lambda e: e.tensor_tensor(out=t2[:, 0:n], in0=ps[b2][:, 0:n], in1=sinv[:, c0:c0 + n], op=ALU.mult),
            r=[("ps", b2), "cf"], w=[("rt_t2", sl)])
        dve(lambda e: e.tensor_tensor(out=outb[:, c0:c0 + n], in0=t1[:, 0:n], in1=t2[:, 0:n], op=ALU.add),
            r=[("rt_t1", sl), ("rt_t2", sl)], w=[outres])

    def stageA(h):
        j, ml = h // 2, h % 2
        if ml == 0:
            p2blocks[j] = [w_next() for _ in range(2)]
        blocks = p2blocks[j]
        g1 = G1[h % 2]
        gp = h % 2
        dma("sp", "ldk%d" % gp, g1["kr"], krs.ap()[h], r=[("krs", h)], w=[(tg, "kr", gp, c0) for (c0, n) in TT])
        dma("sp", "ldv%d" % gp, g1["vT"], vts.ap()[h], r=[("vts", h)], w=[(tg, "vT", gp, c0) for (c0, n) in TT])
        for bi in range(2):
            wv, wres = blocks[bi]
            pending = None
            for ti, (c0, n) in enumerate(TT):
                b = bank_d3()
                for k in range(16):
                    pe(lambda e, b=b, k=k, c0=c0, n=n, wv=wv, ml=ml: e.matmul(
                        ps[b][:, 0:n], lhsT=wv[:, k, ml * 128:(ml + 1) * 128], rhs=aT[:, k, c0:c0 + n],
                        start=(k == 0), stop=(k == 15)), r=[wres, "aT"], w=[("ps", b)])
                if bi == 0:
                    sl = rot_a(b, c0, n)
                    if pending is not None:
                        rot_b(*pending)
                    pending = (sl, c0, n, g1["qr"], (tg, "qr", gp, c0))
                else:
                    act(lambda e, b=b, c0=c0, n=n, g1=g1: e.activation(out=g1["sg"][:, c0:c0 + n], in_=ps[b][:, 0:n],
                                                                       func=AF.Silu), r=[("ps", b)], w=[(tg, "sg", gp, c0)])
                if ti == 2:
                    if pending is not None:
                        rot_b(*pending)
                    if ml == 1:
                        w_done()
                yield

    def stageB(h):
        g1 = G1[h % 2]
        gp = h % 2
        q_res = [(tg, "qr", gp, c0) for (c0, n) in TT]
        k_res = [(tg, "kr", gp, c0) for (c0, n) in TT]
        v_res = [(tg, "vT", gp, c0) for (c0, n) in TT]
        g_res = [(tg, "sg", gp, c0) for (c0, n) in TT]
        dve(lambda e: e.tensor_tensor(out=R["qd"], in0=g1["qr"][:, 0:NP_].rearrange("p (a c) -> p a c", a=8),
                                      in1=qdecv[:, h:h + 1, :].broadcast_to([128, 8, 128]), op=ALU.mult),
            r=q_res + ["cf"], w=[(tg, "qd")])
        for half in range(2):
            for cc in range(4):
                c = half * 4 + cc
                pe(lambda e, cc=cc, c=c: e.transpose(psb(M0)[:, cc * 128:(cc + 1) * 128],
                                                    g1["kr"][:, c * 128:(c + 1) * 128], ident_b),
                   r=k_res + ["cb"], w=[("ps", M0)])
            dve(lambda e, half=half: e.tensor_scalar(
                out=R["kd"][:, half * 4:(half + 1) * 4, :], in0=psb(M0)[:, 0:512].rearrange("p (a c) -> p a c", a=4),
                scalar1=kdecv[:, h:h + 1], scalar2=None, op0=ALU.mult), r=[("ps", M0), "cf"], w=[(tg, "kd")])
            for cc in range(4):
                c = half * 4 + cc
                pe(lambda e, cc=cc, c=c: e.transpose(psb(M1)[:, cc * 128:(cc + 1) * 128],
                                                    g1["vT"][:, c * 128:(c + 1) * 128], ident_b),
                   r=v_res + ["cb"], w=[("ps", M1)])
            act(lambda e, half=half: e.copy(out=R["vtok"][:, half * 4:(half + 1) * 4, :],
                                            in_=psb(M1)[:, 0:512].rearrange("p (a c) -> p a c", a=4)),
                r=[("ps", M1)], w=[(tg, "vtok")])
        act(lambda e: e.copy(out=R["Sb"][:, 0, :], in_=S0[:, h, :]), r=[("S0", h)], w=[(tg, "Sb", 0)])
        yield
        g128 = float(GAMMA[h] ** 128)
        for half in range(2):
            for cc in range(4):
                c = half * 4 + cc
                pe(lambda e, cc=cc, c=c: e.matmul(ps[M2][:, cc * 128:(cc + 1) * 128], lhsT=R["kd"][:, c, :],
                                                 rhs=R["vtok"][:, c, :], start=True, stop=True),
                   r=[(tg, "kd"), (tg, "vtok")], w=[("ps", M2)])
            for cc in range(4):
                c = half * 4 + cc
                prev = S0[:, h, :] if c == 0 else R["Srun"][:, (c - 1) % 2, :]
                prev_res = ("S0", h) if c == 0 else (tg, "Srun", (c - 1) % 2)
                dve(lambda e, cc=cc, c=c, prev=prev: e.scalar_tensor_tensor(
                    out=R["Srun"][:, c % 2, :], in0=prev, scalar=g128, in1=ps[M2][:, cc * 128:(cc + 1) * 128],
                    op0=ALU.mult, op1=ALU.add), r=[prev_res, ("ps", M2)], w=[(tg, "Srun", c % 2)])
                if c < 7:
                    act(lambda e, c=c: e.copy(out=R["Sb"][:, c + 1, :], in_=R["Srun"][:, c % 2, :]),
                        r=[(tg, "Srun", c % 2)], w=[(tg, "Sb", c + 1)])
                else:
                    dma("sp", "o_rs", rstate[h], R["Srun"][:, c % 2, :], r=[(tg, "Srun", c % 2)], w=[("rstate", h)])
        yield
        for half, bsc in ((0, M0), (1, M1)):
            for cc in range(4):
                c = half * 4 + cc
                pe(lambda e, bsc=bsc, cc=cc, c=c: e.matmul(ps[bsc][:, cc * 128:(cc + 1) * 128],
                                                          lhsT=g1["kr"][:, c * 128:(c + 1) * 128],
                                                          rhs=g1["qr"][:, c * 128:(c + 1) * 128], start=True, stop=True),
                   r=k_res + q_res, w=[("ps", bsc)])
            dve(lambda e, bsc=bsc, half=half: e.tensor_tensor(
                out=R["scm"][:, half * 4:(half + 1) * 4, :], in0=ps[bsc][:, :].rearrange("p (a c) -> p a c", a=4),
                in1=intraT[:, h:h + 1, :].broadcast_to([128, 4, 128]), op=ALU.mult),
                r=[("ps", bsc), "cf"], w=[(tg, "scm", half)])
        yield
        obanks = [M0, M1]
        for half in range(2):
            bo = obanks[half]
            for cc in range(4):
                c = half * 4 + cc
                pe(lambda e, bo=bo, cc=cc, c=c: e.matmul(ps[bo][:, cc * 128:(cc + 1) * 128], lhsT=R["vtok"][:, c, :],
                                                        rhs=R["scm"][:, c, :], start=(cc == 0), stop=False,
                                                        skip_group_check=True),
                   r=[(tg, "vtok"), (tg, "scm", half)], w=[("ps", bo)])
        pe(lambda e: e.transpose(psb(M2)[0:NS, 0:128], g1["kr"][:, NP_:NT], ident_b), r=k_res + ["cb"], w=[("ps", M2)])
        pe(lambda e: e.transpose(psb(M2)[0:NS, 128:256], g1["vT"][:, NP_:NT], ident_b), r=v_res + ["cb"], w=[("ps", M2)])
        act(lambda e: e.mul(out=R["ktok_s"][0:NS, :], in_=psb(M2)[0:NS, 0:128], mul=RET_K_SCALE),
            r=[("ps", M2)], w=[(tg, "ktok_s")])
        act(lambda e: e.copy(out=R["vtok_s"][0:NS, :], in_=psb(M2)[0:NS, 128:256]), r=[("ps", M2)], w=[(tg, "vtok_s")])
        for sm in range(NS):
            dma("sp", "st%d" % sm, Sst[sm], st[sm, h], w=[("Sst", sm)])
            dve(lambda e, sm=sm: e.tensor_scalar(out=vm[sm][0:NS, :], in0=R["vtok_s"][0:NS, :],
                                                 scalar1=cfv("onehot")[0:NS, sm:sm + 1], scalar2=None, op0=ALU.mult),
                r=[(tg, "vtok_s"), "cf"], w=[("vm", sm)])
        yield
        for half in range(2):
            bo = obanks[half]
            for cc in range(4):
                c = half * 4 + cc
                pe(lambda e, bo=bo, cc=cc, c=c: e.matmul(ps[bo][:, cc * 128:(cc + 1) * 128], lhsT=R["Sb"][:, c, :],
                                                        rhs=R["qd"][:, c, :], start=False, stop=True,
                                                        skip_group_check=True),
                   r=[(tg, "Sb", c), (tg, "qd")], w=[("ps", bo)])

        def o_read(bo, c0, n):
            act(lambda e: e.activation(out=R["sq"][:, c0:c0 + n], in_=ps[bo][:, 0:n], func=AF.Square),
                r=[("ps", bo)], w=[(tg, "sq", c0)])
            dve(lambda e: e.tensor_copy(out=R["o_sb"][:, c0:c0 + n], in_=ps[bo][:, 0:n]),
                r=[("ps", bo)], w=[(tg, "o_sb", c0)])
        o_read(M0, 0, 512)
        o_read(M1, 512, 512)
        yield
        for sm in range(NS):
            bu = M0 if sm % 2 == 0 else M1
            pe(lambda e, bu=bu, sm=sm: e.matmul(ps[bu][:, 0:128], lhsT=R["ktok_s"][0:NS, :], rhs=vm[sm][0:NS, :],
                                               start=True, stop=True), r=[(tg, "ktok_s"), ("vm", sm)], w=[("ps", bu)])
            dve(lambda e, bu=bu, sm=sm: e.scalar_tensor_tensor(out=Snew[sm], in0=Sst[sm], scalar=float(GAMMA[h]),
                                                               in1=ps[bu][:, 0:128], op0=ALU.mult, op1=ALU.add),
                r=[("Sst", sm), ("ps", bu)], w=[("Snew", sm)])
            dma("sp", "o_ss%d" % sm, ss_out[sm, h], Snew[sm], r=[("Snew", sm)], w=[("ss_out", sm, h)])
            act(lambda e, sm=sm: e.copy(out=Sbs[sm], in_=Snew[sm]), r=[("Snew", sm)], w=[("Sbs", sm)])
        yield
        for sm in range(NS):
            pe(lambda e, sm=sm: e.matmul(ps[M2][:, sm:sm + 1], lhsT=Sbs[sm], rhs=g1["qr"][:, NP_ + sm:NP_ + sm + 1],
                                        start=True, stop=True), r=[("Sbs", sm)] + q_res, w=[("ps", M2)])
        o_read(M2, NP_, NS)
        yield
        for pi, (c0, n) in enumerate(((0, 512), (512, 512), (NP_, NS))):
            bm_ = M0 if pi % 2 == 0 else M1
            pe(lambda e, bm_=bm_, c0=c0, n=n: e.matmul(ps[bm_][:, 0:n], lhsT=cbv("ones128"), rhs=R["sq"][:, c0:c0 + n],
                                                      start=True, stop=True), r=[(tg, "sq", c0), "cb"], w=[("ps", bm_)])
            act(lambda e, bm_=bm_, c0=c0, n=n: e.activation(out=R["rstd"][:, c0:c0 + n], in_=ps[bm_][:, 0:n], func=AF.Ln,
                                                            bias=epsv, scale=1.0), r=[("ps", bm_)], w=[(tg, "rstd", c0)])
            act(lambda e, c0=c0, n=n: e.activation(out=R["rstd"][:, c0:c0 + n], in_=R["rstd"][:, c0:c0 + n], func=AF.Exp,
                                                   scale=-0.5), r=[(tg, "rstd", c0)], w=[(tg, "rstd", c0)])
            dve(lambda e, c0=c0, n=n: e.scalar_tensor_tensor(
                out=R["tt"][:, c0:c0 + n], in0=R["o_sb"][:, c0:c0 + n], scalar=rgv[:, h:h + 1],
                in1=R["rstd"][:, c0:c0 + n], op0=ALU.mult, op1=ALU.mult),
                r=[(tg, "o_sb", c0), (tg, "rstd", c0), "cf"], w=[(tg, "tt", c0)])
            pool(lambda e, c0=c0, n=n: e.tensor_tensor(out=mixT[:, 8 + h, c0:c0 + n], in0=R["tt"][:, c0:c0 + n],
                                                       in1=g1["sg"][:, c0:c0 + n], op=ALU.mult),
                 r=[(tg, "tt", c0)] + g_res, w=["mixT"])
        yield

    def drive(gens):
        gens = [g for g in gens if g is not None]
        while gens:
            for g in list(gens):
                try:
                    next(g)
                except StopIteration:
                    gens.remove(g)

    def drive2(gb, ga):
        pat = [1, 1, 1, 0, 1, 0, 1, 1, 9, 9]
        bi = 0
        b_alive, a_alive = True, ga is not None
        while b_alive or a_alive:
            if b_alive:
                try:
                    next(gb)
                except StopIteration:
                    b_alive = False
            na = pat[min(bi, len(pat) - 1)] if b_alive else 99
            bi += 1
            for _ in range(na):
                if not a_alive:
                    break
                try:
                    next(ga)
                except StopIteration:
                    a_alive = False

    drive([stageA(0)])
    for h in range(8):
        drive2(stageB(h), stageA(h + 1) if h + 1 < 8 else None)
    P.fence()
    dump("mixT", mixT, ["mixT"])
    chk(6)

    Zt = Alloc(T2base, T2base + 12 * 1024)
    Zx = Alloc(0, 0)
    xs_off = None
    xstage = [aT[:, 0:8, :].rearrange("p a c -> p (a c)").bitcast(F32)[:, 0:D],
              aT[:, 8:16, :].rearrange("p a c -> p (a c)").bitcast(F32)[:, 0:D]]
    tiles_x = [(x_main[i * 128:(i + 1) * 128, :], 128, i * 128) for i in range(8)] + [(x_smp, NS, NP_)]
    for i, (src, rows, c0) in enumerate(tiles_x):
        s = i % 2
        dma("sp", "x%d" % s, xstage[s][0:rows, :], src, w=[("xstage", s)])
        for q4 in range(4):
            b = bank_m()
            for kk in range(4):
                k = q4 * 4 + kk
                pe(lambda e, b=b, kk=kk, k=k, s=s, rows=rows: e.transpose(
                    ps[b][:, kk * 128:kk * 128 + rows], xstage[s][0:rows, k * 128:(k + 1) * 128],
                    ident_f[0:rows, 0:rows]), r=[("xstage", s), "cf"], w=[("ps", b)])
            src_ps = lambda b=b, rows=rows: ps[b][:, :].rearrange("p (a c) -> p a c", a=4)[:, :, 0:rows]
            dst = hT[:, q4 * 4:(q4 + 1) * 4, c0:c0 + rows]
            if q4 % 2 == 0:
                act(lambda e, dst=dst, src_ps=src_ps: e.copy(out=dst, in_=src_ps()), r=[("ps", b)], w=[("hT", q4)])
            else:
                dve(lambda e, dst=dst, src_ps=src_ps: e.tensor_copy(out=dst, in_=src_ps()), r=[("ps", b)], w=[("hT", q4)])
    P.fence()

    sqs = [Zt([344], BF16) for _ in range(4)]
    rstdn = Zt([NT], F32)
    fT = aT[:, :, 0:NT]

    class NormState:
        pass

    def norm_begin(tag):
        st_ = NormState()
        st_.tag = tag
        st_.banks = [bank_m() for _ in TT]
        st_.pending = None
        st_.cnt = 0
        return st_

    def norm_flush(st_):
        if st_.pending is not None:
            slot, m, ti, n = st_.pending
            pe(lambda e: e.matmul(ps[st_.banks[ti]][:, 0:n], lhsT=cbv("onesD"), rhs=sqs[slot][:, 0:n],
                                  start=(m == 0), stop=(m == 15)),
               r=[("sqs", slot), "cb"], w=[("ps", st_.banks[ti])])
            st_.pending = None

    def norm_tile(st_, m, ti, c0, n):
        norm_flush(st_)
        slot = st_.cnt % 4
        st_.cnt += 1
        act(lambda e: e.activation(out=sqs[slot][:, 0:n], in_=hT[:, m, c0:c0 + n], func=AF.Square),
            r=[("hTm", m, c0)], w=[("sqs", slot)])
        st_.pending = (slot, m, ti, n)

    def norm_finish(st_, gname):
        tag = st_.tag
        norm_flush(st_)
        for ti, (c0, n) in enumerate(TT):
            act(lambda e, ti=ti, c0=c0, n=n: e.activation(out=rstdn[:, c0:c0 + n], in_=ps[st_.banks[ti]][:, 0:n], func=AF.Ln,
                                                          bias=epsv, scale=1.0),
                r=[("ps", st_.banks[ti])], w=[(tag, "rstdn0", ti)])
            act(lambda e, ti=ti, c0=c0, n=n: e.activation(out=rstdn[:, c0:c0 + n], in_=rstdn[:, c0:c0 + n], func=AF.Exp,
                                                          scale=-0.5),
                r=[(tag, "rstdn0", ti)], w=[(tag, "rstdn")])
        gv = cfv(gname)
        for k in range(16):
            dve(lambda e, k=k: e.scalar_tensor_tensor(out=fT[:, k, :], in0=hT[:, k, :], scalar=gv[:, k:k + 1], in1=rstdn,
                                                      op0=ALU.mult, op1=ALU.mult),
                r=[(tag, "rstdn"), "cf"] + [("hTm", k, c0) for (c0, n) in TT], w=[("fT", k)])

    mix_rhs = lambda k, c0, n: mixT[:, k, c0:c0 + n]
    n2 = norm_begin("n2")
    for j in range(8):
        def evac_out(ml, ti, c0, n, b, j=j):
            m = 2 * j + ml
            dve(lambda e: e.tensor_tensor(out=hT[:, m, c0:c0 + n], in0=ps[b][:, 0:n], in1=hT[:, m, c0:c0 + n], op=ALU.add),
                r=[("ps", b), ("hTm", m, c0)], w=[("hTm", m, c0)])
            norm_tile(n2, m, ti, c0, n)
        dense_block(16, mix_rhs, ["mixT"], evac_out)
    P.fence()
    dump("h1", hT, [])
    chk(7)

    norm_finish(n2, "fn_g")
    P.fence()
    dump("fT", fT, [])
    chk(8)

    uT = mixT
    sgt = [Zt([344], F32) for _ in range(2)]
    f_rhs = lambda k, c0, n: fT[:, k, c0:c0 + n]
    cnt_s = {"i": 0}
    for qi, (b0, nb) in enumerate(QUART):
        for bb in range(nb):
            wg, wgres = w_next()
            wu, wures = w_next()
            for ml in range(2):
                cl = 2 * bb + ml
                for ti, (c0, n) in enumerate(TT):
                    bg = bank_d(); bu = bank_d()
                    for (bk_, wv, wres) in ((bg, wg, wgres), (bu, wu, wures)):
                        for k in range(16):
                            pe(lambda e, bk_=bk_, wv=wv, k=k, ml=ml, c0=c0, n=n: e.matmul(
                                ps[bk_][:, 0:n], lhsT=wv[:, k, ml * 128:(ml + 1) * 128], rhs=fT[:, k, c0:c0 + n],
                                start=(k == 0), stop=(k == 15)), r=[wres], w=[("ps", bk_)])
                    s = cnt_s["i"] % 2
                    cnt_s["i"] += 1
                    act(lambda e, bg=bg, s=s, n=n: e.activation(out=sgt[s][:, 0:n], in_=ps[bg][:, 0:n], func=AF.Silu),
                        r=[("ps", bg)], w=[("sgt", s)])
                    dve(lambda e, bu=bu, s=s, n=n, cl=cl, c0=c0: e.tensor_tensor(out=uT[:, cl, c0:c0 + n], in0=ps[bu][:, 0:n],
                                                                               in1=sgt[s][:, 0:n], op=ALU.mult),
                        r=[("ps", bu), ("sgt", s)], w=[("uT", qi)])
            w_done(2)
        kq = 2 * nb
        u_rhs = lambda k, c0, n: uT[:, k, c0:c0 + n]
        if qi == 3:
            n3 = norm_begin("n3")
        for j in range(8):
            def evac_dn(ml, ti, c0, n, b, j=j):
                m = 2 * j + ml
                dve(lambda e: e.tensor_tensor(out=hT[:, m, c0:c0 + n], in0=ps[b][:, 0:n], in1=hT[:, m, c0:c0 + n],
                                              op=ALU.add), r=[("ps", b), ("hTm", m, c0)], w=[("hTm", m, c0)])
                if qi == 3:
                    norm_tile(n3, m, ti, c0, n)
            dense_block(kq, u_rhs, [("uT", qi)], evac_dn)
    P.fence()
    dump("h2", hT, [])
    chk(9)

    norm_finish(n3, "pn_g")
    pst = [uT[:, 0:1, :].rearrange("p a c -> p (a c)").bitcast(F32)[:, 0:256],
           uT[:, 1:2, :].rearrange("p a c -> p (a c)").bitcast(F32)[:, 0:256]]
    peT = uT[:, 4:6, :]
    tiles_p = [(p_main[i * 128:(i + 1) * 128, :], 128, i * 128) for i in range(8)] + [(p_smp, NS, NP_)]
    for i, (src, rows, c0) in enumerate(tiles_p):
        s = i % 2
        dma("sp", "x%d" % s, pst[s][0:rows, :], src, w=[("pst", s)])
        b = bank_m()
        for kk in range(2):
            pe(lambda e, b=b, kk=kk, s=s, rows=rows: e.transpose(ps[b][:, kk * 128:kk * 128 + rows],
                                                                pst[s][0:rows, kk * 128:(kk + 1) * 128],
                                                                ident_f[0:rows, 0:rows]), r=[("pst", s), "cf"], w=[("ps", b)])
        act(lambda e, b=b, rows=rows, c0=c0: e.copy(out=peT[:, :, c0:c0 + rows],
                                                   in_=ps[b][:, 0:256].rearrange("p (a c) -> p a c", a=2)[:, :, 0:rows]),
            r=[("ps", b)], w=["peT"])
    P.fence()
    wple = uT[:, 8:12, :].rearrange("p a c -> p (a c)")[:, 0:4096].rearrange("p (k n) -> p k n", k=2)
    wpleres = "wple"
    dma("pool", "wple", wple, w_ple.rearrange("(k p) n -> p k n", p=128), w=["wple"])
    sB = [uT[:, 12 + i, :].bitcast(F32)[:, 0:344] for i in range(2)]
    tA = [uT[:, 14 + i, :].bitcast(F32)[:, 0:344] for i in range(2)]
    for j in range(8):
        wv, wres = w_next()
        for ml in range(2):
            m = 2 * j + ml
            for ti, (c0, n) in enumerate(TT):
                bA = 2 * (cnt_s["i"] % 4); bB = bA + 1
                for k in range(2):
                    pe(lambda e, bA=bA, k=k, m=m, c0=c0, n=n: e.matmul(ps[bA][:, 0:n], lhsT=wple[:, k, m * 128:(m + 1) * 128],
                                                                      rhs=peT[:, k, c0:c0 + n], start=(k == 0), stop=(k == 1)),
                       r=[wpleres, "peT"], w=[("ps", bA)])
                for k in range(16):
                    pe(lambda e, bB=bB, k=k, ml=ml, c0=c0, n=n, wv=wv: e.matmul(
                        ps[bB][:, 0:n], lhsT=wv[:, k, ml * 128:(ml + 1) * 128], rhs=fT[:, k, c0:c0 + n],
                        start=(k == 0), stop=(k == 15)), r=[wres], w=[("ps", bB)])
                s = cnt_s["i"] % 2
                cnt_s["i"] += 1
                act(lambda e, bB=bB, s=s, n=n: e.activation(out=sB[s][:, 0:n], in_=ps[bB][:, 0:n], func=AF.Sigmoid),
                    r=[("ps", bB)], w=[("sB", s)])
                dve(lambda e, bA=bA, s=s, n=n: e.tensor_tensor(out=tA[s][:, 0:n], in0=ps[bA][:, 0:n], in1=sB[s][:, 0:n],
                                                              op=ALU.mult), r=[("ps", bA), ("sB", s)], w=[("tA", s)])
                dve(lambda e, s=s, n=n, m=m, c0=c0: e.tensor_tensor(out=hT[:, m, c0:c0 + n], in0=tA[s][:, 0:n],
                                                                   in1=hT[:, m, c0:c0 + n], op=ALU.add),
                    r=[("tA", s), ("hTm", m, c0)], w=[("hTm", m, c0)])
        w_done()
    P.fence()
    dump("h3", hT, [])
    chk(10)

    ystage = [aT[:, 0:8, :].rearrange("p a c -> p (a c)").bitcast(F32)[:, 0:D],
              aT[:, 8:16, :].rearrange("p a c -> p (a c)").bitcast(F32)[:, 0:D]]
    tiles_y = [(y_main[i * 128:(i + 1) * 128, :], 128, i * 128) for i in range(8)] + [(y_smp, NS, NP_)]
    for i, (dst, rows, c0) in enumerate(tiles_y):
        s = i % 2
        for q4 in range(4):
            b = bank_m()
            for kk in range(4):
                k = q4 * 4 + kk
                pe(lambda e, b=b, kk=kk, k=k, rows=rows, c0=c0: e.transpose(ps[b][0:rows, kk * 128:(kk + 1) * 128],
                                                                           hT[:, k, c0:c0 + rows], ident_f),
                   r=["cf"], w=[("ps", b)])
            if q4 % 2 == 0:
                act(lambda e, b=b, s=s, rows=rows, q4=q4: e.copy(out=ystage[s][0:rows, q4 * 512:(q4 + 1) * 512],
                                                                in_=ps[b][0:rows, :]), r=[("ps", b)], w=[("ystage", s, q4)])
            else:
                dve(lambda e, b=b, s=s, rows=rows, q4=q4: e.tensor_copy(out=ystage[s][0:rows, q4 * 512:(q4 + 1) * 512],
                                                                       in_=ps[b][0:rows, :]), r=[("ps", b)],
                    w=[("ystage", s, q4)])
        dma("sp", "y%d" % s, dst, ystage[s][0:rows, :], r=[("ystage", s, q4) for q4 in range(4)], w=[("y", i)])

    assert stop != 99 or wstate["used"] == len(wq), (wstate, len(wq))

    sems = {e: es.enter_context(nc.semaphore("s_" + e)) for e in Prog.ENGS}
    dma_sems = {k: es.enter_context(nc.semaphore("d_" + k)) for k in sorted(dma_sem_names)}
    block = es.enter_context(nc.Block())
    finals = sorted(dma_sem_names)
    P.emit(nc, block, sems, dma_sems, finals)
    es.close()
    print("ops", len(P.ops), "counts", P.final_counts[0])
    return nc


_CACHE = {}


def kernel(**inputs):
    inp = {k: np.asarray(v) for k, v in inputs.items()}
    if "nc" not in _CACHE:
        _CACHE["nc"] = build_program()
    nc = _CACHE["nc"]
    xp = inp["x_prompt"]; xs = inp["x_sample"]
    in_maps = []
    zeros_halo = np.zeros((NHALO, D), np.float32)
    shared = dict(
        an_g=np.ascontiguousarray(inp["attn_norm_g"].reshape(1, D)),
        w_in=np.ascontiguousarray(inp["w_in"][0]), w_out=np.ascontiguousarray(inp["w_out"][0]),
        w_gate=np.ascontiguousarray(inp["w_gate"][0]), w_up=np.ascontiguousarray(inp["w_up"][0]),
        w_down=np.ascontiguousarray(inp["w_down"][0]), w_ple=np.ascontiguousarray(inp["w_ple"][0]),
        w_pg=np.ascontiguousarray(inp["w_ple_gate"][0]),
    )
    for c in range(8):
        b, m = c // 4, c % 4
        t0 = m * 1024
        cf, cb = host_consts(c, inp)
        d = dict(shared)
        d.update(
            x_main=np.ascontiguousarray(xp[b, t0:t0 + 1024]),
            x_halo=np.ascontiguousarray(xp[b, t0 - 128:t0]) if m > 0 else zeros_halo,
            x_smp=np.ascontiguousarray(xs[4 * c:4 * c + 4, 0]),
            p_main=np.ascontiguousarray(inp["p_prompt"][0, b, t0:t0 + 1024]),
            p_smp=np.ascontiguousarray(inp["p_sample"][0, 4 * c:4 * c + 4, 0]),
            ck=np.ascontiguousarray(inp["cache_k_win"][0, 4 * c:4 * c + 4]),
            cv=np.ascontiguousarray(inp["cache_v_win"][0, 4 * c:4 * c + 4]),
            st=np.ascontiguousarray(inp["state_ret"][0, 4 * c:4 * c + 4]),
            cf=cf, cb=cb,
        )
        in_maps.append(d)
    res = run_bass_kernel_spmd(nc, in_maps, core_ids=list(range(8)))
    R = res.results
    _CACHE["last"] = R
    y_p = np.stack([np.concatenate([R[4 * b + m]["y_main"] for m in range(4)], 0) for b in range(2)], 0)
    y_s = np.concatenate([R[c]["y_smp"] for c in range(8)], 0)[:, None, :]
    kwp = np.stack([R[4 * b + 3]["kwin"] for b in range(2)], 0)[None]
    vwp = np.stack([R[4 * b + 3]["vwin"] for b in range(2)], 0)[None]
    rsp = np.stack([R[4 * b + 3]["rstate"] for b in range(2)], 0)[None]
    kws = np.concatenate([R[c]["ks_out"] for c in range(8)], 0)[None]
    vws = np.concatenate([R[c]["vs_out"] for c in range(8)], 0)[None]
    rss = np.concatenate([R[c]["ss_out"] for c in range(8)], 0)[None]
    return (y_p.astype(np.float32), y_s.astype(np.float32), kwp.astype(np.float32), vwp.astype(np.float32),
            rsp.astype(np.float32), kws.astype(np.float32), vws.astype(np.float32), rss.astype(np.float32))
```

```python
import math
from contextlib import ExitStack

import numpy as np
import ml_dtypes

import concourse.bass as bass
import concourse.mybir as mybir
from concourse.bass_utils import run_bass_kernel_spmd

F32 = mybir.dt.float32
BF16 = mybir.dt.bfloat16
U8 = mybir.dt.uint8
AF = mybir.ActivationFunctionType
ALU = mybir.AluOpType
AX = mybir.AxisListType

D = 2048
NP_ = 1024
NS = 4
NT = NP_ + NS
NHALO = 128
NA = NT + NHALO
KC = 16
DFF = 5632
EPS = 1e-6
ATTN_SCALE = 128 ** -0.5
RET_K_SCALE = 128 ** -0.5
PAST_LEN = 16384
TT = [(0, 344), (344, 344), (688, 340)]
TT_H = TT + [(NT, NHALO)]
NSLOT = 4
SLOT_BYTES = 8192
GAMMA = [1.0 - 2.0 ** (-5 - h) for h in range(8)]


class Op:
    __slots__ = ("eng", "fn", "dma", "idx", "waits", "sig", "val", "fence")

    def __init__(self, eng, fn, dma, idx):
        self.eng = eng
        self.fn = fn
        self.dma = dma
        self.idx = idx
        self.waits = []
        self.sig = dma is not None
        self.val = None
        self.fence = None


class Prog:
    ENGS = ("sp", "act", "dve", "pool", "pe")

    def __init__(self):
        self.ops = []
        self.last_w = {}
        self.readers = {}
        self.last_on = {}
        self.dma_ops = {}
        self.pending_fence = {}

    def add(self, eng, fn, r=(), w=(), dma=None, nofence=False):
        op = Op(eng, fn, dma, len(self.ops))
        psr = [x for x in r if isinstance(x, tuple) and x[0] == "ps"]
        if psr:
            r = [x for x in r if not (isinstance(x, tuple) and x[0] == "ps")]
            w = list(w) + psr
        deps = {}
        for res in r:
            lw = self.last_w.get(res)
            if lw is not None:
                deps.setdefault(lw, set()).add("raw")
        for res in w:
            lw = self.last_w.get(res)
            if lw is not None:
                deps.setdefault(lw, set()).add("waw")
            for rd in self.readers.get(res, ()):
                deps.setdefault(rd, set()).add("war")
        best = {}
        for d, kinds in deps.items():
            if d is op:
                continue
            if d.dma is not None:
                op.waits.append(d)
                continue
            if op.dma is None and d.eng == eng:
                if eng == "pe":
                    continue
            b = best.get(d.eng)
            if b is None or d.idx > b.idx:
                best[d.eng] = d
        for d in best.values():
            d.sig = True
            op.waits.append(d)
        for res in r:
            self.readers.setdefault(res, []).append(op)
        for res in w:
            self.last_w[res] = op
            self.readers[res] = []
        if eng in self.pending_fence and not nofence:
            op.fence = self.pending_fence.pop(eng)
        self.ops.append(op)
        if dma is None:
            self.last_on[eng] = op
        else:
            self.dma_ops.setdefault(dma, []).append(op)
        return op

    def fence(self):
        st = {"comp": dict(self.last_on), "dma": {k: v[-1] for k, v in self.dma_ops.items()}}
        for o in st["comp"].values():
            o.sig = True
        for e in self.ENGS:
            self.pending_fence[e] = st

    def emit(self, nc, block, sems, dma_sems, final_waits):
        cnt = {e: 0 for e in self.ENGS}
        dcnt = {}
        for op in self.ops:
            if op.dma is not None:
                dcnt[op.dma] = dcnt.get(op.dma, 0) + (1 if op.dma == "cc" else 16)
                op.val = dcnt[op.dma]
            elif op.sig:
                cnt[op.eng] += 1
                op.val = cnt[op.eng]
        self.final_counts = (cnt, dcnt)

        def semof(op):
            return dma_sems[op.dma] if op.dma is not None else sems[op.eng]

        def run(engname):
            def body(eng):
                waited = {}

                def wait(sem_key, sem, val):
                    if waited.get(sem_key, 0) >= val:
                        return
                    waited[sem_key] = val
                    eng.wait_ge(sem, val)

                for op in self.ops:
                    if op.eng != engname:
                        continue
                    if op.fence is not None:
                        for o in op.fence["comp"].values():
                            if o.eng != engname or engname != "pe":
                                wait(("c", o.eng), sems[o.eng], o.val)
                        for k, o in op.fence["dma"].items():
                            wait(("d", k), dma_sems[k], o.val)
                    for d in op.waits:
                        key = ("d", d.dma) if d.dma is not None else ("c", d.eng)
                        wait(key, semof(d), d.val)
                    ins = op.fn(eng)
                    if op.dma is not None:
                        ins.then_inc(dma_sems[op.dma], 1 if op.dma == "cc" else 16)
                    elif op.sig:
                        ins.then_inc(sems[op.eng], 1)
                if engname == "sp":
                    for k in final_waits:
                        if k in dcnt:
                            eng.wait_ge(dma_sems[k], dcnt[k])
            return body

        block.sync(run("sp"))
        block.scalar(run("act"))
        block.vector(run("dve"))
        block.gpsimd(run("pool"))
        block.tensor(run("pe"))


CF = {}
_off = 0
for _n, _w in [("ident", 128), ("ones_row", 128), ("fn_g", 16), ("pn_g", 16), ("rg", 8), ("qg", 1), ("kg", 1),
               ("sinks", 8), ("intraT", 1024), ("qdec", 1024), ("kdec", 8), ("kdlong", 64), ("coef", 32),
               ("onehot", 4), ("cos", NT), ("sin", NT)]:
    CF[_n] = (_off, _w)
    _off += _w
NCF = _off
CB = {}
_off = 0
for _n, _w in [("ident", 128), ("ones", 128), ("onesD", 128), ("ones128", 128), ("rrot", 128), ("mask", 384)]:
    CB[_n] = (_off, _w)
    _off += _w
NCB = _off


def host_consts(core, inp):
    m = core % 4
    cf = np.zeros((128, NCF), np.float32)

    def put(name, arr):
        o, w = CF[name]
        cf[:, o:o + w] = np.asarray(arr, np.float32).reshape(128, w) if np.ndim(arr) == 2 else np.broadcast_to(
            np.asarray(arr, np.float32).reshape(1, w), (128, w))

    put("ident", np.eye(128, dtype=np.float32))
    put("ones_row", np.ones((128, 128), np.float32))
    put("fn_g", inp["ffn_norm_g"][0].reshape(16, 128).T)
    put("pn_g", inp["ple_norm_g"][0].reshape(16, 128).T)
    put("rg", inp["ret_out_g"][0].reshape(8, 128).T)
    put("qg", inp["q_norm_g"][0].reshape(128, 1))
    put("kg", inp["k_norm_g"][0].reshape(128, 1))
    put("sinks", inp["attn_sinks"][0].reshape(8))
    g = np.array(GAMMA, np.float64)
    j = np.arange(128)
    diff = j[None, :] - j[:, None]
    intraT = np.where(diff[:, None, :] >= 0, g[None, :, None] ** np.maximum(diff, 0)[:, None, :], 0.0) * RET_K_SCALE
    put("intraT", intraT.reshape(128, 1024))
    qdec = g[:, None] ** (j[None, :] + 1.0)
    put("qdec", qdec.reshape(1024))
    put("kdec", (g[None, :] ** (127.0 - j[:, None])) * RET_K_SCALE)
    c = np.arange(8)
    kdl = g[None, :, None] ** (1023.0 - (128.0 * c[None, None, :] + j[:, None, None])) * RET_K_SCALE
    put("kdlong", kdl.reshape(128, 64))
    coef = np.zeros((4, 8))
    for r in range(4):
        if r < m:
            coef[r] = g ** (1024.0 * (m - r - 1))
    put("coef", coef.reshape(32))
    oh = np.zeros((128, 4), np.float32)
    oh[:4] = np.eye(4)
    put("onehot", oh)
    pos = np.concatenate([m * 1024 + np.arange(1024), np.full(4, PAST_LEN)]).astype(np.float32)
    inv = (np.float32(10000.0) ** (-np.arange(64, dtype=np.float32) / np.float32(64))).astype(np.float32)
    ang = (pos[None, :] * inv[:, None]).astype(np.float32).astype(np.float64)
    cos = np.cos(ang)
    sin = np.sin(ang)
    put("cos", np.concatenate([cos, cos], 0))
    put("sin", np.concatenate([-sin, sin], 0))

    cb = np.zeros((128, NCB), np.float32)

    def putb(name, arr):
        o, w = CB[name]
        cb[:, o:o + w] = arr

    putb("ident", np.eye(128))
    putb("ones", np.ones((128, 128)))
    putb("onesD", np.full((128, 128), 1.0 / D))
    putb("ones128", np.full((128, 128), 1.0 / 128))
    rr = np.zeros((128, 128))
    for p in range(128):
        rr[(p + 64) % 128, p] = 1.0
    putb("rrot", rr)
    NEG = -30000.0
    own = np.where(j[:, None] <= j[None, :], 0.0, NEG).astype(np.float32)
    prev = np.where(j[:, None] >= j[None, :], 0.0, NEG).astype(np.float32)
    putb("mask", np.concatenate([own, prev, prev if m != 0 else np.full((128, 128), NEG, np.float32)], 1))
    return cf, cb.astype(ml_dtypes.bfloat16)


class _Stop(Exception):
    pass


def build_program(dbg=None, stop=99, groups=None):
    nc = bass.Bass("TRN2", target_bir_lowering=False)
    P = Prog()
    dbg = dbg or {}

    stopped = [False]

    def chk(k):
        if stop == k:
            stopped[0] = True

    def din(name, shape, dt=F32):
        return nc.dram_tensor(name, list(shape), dt, kind="ExternalInput").ap()

    def dout(name, shape, dt=F32):
        return nc.dram_tensor(name, list(shape), dt, kind="ExternalOutput").ap()

    x_main = din("x_main", [NP_, D]); x_halo = din("x_halo", [NHALO, D]); x_smp = din("x_smp", [NS, D])
    p_main = din("p_main", [NP_, 256]); p_smp = din("p_smp", [NS, 256])
    ck = din("ck", [NS, 128, 2, 128]); cv = din("cv", [NS, 128, 2, 128]); st = din("st", [NS, 8, 128, 128])
    an_g = din("an_g", [1, D])
    w_in = din("w_in", [D, DFF]); w_out = din("w_out", [D, D]); w_gate = din("w_gate", [D, DFF])
    w_up = din("w_up", [D, DFF]); w_down = din("w_down", [DFF, D]); w_ple = din("w_ple", [256, D])
    w_pg = din("w_pg", [D, D])
    cf_d = din("cf", [128, NCF]); cb_d = din("cb", [128, NCB], BF16)

    y_main = dout("y_main", [NP_, D]); y_smp = dout("y_smp", [NS, D])
    kwin = dout("kwin", [128, 2, 128]); vwin = dout("vwin", [128, 2, 128]); rstate = dout("rstate", [8, 128, 128])
    ks_out = dout("ks_out", [NS, 128, 2, 128]); vs_out = dout("vs_out", [NS, 128, 2, 128])
    ss_out = dout("ss_out", [NS, 8, 128, 128])
    krs = nc.dram_tensor("krs", [8, 128, NT], BF16)
    vts = nc.dram_tensor("vts", [8, 128, NT], BF16)
    ag_in = nc.dram_tensor("ag_in", [8 * 128, 128], F32)
    ag_out = nc.dram_tensor("ag_out", [4 * 8 * 128, 128], F32)
    dbg_out = {}
    for name, shape in dbg.items():
        dbg_out[name] = dout("dbg_" + name, shape)

    def dump(name, ap, r=()):
        if name in dbg_out:
            P.add("pool", lambda e: e.dma_start(out=dbg_out[name], in_=ap), r, [("dbg", name)], dma="o_dbg")

    es = ExitStack()
    total = (nc.sbuf_bytes_remaining - 64) // 64 * 64
    arena = es.enter_context(nc.sbuf_tensor("arena", [128, total], U8))
    ps = [es.enter_context(nc.psum_tensor("ps%d" % i, [128, 512], F32)) for i in range(8)]

    class Alloc:
        def __init__(self, base, limit):
            self.p = base
            self.limit = limit

        def __call__(self, shape, dt):
            esz = 4 if dt == F32 else 2
            n = int(np.prod(shape)) * esz
            off = (self.p + 31) // 32 * 32
            self.p = off + n
            assert self.p <= self.limit, (self.p, self.limit)
            v = arena[:, off:off + n].bitcast(dt)
            if len(shape) == 2:
                return v.rearrange("p (a b) -> p a b", a=shape[0])
            if len(shape) == 3:
                return v.rearrange("p (a b c) -> p a b c", a=shape[0], b=shape[1])
            return v

    A = Alloc(0, total)
    cf = A([NCF], F32)
    cb = A([NCB], BF16)
    negc = A([1], F32); esink = A([8], F32); S0 = A([8, 128], F32)
    misc = A([64], F32)
    wslots = [A([SLOT_BYTES // 2], BF16) for _ in range(NSLOT)]
    aT = A([KC, NA], BF16)
    mixT = A([KC, NT], BF16)
    T2base = A.p
    T2 = Alloc(T2base, T2base + 12 * 1024)
    A.p = T2base + 12 * 1024
    HB = (A.p + 31) // 32 * 32
    hT = A([KC, NT], F32)
    HEND = A.p
    print("sbuf used", A.p, "of", total)

    def cfv(name, lo=0, hi=None):
        o, w = CF[name]
        return cf[:, o + lo:o + (w if hi is None else hi)]

    def cbv(name, lo=0, hi=None):
        o, w = CB[name]
        return cb[:, o + lo:o + (w if hi is None else hi)]

    ident_f = cfv("ident"); ident_b = cbv("ident")

    def psb(i, n=1024):
        return ps[i][:, :].bitcast(BF16)[:, 0:n]

    def pe(fn, r=(), w=()): return P.add("pe", fn, r, w)
    def act(fn, r=(), w=()): return P.add("act", fn, r, w)
    def dve(fn, r=(), w=()): return P.add("dve", fn, r, w)
    def pool(fn, r=(), w=()): return P.add("pool", fn, r, w)
    def dma(q, sem, out, in_, r=(), w=(), nofence=False, slow=False):
        if slow:
            return P.add(q, lambda e: e.dma_start(out=out, in_=in_, allow_slow_non_contiguous=True), r, w, dma=sem)
        return P.add(q, lambda e: e.dma_start(out=out, in_=in_), r, w, dma=sem, nofence=nofence)

    dma_sem_names = set()
    _orig_add = P.add

    def add_track(eng, fn, r=(), w=(), dma=None, nofence=False):
        if stopped[0]:
            return None
        if dma is not None:
            dma_sem_names.add(dma)
        return _orig_add(eng, fn, r, w, dma, nofence)
    P.add = add_track

    rot = {"d": 0, "m": 0}

    att_dense3 = [False]

    def bank_d():
        b = rot["d"] % (3 if att_dense3[0] else 4)
        rot["d"] += 1
        return b

    def bank_m():
        b = 4 + rot["m"] % 4
        rot["m"] += 1
        return b

    wq = []
    wstate = {"issued": 0, "used": 0, "done": 0}
    w_extra = []

    def w_issue_upto(n):
        while wstate["issued"] < min(n, len(wq)):
            i = wstate["issued"]
            src, kc, ncols = wq[i]
            s = i % NSLOT
            dst = wslots[s][:, 0:kc * ncols].rearrange("p (k n) -> p k n", k=kc)
            dma("pool", "w%d" % s, dst, src.rearrange("(k p) n -> p k n", p=128), r=list(w_extra), w=[("w", s)], nofence=True)
            wstate["issued"] += 1

    def w_done(n=1):
        wstate["done"] += n
        w_issue_upto(wstate["done"] + NSLOT)

    def w_next():
        i = wstate["used"]
        assert i < wstate["done"] + NSLOT
        w_issue_upto(i + 1)
        src, kc, ncols = wq[i]
        s = i % NSLOT
        wstate["used"] += 1
        return wslots[s][:, 0:kc * ncols].rearrange("p (k n) -> p k n", k=kc), ("w", s)

    def wblk(wap, k0, k1, c0, ncols):
        return (wap[k0 * 128:k1 * 128, c0:c0 + ncols], k1 - k0, ncols)

    C_AQ, C_AK, C_AV, C_RQ, C_RK, C_RV, C_RG = 0, 1024, 1280, 1536, 2560, 3584, 4608
    for j in range(4):
        wq.append(wblk(w_in, 0, 16, C_RK + 256 * j, 256))
        wq.append(wblk(w_in, 0, 16, C_RV + 256 * j, 256))
    wq.append(wblk(w_in, 0, 16, C_AK, 256))
    wq.append(wblk(w_in, 0, 16, C_AV, 256))
    for j in range(4):
        wq.append(wblk(w_in, 0, 16, C_AQ + 256 * j, 256))
    for j in range(4):
        for cbase in (C_RQ, C_RG):
            wq.append(wblk(w_in, 0, 16, cbase + 256 * j, 256))
    for j in range(8):
        wq.append(wblk(w_out, 0, 16, 256 * j, 256))
    QUART = [(0, 6), (6, 6), (12, 5), (17, 5)]
    for (b0, nb) in QUART:
        for b in range(b0, b0 + nb):
            wq.append(wblk(w_gate, 0, 16, 256 * b, 256))
            wq.append(wblk(w_up, 0, 16, 256 * b, 256))
        for j in range(8):
            wq.append(wblk(w_down, 2 * b0, 2 * (b0 + nb), 256 * j, 256))
    for j in range(8):
        wq.append(wblk(w_pg, 0, 16, 256 * j, 256))

    def dense_block(kc, rhs_fn, rhs_res, evac, tiles=TT, nm=2):
        wv, wres = w_next()
        for ml in range(nm):
            for ti, (c0, n) in enumerate(tiles):
                b = bank_d()
                for k in range(kc):
                    pe(lambda e, b=b, k=k, ml=ml, c0=c0, n=n, wv=wv: e.matmul(
                        ps[b][:, 0:n], lhsT=wv[:, k, ml * 128:(ml + 1) * 128], rhs=rhs_fn(k, c0, n),
                        start=(k == 0), stop=(k == kc - 1)),
                       r=[wres] + list(rhs_res), w=[("ps", b)])
                evac(ml, ti, c0, n, b)
        w_done()

    dma("sp", "c_cf", cf, cf_d, w=["cf"])
    dma("sp", "c_cb", cb, cb_d, w=["cb"])

    Z = Alloc(HB, total)
    NXT = 4
    xt = [Z([D], F32) for _ in range(NXT)]
    junk = Z([D], BF16)
    xn = [Z([D], BF16) for _ in range(2)]
    gbc = Z([D], F32)
    ss = misc[:, 0:10]; rstd1 = misc[:, 10:20]; tmpa = misc[:, 20:30]
    gpa = misc[:, 30:31]; mx = misc[:, 31:32]; negc1 = misc[:, 32:33]; epsv = misc[:, 34:35]
    dve(lambda e: e.memset(epsv, EPS), w=["epsv"])

    dma("sp", "c_gbc", gbc, an_g.partition_broadcast(128), w=["gbc"])

    dve(lambda e: e.tensor_tensor(out=gpa, in0=cfv("qg"), in1=cfv("kg"), op=ALU.mult), r=["cf"], w=["gpa"])
    gpa2 = misc[:, 33:34]
    dve(lambda e: e.tensor_tensor(out=gpa2, in0=gpa, in1=gpa, op=ALU.mult), r=["gpa"], w=["gpa2"])
    b = bank_m()
    pe(lambda e, b=b: e.transpose(ps[b][0:1, 0:128], gpa2, ident_f), r=["gpa2", "cf"], w=[("ps", b)])
    dve(lambda e, b=b: e.tensor_reduce(out=mx[0:1, :], in_=ps[b][0:1, 0:128], axis=AX.X, op=ALU.max),
        r=[("ps", b)], w=["mx"])
    act(lambda e: e.activation(out=mx[0:1, :], in_=mx[0:1, :], func=AF.Sqrt), r=["mx"], w=["mx"])
    dve(lambda e: e.tensor_scalar(out=negc1[0:1, :], in0=mx[0:1, :], scalar1=-(ATTN_SCALE * 128.0), scalar2=None,
                                  op0=ALU.mult), r=["mx"], w=["negc1"])
    b = bank_m()
    pe(lambda e, b=b: e.matmul(ps[b][:, 0:1], lhsT=cfv("ones_row")[0:1, :], rhs=negc1[0:1, :], start=True, stop=True),
       r=["negc1", "cf"], w=[("ps", b)])
    dve(lambda e, b=b: e.tensor_copy(out=negc, in_=ps[b][:, 0:1]), r=[("ps", b)], w=["negc"])
    act(lambda e: e.activation(out=esink, in_=cfv("sinks"), func=AF.Exp, bias=negc, scale=1.0),
        r=["negc", "cf"], w=["esink"])

    tiles1 = [(x_main[i * 128:(i + 1) * 128, :], 128, i * 128) for i in range(8)]
    tiles1.append((x_halo, 128, NT))
    tiles1.append((x_smp, NS, NP_))
    def p1_load(i, src, rows, c0):
        s4 = i % NXT
        dma("sp", "x%d" % s4, xt[s4][0:rows, :], src, w=[("xt", s4)])

    def p1_stage1(i, src, rows, c0):
        s = i % 2
        s4 = i % NXT
        act(lambda e, s4=s4, rows=rows, i=i: e.activation(out=junk[0:rows, :], in_=xt[s4][0:rows, :], func=AF.Square,
                                                         accum_out=ss[0:rows, i:i + 1]),
            r=[("xt", s4)], w=["junk", ("ss", i)])
        act(lambda e, rows=rows, i=i: e.activation(out=tmpa[0:rows, i:i + 1], in_=ss[0:rows, i:i + 1], func=AF.Sqrt,
                                                   bias=epsv[0:rows, :], scale=1.0 / D),
            r=[("ss", i), "epsv"], w=[("tmpa", i)])
        dve(lambda e, rows=rows, i=i: e.reciprocal(out=rstd1[0:rows, i:i + 1], in_=tmpa[0:rows, i:i + 1]),
            r=[("tmpa", i)], w=[("rstd1", i)])
        dve(lambda e, s=s, s4=s4, rows=rows, i=i: e.scalar_tensor_tensor(
            out=xn[s][0:rows, :], in0=xt[s4][0:rows, :], scalar=rstd1[0:rows, i:i + 1], in1=gbc[0:rows, :],
            op0=ALU.mult, op1=ALU.mult), r=[("xt", s4), ("rstd1", i), "gbc"], w=[("xn", s)])

    def p1_stage2(i, src, rows, c0):
        s = i % 2
        for half in range(2):
            b = bank_m()
            for kk in range(8):
                k = half * 8 + kk
                pe(lambda e, b=b, kk=kk, k=k, s=s, rows=rows: e.transpose(
                    psb(b)[:, kk * 128:kk * 128 + rows], xn[s][0:rows, k * 128:(k + 1) * 128],
                    ident_b[0:rows, 0:rows]), r=[("xn", s), "cb"], w=[("ps", b)])
            src_ps = lambda b=b, rows=rows: psb(b).rearrange("p (a c) -> p a c", a=8)[:, :, 0:rows]
            dst = aT[:, half * 8:(half + 1) * 8, c0:c0 + rows]
            if half == 0:
                act(lambda e, dst=dst, src_ps=src_ps: e.copy(out=dst, in_=src_ps()), r=[("ps", b)], w=[("aT", c0, half)])
            else:
                dve(lambda e, dst=dst, src_ps=src_ps: e.tensor_copy(out=dst, in_=src_ps()), r=[("ps", b)], w=[("aT", c0, half)])

    for i in range(NXT):
        p1_load(i, *tiles1[i])
    p1_stage1(0, *tiles1[0])
    for i in range(len(tiles1)):
        if i + 1 < len(tiles1):
            p1_stage1(i + 1, *tiles1[i + 1])
        if i + NXT < len(tiles1):
            p1_load(i + NXT, *tiles1[i + NXT])
        if i in (3, 5, 7, 8):
            w_extra[:] = [("xt", (i + 1) % NXT)]
            w_issue_upto(wstate["issued"] + 1)
            w_extra[:] = []
        p1_stage2(i, *tiles1[i])
    w_issue_upto(NSLOT)
    P.fence()
    dump("aT", aT, ["aT"])
    chk(1)

    aT_rhs = lambda k, c0, n: aT[:, k, c0:c0 + n]
    cosv = cfv("cos"); sinv = cfv("sin")

    rot_pending = [None]

    def rot_flush():
        if rot_pending[0] is not None:
            f = rot_pending[0]
            rot_pending[0] = None
            f()

    def rotary_ops(tag, b, c0, n, xb, t1, t2, outb):
        act(lambda e: e.copy(out=xb[:, c0:c0 + n], in_=ps[b][:, 0:n]), r=[("ps", b)], w=[(tag, "xb", c0)])
        dve(lambda e: e.tensor_tensor(out=t1[:, c0:c0 + n], in0=ps[b][:, 0:n], in1=cosv[:, c0:c0 + n], op=ALU.mult),
            r=[("ps", b), "cf"], w=[(tag, "t1", c0)])
        rot_flush()

        def part_b():
            b2 = bank_m()
            pe(lambda e: e.matmul(ps[b2][:, 0:n], lhsT=cbv("rrot"), rhs=xb[:, c0:c0 + n], start=True, stop=True),
               r=[(tag, "xb", c0), "cb"], w=[("ps", b2)])
            dve(lambda e: e.tensor_tensor(out=t2[:, c0:c0 + n], in0=ps[b2][:, 0:n], in1=sinv[:, c0:c0 + n], op=ALU.mult),
                r=[("ps", b2), "cf"], w=[(tag, "t2", c0)])
            dve(lambda e: e.tensor_tensor(out=outb[:, c0:c0 + n], in0=t1[:, c0:c0 + n], in1=t2[:, c0:c0 + n], op=ALU.add),
                r=[(tag, "t1", c0), (tag, "t2", c0)], w=[(tag, "rot", c0)])
        rot_pending[0] = part_b

    Z = Alloc(HB, total)
    p1 = []
    for hp in range(2):
        p1.append(dict(xb=Z([NT], BF16), t1=Z([NT], F32), t2=Z([NT], F32), kr=Z([NT], BF16),
                       kD=Z([8, 128], BF16), vT=Z([NT], BF16), vtok=Z([8, 128], BF16), sloc=Z([128], F32)))
    kdl = cfv("kdlong").rearrange("p (h c) -> p h c", h=8)

    for j in range(4):
        hs = (2 * j, 2 * j + 1)

        def evac_k(ml, ti, c0, n, b, hs=hs):
            bf = p1[ml]
            rotary_ops(("p1", ml), b, c0, n, bf["xb"], bf["t1"], bf["t2"], bf["kr"])

        def evac_v(ml, ti, c0, n, b, hs=hs):
            bf = p1[ml]
            act(lambda e: e.copy(out=bf["vT"][:, c0:c0 + n], in_=ps[b][:, 0:n]), r=[("ps", b)],
                w=[("p1", ml, "vT", c0)])
        dense_block(16, aT_rhs, ["aT"], evac_k)
        chk(20)
        dense_block(16, aT_rhs, ["aT"], evac_v)
        rot_flush()
        chk(21)
        for ml in range(2):
            h = hs[ml]
            bf = p1[ml]
            rk_res = [(("p1", ml), "rot", c0) for (c0, n) in TT]
            rv_res = [("p1", ml, "vT", c0) for (c0, n) in TT]
            dma("sp", "spk%d" % ml, krs.ap()[h], bf["kr"], r=rk_res, w=[("krs", h)])
            dma("sp", "spv%d" % ml, vts.ap()[h], bf["vT"], r=rv_res, w=[("vts", h)])
            for half in range(2):
                bk = bank_m()
                for cc in range(4):
                    c = half * 4 + cc
                    pe(lambda e, bk=bk, cc=cc, c=c, bf=bf: e.transpose(
                        psb(bk)[:, cc * 128:(cc + 1) * 128], bf["kr"][:, c * 128:(c + 1) * 128], ident_b),
                       r=rk_res + ["cb"], w=[("ps", bk)])
                for cc in range(4):
                    c = half * 4 + cc
                    dve(lambda e, bk=bk, cc=cc, c=c, bf=bf, h=h: e.tensor_scalar(
                        out=bf["kD"][:, c, :], in0=psb(bk)[:, cc * 128:(cc + 1) * 128], scalar1=kdl[:, h, c:c + 1],
                        scalar2=None, op0=ALU.mult), r=[("ps", bk), "cf"], w=[("p1", ml, "kD", c)])
                bv = bank_m()
                for cc in range(4):
                    c = half * 4 + cc
                    pe(lambda e, bv=bv, cc=cc, c=c, bf=bf: e.transpose(
                        psb(bv)[:, cc * 128:(cc + 1) * 128], bf["vT"][:, c * 128:(c + 1) * 128], ident_b),
                       r=rv_res + ["cb"], w=[("ps", bv)])
                act(lambda e, bv=bv, half=half, bf=bf: e.copy(
                    out=bf["vtok"][:, half * 4:(half + 1) * 4, :],
                    in_=psb(bv)[:, 0:512].rearrange("p (a c) -> p a c", a=4)), r=[("ps", bv)], w=[("p1", ml, "vtok", half)])
            chk(22)
            bs = bank_m()
            for c in range(8):
                pe(lambda e, bs=bs, c=c, bf=bf: e.matmul(ps[bs][:, 0:128], lhsT=bf["kD"][:, c, :], rhs=bf["vtok"][:, c, :],
                                                        start=(c == 0), stop=(c == 7)),
                   r=[("p1", ml, "kD", c), ("p1", ml, "vtok", c // 4)], w=[("ps", bs)])
            dve(lambda e, bs=bs, bf=bf: e.tensor_copy(out=bf["sloc"], in_=ps[bs][:, 0:128]), r=[("ps", bs)],
                w=[("p1", ml, "sloc")])
            chk(23)
            dma("sp", "agi", ag_in.ap()[h * 128:(h + 1) * 128, :], bf["sloc"], r=[("p1", ml, "sloc")], w=["ag_in"])
            chk(24)

    dump("ag_in", ag_in.ap(), ["ag_in"])
    chk(2)
    P.fence()
    P.add("pool", lambda e: e.collective_compute("AllGather", ALU.bypass, replica_groups=groups or [[0, 1, 2, 3], [4, 5, 6, 7]],
                                                 ins=[ag_in.ap().opt()], outs=[ag_out.ap().opt()]),
          r=["ag_in"], w=["ag_out"], dma="cc")
    chk(3)

    Z = Alloc(HB, total)
    zf = [Z([NA], F32) for _ in range(2)]
    sq = [Z([NA], BF16) for _ in range(2)]
    rstdb = [Z([NA], F32)] * 2
    knT = Z([2, NA], BF16)
    kn32 = Z([2, 132], F32)
    v32 = Z([2, 132], F32)
    vTb = Z([2, NA], BF16)
    vtokA = Z([2, 9, 128], BF16)
    qnT = Z([4, NT], BF16)
    qnT2 = Z([4, NT], BF16)
    PTm = [Z([2, 512], BF16) for _ in range(2)]
    rec = [Z([512], F32) for _ in range(2)]
    win_t = Z([2, 128], F32)
    vstok = Z([2, 128], BF16)
    kc_b = [Z([128], BF16) for _ in range(NS)]
    vc_b = [Z([128], BF16) for _ in range(NS)]
    kcT = Z([4, 128], BF16)
    PTc = Z([16], BF16)
    Pn = Z([16], BF16)
    recs = Z([16], F32)
    cnt = {"qk": 0}

    qk_pending = [None]

    def qk_flush():
        if qk_pending[0] is not None:
            f = qk_pending[0]
            qk_pending[0] = None
            f()

    def qknorm(b, c0, n, gname, outbf, tagres, out32=None):
        s = cnt["qk"] % 2
        cnt["qk"] += 1
        act(lambda e: e.activation(out=sq[s][:, 0:n], in_=ps[b][:, 0:n], func=AF.Square), r=[("ps", b)], w=[("sq", s)])
        dve(lambda e: e.tensor_copy(out=zf[s][:, 0:n], in_=ps[b][:, 0:n]), r=[("ps", b)], w=[("zf", s)])
        qk_flush()

        def part_b():
            b2 = 3
            pe(lambda e: e.matmul(ps[b2][:, 0:n], lhsT=cbv("ones128"), rhs=sq[s][:, 0:n], start=True, stop=True),
               r=[("sq", s), "cb"], w=[("ps", b2)])
            act(lambda e: e.activation(out=rstdb[0][:, 0:n], in_=ps[b2][:, 0:n], func=AF.Ln, bias=epsv, scale=1.0),
                r=[("ps", b2)], w=["rstdb"])
            act(lambda e: e.activation(out=rstdb[0][:, 0:n], in_=rstdb[0][:, 0:n], func=AF.Exp, scale=-0.5),
                r=["rstdb"], w=["rstdb"])
            dve(lambda e: e.scalar_tensor_tensor(out=outbf, in0=zf[s][:, 0:n], scalar=cfv(gname), in1=rstdb[0][:, 0:n],
                                                 op0=ALU.mult, op1=ALU.mult),
                r=[("zf", s), "rstdb", "cf"], w=[tagres])
            if out32 is not None:
                lo, hi, dst = out32
                dve(lambda e: e.scalar_tensor_tensor(out=dst, in0=zf[s][:, lo:hi], scalar=cfv(gname),
                                                     in1=rstdb[0][:, lo:hi], op0=ALU.mult, op1=ALU.mult),
                    r=[("zf", s), "rstdb", "cf"], w=[("kn32", tagres)])
        qk_pending[0] = part_b

    def evac_ak(ml, ti, c0, n, b):
        o32 = None
        if ti == 2:
            o32 = (208, 340, kn32[:, ml, :])
        qknorm(b, c0, n, "kg", knT[:, ml, c0:c0 + n], ("knT", ml, c0), o32)

    def evac_av(ml, ti, c0, n, b):
        act(lambda e: e.copy(out=vTb[:, ml, c0:c0 + n], in_=ps[b][:, 0:n]), r=[("ps", b)], w=[("vTb", ml, c0)])
        if ti == 2:
            dve(lambda e: e.tensor_copy(out=v32[:, ml, :], in_=ps[b][:, 208:340]), r=[("ps", b)], w=[("v32", ml)])

    att_dense3 = [True]
    dense_block(16, aT_rhs, ["aT"], evac_ak, tiles=TT_H)
    qk_flush()
    dense_block(16, aT_rhs, ["aT"], evac_av, tiles=TT_H)
    vres = lambda g: [("vTb", g, c0) for (c0, n) in TT_H]
    kres = lambda g: [("knT", g, c0) for (c0, n) in TT_H]
    for g in range(2):
        for grp in range(3):
            blks = [0, 1, 2, 3] if grp == 0 else ([4, 5, 6, 7] if grp == 1 else [8])
            bv = bank_m()
            for ii, blk in enumerate(blks):
                col = NT if blk == 0 else (blk - 1) * 128
                pe(lambda e, bv=bv, ii=ii, col=col, g=g: e.transpose(
                    psb(bv)[:, ii * 128:(ii + 1) * 128], vTb[:, g, col:col + 128], ident_b),
                   r=vres(g) + ["cb"], w=[("ps", bv)])
            nb = len(blks)
            act(lambda e, bv=bv, g=g, blks=blks, nb=nb: e.copy(
                out=vtokA[:, g, blks[0]:blks[0] + nb, :],
                in_=psb(bv)[:, 0:nb * 128].rearrange("p (a c) -> p a c", a=nb)), r=[("ps", bv)], w=[("vtokA", g)])
        bv = bank_m()
        pe(lambda e, bv=bv, g=g: e.transpose(psb(bv)[0:NS, 0:128], vTb[:, g, NP_:NP_ + NS], ident_b),
           r=vres(g) + ["cb"], w=[("ps", bv)])
        act(lambda e, bv=bv, g=g: e.copy(out=vstok[0:NS, g, :], in_=psb(bv)[0:NS, 0:128]), r=[("ps", bv)],
            w=[("vstok", g)])
    for (src32, dst, nm) in ((kn32, kwin, "kw"), (v32, vwin, "vw")):
        for g in range(2):
            bw = bank_m()
            pe(lambda e, bw=bw, g=g, src32=src32: e.transpose(ps[bw][:, 0:128], src32[:, g, 0:128], ident_f),
               r=[("kn32", ("knT", g, 688)), ("v32", g), "cf"], w=[("ps", bw)])
            dve(lambda e, bw=bw, g=g: e.tensor_copy(out=win_t[:, g, :], in_=ps[bw][:, 0:128]), r=[("ps", bw)],
                w=[("win_t", g)])
        dma("sp", "o_" + nm, dst, win_t, r=[("win_t", 0), ("win_t", 1)], w=["out_" + nm])
    dma("sp", "o_ks", ks_out[:, 0:127, :, :], ck[:, 1:128, :, :], w=["ks_out_a"])
    dma("sp", "o_vs", vs_out[:, 0:127, :, :], cv[:, 1:128, :, :], w=["vs_out_a"])
    for g in range(2):
        for s in range(NS):
            dma("sp", "o_ks", ks_out[s, 127, g, :].rearrange("(d o) -> d o", o=1), kn32[:, g, 128 + s:129 + s],
                r=[("kn32", ("knT", g, 688))], w=[("ks_out_b", g, s)], slow=True)
            dma("sp", "o_vs", vs_out[s, 127, g, :].rearrange("(d o) -> d o", o=1), v32[:, g, 128 + s:129 + s],
                r=[("v32", g)], w=[("vs_out_b", g, s)], slow=True)

    mask = cbv("mask").rearrange("p (a c) -> p a c", a=3)
    v3 = lambda ap: ap[:, 0:1024].rearrange("p (a c) -> p a c", a=8)
    esr_f = v3(zf[0]); esr_d = v3(zf[1]); esr_hi = v3(sq[0]); esr_lo = v3(sq[1])
    esr2 = Z([8, 128], BF16)
    oh = cfv("onehot")
    dve(lambda e: e.tensor_copy(out=esr_f[0:2], in_=esink[0:2, :].unsqueeze(2).broadcast_to([2, 8, 128])),
        r=["esink"], w=[("zf", 0)])
    dve(lambda e: e.tensor_copy(out=esr_hi[0:2], in_=esr_f[0:2]), r=[("zf", 0)], w=[("sq", 0)])
    dve(lambda e: e.tensor_tensor(out=esr_d[0:2], in0=esr_f[0:2], in1=esr_hi[0:2], op=ALU.subtract),
        r=[("zf", 0), ("sq", 0)], w=[("zf", 1)])
    dve(lambda e: e.tensor_copy(out=esr_lo[0:2], in_=esr_d[0:2]), r=[("zf", 1)], w=[("sq", 1)])
    dve(lambda e: e.tensor_scalar(out=esr2[0:2], in0=esr_hi[0:2], scalar1=oh[0:2, 0:1], scalar2=None, op0=ALU.mult),
        r=[("sq", 0), "cf"], w=["esr2a"])
    dve(lambda e: e.scalar_tensor_tensor(out=esr2[0:2], in0=esr_lo[0:2], scalar=oh[0:2, 1:2], in1=esr2[0:2],
                                         op0=ALU.mult, op1=ALU.add), r=[("sq", 1), "esr2a", "cf"], w=["esr2"])
    def make_att(g, qnT, qres):
        def att_S(blk, g=g):
            s = blk % 2
            q_rhs = qnT[:, :, blk * 128:(blk + 1) * 128]
            k_own = knT[:, g, blk * 128:(blk + 1) * 128]
            k_prev = knT[:, g, NT:NT + 128] if blk == 0 else knT[:, g, (blk - 1) * 128:blk * 128]
            bo = bank_m(); bp = bank_m()
            mprev = 2 if blk == 0 else 1
            for (bb_, kk_, mi) in ((bo, k_own, 0), (bp, k_prev, mprev)):
                pe(lambda e, bb_=bb_, kk_=kk_, q_rhs=q_rhs: e.matmul(ps[bb_][:, :], lhsT=kk_, rhs=q_rhs, start=True, stop=False),
                   r=qres + kres(g), w=[("ps", bb_)])
                pe(lambda e, bb_=bb_, mi=mi: e.matmul(ps[bb_][:, :], lhsT=ident_b,
                                                     rhs=mask[:, mi:mi + 1, :].broadcast_to([128, 4, 128]),
                                                     start=False, stop=True), r=["cb"], w=[("ps", bb_)])
            act(lambda e, bo=bo, s=s: e.activation(out=PTm[s][:, 0, :], in_=ps[bo][:, :], func=AF.Exp, bias=negc,
                                                   scale=ATTN_SCALE), r=[("ps", bo), "negc"], w=[("PTm", s, 0)])
            act(lambda e, bp=bp, s=s: e.activation(out=PTm[s][:, 1, :], in_=ps[bp][:, :], func=AF.Exp, bias=negc,
                                                   scale=ATTN_SCALE), r=[("ps", bp), "negc"], w=[("PTm", s, 1)])

        def att_PV(blk, g=g):
            s = blk % 2
            bO = bank_m(); bD = bank_m()
            for t, vb in ((0, blk + 1), (1, blk)):
                pe(lambda e, bO=bO, t=t, vb=vb, s=s, g=g: e.matmul(ps[bO][:, :], lhsT=vtokA[:, g, vb, :], rhs=PTm[s][:, t, :],
                                                                  start=(t == 0), stop=(t == 1)),
                   r=[("PTm", s, t), ("vtokA", g)], w=[("ps", bO)])
            for t in range(2):
                pe(lambda e, bD=bD, t=t, s=s: e.matmul(ps[bD][:, :], lhsT=cbv("ones"), rhs=PTm[s][:, t, :],
                                                      start=(t == 0), stop=False),
                   r=[("PTm", s, t), "cb"], w=[("ps", bD)])
            pe(lambda e, bD=bD, g=g: e.matmul(ps[bD][:, :], lhsT=cbv("ones")[0:2, :], rhs=esr2[0:2, 4 * g:4 * g + 4, :],
                                             start=False, stop=True), r=["esr2", "cb"], w=[("ps", bD)])
            if blk % 2 == 0:
                act(lambda e, bD=bD, s=s: e.activation(out=rec[s], in_=ps[bD][:, :], func=AF.Ln),
                    r=[("ps", bD)], w=[("rec", s)])
                act(lambda e, s=s: e.activation(out=rec[s], in_=rec[s], func=AF.Exp, scale=-1.0), r=[("rec", s)],
                    w=[("rec", s)])
            else:
                dve(lambda e, bD=bD, s=s: e.reciprocal(out=rec[s], in_=ps[bD][:, :]), r=[("ps", bD)], w=[("rec", s)])
            dve(lambda e, bO=bO, s=s, g=g, blk=blk: e.tensor_tensor(
                out=mixT[:, 4 * g:4 * g + 4, blk * 128:(blk + 1) * 128],
                in0=ps[bO][:, :].rearrange("p (a c) -> p a c", a=4),
                in1=rec[s][:, :].rearrange("p (a c) -> p a c", a=4), op=ALU.mult),
                r=[("ps", bO), ("rec", s)], w=[("mixT", "a", g, blk)])


        def att_gen():
            att_S(0)
            yield
            for blk in range(8):
                if blk + 1 < 8:
                    att_S(blk + 1)
                    yield
                att_PV(blk)
                yield

        def att_sample():
            for sm in range(NS):
                dma("pool", "kc%d" % sm, kc_b[sm], ck[sm, :, g, :], w=[("kc_b", sm)])
                dma("pool", "vc%d" % sm, vc_b[sm], cv[sm, :, g, :], w=[("vc_b", sm)])
            bt = bank_m()
            for sm in range(NS):
                pe(lambda e, bt=bt, sm=sm: e.transpose(psb(bt)[:, sm * 128:(sm + 1) * 128], kc_b[sm], ident_b),
                   r=[("kc_b", sm), "cb"], w=[("ps", bt)])
            act(lambda e, bt=bt: e.copy(out=kcT, in_=psb(bt)[:, 0:512].rearrange("p (a c) -> p a c", a=4)),
                r=[("ps", bt)], w=["kcT"])
            bS = bank_m()
            for sm in range(NS):
                pe(lambda e, bS=bS, sm=sm: e.matmul(ps[bS][:, sm * 4:(sm + 1) * 4], lhsT=kcT[:, sm, :], rhs=qnT[:, :, NP_ + sm],
                                                   start=True, stop=True, skip_group_check=True),
                   r=["kcT"] + qres, w=[("ps", bS)])
            for sm in range(NS):
                pe(lambda e, bS=bS, sm=sm, g=g: e.matmul(ps[bS][0:NS, 32 + sm * 4:32 + (sm + 1) * 4],
                                                        lhsT=knT[:, g, NP_:NP_ + NS], rhs=qnT[:, :, NP_ + sm],
                                                        start=True, stop=True, skip_group_check=True),
                   r=kres(g) + qres, w=[("ps", bS)])
            act(lambda e, bS=bS: e.activation(out=PTc, in_=ps[bS][:, 0:16], func=AF.Exp, bias=negc, scale=ATTN_SCALE),
                r=[("ps", bS), "negc"], w=["PTc"])
            act(lambda e, bS=bS: e.activation(out=Pn[0:NS, :], in_=ps[bS][0:NS, 32:48], func=AF.Exp,
                                              bias=negc[0:NS, :], scale=ATTN_SCALE), r=[("ps", bS), "negc"], w=["Pn"])
            dve(lambda e: e.tensor_tensor(out=Pn[0:NS, :].rearrange("p (s h) -> p s h", s=4),
                                          in0=Pn[0:NS, :].rearrange("p (s h) -> p s h", s=4),
                                          in1=oh[0:NS, 0:4].unsqueeze(2).broadcast_to([NS, 4, 4]), op=ALU.mult),
                r=["Pn", "cf"], w=["Pn"])
            bO = bank_m(); bD = bank_m()
            for sm in range(NS):
                pe(lambda e, bO=bO, sm=sm: e.matmul(ps[bO][:, sm * 4:(sm + 1) * 4], lhsT=vc_b[sm], rhs=PTc[:, sm * 4:(sm + 1) * 4],
                                                   start=(sm == 0), stop=False, skip_group_check=True),
                   r=[("vc_b", sm), "PTc"], w=[("ps", bO)])
                pe(lambda e, bO=bO, sm=sm, g=g: e.matmul(ps[bO][:, sm * 4:(sm + 1) * 4], lhsT=vstok[0:NS, g, :],
                                                        rhs=Pn[0:NS, sm * 4:(sm + 1) * 4], start=False, stop=True,
                                                        skip_group_check=True),
                   r=[("vstok", g), "Pn"], w=[("ps", bO)])
            for sm in range(NS):
                pe(lambda e, bD=bD, sm=sm: e.matmul(ps[bD][:, sm * 4:(sm + 1) * 4], lhsT=cbv("ones"), rhs=PTc[:, sm * 4:(sm + 1) * 4],
                                                   start=(sm == 0), stop=False, skip_group_check=True),
                   r=["PTc", "cb"], w=[("ps", bD)])
                pe(lambda e, bD=bD, sm=sm: e.matmul(ps[bD][:, sm * 4:(sm + 1) * 4], lhsT=cbv("ones")[0:NS, :],
                                                   rhs=Pn[0:NS, sm * 4:(sm + 1) * 4], start=False, stop=False,
                                                   skip_group_check=True), r=["Pn", "cb"], w=[("ps", bD)])
            pe(lambda e, bD=bD, g=g: e.matmul(ps[bD][:, 0:16], lhsT=cbv("ones")[0:2, :],
                                             rhs=esr2[0:2, 4 * g:4 * g + 4, 0].unsqueeze(1).broadcast_to([2, 4, 4]),
                                             start=False, stop=True, skip_group_check=True),
               r=["esr2", "cb"], w=[("ps", bD)])
            act(lambda e, bD=bD: e.activation(out=recs, in_=ps[bD][:, 0:16], func=AF.Ln), r=[("ps", bD)], w=["recs"])
            act(lambda e: e.activation(out=recs, in_=recs, func=AF.Exp, scale=-1.0), r=["recs"], w=["recs"])
            dve(lambda e, bO=bO, g=g: e.tensor_tensor(
                out=mixT[:, 4 * g:4 * g + 4, NP_:NP_ + NS], in0=ps[bO][:, 0:16].rearrange("p (s h) -> p h s", s=4),
                in1=recs[:, :].rearrange("p (s h) -> p h s", s=4), op=ALU.mult),
                r=[("ps", bO), "recs"], w=[("mixT", "as", g)])

        return att_gen, att_sample

    qbufs = [qnT, qnT2]
    qresf = lambda gi: [("qnT", gi, hh, c0) for hh in range(4) for (c0, n) in TT]

    def qproj_gen(gi):
        qb = qbufs[gi]
        for jj in range(2):
            wv, wres = w_next()
            for ml in range(2):
                hh = jj * 2 + ml
                for ti, (c0, n) in enumerate(TT):
                    b = bank_d()
                    for k in range(16):
                        pe(lambda e, b=b, k=k, ml=ml, c0=c0, n=n, wv=wv: e.matmul(
                            ps[b][:, 0:n], lhsT=wv[:, k, ml * 128:(ml + 1) * 128], rhs=aT[:, k, c0:c0 + n],
                            start=(k == 0), stop=(k == 15)), r=[wres, "aT"], w=[("ps", b)])
                    qknorm(b, c0, n, "qg", qb[:, hh, c0:c0 + n], ("qnT", gi, hh, c0))
                    yield
            w_done()
        qk_flush()

    def drive_mix(ga, gb, pat):
        a_alive, b_alive = True, gb is not None
        while a_alive or b_alive:
            if a_alive:
                try:
                    next(ga)
                except StopIteration:
                    a_alive = False
            for _ in range(pat if a_alive else 99):
                if not b_alive:
                    break
                try:
                    next(gb)
                except StopIteration:
                    b_alive = False

    for _ in qproj_gen(0):
        pass
    attg0, atts0 = make_att(0, qbufs[0], qresf(0))
    attg1, atts1 = make_att(1, qbufs[1], qresf(1))
    drive_mix(attg0(), qproj_gen(1), 1)
    atts0()
    for _ in attg1():
        pass
    atts1()
    att_dense3[0] = False
    P.fence()
    dump("mixA", mixT[:, 0:8, :], ["mixT"])
    chk(4)

    Z = Alloc(HB, total)
    agl = [Z([8, 128], F32) for _ in range(2)]
    coef = cfv("coef")
    ago = ag_out.ap()
    for r_ in range(4):
        s = r_ % 2
        dma("sp", "agl%d" % s, agl[s], ago[r_ * 1024:(r_ + 1) * 1024, :].rearrange("(h p) n -> p h n", p=128),
            r=["ag_out"], w=[("agl", s)])
        for h in range(8):
            if r_ == 0:
                dve(lambda e, s=s, h=h, r_=r_: e.tensor_scalar(out=S0[:, h, :], in0=agl[s][:, h, :],
                                                              scalar1=coef[:, r_ * 8 + h:r_ * 8 + h + 1], scalar2=None,
                                                              op0=ALU.mult), r=[("agl", s), "cf"], w=[("S0", h)])
            else:
                dve(lambda e, s=s, h=h, r_=r_: e.scalar_tensor_tensor(
                    out=S0[:, h, :], in0=agl[s][:, h, :], scalar=coef[:, r_ * 8 + h:r_ * 8 + h + 1], in1=S0[:, h, :],
                    op0=ALU.mult, op1=ALU.add), r=[("agl", s), "cf", ("S0", h)], w=[("S0", h)])

    dump("S0", S0, [("S0", h) for h in range(8)])
    chk(5)
    G1 = [dict(qr=Z([NT], BF16), kr=Z([NT], BF16), vT=Z([NT], BF16), sg=Z([NT], F32)) for _ in range(2)]
    RT = dict(xb=[Z([344], BF16) for _ in range(3)], t1=[Z([344], F32) for _ in range(3)],
              t2=[Z([344], F32) for _ in range(3)])
    R = dict(qd=Z([8, 128], BF16), kd=Z([8, 128], BF16), vtok=Z([8, 128], BF16), scm=Z([8, 128], BF16),
             Sb=Z([8, 128], BF16), Srun=Z([2, 128], F32), o_sb=Z([NT], F32), sq=Z([NT], BF16), rstd=Z([NT], F32),
             tt=Z([NT], F32), ktok_s=Z([128], BF16), vtok_s=Z([128], BF16))
    vm = [Z([128], BF16) for _ in range(NS)]
    Sst = [Z([128], F32) for _ in range(NS)]
    Snew = [Z([128], F32) for _ in range(NS)]
    Sbs = [Z([128], BF16) for _ in range(NS)]
    intraT = cfv("intraT").rearrange("p (h c) -> p h c", h=8)
    qdecv = cfv("qdec").rearrange("p (h c) -> p h c", h=8)
    kdecv = cfv("kdec")
    rgv = cfv("rg")
    tg = "p2"
    p2c = {"d": 0, "r": 0, "t": 0}

    def bank_d3():
        b = p2c["d"] % 3
        p2c["d"] += 1
        return b

    def bank_r():
        b = 3 + p2c["r"] % 2
        p2c["r"] += 1
        return b
    M0, M1, M2 = 5, 6, 7
    p2blocks = {}

    def rot_a(b, c0, n):
        sl = p2c["t"] % 3
        p2c["t"] += 1
        xb, t1 = RT["xb"][sl], RT["t1"][sl]
        act(lambda e: e.copy(out=xb[:, 0:n], in_=ps[b][:, 0:n]), r=[("ps", b)], w=[("rt_xb", sl)])
        dve(lambda e: e.tensor_tensor(out=t1[:, 0:n], in0=ps[b][:, 0:n], in1=cosv[:, c0:c0 + n], op=ALU.mult),
            r=[("ps", b), "cf"], w=[("rt_t1", sl)])
        return sl

    def rot_b(sl, c0, n, outb, outres):
        xb, t1, t2 = RT["xb"][sl], RT["t1"][sl], RT["t2"][sl]
        b2 = bank_r()
        pe(lambda e: e.matmul(ps[b2][:, 0:n], lhsT=cbv("rrot"), rhs=xb[:, 0:n], start=True, stop=True),
           r=[("rt_xb", sl), "cb"], w=[("ps", b2)])
        dve(lambda e: e.tensor_tensor(out=t2[:, 0:n], in0=ps[b2][:, 0:n], in1=sinv[:, c0:c0 + n], op=ALU.mult),
            r=[("ps", b2), "cf"], w=[("rt_t2", sl)])
        dve(lambda e: e.tensor_tensor(out=outb[:, c0:c0 + n], in0=t1[:, 0:n], in1=t2[:, 0:n], op=ALU.add),
            r=[("rt_t1", sl), ("rt_t2", sl)], w=[outres])

    def stageA(h):
        j, ml = h // 2, h % 2
        if ml == 0:
            p2blocks[j] = [w_next() for _ in range(2)]
        blocks = p2blocks[j]
        g1 = G1[h % 2]
        gp = h % 2
        dma("sp", "ldk%d" % gp, g1["kr"], krs.ap()[h], r=[("krs", h)], w=[(tg, "kr", gp, c0) for (c0, n) in TT])
        dma("sp", "ldv%d" % gp, g1["vT"], vts.ap()[h], r=[("vts", h)], w=[(tg, "vT", gp, c0) for (c0, n) in TT])
        for bi in range(2):
            wv, wres = blocks[bi]
            pending = None
            for ti, (c0, n) in enumerate(TT):
                b = bank_d3()
                for k in range(16):
                    pe(lambda e, b=b, k=k, c0=c0, n=n, wv=wv, ml=ml: e.matmul(
                        ps[b][:, 0:n], lhsT=wv[:, k, ml * 128:(ml + 1) * 128], rhs=aT[:, k, c0:c0 + n],
                        start=(k == 0), stop=(k == 15)), r=[wres, "aT"], w=[("ps", b)])
                if bi == 0:
                    sl = rot_a(b, c0, n)
                    if pending is not None:
                        rot_b(*pending)
                    pending = (sl, c0, n, g1["qr"], (tg, "qr", gp, c0))
                else:
                    act(lambda e, b=b, c0=c0, n=n, g1=g1: e.activation(out=g1["sg"][:, c0:c0 + n], in_=ps[b][:, 0:n],
                                                                       func=AF.Silu), r=[("ps", b)], w=[(tg, "sg", gp, c0)])
                if ti == 2:
                    if pending is not None:
                        rot_b(*pending)
                    if ml == 1:
                        w_done()
                yield

    def stageB(h):
        g1 = G1[h % 2]
        gp = h % 2
        q_res = [(tg, "qr", gp, c0) for (c0, n) in TT]
        k_res = [(tg, "kr", gp, c0) for (c0, n) in TT]
        v_res = [(tg, "vT", gp, c0) for (c0, n) in TT]
        g_res = [(tg, "sg", gp, c0) for (c0, n) in TT]
        dve(lambda e: e.tensor_tensor(out=R["qd"], in0=g1["qr"][:, 0:NP_].rearrange("p (a c) -> p a c", a=8),
                                      in1=qdecv[:, h:h + 1, :].broadcast_to([128, 8, 128]), op=ALU.mult),
            r=q_res + ["cf"], w=[(tg, "qd")])
        for half in range(2):
            for cc in range(4):
                c = half * 4 + cc
                pe(lambda e, cc=cc, c=c: e.transpose(psb(M0)[:, cc * 128:(cc + 1) * 128],
                                                    g1["kr"][:, c * 128:(c + 1) * 128], ident_b),
                   r=k_res + ["cb"], w=[("ps", M0)])
            dve(lambda e, half=half: e.tensor_scalar(
                out=R["kd"][:, half * 4:(half + 1) * 4, :], in0=psb(M0)[:, 0:512].rearrange("p (a c) -> p a c", a=4),
                scalar1=kdecv[:, h:h + 1], scalar2=None, op0=ALU.mult), r=[("ps", M0), "cf"], w=[(tg, "kd", half)])
            for cc in range(4):
                c = half * 4 + cc
                pe(lambda e, cc=cc, c=c: e.transpose(psb(M1)[:, cc * 128:(cc + 1) * 128],
                                                    g1["vT"][:, c * 128:(c + 1) * 128], ident_b),
                   r=v_res + ["cb"], w=[("ps", M1)])
            act(lambda e, half=half: e.copy(out=R["vtok"][:, half * 4:(half + 1) * 4, :],
                                            in_=psb(M1)[:, 0:512].rearrange("p (a c) -> p a c", a=4)),
                r=[("ps", M1)], w=[(tg, "vtok", half)])
        act(lambda e: e.copy(out=R["Sb"][:, 0, :], in_=S0[:, h, :]), r=[("S0", h)], w=[(tg, "Sb", 0)])
        yield
        g128 = float(GAMMA[h] ** 128)
        for half in range(2):
            for cc in range(4):
                c = half * 4 + cc
                pe(lambda e, cc=cc, c=c: e.matmul(ps[M2][:, cc * 128:(cc + 1) * 128], lhsT=R["kd"][:, c, :],
                                                 rhs=R["vtok"][:, c, :], start=True, stop=True),
                   r=[(tg, "kd", half), (tg, "vtok", half)], w=[("ps", M2)])
            for cc in range(4):
                c = half * 4 + cc
                prev = S0[:, h, :] if c == 0 else R["Srun"][:, (c - 1) % 2, :]
                prev_res = ("S0", h) if c == 0 else (tg, "Srun", (c - 1) % 2)
                dve(lambda e, cc=cc, c=c, prev=prev: e.scalar_tensor_tensor(
                    out=R["Srun"][:, c % 2, :], in0=prev, scalar=g128, in1=ps[M2][:, cc * 128:(cc + 1) * 128],
                    op0=ALU.mult, op1=ALU.add), r=[prev_res, ("ps", M2)], w=[(tg, "Srun", c % 2)])
                if c < 7:
                    act(lambda e, c=c: e.copy(out=R["Sb"][:, c + 1, :], in_=R["Srun"][:, c % 2, :]),
                        r=[(tg, "Srun", c % 2)], w=[(tg, "Sb", c + 1)])
                else:
                    dma("sp", "o_rs", rstate[h], R["Srun"][:, c % 2, :], r=[(tg, "Srun", c % 2)], w=[("rstate", h)])
        yield
        for half, bsc in ((0, M0), (1, M1)):
            for cc in range(4):
                c = half * 4 + cc
                pe(lambda e, bsc=bsc, cc=cc, c=c: e.matmul(ps[bsc][:, cc * 128:(cc + 1) * 128],
                                                          lhsT=g1["kr"][:, c * 128:(c + 1) * 128],
                                                          rhs=g1["qr"][:, c * 128:(c + 1) * 128], start=True, stop=True),
                   r=k_res + q_res, w=[("ps", bsc)])
            dve(lambda e, bsc=bsc, half=half: e.tensor_tensor(
                out=R["scm"][:, half * 4:(half + 1) * 4, :], in0=ps[bsc][:, :].rearrange("p (a c) -> p a c", a=4),
                in1=intraT[:, h:h + 1, :].broadcast_to([128, 4, 128]), op=ALU.mult),
                r=[("ps", bsc), "cf"], w=[(tg, "scm", half)])
        yield
        obanks = [M0, M1]
        for half in range(2):
            bo = obanks[half]
            for cc in range(4):
                c = half * 4 + cc
                pe(lambda e, bo=bo, cc=cc, c=c: e.matmul(ps[bo][:, cc * 128:(cc + 1) * 128], lhsT=R["vtok"][:, c, :],
                                                        rhs=R["scm"][:, c, :], start=(cc == 0), stop=False,
                                                        skip_group_check=True),
                   r=[(tg, "vtok", half), (tg, "scm", half)], w=[("ps", bo)])
        pe(lambda e: e.transpose(psb(M2)[0:NS, 0:128], g1["kr"][:, NP_:NT], ident_b), r=k_res + ["cb"], w=[("ps", M2)])
        pe(lambda e: e.transpose(psb(M2)[0:NS, 128:256], g1["vT"][:, NP_:NT], ident_b), r=v_res + ["cb"], w=[("ps", M2)])
        act(lambda e: e.mul(out=R["ktok_s"][0:NS, :], in_=psb(M2)[0:NS, 0:128], mul=RET_K_SCALE),
            r=[("ps", M2)], w=[(tg, "ktok_s")])
        act(lambda e: e.copy(out=R["vtok_s"][0:NS, :], in_=psb(M2)[0:NS, 128:256]), r=[("ps", M2)], w=[(tg, "vtok_s")])
        for sm in range(NS):
            dma("sp", "st%d" % sm, Sst[sm], st[sm, h], w=[("Sst", sm)])
            dve(lambda e, sm=sm: e.tensor_scalar(out=vm[sm][0:NS, :], in0=R["vtok_s"][0:NS, :],
                                                 scalar1=cfv("onehot")[0:NS, sm:sm + 1], scalar2=None, op0=ALU.mult),
                r=[(tg, "vtok_s"), "cf"], w=[("vm", sm)])
        yield
        for half in range(2):
            bo = obanks[half]
            for cc in range(4):
                c = half * 4 + cc
                pe(lambda e, bo=bo, cc=cc, c=c: e.matmul(ps[bo][:, cc * 128:(cc + 1) * 128], lhsT=R["Sb"][:, c, :],
                                                        rhs=R["qd"][:, c, :], start=False, stop=True,
                                                        skip_group_check=True),
                   r=[(tg, "Sb", c), (tg, "qd")], w=[("ps", bo)])

        def o_read(bo, c0, n):
            act(lambda e: e.activation(out=R["sq"][:, c0:c0 + n], in_=ps[bo][:, 0:n], func=AF.Square),
                r=[("ps", bo)], w=[(tg, "sq", c0)])
            dve(lambda e: e.tensor_copy(out=R["o_sb"][:, c0:c0 + n], in_=ps[bo][:, 0:n]),
                r=[("ps", bo)], w=[(tg, "o_sb", c0)])
        o_read(M0, 0, 512)
        o_read(M1, 512, 512)
        yield
        for sm in range(NS):
            bu = M0 if sm % 2 == 0 else M1
            pe(lambda e, bu=bu, sm=sm: e.matmul(ps[bu][:, 0:128], lhsT=R["ktok_s"][0:NS, :], rhs=vm[sm][0:NS, :],
                                               start=True, stop=True), r=[(tg, "ktok_s"), ("vm", sm)], w=[("ps", bu)])
            dve(lambda e, bu=bu, sm=sm: e.scalar_tensor_tensor(out=Snew[sm], in0=Sst[sm], scalar=float(GAMMA[h]),
                                                               in1=ps[bu][:, 0:128], op0=ALU.mult, op1=ALU.add),
                r=[("Sst", sm), ("ps", bu)], w=[("Snew", sm)])
            dma("sp", "o_ss%d" % sm, ss_out[sm, h], Snew[sm], r=[("Snew", sm)], w=[("ss_out", sm, h)])
            act(lambda e, sm=sm: e.copy(out=Sbs[sm], in_=Snew[sm]), r=[("Snew", sm)], w=[("Sbs", sm)])
        yield
        for sm in range(NS):
            pe(lambda e, sm=sm: e.matmul(ps[M2][:, sm:sm + 1], lhsT=Sbs[sm], rhs=g1["qr"][:, NP_ + sm:NP_ + sm + 1],
                                        start=True, stop=True), r=[("Sbs", sm)] + q_res, w=[("ps", M2)])
        o_read(M2, NP_, NS)
        yield
        for pi, (c0, n) in enumerate(((0, 512), (512, 512), (NP_, NS))):
            bm_ = M0 if pi % 2 == 0 else M1
            pe(lambda e, bm_=bm_, c0=c0, n=n: e.matmul(ps[bm_][:, 0:n], lhsT=cbv("ones128"), rhs=R["sq"][:, c0:c0 + n],
                                                      start=True, stop=True), r=[(tg, "sq", c0), "cb"], w=[("ps", bm_)])
            act(lambda e, bm_=bm_, c0=c0, n=n: e.activation(out=R["rstd"][:, c0:c0 + n], in_=ps[bm_][:, 0:n], func=AF.Ln,
                                                            bias=epsv, scale=1.0), r=[("ps", bm_)], w=[(tg, "rstd", c0)])
            act(lambda e, c0=c0, n=n: e.activation(out=R["rstd"][:, c0:c0 + n], in_=R["rstd"][:, c0:c0 + n], func=AF.Exp,
                                                   scale=-0.5), r=[(tg, "rstd", c0)], w=[(tg, "rstd", c0)])
            dve(lambda e, c0=c0, n=n: e.scalar_tensor_tensor(
                out=R["tt"][:, c0:c0 + n], in0=R["o_sb"][:, c0:c0 + n], scalar=rgv[:, h:h + 1],
                in1=R["rstd"][:, c0:c0 + n], op0=ALU.mult, op1=ALU.mult),
                r=[(tg, "o_sb", c0), (tg, "rstd", c0), "cf"], w=[(tg, "tt", c0)])
            pool(lambda e, c0=c0, n=n: e.tensor_tensor(out=mixT[:, 8 + h, c0:c0 + n], in0=R["tt"][:, c0:c0 + n],
                                                       in1=g1["sg"][:, c0:c0 + n], op=ALU.mult),
                 r=[(tg, "tt", c0)] + g_res, w=[("mixT", "r", h, c0)])
        yield

    def drive(gens):
        gens = [g for g in gens if g is not None]
        while gens:
            for g in list(gens):
                try:
                    next(g)
                except StopIteration:
                    gens.remove(g)

    def drive2(gb, ga):
        pat = [1, 1, 1, 0, 1, 0, 1, 1, 9, 9]
        bi = 0
        b_alive, a_alive = True, ga is not None
        while b_alive or a_alive:
            if b_alive:
                try:
                    next(gb)
                except StopIteration:
                    b_alive = False
            na = pat[min(bi, len(pat) - 1)] if b_alive else 99
            bi += 1
            for _ in range(na):
                if not a_alive:
                    break
                try:
                    next(ga)
                except StopIteration:
                    a_alive = False

    drive([stageA(0)])
    for h in range(8):
        drive2(stageB(h), stageA(h + 1) if h + 1 < 8 else None)
    P.fence()
    dump("mixT", mixT, ["mixT"])
    chk(6)

    Zt = Alloc(T2base, T2base + 12 * 1024)
    Zx = Alloc(0, 0)
    xs_off = None
    xstage = [aT[:, 0:8, :].rearrange("p a c -> p (a c)").bitcast(F32)[:, 0:D],
              aT[:, 8:16, :].rearrange("p a c -> p (a c)").bitcast(F32)[:, 0:D]]
    tiles_x = [(x_main[i * 128:(i + 1) * 128, :], 128, i * 128) for i in range(8)] + [(x_smp, NS, NP_)]
    for i, (src, rows, c0) in enumerate(tiles_x):
        s = i % 2
        dma("sp", "x%d" % s, xstage[s][0:rows, :], src, w=[("xstage", s)])
        for q4 in range(4):
            b = bank_m()
            for kk in range(4):
                k = q4 * 4 + kk
                pe(lambda e, b=b, kk=kk, k=k, s=s, rows=rows: e.transpose(
                    ps[b][:, kk * 128:kk * 128 + rows], xstage[s][0:rows, k * 128:(k + 1) * 128],
                    ident_f[0:rows, 0:rows]), r=[("xstage", s), "cf"], w=[("ps", b)])
            src_ps = lambda b=b, rows=rows: ps[b][:, :].rearrange("p (a c) -> p a c", a=4)[:, :, 0:rows]
            dst = hT[:, q4 * 4:(q4 + 1) * 4, c0:c0 + rows]
            if q4 % 2 == 0:
                act(lambda e, dst=dst, src_ps=src_ps: e.copy(out=dst, in_=src_ps()), r=[("ps", b)], w=[("hT", q4)])
            else:
                dve(lambda e, dst=dst, src_ps=src_ps: e.tensor_copy(out=dst, in_=src_ps()), r=[("ps", b)], w=[("hT", q4)])
    P.fence()

    sqs = [Zt([344], BF16) for _ in range(4)]
    rstdn = Zt([NT], F32)
    fT = aT[:, :, 0:NT]

    class NormState:
        pass

    def norm_begin(tag):
        st_ = NormState()
        st_.tag = tag
        st_.banks = [bank_m() for _ in TT]
        st_.pending = None
        st_.cnt = 0
        return st_

    def norm_flush(st_):
        if st_.pending is not None:
            slot, m, ti, n = st_.pending
            pe(lambda e: e.matmul(ps[st_.banks[ti]][:, 0:n], lhsT=cbv("onesD"), rhs=sqs[slot][:, 0:n],
                                  start=(m == 0), stop=(m == 15)),
               r=[("sqs", slot), "cb"], w=[("ps", st_.banks[ti])])
            st_.pending = None

    def norm_tile(st_, m, ti, c0, n):
        norm_flush(st_)
        slot = st_.cnt % 4
        st_.cnt += 1
        act(lambda e: e.activation(out=sqs[slot][:, 0:n], in_=hT[:, m, c0:c0 + n], func=AF.Square),
            r=[("hTm", m, c0)], w=[("sqs", slot)])
        st_.pending = (slot, m, ti, n)

    def norm_finish(st_, gname):
        tag = st_.tag
        norm_flush(st_)
        for ti, (c0, n) in enumerate(TT):
            act(lambda e, ti=ti, c0=c0, n=n: e.activation(out=rstdn[:, c0:c0 + n], in_=ps[st_.banks[ti]][:, 0:n], func=AF.Ln,
                                                          bias=epsv, scale=1.0),
                r=[("ps", st_.banks[ti])], w=[(tag, "rstdn0", ti)])
            act(lambda e, ti=ti, c0=c0, n=n: e.activation(out=rstdn[:, c0:c0 + n], in_=rstdn[:, c0:c0 + n], func=AF.Exp,
                                                          scale=-0.5),
                r=[(tag, "rstdn0", ti)], w=[(tag, "rstdn")])
        gv = cfv(gname)
        for k in range(16):
            dve(lambda e, k=k: e.scalar_tensor_tensor(out=fT[:, k, :], in0=hT[:, k, :], scalar=gv[:, k:k + 1], in1=rstdn,
                                                      op0=ALU.mult, op1=ALU.mult),
                r=[(tag, "rstdn"), "cf"] + [("hTm", k, c0) for (c0, n) in TT], w=[("fT", k)])

    mix_rhs = lambda k, c0, n: mixT[:, k, c0:c0 + n]
    n2 = norm_begin("n2")
    for j in range(8):
        def evac_out(ml, ti, c0, n, b, j=j):
            m = 2 * j + ml
            dve(lambda e: e.tensor_tensor(out=hT[:, m, c0:c0 + n], in0=ps[b][:, 0:n], in1=hT[:, m, c0:c0 + n], op=ALU.add),
                r=[("ps", b), ("hTm", m, c0)], w=[("hTm", m, c0)])
            norm_tile(n2, m, ti, c0, n)
        dense_block(16, mix_rhs, ["mixT"], evac_out)
    P.fence()
    dump("h1", hT, [])
    chk(7)

    norm_finish(n2, "fn_g")
    P.fence()
    dump("fT", fT, [])
    chk(8)

    uT = mixT
    sgt = [Zt([344], F32) for _ in range(2)]
    f_rhs = lambda k, c0, n: fT[:, k, c0:c0 + n]
    cnt_s = {"i": 0}
    for qi, (b0, nb) in enumerate(QUART):
        for bb in range(nb):
            wg, wgres = w_next()
            wu, wures = w_next()
            for ml in range(2):
                cl = 2 * bb + ml
                for ti, (c0, n) in enumerate(TT):
                    bg = bank_d(); bu = bank_d()
                    for (bk_, wv, wres) in ((bg, wg, wgres), (bu, wu, wures)):
                        for k in range(16):
                            pe(lambda e, bk_=bk_, wv=wv, k=k, ml=ml, c0=c0, n=n: e.matmul(
                                ps[bk_][:, 0:n], lhsT=wv[:, k, ml * 128:(ml + 1) * 128], rhs=fT[:, k, c0:c0 + n],
                                start=(k == 0), stop=(k == 15)), r=[wres], w=[("ps", bk_)])
                    s = cnt_s["i"] % 2
                    cnt_s["i"] += 1
                    act(lambda e, bg=bg, s=s, n=n: e.activation(out=sgt[s][:, 0:n], in_=ps[bg][:, 0:n], func=AF.Silu),
                        r=[("ps", bg)], w=[("sgt", s)])
                    dve(lambda e, bu=bu, s=s, n=n, cl=cl, c0=c0: e.tensor_tensor(out=uT[:, cl, c0:c0 + n], in0=ps[bu][:, 0:n],
                                                                               in1=sgt[s][:, 0:n], op=ALU.mult),
                        r=[("ps", bu), ("sgt", s)], w=[("uT", qi)])
            w_done(2)
        kq = 2 * nb
        u_rhs = lambda k, c0, n: uT[:, k, c0:c0 + n]
        if qi == 3:
            n3 = norm_begin("n3")
        for j in range(8):
            def evac_dn(ml, ti, c0, n, b, j=j):
                m = 2 * j + ml
                dve(lambda e: e.tensor_tensor(out=hT[:, m, c0:c0 + n], in0=ps[b][:, 0:n], in1=hT[:, m, c0:c0 + n],
                                              op=ALU.add), r=[("ps", b), ("hTm", m, c0)], w=[("hTm", m, c0)])
                if qi == 3:
                    norm_tile(n3, m, ti, c0, n)
            dense_block(kq, u_rhs, [("uT", qi)], evac_dn)
    P.fence()
    dump("h2", hT, [])
    chk(9)

    norm_finish(n3, "pn_g")
    pst = [uT[:, 0:1, :].rearrange("p a c -> p (a c)").bitcast(F32)[:, 0:256],
           uT[:, 1:2, :].rearrange("p a c -> p (a c)").bitcast(F32)[:, 0:256]]
    peT = uT[:, 4:6, :]
    tiles_p = [(p_main[i * 128:(i + 1) * 128, :], 128, i * 128) for i in range(8)] + [(p_smp, NS, NP_)]
    for i, (src, rows, c0) in enumerate(tiles_p):
        s = i % 2
        dma("sp", "x%d" % s, pst[s][0:rows, :], src, w=[("pst", s)])
        b = bank_m()
        for kk in range(2):
            pe(lambda e, b=b, kk=kk, s=s, rows=rows: e.transpose(ps[b][:, kk * 128:kk * 128 + rows],
                                                                pst[s][0:rows, kk * 128:(kk + 1) * 128],
                                                                ident_f[0:rows, 0:rows]), r=[("pst", s), "cf"], w=[("ps", b)])
        act(lambda e, b=b, rows=rows, c0=c0: e.copy(out=peT[:, :, c0:c0 + rows],
                                                   in_=ps[b][:, 0:256].rearrange("p (a c) -> p a c", a=2)[:, :, 0:rows]),
            r=[("ps", b)], w=["peT"])
    P.fence()
    wple = uT[:, 8:12, :].rearrange("p a c -> p (a c)")[:, 0:4096].rearrange("p (k n) -> p k n", k=2)
    wpleres = "wple"
    dma("pool", "wple", wple, w_ple.rearrange("(k p) n -> p k n", p=128), w=["wple"])
    sB = [uT[:, 12 + i, :].bitcast(F32)[:, 0:344] for i in range(2)]
    tA = [uT[:, 14 + i, :].bitcast(F32)[:, 0:344] for i in range(2)]
    for j in range(8):
        wv, wres = w_next()
        for ml in range(2):
            m = 2 * j + ml
            for ti, (c0, n) in enumerate(TT):
                bA = 2 * (cnt_s["i"] % 4); bB = bA + 1
                for k in range(2):
                    pe(lambda e, bA=bA, k=k, m=m, c0=c0, n=n: e.matmul(ps[bA][:, 0:n], lhsT=wple[:, k, m * 128:(m + 1) * 128],
                                                                      rhs=peT[:, k, c0:c0 + n], start=(k == 0), stop=(k == 1)),
                       r=[wpleres, "peT"], w=[("ps", bA)])
                for k in range(16):
                    pe(lambda e, bB=bB, k=k, ml=ml, c0=c0, n=n, wv=wv: e.matmul(
                        ps[bB][:, 0:n], lhsT=wv[:, k, ml * 128:(ml + 1) * 128], rhs=fT[:, k, c0:c0 + n],
                        start=(k == 0), stop=(k == 15)), r=[wres], w=[("ps", bB)])
                s = cnt_s["i"] % 2
                cnt_s["i"] += 1
                act(lambda e, bB=bB, s=s, n=n: e.activation(out=sB[s][:, 0:n], in_=ps[bB][:, 0:n], func=AF.Sigmoid),
                    r=[("ps", bB)], w=[("sB", s)])
                dve(lambda e, bA=bA, s=s, n=n: e.tensor_tensor(out=tA[s][:, 0:n], in0=ps[bA][:, 0:n], in1=sB[s][:, 0:n],
                                                              op=ALU.mult), r=[("ps", bA), ("sB", s)], w=[("tA", s)])
                dve(lambda e, s=s, n=n, m=m, c0=c0: e.tensor_tensor(out=hT[:, m, c0:c0 + n], in0=tA[s][:, 0:n],
                                                                   in1=hT[:, m, c0:c0 + n], op=ALU.add),
                    r=[("tA", s), ("hTm", m, c0)], w=[("hTm", m, c0)])
        w_done()
    P.fence()
    dump("h3", hT, [])
    chk(10)

    ystage = [aT[:, 0:8, :].rearrange("p a c -> p (a c)").bitcast(F32)[:, 0:D],
              aT[:, 8:16, :].rearrange("p a c -> p (a c)").bitcast(F32)[:, 0:D]]
    tiles_y = [(y_main[i * 128:(i + 1) * 128, :], 128, i * 128) for i in range(8)] + [(y_smp, NS, NP_)]
    for i, (dst, rows, c0) in enumerate(tiles_y):
        s = i % 2
        for q4 in range(4):
            b = bank_m()
            for kk in range(4):
                k = q4 * 4 + kk
                pe(lambda e, b=b, kk=kk, k=k, rows=rows, c0=c0: e.transpose(ps[b][0:rows, kk * 128:(kk + 1) * 128],
                                                                           hT[:, k, c0:c0 + rows], ident_f),
                   r=["cf"], w=[("ps", b)])
            if q4 % 2 == 0:
                act(lambda e, b=b, s=s, rows=rows, q4=q4: e.copy(out=ystage[s][0:rows, q4 * 512:(q4 + 1) * 512],
                                                                in_=ps[b][0:rows, :]), r=[("ps", b)], w=[("ystage", s, q4)])
            else:
                dve(lambda e, b=b, s=s, rows=rows, q4=q4: e.tensor_copy(out=ystage[s][0:rows, q4 * 512:(q4 + 1) * 512],
                                                                       in_=ps[b][0:rows, :]), r=[("ps", b)],
                    w=[("ystage", s, q4)])
        dma("sp", "y%d" % s, dst, ystage[s][0:rows, :], r=[("ystage", s, q4) for q4 in range(4)], w=[("y", i)])

    assert stop != 99 or wstate["used"] == len(wq), (wstate, len(wq))

    sems = {e: es.enter_context(nc.semaphore("s_" + e)) for e in Prog.ENGS}
    dma_sems = {k: es.enter_context(nc.semaphore("d_" + k)) for k in sorted(dma_sem_names)}
    block = es.enter_context(nc.Block())
    finals = sorted(dma_sem_names)
    P.emit(nc, block, sems, dma_sems, finals)
    es.close()
    print("ops", len(P.ops), "counts", P.final_counts[0])
    return nc


_CACHE = {}


def kernel(**inputs):
    inp = {k: np.asarray(v) for k, v in inputs.items()}
    if "nc" not in _CACHE:
        _CACHE["nc"] = build_program()
    nc = _CACHE["nc"]
    xp = inp["x_prompt"]; xs = inp["x_sample"]
    in_maps = []
    zeros_halo = np.zeros((NHALO, D), np.float32)
    shared = dict(
        an_g=np.ascontiguousarray(inp["attn_norm_g"].reshape(1, D)),
        w_in=np.ascontiguousarray(inp["w_in"][0]), w_out=np.ascontiguousarray(inp["w_out"][0]),
        w_gate=np.ascontiguousarray(inp["w_gate"][0]), w_up=np.ascontiguousarray(inp["w_up"][0]),
        w_down=np.ascontiguousarray(inp["w_down"][0]), w_ple=np.ascontiguousarray(inp["w_ple"][0]),
        w_pg=np.ascontiguousarray(inp["w_ple_gate"][0]),
    )
    for c in range(8):
        b, m = c // 4, c % 4
        t0 = m * 1024
        cf, cb = host_consts(c, inp)
        d = dict(shared)
        d.update(
            x_main=np.ascontiguousarray(xp[b, t0:t0 + 1024]),
            x_halo=np.ascontiguousarray(xp[b, t0 - 128:t0]) if m > 0 else zeros_halo,
            x_smp=np.ascontiguousarray(xs[4 * c:4 * c + 4, 0]),
            p_main=np.ascontiguousarray(inp["p_prompt"][0, b, t0:t0 + 1024]),
            p_smp=np.ascontiguousarray(inp["p_sample"][0, 4 * c:4 * c + 4, 0]),
            ck=np.ascontiguousarray(inp["cache_k_win"][0, 4 * c:4 * c + 4]),
            cv=np.ascontiguousarray(inp["cache_v_win"][0, 4 * c:4 * c + 4]),
            st=np.ascontiguousarray(inp["state_ret"][0, 4 * c:4 * c + 4]),
            cf=cf, cb=cb,
        )
        in_maps.append(d)
    res = run_bass_kernel_spmd(nc, in_maps, core_ids=list(range(8)))
    R = res.results
    _CACHE["last"] = R
    y_p = np.stack([np.concatenate([R[4 * b + m]["y_main"] for m in range(4)], 0) for b in range(2)], 0)
    y_s = np.concatenate([R[c]["y_smp"] for c in range(8)], 0)[:, None, :]
    kwp = np.stack([R[4 * b + 3]["kwin"] for b in range(2)], 0)[None]
    vwp = np.stack([R[4 * b + 3]["vwin"] for b in range(2)], 0)[None]
    rsp = np.stack([R[4 * b + 3]["rstate"] for b in range(2)], 0)[None]
    kws = np.concatenate([R[c]["ks_out"] for c in range(8)], 0)[None]
    vws = np.concatenate([R[c]["vs_out"] for c in range(8)], 0)[None]
    rss = np.concatenate([R[c]["ss_out"] for c in range(8)], 0)[None]
    return (y_p.astype(np.float32), y_s.astype(np.float32), kwp.astype(np.float32), vwp.astype(np.float32),
            rsp.astype(np.float32), kws.astype(np.float32), vws.astype(np.float32), rss.astype(np.float32))
```

```python
import math
from contextlib import ExitStack

import numpy as np
import ml_dtypes

import concourse.bass as bass
import concourse.mybir as mybir
from concourse.bass_utils import run_bass_kernel_spmd

F32 = mybir.dt.float32
BF16 = mybir.dt.bfloat16
U8 = mybir.dt.uint8
AF = mybir.ActivationFunctionType
ALU = mybir.AluOpType
AX = mybir.AxisListType

D = 2048
NP_ = 1024
NS = 4
NT = NP_ + NS
NHALO = 128
NA = NT + NHALO
KC = 16
DFF = 5632
EPS = 1e-6
ATTN_SCALE = 128 ** -0.5
RET_K_SCALE = 128 ** -0.5
PAST_LEN = 16384
TT = [(0, 344), (344, 344), (688, 340)]
TT_H = TT + [(NT, NHALO)]
NSLOT = 4
SLOT_BYTES = 8192
GAMMA = [1.0 - 2.0 ** (-5 - h) for h in range(8)]


class Op:
    __slots__ = ("eng", "fn", "dma", "idx", "waits", "sig", "val", "fence")

    def __init__(self, eng, fn, dma, idx):
        self.eng = eng
        self.fn = fn
        self.dma = dma
        self.idx = idx
        self.waits = []
        self.sig = dma is not None
        self.val = None
        self.fence = None


class Prog:
    ENGS = ("sp", "act", "dve", "pool", "pe")

    def __init__(self):
        self.ops = []
        self.last_w = {}
        self.readers = {}
        self.last_on = {}
        self.dma_ops = {}
        self.pending_fence = {}

    def add(self, eng, fn, r=(), w=(), dma=None, nofence=False):
        op = Op(eng, fn, dma, len(self.ops))
        psr = [x for x in r if isinstance(x, tuple) and x[0] == "ps"]
        if psr:
            r = [x for x in r if not (isinstance(x, tuple) and x[0] == "ps")]
            w = list(w) + psr
        deps = {}
        for res in r:
            lw = self.last_w.get(res)
            if lw is not None:
                deps.setdefault(lw, set()).add("raw")
        for res in w:
            lw = self.last_w.get(res)
            if lw is not None:
                deps.setdefault(lw, set()).add("waw")
            for rd in self.readers.get(res, ()):
                deps.setdefault(rd, set()).add("war")
        best = {}
        for d, kinds in deps.items():
            if d is op:
                continue
            if d.dma is not None:
                op.waits.append(d)
                continue
            if op.dma is None and d.eng == eng:
                if eng == "pe":
                    continue
            b = best.get(d.eng)
            if b is None or d.idx > b.idx:
                best[d.eng] = d
        for d in best.values():
            d.sig = True
            op.waits.append(d)
        for res in r:
            self.readers.setdefault(res, []).append(op)
        for res in w:
            self.last_w[res] = op
            self.readers[res] = []
        if eng in self.pending_fence and not nofence:
            op.fence = self.pending_fence.pop(eng)
        self.ops.append(op)
        if dma is None:
            self.last_on[eng] = op
        else:
            self.dma_ops.setdefault(dma, []).append(op)
        return op

    def fence(self):
        st = {"comp": dict(self.last_on), "dma": {k: v[-1] for k, v in self.dma_ops.items()}}
        for o in st["comp"].values():
            o.sig = True
        for e in self.ENGS:
            self.pending_fence[e] = st

    def emit(self, nc, block, sems, dma_sems, final_waits):
        cnt = {e: 0 for e in self.ENGS}
        dcnt = {}
        for op in self.ops:
            if op.dma is not None:
                dcnt[op.dma] = dcnt.get(op.dma, 0) + (1 if op.dma == "cc" else 16)
                op.val = dcnt[op.dma]
            elif op.sig:
                cnt[op.eng] += 1
                op.val = cnt[op.eng]
        self.final_counts = (cnt, dcnt)

        def semof(op):
            return dma_sems[op.dma] if op.dma is not None else sems[op.eng]

        def run(engname):
            def body(eng):
                waited = {}

                def wait(sem_key, sem, val):
                    if waited.get(sem_key, 0) >= val:
                        return
                    waited[sem_key] = val
                    eng.wait_ge(sem, val)

                for op in self.ops:
                    if op.eng != engname:
                        continue
                    if op.fence is not None:
                        for o in op.fence["comp"].values():
                            if o.eng != engname or engname != "pe":
                                wait(("c", o.eng), sems[o.eng], o.val)
                        for k, o in op.fence["dma"].items():
                            wait(("d", k), dma_sems[k], o.val)
                    for d in op.waits:
                        key = ("d", d.dma) if d.dma is not None else ("c", d.eng)
                        wait(key, semof(d), d.val)
                    ins = op.fn(eng)
                    if op.dma is not None:
                        ins.then_inc(dma_sems[op.dma], 1 if op.dma == "cc" else 16)
                    elif op.sig:
                        ins.then_inc(sems[op.eng], 1)
                if engname == "sp":
                    for k in final_waits:
                        if k in dcnt:
                            eng.wait_ge(dma_sems[k], dcnt[k])
            return body

        block.sync(run("sp"))
        block.scalar(run("act"))
        block.vector(run("dve"))
        block.gpsimd(run("pool"))
        block.tensor(run("pe"))


CF = {}
_off = 0
for _n, _w in [("ident", 128), ("ones_row", 128), ("fn_g", 16), ("pn_g", 16), ("rg", 8), ("qg", 1), ("kg", 1),
               ("sinks", 8), ("intraT", 1024), ("qdec", 1024), ("kdec", 8), ("kdlong", 64), ("coef", 32),
               ("onehot", 4), ("cos", NT), ("sin", NT)]:
    CF[_n] = (_off, _w)
    _off += _w
NCF = _off
CB = {}
_off = 0
for _n, _w in [("ident", 128), ("ones", 128), ("onesD", 128), ("ones128", 128), ("rrot", 128), ("mask", 384)]:
    CB[_n] = (_off, _w)
    _off += _w
NCB = _off


def host_consts(core, inp):
    m = core % 4
    cf = np.zeros((128, NCF), np.float32)

    def put(name, arr):
        o, w = CF[name]
        cf[:, o:o + w] = np.asarray(arr, np.float32).reshape(128, w) if np.ndim(arr) == 2 else np.broadcast_to(
            np.asarray(arr, np.float32).reshape(1, w), (128, w))

    put("ident", np.eye(128, dtype=np.float32))
    put("ones_row", np.ones((128, 128), np.float32))
    put("fn_g", inp["ffn_norm_g"][0].reshape(16, 128).T)
    put("pn_g", inp["ple_norm_g"][0].reshape(16, 128).T)
    put("rg", inp["ret_out_g"][0].reshape(8, 128).T)
    put("qg", inp["q_norm_g"][0].reshape(128, 1))
    put("kg", inp["k_norm_g"][0].reshape(128, 1))
    put("sinks", inp["attn_sinks"][0].reshape(8))
    g = np.array(GAMMA, np.float64)
    j = np.arange(128)
    diff = j[None, :] - j[:, None]
    intraT = np.where(diff[:, None, :] >= 0, g[None, :, None] ** np.maximum(diff, 0)[:, None, :], 0.0) * RET_K_SCALE
    put("intraT", intraT.reshape(128, 1024))
    qdec = g[:, None] ** (j[None, :] + 1.0)
    put("qdec", qdec.reshape(1024))
    put("kdec", (g[None, :] ** (127.0 - j[:, None])) * RET_K_SCALE)
    c = np.arange(8)
    kdl = g[None, :, None] ** (1023.0 - (128.0 * c[None, None, :] + j[:, None, None])) * RET_K_SCALE
    put("kdlong", kdl.reshape(128, 64))
    coef = np.zeros((4, 8))
    for r in range(4):
        if r < m:
            coef[r] = g ** (1024.0 * (m - r - 1))
    put("coef", coef.reshape(32))
    oh = np.zeros((128, 4), np.float32)
    oh[:4] = np.eye(4)
    put("onehot", oh)
    pos = np.concatenate([m * 1024 + np.arange(1024), np.full(4, PAST_LEN)]).astype(np.float32)
    inv = (np.float32(10000.0) ** (-np.arange(64, dtype=np.float32) / np.float32(64))).astype(np.float32)
    ang = (pos[None, :] * inv[:, None]).astype(np.float32).astype(np.float64)
    cos = np.cos(ang)
    sin = np.sin(ang)
    put("cos", np.concatenate([cos, cos], 0))
    put("sin", np.concatenate([-sin, sin], 0))

    cb = np.zeros((128, NCB), np.float32)

    def putb(name, arr):
        o, w = CB[name]
        cb[:, o:o + w] = arr

    putb("ident", np.eye(128))
    putb("ones", np.ones((128, 128)))
    putb("onesD", np.full((128, 128), 1.0 / D))
    putb("ones128", np.full((128, 128), 1.0 / 128))
    rr = np.zeros((128, 128))
    for p in range(128):
        rr[(p + 64) % 128, p] = 1.0
    putb("rrot", rr)
    NEG = -30000.0
    own = np.where(j[:, None] <= j[None, :], 0.0, NEG).astype(np.float32)
    prev = np.where(j[:, None] >= j[None, :], 0.0, NEG).astype(np.float32)
    putb("mask", np.concatenate([own, prev, prev if m != 0 else np.full((128, 128), NEG, np.float32)], 1))
    return cf, cb.astype(ml_dtypes.bfloat16)


class _Stop(Exception):
    pass


def build_program(dbg=None, stop=99, groups=None):
    nc = bass.Bass("TRN2", target_bir_lowering=False)
    P = Prog()
    dbg = dbg or {}

    stopped = [False]

    def chk(k):
        if stop == k:
            stopped[0] = True

    def din(name, shape, dt=F32):
        return nc.dram_tensor(name, list(shape), dt, kind="ExternalInput").ap()

    def dout(name, shape, dt=F32):
        return nc.dram_tensor(name, list(shape), dt, kind="ExternalOutput").ap()

    x_main = din("x_main", [NP_, D]); x_halo = din("x_halo", [NHALO, D]); x_smp = din("x_smp", [NS, D])
    p_main = din("p_main", [NP_, 256]); p_smp = din("p_smp", [NS, 256])
    ck = din("ck", [NS, 128, 2, 128]); cv = din("cv", [NS, 128, 2, 128]); st = din("st", [NS, 8, 128, 128])
    an_g = din("an_g", [1, D])
    w_in = din("w_in", [D, DFF]); w_out = din("w_out", [D, D]); w_gate = din("w_gate", [D, DFF])
    w_up = din("w_up", [D, DFF]); w_down = din("w_down", [DFF, D]); w_ple = din("w_ple", [256, D])
    w_pg = din("w_pg", [D, D])
    cf_d = din("cf", [128, NCF]); cb_d = din("cb", [128, NCB], BF16)

    y_main = dout("y_main", [NP_, D]); y_smp = dout("y_smp", [NS, D])
    kwin = dout("kwin", [128, 2, 128]); vwin = dout("vwin", [128, 2, 128]); rstate = dout("rstate", [8, 128, 128])
    ks_out = dout("ks_out", [NS, 128, 2, 128]); vs_out = dout("vs_out", [NS, 128, 2, 128])
    ss_out = dout("ss_out", [NS, 8, 128, 128])
    krs = nc.dram_tensor("krs", [8, 128, NT], BF16)
    vts = nc.dram_tensor("vts", [8, 128, NT], BF16)
    ag_in = nc.dram_tensor("ag_in", [8 * 128, 128], F32)
    ag_out = nc.dram_tensor("ag_out", [4 * 8 * 128, 128], F32)
    dbg_out = {}
    for name, shape in dbg.items():
        dbg_out[name] = dout("dbg_" + name, shape)

    def dump(name, ap, r=()):
        if name in dbg_out:
            P.add("pool", lambda e: e.dma_start(out=dbg_out[name], in_=ap), r, [("dbg", name)], dma="o_dbg")

    es = ExitStack()
    total = (nc.sbuf_bytes_remaining - 64) // 64 * 64
    arena = es.enter_context(nc.sbuf_tensor("arena", [128, total], U8))
    ps = [es.enter_context(nc.psum_tensor("ps%d" % i, [128, 512], F32)) for i in range(8)]

    class Alloc:
        def __init__(self, base, limit):
            self.p = base
            self.limit = limit

        def __call__(self, shape, dt):
            esz = 4 if dt == F32 else 2
            n = int(np.prod(shape)) * esz
            off = (self.p + 31) // 32 * 32
            self.p = off + n
            assert self.p <= self.limit, (self.p, self.limit)
            v = arena[:, off:off + n].bitcast(dt)
            if len(shape) == 2:
                return v.rearrange("p (a b) -> p a b", a=shape[0])
            if len(shape) == 3:
                return v.rearrange("p (a b c) -> p a b c", a=shape[0], b=shape[1])
            return v

    A = Alloc(0, total)
    cf = A([NCF], F32)
    cb = A([NCB], BF16)
    negc = A([1], F32); esink = A([8], F32); S0 = A([8, 128], F32)
    misc = A([64], F32)
    wslots = [A([SLOT_BYTES // 2], BF16) for _ in range(NSLOT)]
    aT = A([KC, NA], BF16)
    mixT = A([KC, NT], BF16)
    T2base = A.p
    T2 = Alloc(T2base, T2base + 12 * 1024)
    A.p = T2base + 12 * 1024
    HB = (A.p + 31) // 32 * 32
    hT = A([KC, NT], F32)
    HEND = A.p
    print("sbuf used", A.p, "of", total)

    def cfv(name, lo=0, hi=None):
        o, w = CF[name]
        return cf[:, o + lo:o + (w if hi is None else hi)]

    def cbv(name, lo=0, hi=None):
        o, w = CB[name]
        return cb[:, o + lo:o + (w if hi is None else hi)]

    ident_f = cfv("ident"); ident_b = cbv("ident")

    def psb(i, n=1024):
        return ps[i][:, :].bitcast(BF16)[:, 0:n]

    def pe(fn, r=(), w=()): return P.add("pe", fn, r, w)
    def act(fn, r=(), w=()): return P.add("act", fn, r, w)
    def dve(fn, r=(), w=()): return P.add("dve", fn, r, w)
    def pool(fn, r=(), w=()): return P.add("pool", fn, r, w)
    def dma(q, sem, out, in_, r=(), w=(), nofence=False, slow=False):
        if slow:
            return P.add(q, lambda e: e.dma_start(out=out, in_=in_, allow_slow_non_contiguous=True), r, w, dma=sem)
        return P.add(q, lambda e: e.dma_start(out=out, in_=in_), r, w, dma=sem, nofence=nofence)

    dma_sem_names = set()
    _orig_add = P.add

    def add_track(eng, fn, r=(), w=(), dma=None, nofence=False):
        if stopped[0]:
            return None
        if dma is not None:
            dma_sem_names.add(dma)
        return _orig_add(eng, fn, r, w, dma, nofence)
    P.add = add_track

    rot = {"d": 0, "m": 0}

    att_dense3 = [False]

    def bank_d():
        b = rot["d"] % (3 if att_dense3[0] else 4)
        rot["d"] += 1
        return b

    def bank_m():
        b = 4 + rot["m"] % 4
        rot["m"] += 1
        return b

    wq = []
    wstate = {"issued": 0, "used": 0, "done": 0}
    w_extra = []

    def w_issue_upto(n):
        while wstate["issued"] < min(n, len(wq)):
            i = wstate["issued"]
            src, kc, ncols = wq[i]
            s = i % NSLOT
            dst = wslots[s][:, 0:kc * ncols].rearrange("p (k n) -> p k n", k=kc)
            dma("pool", "w%d" % s, dst, src.rearrange("(k p) n -> p k n", p=128), r=list(w_extra), w=[("w", s)], nofence=True)
            wstate["issued"] += 1

    def w_done(n=1):
        wstate["done"] += n
        w_issue_upto(wstate["done"] + NSLOT)

    def w_next():
        i = wstate["used"]
        assert i < wstate["done"] + NSLOT
        w_issue_upto(i + 1)
        src, kc, ncols = wq[i]
        s = i % NSLOT
        wstate["used"] += 1
        return wslots[s][:, 0:kc * ncols].rearrange("p (k n) -> p k n", k=kc), ("w", s)

    def wblk(wap, k0, k1, c0, ncols):
        return (wap[k0 * 128:k1 * 128, c0:c0 + ncols], k1 - k0, ncols)

    C_AQ, C_AK, C_AV, C_RQ, C_RK, C_RV, C_RG = 0, 1024, 1280, 1536, 2560, 3584, 4608
    for j in range(4):
        wq.append(wblk(w_in, 0, 16, C_RK + 256 * j, 256))
        wq.append(wblk(w_in, 0, 16, C_RV + 256 * j, 256))
    wq.append(wblk(w_in, 0, 16, C_AK, 256))
    wq.append(wblk(w_in, 0, 16, C_AV, 256))
    for j in range(4):
        wq.append(wblk(w_in, 0, 16, C_AQ + 256 * j, 256))
    for j in range(4):
        for cbase in (C_RQ, C_RG):
            wq.append(wblk(w_in, 0, 16, cbase + 256 * j, 256))
    for j in range(8):
        wq.append(wblk(w_out, 0, 16, 256 * j, 256))
    QUART = [(0, 6), (6, 6), (12, 5), (17, 5)]
    for (b0, nb) in QUART:
        for b in range(b0, b0 + nb):
            wq.append(wblk(w_gate, 0, 16, 256 * b, 256))
            wq.append(wblk(w_up, 0, 16, 256 * b, 256))
        for j in range(8):
            wq.append(wblk(w_down, 2 * b0, 2 * (b0 + nb), 256 * j, 256))
    for j in range(8):
        wq.append(wblk(w_pg, 0, 16, 256 * j, 256))

    def dense_block(kc, rhs_fn, rhs_res, evac, tiles=TT, nm=2):
        wv, wres = w_next()
        for ml in range(nm):
            for ti, (c0, n) in enumerate(tiles):
                b = bank_d()
                for k in range(kc):
                    pe(lambda e, b=b, k=k, ml=ml, c0=c0, n=n, wv=wv: e.matmul(
                        ps[b][:, 0:n], lhsT=wv[:, k, ml * 128:(ml + 1) * 128], rhs=rhs_fn(k, c0, n),
                        start=(k == 0), stop=(k == kc - 1)),
                       r=[wres] + list(rhs_res), w=[("ps", b)])
                evac(ml, ti, c0, n, b)
        w_done()

    dma("sp", "c_cf", cf, cf_d, w=["cf"])
    dma("sp", "c_cb", cb, cb_d, w=["cb"])

    Z = Alloc(HB, total)
    NXT = 4
    xt = [Z([D], F32) for _ in range(NXT)]
    junk = Z([D], BF16)
    xn = [Z([D], BF16) for _ in range(2)]
    gbc = Z([D], F32)
    ss = misc[:, 0:10]; rstd1 = misc[:, 10:20]; tmpa = misc[:, 20:30]
    gpa = misc[:, 30:31]; mx = misc[:, 31:32]; negc1 = misc[:, 32:33]; epsv = misc[:, 34:35]
    dve(lambda e: e.memset(epsv, EPS), w=["epsv"])

    dma("sp", "c_gbc", gbc, an_g.partition_broadcast(128), w=["gbc"])

    dve(lambda e: e.tensor_tensor(out=gpa, in0=cfv("qg"), in1=cfv("kg"), op=ALU.mult), r=["cf"], w=["gpa"])
    gpa2 = misc[:, 33:34]
    dve(lambda e: e.tensor_tensor(out=gpa2, in0=gpa, in1=gpa, op=ALU.mult), r=["gpa"], w=["gpa2"])
    b = bank_m()
    pe(lambda e, b=b: e.transpose(ps[b][0:1, 0:128], gpa2, ident_f), r=["gpa2", "cf"], w=[("ps", b)])
    dve(lambda e, b=b: e.tensor_reduce(out=mx[0:1, :], in_=ps[b][0:1, 0:128], axis=AX.X, op=ALU.max),
        r=[("ps", b)], w=["mx"])
    act(lambda e: e.activation(out=mx[0:1, :], in_=mx[0:1, :], func=AF.Sqrt), r=["mx"], w=["mx"])
    dve(lambda e: e.tensor_scalar(out=negc1[0:1, :], in0=mx[0:1, :], scalar1=-(ATTN_SCALE * 128.0), scalar2=None,
                                  op0=ALU.mult), r=["mx"], w=["negc1"])
    b = bank_m()
    pe(lambda e, b=b: e.matmul(ps[b][:, 0:1], lhsT=cfv("ones_row")[0:1, :], rhs=negc1[0:1, :], start=True, stop=True),
       r=["negc1", "cf"], w=[("ps", b)])
    dve(lambda e, b=b: e.tensor_copy(out=negc, in_=ps[b][:, 0:1]), r=[("ps", b)], w=["negc"])
    act(lambda e: e.activation(out=esink, in_=cfv("sinks"), func=AF.Exp, bias=negc, scale=1.0),
        r=["negc", "cf"], w=["esink"])

    tiles1 = [(x_main[i * 128:(i + 1) * 128, :], 128, i * 128) for i in range(8)]
    tiles1.append((x_halo, 128, NT))
    tiles1.append((x_smp, NS, NP_))
    def p1_load(i, src, rows, c0):
        s4 = i % NXT
        dma("sp", "x%d" % s4, xt[s4][0:rows, :], src, w=[("xt", s4)])

    def p1_stage1(i, src, rows, c0):
        s = i % 2
        s4 = i % NXT
        act(lambda e, s4=s4, rows=rows, i=i: e.activation(out=junk[0:rows, :], in_=xt[s4][0:rows, :], func=AF.Square,
                                                         accum_out=ss[0:rows, i:i + 1]),
            r=[("xt", s4)], w=["junk", ("ss", i)])
        act(lambda e, rows=rows, i=i: e.activation(out=tmpa[0:rows, i:i + 1], in_=ss[0:rows, i:i + 1], func=AF.Sqrt,
                                                   bias=epsv[0:rows, :], scale=1.0 / D),
            r=[("ss", i), "epsv"], w=[("tmpa", i)])
        dve(lambda e, rows=rows, i=i: e.reciprocal(out=rstd1[0:rows, i:i + 1], in_=tmpa[0:rows, i:i + 1]),
            r=[("tmpa", i)], w=[("rstd1", i)])
        dve(lambda e, s=s, s4=s4, rows=rows, i=i: e.scalar_tensor_tensor(
            out=xn[s][0:rows, :], in0=xt[s4][0:rows, :], scalar=rstd1[0:rows, i:i + 1], in1=gbc[0:rows, :],
            op0=ALU.mult, op1=ALU.mult), r=[("xt", s4), ("rstd1", i), "gbc"], w=[("xn", s)])

    def p1_stage2(i, src, rows, c0):
        s = i % 2
        for half in range(2):
            b = bank_m()
            for kk in range(8):
                k = half * 8 + kk
                pe(lambda e, b=b, kk=kk, k=k, s=s, rows=rows: e.transpose(
                    psb(b)[:, kk * 128:kk * 128 + rows], xn[s][0:rows, k * 128:(k + 1) * 128],
                    ident_b[0:rows, 0:rows]), r=[("xn", s), "cb"], w=[("ps", b)])
            src_ps = lambda b=b, rows=rows: psb(b).rearrange("p (a c) -> p a c", a=8)[:, :, 0:rows]
            dst = aT[:, half * 8:(half + 1) * 8, c0:c0 + rows]
            if half == 0:
                act(lambda e, dst=dst, src_ps=src_ps: e.copy(out=dst, in_=src_ps()), r=[("ps", b)], w=[("aT", c0, half)])
            else:
                dve(lambda e, dst=dst, src_ps=src_ps: e.tensor_copy(out=dst, in_=src_ps()), r=[("ps", b)], w=[("aT", c0, half)])

    for i in range(NXT):
        p1_load(i, *tiles1[i])
    p1_stage1(0, *tiles1[0])
    for i in range(len(tiles1)):
        if i + 1 < len(tiles1):
            p1_stage1(i + 1, *tiles1[i + 1])
        if i + NXT < len(tiles1):
            p1_load(i + NXT, *tiles1[i + NXT])
        if i in (3, 5, 7, 8):
            w_extra[:] = [("xt", (i + 1) % NXT)]
            w_issue_upto(wstate["issued"] + 1)
            w_extra[:] = []
        p1_stage2(i, *tiles1[i])
    w_issue_upto(NSLOT)
    P.fence()
    dump("aT", aT, ["aT"])
    chk(1)

    aT_rhs = lambda k, c0, n: aT[:, k, c0:c0 + n]
    cosv = cfv("cos"); sinv = cfv("sin")

    rot_pending = [None]

    def rot_flush():
        if rot_pending[0] is not None:
            f = rot_pending[0]
            rot_pending[0] = None
            f()

    def rotary_ops(tag, b, c0, n, xb, t1, t2, outb):
        act(lambda e: e.copy(out=xb[:, c0:c0 + n], in_=ps[b][:, 0:n]), r=[("ps", b)], w=[(tag, "xb", c0)])
        dve(lambda e: e.tensor_tensor(out=t1[:, c0:c0 + n], in0=ps[b][:, 0:n], in1=cosv[:, c0:c0 + n], op=ALU.mult),
            r=[("ps", b), "cf"], w=[(tag, "t1", c0)])
        rot_flush()

        def part_b():
            b2 = bank_m()
            pe(lambda e: e.matmul(ps[b2][:, 0:n], lhsT=cbv("rrot"), rhs=xb[:, c0:c0 + n], start=True, stop=True),
               r=[(tag, "xb", c0), "cb"], w=[("ps", b2)])
            dve(lambda e: e.tensor_tensor(out=t2[:, c0:c0 + n], in0=ps[b2][:, 0:n], in1=sinv[:, c0:c0 + n], op=ALU.mult),
                r=[("ps", b2), "cf"], w=[(tag, "t2", c0)])
            dve(lambda e: e.tensor_tensor(out=outb[:, c0:c0 + n], in0=t1[:, c0:c0 + n], in1=t2[:, c0:c0 + n], op=ALU.add),
                r=[(tag, "t1", c0), (tag, "t2", c0)], w=[(tag, "rot", c0)])
        rot_pending[0] = part_b

    Z = Alloc(HB, total)
    p1 = []
    for hp in range(2):
        p1.append(dict(xb=Z([NT], BF16), t1=Z([NT], F32), t2=Z([NT], F32), kr=Z([NT], BF16),
                       kD=Z([8, 128], BF16), vT=Z([NT], BF16), vtok=Z([8, 128], BF16), sloc=Z([128], F32)))
    kdl = cfv("kdlong").rearrange("p (h c) -> p h c", h=8)

    for j in range(4):
        hs = (2 * j, 2 * j + 1)

        def evac_k(ml, ti, c0, n, b, hs=hs):
            bf = p1[ml]
            rotary_ops(("p1", ml), b, c0, n, bf["xb"], bf["t1"], bf["t2"], bf["kr"])

        def evac_v(ml, ti, c0, n, b, hs=hs):
            bf = p1[ml]
            act(lambda e: e.copy(out=bf["vT"][:, c0:c0 + n], in_=ps[b][:, 0:n]), r=[("ps", b)],
                w=[("p1", ml, "vT", c0)])
        dense_block(16, aT_rhs, ["aT"], evac_k)
        chk(20)
        dense_block(16, aT_rhs, ["aT"], evac_v)
        rot_flush()
        chk(21)
        for ml in range(2):
            h = hs[ml]
            bf = p1[ml]
            rk_res = [(("p1", ml), "rot", c0) for (c0, n) in TT]
            rv_res = [("p1", ml, "vT", c0) for (c0, n) in TT]
            dma("sp", "spk%d" % ml, krs.ap()[h], bf["kr"], r=rk_res, w=[("krs", h)])
            dma("sp", "spv%d" % ml, vts.ap()[h], bf["vT"], r=rv_res, w=[("vts", h)])
            for half in range(2):
                bk = bank_m()
                for cc in range(4):
                    c = half * 4 + cc
                    pe(lambda e, bk=bk, cc=cc, c=c, bf=bf: e.transpose(
                        psb(bk)[:, cc * 128:(cc + 1) * 128], bf["kr"][:, c * 128:(c + 1) * 128], ident_b),
                       r=rk_res + ["cb"], w=[("ps", bk)])
                for cc in range(4):
                    c = half * 4 + cc
                    dve(lambda e, bk=bk, cc=cc, c=c, bf=bf, h=h: e.tensor_scalar(
                        out=bf["kD"][:, c, :], in0=psb(bk)[:, cc * 128:(cc + 1) * 128], scalar1=kdl[:, h, c:c + 1],
                        scalar2=None, op0=ALU.mult), r=[("ps", bk), "cf"], w=[("p1", ml, "kD", c)])
                bv = bank_m()
                for cc in range(4):
                    c = half * 4 + cc
                    pe(lambda e, bv=bv, cc=cc, c=c, bf=bf: e.transpose(
                        psb(bv)[:, cc * 128:(cc + 1) * 128], bf["vT"][:, c * 128:(c + 1) * 128], ident_b),
                       r=rv_res + ["cb"], w=[("ps", bv)])
                act(lambda e, bv=bv, half=half, bf=bf: e.copy(
                    out=bf["vtok"][:, half * 4:(half + 1) * 4, :],
                    in_=psb(bv)[:, 0:512].rearrange("p (a c) -> p a c", a=4)), r=[("ps", bv)], w=[("p1", ml, "vtok", half)])
            chk(22)
            bs = bank_m()
            for c in range(8):
                pe(lambda e, bs=bs, c=c, bf=bf: e.matmul(ps[bs][:, 0:128], lhsT=bf["kD"][:, c, :], rhs=bf["vtok"][:, c, :],
                                                        start=(c == 0), stop=(c == 7)),
                   r=[("p1", ml, "kD", c), ("p1", ml, "vtok", c // 4)], w=[("ps", bs)])
            dve(lambda e, bs=bs, bf=bf: e.tensor_copy(out=bf["sloc"], in_=ps[bs][:, 0:128]), r=[("ps", bs)],
                w=[("p1", ml, "sloc")])
            chk(23)
            dma("sp", "agi", ag_in.ap()[h * 128:(h + 1) * 128, :], bf["sloc"], r=[("p1", ml, "sloc")], w=["ag_in"])
            chk(24)

    dump("ag_in", ag_in.ap(), ["ag_in"])
    chk(2)
    P.fence()
    P.add("pool", lambda e: e.collective_compute("AllGather", ALU.bypass, replica_groups=groups or [[0, 1, 2, 3], [4, 5, 6, 7]],
                                                 ins=[ag_in.ap().opt()], outs=[ag_out.ap().opt()]),
          r=["ag_in"], w=["ag_out"], dma="cc")
    chk(3)

    Z = Alloc(HB, total)
    zf = [Z([NA], F32) for _ in range(2)]
    sq = [Z([NA], BF16) for _ in range(2)]
    rstdb = [Z([NA], F32)] * 2
    knT = Z([2, NA], BF16)
    kn32 = Z([2, 132], F32)
    v32 = Z([2, 132], F32)
    vTb = Z([2, NA], BF16)
    vtokA = Z([2, 9, 128], BF16)
    qnT = Z([4, NT], BF16)
    qnT2 = Z([4, NT], BF16)
    PTm = [Z([2, 512], BF16) for _ in range(2)]
    rec = [Z([512], F32) for _ in range(2)]
    win_t = Z([2, 128], F32)
    vstok = Z([2, 128], BF16)
    kc_b = [Z([128], BF16) for _ in range(NS)]
    vc_b = [Z([128], BF16) for _ in range(NS)]
    kcT = Z([4, 128], BF16)
    PTc = Z([16], BF16)
    Pn = Z([16], BF16)
    recs = Z([16], F32)
    cnt = {"qk": 0}

    qk_pending = [None]

    def qk_flush():
        if qk_pending[0] is not None:
            f = qk_pending[0]
            qk_pending[0] = None
            f()

    def qknorm(b, c0, n, gname, outbf, tagres, out32=None):
        s = cnt["qk"] % 2
        cnt["qk"] += 1
        act(lambda e: e.activation(out=sq[s][:, 0:n], in_=ps[b][:, 0:n], func=AF.Square), r=[("ps", b)], w=[("sq", s)])
        dve(lambda e: e.tensor_copy(out=zf[s][:, 0:n], in_=ps[b][:, 0:n]), r=[("ps", b)], w=[("zf", s)])
        qk_flush()

        def part_b():
            b2 = 3
            pe(lambda e: e.matmul(ps[b2][:, 0:n], lhsT=cbv("ones128"), rhs=sq[s][:, 0:n], start=True, stop=True),
               r=[("sq", s), "cb"], w=[("ps", b2)])
            act(lambda e: e.activation(out=rstdb[0][:, 0:n], in_=ps[b2][:, 0:n], func=AF.Ln, bias=epsv, scale=1.0),
                r=[("ps", b2)], w=["rstdb"])
            act(lambda e: e.activation(out=rstdb[0][:, 0:n], in_=rstdb[0][:, 0:n], func=AF.Exp, scale=-0.5),
                r=["rstdb"], w=["rstdb"])
            dve(lambda e: e.scalar_tensor_tensor(out=outbf, in0=zf[s][:, 0:n], scalar=cfv(gname), in1=rstdb[0][:, 0:n],
                                                 op0=ALU.mult, op1=ALU.mult),
                r=[("zf", s), "rstdb", "cf"], w=[tagres])
            if out32 is not None:
                lo, hi, dst = out32
                dve(lambda e: e.scalar_tensor_tensor(out=dst, in0=zf[s][:, lo:hi], scalar=cfv(gname),
                                                     in1=rstdb[0][:, lo:hi], op0=ALU.mult, op1=ALU.mult),
                    r=[("zf", s), "rstdb", "cf"], w=[("kn32", tagres)])
        qk_pending[0] = part_b

    def evac_ak(ml, ti, c0, n, b):
        o32 = None
        if ti == 2:
            o32 = (208, 340, kn32[:, ml, :])
        qknorm(b, c0, n, "kg", knT[:, ml, c0:c0 + n], ("knT", ml, c0), o32)

    def evac_av(ml, ti, c0, n, b):
        act(lambda e: e.copy(out=vTb[:, ml, c0:c0 + n], in_=ps[b][:, 0:n]), r=[("ps", b)], w=[("vTb", ml, c0)])
        if ti == 2:
            dve(lambda e: e.tensor_copy(out=v32[:, ml, :], in_=ps[b][:, 208:340]), r=[("ps", b)], w=[("v32", ml)])

    att_dense3 = [True]
    dense_block(16, aT_rhs, ["aT"], evac_ak, tiles=TT_H)
    qk_flush()
    dense_block(16, aT_rhs, ["aT"], evac_av, tiles=TT_H)
    vres = lambda g: [("vTb", g, c0) for (c0, n) in TT_H]
    kres = lambda g: [("knT", g, c0) for (c0, n) in TT_H]
    for g in range(2):
        for grp in range(3):
            blks = [0, 1, 2, 3] if grp == 0 else ([4, 5, 6, 7] if grp == 1 else [8])
            bv = bank_m()
            for ii, blk in enumerate(blks):
                col = NT if blk == 0 else (blk - 1) * 128
                pe(lambda e, bv=bv, ii=ii, col=col, g=g: e.transpose(
                    psb(bv)[:, ii * 128:(ii + 1) * 128], vTb[:, g, col:col + 128], ident_b),
                   r=vres(g) + ["cb"], w=[("ps", bv)])
            nb = len(blks)
            act(lambda e, bv=bv, g=g, blks=blks, nb=nb: e.copy(
                out=vtokA[:, g, blks[0]:blks[0] + nb, :],
                in_=psb(bv)[:, 0:nb * 128].rearrange("p (a c) -> p a c", a=nb)), r=[("ps", bv)], w=[("vtokA", g)])
        bv = bank_m()
        pe(lambda e, bv=bv, g=g: e.transpose(psb(bv)[0:NS, 0:128], vTb[:, g, NP_:NP_ + NS], ident_b),
           r=vres(g) + ["cb"], w=[("ps", bv)])
        act(lambda e, bv=bv, g=g: e.copy(out=vstok[0:NS, g, :], in_=psb(bv)[0:NS, 0:128]), r=[("ps", bv)],
            w=[("vstok", g)])
    for (src32, dst, nm) in ((kn32, kwin, "kw"), (v32, vwin, "vw")):
        for g in range(2):
            bw = bank_m()
            pe(lambda e, bw=bw, g=g, src32=src32: e.transpose(ps[bw][:, 0:128], src32[:, g, 0:128], ident_f),
               r=[("kn32", ("knT", g, 688)), ("v32", g), "cf"], w=[("ps", bw)])
            dve(lambda e, bw=bw, g=g: e.tensor_copy(out=win_t[:, g, :], in_=ps[bw][:, 0:128]), r=[("ps", bw)],
                w=[("win_t", g)])
        dma("sp", "o_" + nm, dst, win_t, r=[("win_t", 0), ("win_t", 1)], w=["out_" + nm])
    dma("sp", "o_ks", ks_out[:, 0:127, :, :], ck[:, 1:128, :, :], w=["ks_out_a"])
    dma("sp", "o_vs", vs_out[:, 0:127, :, :], cv[:, 1:128, :, :], w=["vs_out_a"])
    for g in range(2):
        for s in range(NS):
            dma("sp", "o_ks", ks_out[s, 127, g, :].rearrange("(d o) -> d o", o=1), kn32[:, g, 128 + s:129 + s],
                r=[("kn32", ("knT", g, 688))], w=[("ks_out_b", g, s)], slow=True)
            dma("sp", "o_vs", vs_out[s, 127, g, :].rearrange("(d o) -> d o", o=1), v32[:, g, 128 + s:129 + s],
                r=[("v32", g)], w=[("vs_out_b", g, s)], slow=True)

    mask = cbv("mask").rearrange("p (a c) -> p a c", a=3)
    v3 = lambda ap: ap[:, 0:1024].rearrange("p (a c) -> p a c", a=8)
    esr_f = v3(zf[0]); esr_d = v3(zf[1]); esr_hi = v3(sq[0]); esr_lo = v3(sq[1])
    esr2 = Z([8, 128], BF16)
    oh = cfv("onehot")
    dve(lambda e: e.tensor_copy(out=esr_f[0:2], in_=esink[0:2, :].unsqueeze(2).broadcast_to([2, 8, 128])),
        r=["esink"], w=[("zf", 0)])
    dve(lambda e: e.tensor_copy(out=esr_hi[0:2], in_=esr_f[0:2]), r=[("zf", 0)], w=[("sq", 0)])
    dve(lambda e: e.tensor_tensor(out=esr_d[0:2], in0=esr_f[0:2], in1=esr_hi[0:2], op=ALU.subtract),
        r=[("zf", 0), ("sq", 0)], w=[("zf", 1)])
    dve(lambda e: e.tensor_copy(out=esr_lo[0:2], in_=esr_d[0:2]), r=[("zf", 1)], w=[("sq", 1)])
    dve(lambda e: e.tensor_scalar(out=esr2[0:2], in0=esr_hi[0:2], scalar1=oh[0:2, 0:1], scalar2=None, op0=ALU.mult),
        r=[("sq", 0), "cf"], w=["esr2a"])
    dve(lambda e: e.scalar_tensor_tensor(out=esr2[0:2], in0=esr_lo[0:2], scalar=oh[0:2, 1:2], in1=esr2[0:2],
                                         op0=ALU.mult, op1=ALU.add), r=[("sq", 1), "esr2a", "cf"], w=["esr2"])
    def make_att(g, qnT, qres):
        def att_S(blk, g=g):
            s = blk % 2
            q_rhs = qnT[:, :, blk * 128:(blk + 1) * 128]
            k_own = knT[:, g, blk * 128:(blk + 1) * 128]
            k_prev = knT[:, g, NT:NT + 128] if blk == 0 else knT[:, g, (blk - 1) * 128:blk * 128]
            bo = bank_m(); bp = bank_m()
            mprev = 2 if blk == 0 else 1
            for (bb_, kk_, mi) in ((bo, k_own, 0), (bp, k_prev, mprev)):
                pe(lambda e, bb_=bb_, kk_=kk_, q_rhs=q_rhs: e.matmul(ps[bb_][:, :], lhsT=kk_, rhs=q_rhs, start=True, stop=False),
                   r=qres + kres(g), w=[("ps", bb_)])
                pe(lambda e, bb_=bb_, mi=mi: e.matmul(ps[bb_][:, :], lhsT=ident_b,
                                                     rhs=mask[:, mi:mi + 1, :].broadcast_to([128, 4, 128]),
                                                     start=False, stop=True), r=["cb"], w=[("ps", bb_)])
            act(lambda e, bo=bo, s=s: e.activation(out=PTm[s][:, 0, :], in_=ps[bo][:, :], func=AF.Exp, bias=negc,
                                                   scale=ATTN_SCALE), r=[("ps", bo), "negc"], w=[("PTm", s, 0)])
            act(lambda e, bp=bp, s=s: e.activation(out=PTm[s][:, 1, :], in_=ps[bp][:, :], func=AF.Exp, bias=negc,
                                                   scale=ATTN_SCALE), r=[("ps", bp), "negc"], w=[("PTm", s, 1)])

        def att_PV(blk, g=g):
            s = blk % 2
            bO = bank_m(); bD = bank_m()
            for t, vb in ((0, blk + 1), (1, blk)):
                pe(lambda e, bO=bO, t=t, vb=vb, s=s, g=g: e.matmul(ps[bO][:, :], lhsT=vtokA[:, g, vb, :], rhs=PTm[s][:, t, :],
                                                                  start=(t == 0), stop=(t == 1)),
                   r=[("PTm", s, t), ("vtokA", g)], w=[("ps", bO)])
            for t in range(2):
                pe(lambda e, bD=bD, t=t, s=s: e.matmul(ps[bD][:, :], lhsT=cbv("ones"), rhs=PTm[s][:, t, :],
                                                      start=(t == 0), stop=False),
                   r=[("PTm", s, t), "cb"], w=[("ps", bD)])
            pe(lambda e, bD=bD, g=g: e.matmul(ps[bD][:, :], lhsT=cbv("ones")[0:2, :], rhs=esr2[0:2, 4 * g:4 * g + 4, :],
                                             start=False, stop=True), r=["esr2", "cb"], w=[("ps", bD)])
            if blk % 2 == 0:
                act(lambda e, bD=bD, s=s: e.activation(out=rec[s], in_=ps[bD][:, :], func=AF.Ln),
                    r=[("ps", bD)], w=[("rec", s)])
                act(lambda e, s=s: e.activation(out=rec[s], in_=rec[s], func=AF.Exp, scale=-1.0), r=[("rec", s)],
                    w=[("rec", s)])
            else:
                dve(lambda e, bD=bD, s=s: e.reciprocal(out=rec[s], in_=ps[bD][:, :]), r=[("ps", bD)], w=[("rec", s)])
            dve(lambda e, bO=bO, s=s, g=g, blk=blk: e.tensor_tensor(
                out=mixT[:, 4 * g:4 * g + 4, blk * 128:(blk + 1) * 128],
                in0=ps[bO][:, :].rearrange("p (a c) -> p a c", a=4),
                in1=rec[s][:, :].rearrange("p (a c) -> p a c", a=4), op=ALU.mult),
                r=[("ps", bO), ("rec", s)], w=[("mixT", "a", g, blk)])


        def att_gen():
            att_S(0)
            yield
            for blk in range(8):
                if blk + 1 < 8:
                    att_S(blk + 1)
                    yield
                att_PV(blk)
                yield

        def att_sample():
            for sm in range(NS):
                dma("pool", "kc%d" % sm, kc_b[sm], ck[sm, :, g, :], w=[("kc_b", sm)])
                dma("pool", "vc%d" % sm, vc_b[sm], cv[sm, :, g, :], w=[("vc_b", sm)])
            bt = bank_m()
            for sm in range(NS):
                pe(lambda e, bt=bt, sm=sm: e.transpose(psb(bt)[:, sm * 128:(sm + 1) * 128], kc_b[sm], ident_b),
                   r=[("kc_b", sm), "cb"], w=[("ps", bt)])
            act(lambda e, bt=bt: e.copy(out=kcT, in_=psb(bt)[:, 0:512].rearrange("p (a c) -> p a c", a=4)),
                r=[("ps", bt)], w=["kcT"])
            bS = bank_m()
            for sm in range(NS):
                pe(lambda e, bS=bS, sm=sm: e.matmul(ps[bS][:, sm * 4:(sm + 1) * 4], lhsT=kcT[:, sm, :], rhs=qnT[:, :, NP_ + sm],
                                                   start=True, stop=True, skip_group_check=True),
                   r=["kcT"] + qres, w=[("ps", bS)])
            for sm in range(NS):
                pe(lambda e, bS=bS, sm=sm, g=g: e.matmul(ps[bS][0:NS, 32 + sm * 4:32 + (sm + 1) * 4],
                                                        lhsT=knT[:, g, NP_:NP_ + NS], rhs=qnT[:, :, NP_ + sm],
                                                        start=True, stop=True, skip_group_check=True),
                   r=kres(g) + qres, w=[("ps", bS)])
            act(lambda e, bS=bS: e.activation(out=PTc, in_=ps[bS][:, 0:16], func=AF.Exp, bias=negc, scale=ATTN_SCALE),
                r=[("ps", bS), "negc"], w=["PTc"])
            act(lambda e, bS=bS: e.activation(out=Pn[0:NS, :], in_=ps[bS][0:NS, 32:48], func=AF.Exp,
                                              bias=negc[0:NS, :], scale=ATTN_SCALE), r=[("ps", bS), "negc"], w=["Pn"])
            dve(lambda e: e.tensor_tensor(out=Pn[0:NS, :].rearrange("p (s h) -> p s h", s=4),
                                          in0=Pn[0:NS, :].rearrange("p (s h) -> p s h", s=4),
                                          in1=oh[0:NS, 0:4].unsqueeze(2).broadcast_to([NS, 4, 4]), op=ALU.mult),
                r=["Pn", "cf"], w=["Pn"])
            bO = bank_m(); bD = bank_m()
            for sm in range(NS):
                pe(lambda e, bO=bO, sm=sm: e.matmul(ps[bO][:, sm * 4:(sm + 1) * 4], lhsT=vc_b[sm], rhs=PTc[:, sm * 4:(sm + 1) * 4],
                                                   start=(sm == 0), stop=False, skip_group_check=True),
                   r=[("vc_b", sm), "PTc"], w=[("ps", bO)])
                pe(lambda e, bO=bO, sm=sm, g=g: e.matmul(ps[bO][:, sm * 4:(sm + 1) * 4], lhsT=vstok[0:NS, g, :],
                                                        rhs=Pn[0:NS, sm * 4:(sm + 1) * 4], start=False, stop=True,
                                                        skip_group_check=True),
                   r=[("vstok", g), "Pn"], w=[("ps", bO)])
            for sm in range(NS):
                pe(lambda e, bD=bD, sm=sm: e.matmul(ps[bD][:, sm * 4:(sm + 1) * 4], lhsT=cbv("ones"), rhs=PTc[:, sm * 4:(sm + 1) * 4],
                                                   start=(sm == 0), stop=False, skip_group_check=True),
                   r=["PTc", "cb"], w=[("ps", bD)])
                pe(lambda e, bD=bD, sm=sm: e.matmul(ps[bD][:, sm * 4:(sm + 1) * 4], lhsT=cbv("ones")[0:NS, :],
                                                   rhs=Pn[0:NS, sm * 4:(sm + 1) * 4], start=False, stop=False,
                                                   skip_group_check=True), r=["Pn", "cb"], w=[("ps", bD)])
            pe(lambda e, bD=bD, g=g: e.matmul(ps[bD][:, 0:16], lhsT=cbv("ones")[0:2, :],
                                             rhs=esr2[0:2, 4 * g:4 * g + 4, 0].unsqueeze(1).broadcast_to([2, 4, 4]),
                                             start=False, stop=True, skip_group_check=True),
               r=["esr2", "cb"], w=[("ps", bD)])
            act(lambda e, bD=bD: e.activation(out=recs, in_=ps[bD][:, 0:16], func=AF.Ln), r=[("ps", bD)], w=["recs"])
            act(lambda e: e.activation(out=recs, in_=recs, func=AF.Exp, scale=-1.0), r=["recs"], w=["recs"])
            dve(lambda e, bO=bO, g=g: e.tensor_tensor(
                out=mixT[:, 4 * g:4 * g + 4, NP_:NP_ + NS], in0=ps[bO][:, 0:16].rearrange("p (s h) -> p h s", s=4),
                in1=recs[:, :].rearrange("p (s h) -> p h s", s=4), op=ALU.mult),
                r=[("ps", bO), "recs"], w=[("mixT", "as", g)])

        return att_gen, att_sample

    qbufs = [qnT, qnT2]
    qresf = lambda gi: [("qnT", gi, hh, c0) for hh in range(4) for (c0, n) in TT]

    def qproj_gen(gi):
        qb = qbufs[gi]
        for jj in range(2):
            wv, wres = w_next()
            for ml in range(2):
                hh = jj * 2 + ml
                for ti, (c0, n) in enumerate(TT):
                    b = bank_d()
                    for k in range(16):
                        pe(lambda e, b=b, k=k, ml=ml, c0=c0, n=n, wv=wv: e.matmul(
                            ps[b][:, 0:n], lhsT=wv[:, k, ml * 128:(ml + 1) * 128], rhs=aT[:, k, c0:c0 + n],
                            start=(k == 0), stop=(k == 15)), r=[wres, "aT"], w=[("ps", b)])
                    qknorm(b, c0, n, "qg", qb[:, hh, c0:c0 + n], ("qnT", gi, hh, c0))
                    yield
            w_done()
        qk_flush()

    def drive_mix(ga, gb, pat):
        a_alive, b_alive = True, gb is not None
        while a_alive or b_alive:
            if a_alive:
                try:
                    next(ga)
                except StopIteration:
                    a_alive = False
            for _ in range(pat if a_alive else 99):
                if not b_alive:
                    break
                try:
                    next(gb)
                except StopIteration:
                    b_alive = False

    for _ in qproj_gen(0):
        pass
    attg0, atts0 = make_att(0, qbufs[0], qresf(0))
    attg1, atts1 = make_att(1, qbufs[1], qresf(1))
    drive_mix(attg0(), qproj_gen(1), 1)
    atts0()
    for _ in attg1():
        pass
    atts1()
    att_dense3[0] = False
    P.fence()
    dump("mixA", mixT[:, 0:8, :], ["mixT"])
    chk(4)

    Z = Alloc(HB, total)
    agl = [Z([8, 128], F32) for _ in range(2)]
    coef = cfv("coef")
    ago = ag_out.ap()
    for r_ in range(4):
        s = r_ % 2
        dma("sp", "agl%d" % s, agl[s], ago[r_ * 1024:(r_ + 1) * 1024, :].rearrange("(h p) n -> p h n", p=128),
            r=["ag_out"], w=[("agl", s)])
        for h in range(8):
            if r_ == 0:
                dve(lambda e, s=s, h=h, r_=r_: e.tensor_scalar(out=S0[:, h, :], in0=agl[s][:, h, :],
                                                              scalar1=coef[:, r_ * 8 + h:r_ * 8 + h + 1], scalar2=None,
                                                              op0=ALU.mult), r=[("agl", s), "cf"], w=[("S0", h)])
            else:
                dve(lambda e, s=s, h=h, r_=r_: e.scalar_tensor_tensor(
                    out=S0[:, h, :], in0=agl[s][:, h, :], scalar=coef[:, r_ * 8 + h:r_ * 8 + h + 1], in1=S0[:, h, :],
                    op0=ALU.mult, op1=ALU.add), r=[("agl", s), "cf", ("S0", h)], w=[("S0", h)])

    dump("S0", S0, [("S0", h) for h in range(8)])
    chk(5)
    G1 = [dict(qr=Z([NT], BF16), kr=Z([NT], BF16), vT=Z([NT], BF16), sg=Z([NT], F32)) for _ in range(2)]
    RT = dict(xb=[Z([344], BF16) for _ in range(3)], t1=[Z([344], F32) for _ in range(3)],
              t2=[Z([344], F32) for _ in range(3)])
    R = dict(qd=Z([8, 128], BF16), kd=Z([8, 128], BF16), vtok=Z([8, 128], BF16), scm=Z([8, 128], BF16),
             Sb=Z([8, 128], BF16), Srun=Z([2, 128], F32), o_sb=Z([NT], F32), sq=Z([NT], BF16), rstd=Z([NT], F32),
             tt=Z([NT], F32), ktok_s=Z([128], BF16), vtok_s=Z([128], BF16))
    vm = [Z([128], BF16) for _ in range(NS)]
    Sst = [Z([128], F32) for _ in range(NS)]
    Snew = [Z([128], F32) for _ in range(NS)]
    Sbs = [Z([128], BF16) for _ in range(NS)]
    intraT = cfv("intraT").rearrange("p (h c) -> p h c", h=8)
    qdecv = cfv("qdec").rearrange("p (h c) -> p h c", h=8)
    kdecv = cfv("kdec")
    rgv = cfv("rg")
    tg = "p2"
    p2c = {"d": 0, "r": 0, "t": 0}

    def bank_d3():
        b = p2c["d"] % 3
        p2c["d"] += 1
        return b

    def bank_r():
        b = 3 + p2c["r"] % 2
        p2c["r"] += 1
        return b
    M0, M1, M2 = 5, 6, 7
    p2blocks = {}

    def rot_a(b, c0, n):
        sl = p2c["t"] % 3
        p2c["t"] += 1
        xb, t1 = RT["xb"][sl], RT["t1"][sl]
        act(lambda e: e.copy(out=xb[:, 0:n], in_=ps[b][:, 0:n]), r=[("ps", b)], w=[("rt_xb", sl)])
        dve(lambda e: e.tensor_tensor(out=t1[:, 0:n], in0=ps[b][:, 0:n], in1=cosv[:, c0:c0 + n], op=ALU.mult),
            r=[("ps", b), "cf"], w=[("rt_t1", sl)])
        return sl

    def rot_b(sl, c0, n, outb, outres):
        xb, t1, t2 = RT["xb"][sl], RT["t1"][sl], RT["t2"][sl]
        b2 = bank_r()
        pe(lambda e: e.matmul(ps[b2][:, 0:n], lhsT=cbv("rrot"), rhs=xb[:, 0:n], start=True, stop=True),
           r=[("rt_xb", sl), "cb"], w=[("ps", b2)])
        dve(lambda e: e.tensor_tensor(out=t2[:, 0:n], in0=ps[b2][:, 0:n], in1=sinv[:, c0:c0 + n], op=ALU.mult),
            r=[("ps", b2), "cf"], w=[("rt_t2", sl)])
        dve(lambda e: e.tensor_tensor(out=outb[:, c0:c0 + n], in0=t1[:, 0:n], in1=t2[:, 0:n], op=ALU.add),
            r=[("rt_t1", sl), ("rt_t2", sl)], w=[outres])

    def stageA(h):
        j, ml = h // 2, h % 2
        if ml == 0:
            p2blocks[j] = [w_next() for _ in range(2)]
        blocks = p2blocks[j]
        g1 = G1[h % 2]
        gp = h % 2
        dma("sp", "ldk%d" % gp, g1["kr"], krs.ap()[h], r=[("krs", h)], w=[(tg, "kr", gp, c0) for (c0, n) in TT])
        dma("sp", "ldv%d" % gp, g1["vT"], vts.ap()[h], r=[("vts", h)], w=[(tg, "vT", gp, c0) for (c0, n) in TT])
        for bi in range(2):
            wv, wres = blocks[bi]
            pending = None
            for ti, (c0, n) in enumerate(TT):
                b = bank_d3()
                for k in range(16):
                    pe(lambda e, b=b, k=k, c0=c0, n=n, wv=wv, ml=ml: e.matmul(
                        ps[b][:, 0:n], lhsT=wv[:, k, ml * 128:(ml + 1) * 128], rhs=aT[:, k, c0:c0 + n],
                        start=(k == 0), stop=(k == 15)), r=[wres, "aT"], w=[("ps", b)])
                if bi == 0:
                    sl = rot_a(b, c0, n)
                    if pending is not None:
                        rot_b(*pending)
                    pending = (sl, c0, n, g1["qr"], (tg, "qr", gp, c0))
                else:
                    act(lambda e, b=b, c0=c0, n=n, g1=g1: e.activation(out=g1["sg"][:, c0:c0 + n], in_=ps[b][:, 0:n],
                                                                       func=AF.Silu), r=[("ps", b)], w=[(tg, "sg", gp, c0)])
                if ti == 2:
                    if pending is not None:
                        rot_b(*pending)
                    if ml == 1:
                        w_done()
                yield

    def stageB(h):
        g1 = G1[h % 2]
        gp = h % 2
        q_res = [(tg, "qr", gp, c0) for (c0, n) in TT]
        k_res = [(tg, "kr", gp, c0) for (c0, n) in TT]
        v_res = [(tg, "vT", gp, c0) for (c0, n) in TT]
        g_res = [(tg, "sg", gp, c0) for (c0, n) in TT]
        dve(lambda e: e.tensor_tensor(out=R["qd"], in0=g1["qr"][:, 0:NP_].rearrange("p (a c) -> p a c", a=8),
                                      in1=qdecv[:, h:h + 1, :].broadcast_to([128, 8, 128]), op=ALU.mult),
            r=q_res + ["cf"], w=[(tg, "qd")])
        for half in range(2):
            for cc in range(4):
                c = half * 4 + cc
                pe(lambda e, cc=cc, c=c: e.transpose(psb(M0)[:, cc * 128:(cc + 1) * 128],
                                                    g1["kr"][:, c * 128:(c + 1) * 128], ident_b),
                   r=k_res + ["cb"], w=[("ps", M0)])
            dve(lambda e, half=half: e.tensor_scalar(
                out=R["kd"][:, half * 4:(half + 1) * 4, :], in0=psb(M0)[:, 0:512].rearrange("p (a c) -> p a c", a=4),
                scalar1=kdecv[:, h:h + 1], scalar2=None, op0=ALU.mult), r=[("ps", M0), "cf"], w=[(tg, "kd", half)])
            for cc in range(4):
                c = half * 4 + cc
                pe(lambda e, cc=cc, c=c: e.transpose(psb(M1)[:, cc * 128:(cc + 1) * 128],
                                                    g1["vT"][:, c * 128:(c + 1) * 128], ident_b),
                   r=v_res + ["cb"], w=[("ps", M1)])
            act(lambda e, half=half: e.copy(out=R["vtok"][:, half * 4:(half + 1) * 4, :],
                                            in_=psb(M1)[:, 0:512].rearrange("p (a c) -> p a c", a=4)),
                r=[("ps", M1)], w=[(tg, "vtok", half)])
        act(lambda e: e.copy(out=R["Sb"][:, 0, :], in_=S0[:, h, :]), r=[("S0", h)], w=[(tg, "Sb", 0)])
        yield
        g128 = float(GAMMA[h] ** 128)
        for half in range(2):
            for cc in range(4):
                c = half * 4 + cc
                pe(lambda e, cc=cc, c=c: e.matmul(ps[M2][:, cc * 128:(cc + 1) * 128], lhsT=R["kd"][:, c, :],
                                                 rhs=R["vtok"][:, c, :], start=True, stop=True),
                   r=[(tg, "kd", half), (tg, "vtok", half)], w=[("ps", M2)])
            for cc in range(4):
                c = half * 4 + cc
                prev = S0[:, h, :] if c == 0 else R["Srun"][:, (c - 1) % 2, :]
                prev_res = ("S0", h) if c == 0 else (tg, "Srun", (c - 1) % 2)
                dve(lambda e, cc=cc, c=c, prev=prev: e.scalar_tensor_tensor(
                    out=R["Srun"][:, c % 2, :], in0=prev, scalar=g128, in1=ps[M2][:, cc * 128:(cc + 1) * 128],
                    op0=ALU.mult, op1=ALU.add), r=[prev_res, ("ps", M2)], w=[(tg, "Srun", c % 2)])
                if c < 7:
                    act(lambda e, c=c: e.copy(out=R["Sb"][:, c + 1, :], in_=R["Srun"][:, c % 2, :]),
                        r=[(tg, "Srun", c % 2)], w=[(tg, "Sb", c + 1)])
                else:
                    dma("sp", "o_rs", rstate[h], R["Srun"][:, c % 2, :], r=[(tg, "Srun", c % 2)], w=[("rstate", h)])
        yield
        for half, bsc in ((0, M0), (1, M1)):
            for cc in range(4):
                c = half * 4 + cc
                pe(lambda e, bsc=bsc, cc=cc, c=c: e.matmul(ps[bsc][:, cc * 128:(cc + 1) * 128],
                                                          lhsT=g1["kr"][:, c * 128:(c + 1) * 128],
                                                          rhs=g1["qr"][:, c * 128:(c + 1) * 128], start=True, stop=True),
                   r=k_res + q_res, w=[("ps", bsc)])
            dve(lambda e, bsc=bsc, half=half: e.tensor_tensor(
                out=R["scm"][:, half * 4:(half + 1) * 4, :], in0=ps[bsc][:, :].rearrange("p (a c) -> p a c", a=4),
                in1=intraT[:, h:h + 1, :].broadcast_to([128, 4, 128]), op=ALU.mult),
                r=[("ps", bsc), "cf"], w=[(tg, "scm", half)])
        yield
        obanks = [M0, M1]
        for half in range(2):
            bo = obanks[half]
            for cc in range(4):
                c = half * 4 + cc
                pe(lambda e, bo=bo, cc=cc, c=c: e.matmul(ps[bo][:, cc * 128:(cc + 1) * 128], lhsT=R["vtok"][:, c, :],
                                                        rhs=R["scm"][:, c, :], start=(cc == 0), stop=False,
                                                        skip_group_check=True),
                   r=[(tg, "vtok", half), (tg, "scm", half)], w=[("ps", bo)])
        pe(lambda e: e.transpose(psb(M2)[0:NS, 0:128], g1["kr"][:, NP_:NT], ident_b), r=k_res + ["cb"], w=[("ps", M2)])
        pe(lambda e: e.transpose(psb(M2)[0:NS, 128:256], g1["vT"][:, NP_:NT], ident_b), r=v_res + ["cb"], w=[("ps", M2)])
        act(lambda e: e.mul(out=R["ktok_s"][0:NS, :], in_=psb(M2)[0:NS, 0:128], mul=RET_K_SCALE),
            r=[("ps", M2)], w=[(tg, "ktok_s")])
        act(lambda e: e.copy(out=R["vtok_s"][0:NS, :], in_=psb(M2)[0:NS, 128:256]), r=[("ps", M2)], w=[(tg, "vtok_s")])
        for sm in range(NS):
            dma("sp", "st%d" % sm, Sst[sm], st[sm, h], w=[("Sst", sm)])
            dve(lambda e, sm=sm: e.tensor_scalar(out=vm[sm][0:NS, :], in0=R["vtok_s"][0:NS, :],
                                                 scalar1=cfv("onehot")[0:NS, sm:sm + 1], scalar2=None, op0=ALU.mult),
                r=[(tg, "vtok_s"), "cf"], w=[("vm", sm)])
        yield
        for half in range(2):
            bo = obanks[half]
            for cc in range(4):
                c = half * 4 + cc
                pe(lambda e, bo=bo, cc=cc, c=c: e.matmul(ps[bo][:, cc * 128:(cc + 1) * 128], lhsT=R["Sb"][:, c, :],
                                                        rhs=R["qd"][:, c, :], start=False, stop=True,
                                                        skip_group_check=True),
                   r=[(tg, "Sb", c), (tg, "qd")], w=[("ps", bo)])

        def o_read(bo, c0, n):
            act(lambda e: e.activation(out=R["sq"][:, c0:c0 + n], in_=ps[bo][:, 0:n], func=AF.Square),
                r=[("ps", bo)], w=[(tg, "sq", c0)])
            dve(lambda e: e.tensor_copy(out=R["o_sb"][:, c0:c0 + n], in_=ps[bo][:, 0:n]),
                r=[("ps", bo)], w=[(tg, "o_sb", c0)])
        o_read(M0, 0, 512)
        o_read(M1, 512, 512)
        yield
        for sm in range(NS):
            bu = M0 if sm % 2 == 0 else M1
            pe(lambda e, bu=bu, sm=sm: e.matmul(ps[bu][:, 0:128], lhsT=R["ktok_s"][0:NS, :], rhs=vm[sm][0:NS, :],
                                               start=True, stop=True), r=[(tg, "ktok_s"), ("vm", sm)], w=[("ps", bu)])
            dve(lambda e, bu=bu, sm=sm: e.scalar_tensor_tensor(out=Snew[sm], in0=Sst[sm], scalar=float(GAMMA[h]),
                                                               in1=ps[bu][:, 0:128], op0=ALU.mult, op1=ALU.add),
                r=[("Sst", sm), ("ps", bu)], w=[("Snew", sm)])
            dma("sp", "o_ss%d" % sm, ss_out[sm, h], Snew[sm], r=[("Snew", sm)], w=[("ss_out", sm, h)])
            act(lambda e, sm=sm: e.copy(out=Sbs[sm], in_=Snew[sm]), r=[("Snew", sm)], w=[("Sbs", sm)])
        yield
        for sm in range(NS):
            pe(lambda e, sm=sm: e.matmul(ps[M2][:, sm:sm + 1], lhsT=Sbs[sm], rhs=g1["qr"][:, NP_ + sm:NP_ + sm + 1],
                                        start=True, stop=True), r=[("Sbs", sm)] + q_res, w=[("ps", M2)])
        o_read(M2, NP_, NS)
        yield
        for pi, (c0, n) in enumerate(((0, 512), (512, 512), (NP_, NS))):
            bm_ = M0 if pi % 2 == 0 else M1
            pe(lambda e, bm_=bm_, c0=c0, n=n: e.matmul(ps[bm_][:, 0:n], lhsT=cbv("ones128"), rhs=R["sq"][:, c0:c0 + n],
                                                      start=True, stop=True), r=[(tg, "sq", c0), "cb"], w=[("ps", bm_)])
            act(lambda e, bm_=bm_, c0=c0, n=n: e.activation(out=R["rstd"][:, c0:c0 + n], in_=ps[bm_][:, 0:n], func=AF.Ln,
                                                            bias=epsv, scale=1.0), r=[("ps", bm_)], w=[(tg, "rstd", c0)])
            act(lambda e, c0=c0, n=n: e.activation(out=R["rstd"][:, c0:c0 + n], in_=R["rstd"][:, c0:c0 + n], func=AF.Exp,
                                                   scale=-0.5), r=[(tg, "rstd", c0)], w=[(tg, "rstd", c0)])
            dve(lambda e, c0=c0, n=n: e.scalar_tensor_tensor(
                out=R["tt"][:, c0:c0 + n], in0=R["o_sb"][:, c0:c0 + n], scalar=rgv[:, h:h + 1],
                in1=R["rstd"][:, c0:c0 + n], op0=ALU.mult, op1=ALU.mult),
                r=[(tg, "o_sb", c0), (tg, "rstd", c0), "cf"], w=[(tg, "tt", c0)])
            pool(lambda e, c0=c0, n=n: e.tensor_tensor(out=mixT[:, 8 + h, c0:c0 + n], in0=R["tt"][:, c0:c0 + n],
                                                       in1=g1["sg"][:, c0:c0 + n], op=ALU.mult),
                 r=[(tg, "tt", c0)] + g_res, w=[("mixT", "r", h, c0)])
        yield

    def drive(gens):
        gens = [g for g in gens if g is not None]
        while gens:
            for g in list(gens):
                try:
                    next(g)
                except StopIteration:
                    gens.remove(g)

    def drive2(gb, ga):
        pat = [1, 1, 1, 0, 1, 0, 1, 1, 9, 9]
        bi = 0
        b_alive, a_alive = True, ga is not None
        while b_alive or a_alive:
            if b_alive:
                try:
                    next(gb)
                except StopIteration:
                    b_alive = False
            na = pat[min(bi, len(pat) - 1)] if b_alive else 99
            bi += 1
            for _ in range(na):
                if not a_alive:
                    break
                try:
                    next(ga)
                except StopIteration:
                    a_alive = False

    drive([stageA(0)])
    for h in range(8):
        drive2(stageB(h), stageA(h + 1) if h + 1 < 8 else None)
    P.fence()
    dump("mixT", mixT, ["mixT"])
    chk(6)

    Zt = Alloc(T2base, T2base + 12 * 1024)
    Zx = Alloc(0, 0)
    xs_off = None
    aT_f32 = aT.rearrange("p a c -> p (a c)").bitcast(F32)
    xstage = [aT_f32[:, j * D:(j + 1) * D] for j in range(4)]
    tiles_x = [(x_main[i * 128:(i + 1) * 128, :], 128, i * 128) for i in range(8)] + [(x_smp, NS, NP_)]
    for i in range(3):
        dma("sp", "x%d" % (i % 4), xstage[i % 4][0:tiles_x[i][1], :], tiles_x[i][0], w=[("xstage", i % 4)])
    for i, (src, rows, c0) in enumerate(tiles_x):
        s = i % 4
        if i + 3 < len(tiles_x):
            j3 = i + 3
            dma("sp", "x%d" % (j3 % 4), xstage[j3 % 4][0:tiles_x[j3][1], :], tiles_x[j3][0], w=[("xstage", j3 % 4)])
        for q4 in range(4):
            b = bank_m()
            for kk in range(4):
                k = q4 * 4 + kk
                pe(lambda e, b=b, kk=kk, k=k, s=s, rows=rows: e.transpose(
                    ps[b][:, kk * 128:kk * 128 + rows], xstage[s][0:rows, k * 128:(k + 1) * 128],
                    ident_f[0:rows, 0:rows]), r=[("xstage", s), "cf"], w=[("ps", b)])
            src_ps = lambda b=b, rows=rows: ps[b][:, :].rearrange("p (a c) -> p a c", a=4)[:, :, 0:rows]
            dst = hT[:, q4 * 4:(q4 + 1) * 4, c0:c0 + rows]
            if q4 % 2 == 0:
                act(lambda e, dst=dst, src_ps=src_ps: e.copy(out=dst, in_=src_ps()), r=[("ps", b)], w=[("hT", q4, i)])
            else:
                dve(lambda e, dst=dst, src_ps=src_ps: e.tensor_copy(out=dst, in_=src_ps()), r=[("ps", b)], w=[("hT", q4, i)])
    P.fence()

    sqs = [Zt([344], BF16) for _ in range(4)]
    rstdn = Zt([NT], F32)
    fT = aT[:, :, 0:NT]

    class NormState:
        pass

    def norm_begin(tag):
        st_ = NormState()
        st_.tag = tag
        st_.banks = [bank_m() for _ in TT]
        st_.pending = None
        st_.cnt = 0
        return st_

    def norm_flush(st_):
        if st_.pending is not None:
            slot, m, ti, n = st_.pending
            pe(lambda e: e.matmul(ps[st_.banks[ti]][:, 0:n], lhsT=cbv("onesD"), rhs=sqs[slot][:, 0:n],
                                  start=(m == 0), stop=(m == 15)),
               r=[("sqs", slot), "cb"], w=[("ps", st_.banks[ti])])
            st_.pending = None

    def norm_tile(st_, m, ti, c0, n):
        norm_flush(st_)
        slot = st_.cnt % 4
        st_.cnt += 1
        act(lambda e: e.activation(out=sqs[slot][:, 0:n], in_=hT[:, m, c0:c0 + n], func=AF.Square),
            r=[("hTm", m, c0)], w=[("sqs", slot)])
        st_.pending = (slot, m, ti, n)

    def norm_finish(st_, gname):
        tag = st_.tag
        norm_flush(st_)
        for ti, (c0, n) in enumerate(TT):
            act(lambda e, ti=ti, c0=c0, n=n: e.activation(out=rstdn[:, c0:c0 + n], in_=ps[st_.banks[ti]][:, 0:n], func=AF.Ln,
                                                          bias=epsv, scale=1.0),
                r=[("ps", st_.banks[ti])], w=[(tag, "rstdn0", ti)])
            act(lambda e, ti=ti, c0=c0, n=n: e.activation(out=rstdn[:, c0:c0 + n], in_=rstdn[:, c0:c0 + n], func=AF.Exp,
                                                          scale=-0.5),
                r=[(tag, "rstdn0", ti)], w=[(tag, "rstdn")])
        gv = cfv(gname)
        for k in range(16):
            dve(lambda e, k=k: e.scalar_tensor_tensor(out=fT[:, k, :], in0=hT[:, k, :], scalar=gv[:, k:k + 1], in1=rstdn,
                                                      op0=ALU.mult, op1=ALU.mult),
                r=[(tag, "rstdn"), "cf"] + [("hTm", k, c0) for (c0, n) in TT], w=[("fT", k)])

    mix_rhs = lambda k, c0, n: mixT[:, k, c0:c0 + n]
    n2 = norm_begin("n2")
    for j in range(8):
        def evac_out(ml, ti, c0, n, b, j=j):
            m = 2 * j + ml
            dve(lambda e: e.tensor_tensor(out=hT[:, m, c0:c0 + n], in0=ps[b][:, 0:n], in1=hT[:, m, c0:c0 + n], op=ALU.add),
                r=[("ps", b), ("hTm", m, c0)], w=[("hTm", m, c0)])
            norm_tile(n2, m, ti, c0, n)
        dense_block(16, mix_rhs, ["mixT"], evac_out)
    P.fence()
    dump("h1", hT, [])
    chk(7)

    norm_finish(n2, "fn_g")
    P.fence()
    dump("fT", fT, [])
    chk(8)

    uT = mixT
    sgt = [Zt([344], F32) for _ in range(2)]
    f_rhs = lambda k, c0, n: fT[:, k, c0:c0 + n]
    cnt_s = {"i": 0}
    for qi, (b0, nb) in enumerate(QUART):
        for bb in range(nb):
            wg, wgres = w_next()
            wu, wures = w_next()
            for ml in range(2):
                cl = 2 * bb + ml
                for ti, (c0, n) in enumerate(TT):
                    bg = bank_d(); bu = bank_d()
                    for (bk_, wv, wres) in ((bg, wg, wgres), (bu, wu, wures)):
                        for k in range(16):
                            pe(lambda e, bk_=bk_, wv=wv, k=k, ml=ml, c0=c0, n=n: e.matmul(
                                ps[bk_][:, 0:n], lhsT=wv[:, k, ml * 128:(ml + 1) * 128], rhs=fT[:, k, c0:c0 + n],
                                start=(k == 0), stop=(k == 15)), r=[wres], w=[("ps", bk_)])
                    s = cnt_s["i"] % 2
                    cnt_s["i"] += 1
                    act(lambda e, bg=bg, s=s, n=n: e.activation(out=sgt[s][:, 0:n], in_=ps[bg][:, 0:n], func=AF.Silu),
                        r=[("ps", bg)], w=[("sgt", s)])
                    dve(lambda e, bu=bu, s=s, n=n, cl=cl, c0=c0: e.tensor_tensor(out=uT[:, cl, c0:c0 + n], in0=ps[bu][:, 0:n],
                                                                               in1=sgt[s][:, 0:n], op=ALU.mult),
                        r=[("ps", bu), ("sgt", s)], w=[("uT", qi)])
            w_done(2)
        kq = 2 * nb
        u_rhs = lambda k, c0, n: uT[:, k, c0:c0 + n]
        if qi == 3:
            n3 = norm_begin("n3")
        for j in range(8):
            def evac_dn(ml, ti, c0, n, b, j=j):
                m = 2 * j + ml
                dve(lambda e: e.tensor_tensor(out=hT[:, m, c0:c0 + n], in0=ps[b][:, 0:n], in1=hT[:, m, c0:c0 + n],
                                              op=ALU.add), r=[("ps", b), ("hTm", m, c0)], w=[("hTm", m, c0)])
                if qi == 3:
                    norm_tile(n3, m, ti, c0, n)
            dense_block(kq, u_rhs, [("uT", qi)], evac_dn)
    P.fence()
    dump("h2", hT, [])
    chk(9)

    norm_finish(n3, "pn_g")
    pst = [uT[:, 0:1, :].rearrange("p a c -> p (a c)").bitcast(F32)[:, 0:256],
           uT[:, 1:2, :].rearrange("p a c -> p (a c)").bitcast(F32)[:, 0:256]]
    peT = uT[:, 4:6, :]
    tiles_p = [(p_main[i * 128:(i + 1) * 128, :], 128, i * 128) for i in range(8)] + [(p_smp, NS, NP_)]
    for i, (src, rows, c0) in enumerate(tiles_p):
        s = i % 2
        dma("sp", "x%d" % s, pst[s][0:rows, :], src, w=[("pst", s)])
        b = bank_m()
        for kk in range(2):
            pe(lambda e, b=b, kk=kk, s=s, rows=rows: e.transpose(ps[b][:, kk * 128:kk * 128 + rows],
                                                                pst[s][0:rows, kk * 128:(kk + 1) * 128],
                                                                ident_f[0:rows, 0:rows]), r=[("pst", s), "cf"], w=[("ps", b)])
        act(lambda e, b=b, rows=rows, c0=c0: e.copy(out=peT[:, :, c0:c0 + rows],
                                                   in_=ps[b][:, 0:256].rearrange("p (a c) -> p a c", a=2)[:, :, 0:rows]),
            r=[("ps", b)], w=["peT"])
    P.fence()
    wple = uT[:, 8:12, :].rearrange("p a c -> p (a c)")[:, 0:4096].rearrange("p (k n) -> p k n", k=2)
    wpleres = "wple"
    dma("pool", "wple", wple, w_ple.rearrange("(k p) n -> p k n", p=128), w=["wple"])
    sB = [uT[:, 12 + i, :].bitcast(F32)[:, 0:344] for i in range(2)]
    tA = [uT[:, 14 + i, :].bitcast(F32)[:, 0:344] for i in range(2)]
    for j in range(8):
        wv, wres = w_next()
        for ml in range(2):
            m = 2 * j + ml
            for ti, (c0, n) in enumerate(TT):
                bA = 2 * (cnt_s["i"] % 4); bB = bA + 1
                for k in range(2):
                    pe(lambda e, bA=bA, k=k, m=m, c0=c0, n=n: e.matmul(ps[bA][:, 0:n], lhsT=wple[:, k, m * 128:(m + 1) * 128],
                                                                      rhs=peT[:, k, c0:c0 + n], start=(k == 0), stop=(k == 1)),
                       r=[wpleres, "peT"], w=[("ps", bA)])
                for k in range(16):
                    pe(lambda e, bB=bB, k=k, ml=ml, c0=c0, n=n, wv=wv: e.matmul(
                        ps[bB][:, 0:n], lhsT=wv[:, k, ml * 128:(ml + 1) * 128], rhs=fT[:, k, c0:c0 + n],
                        start=(k == 0), stop=(k == 15)), r=[wres], w=[("ps", bB)])
                s = cnt_s["i"] % 2
                cnt_s["i"] += 1
                act(lambda e, bB=bB, s=s, n=n: e.activation(out=sB[s][:, 0:n], in_=ps[bB][:, 0:n], func=AF.Sigmoid),
                    r=[("ps", bB)], w=[("sB", s)])
                dve(lambda e, bA=bA, s=s, n=n: e.tensor_tensor(out=tA[s][:, 0:n], in0=ps[bA][:, 0:n], in1=sB[s][:, 0:n],
                                                              op=ALU.mult), r=[("ps", bA), ("sB", s)], w=[("tA", s)])
                dve(lambda e, s=s, n=n, m=m, c0=c0: e.tensor_tensor(out=hT[:, m, c0:c0 + n], in0=tA[s][:, 0:n],
                                                                   in1=hT[:, m, c0:c0 + n], op=ALU.add),
                    r=[("tA", s), ("hTm", m, c0)], w=[("hTm", m, c0)])
        w_done()
    P.fence()
    dump("h3", hT, [])
    chk(10)

    ystage = [aT_f32[:, j * D:(j + 1) * D] for j in range(4)]
    tiles_y = [(y_main[i * 128:(i + 1) * 128, :], 128, i * 128) for i in range(8)] + [(y_smp, NS, NP_)]
    for i, (dst, rows, c0) in enumerate(tiles_y):
        s = i % 4
        for q4 in range(4):
            b = bank_m()
            for kk in range(4):
                k = q4 * 4 + kk
                pe(lambda e, b=b, kk=kk, k=k, rows=rows, c0=c0: e.transpose(ps[b][0:rows, kk * 128:(kk + 1) * 128],
                                                                           hT[:, k, c0:c0 + rows], ident_f),
                   r=["cf"], w=[("ps", b)])
            if q4 % 2 == 0:
                act(lambda e, b=b, s=s, rows=rows, q4=q4: e.copy(out=ystage[s][0:rows, q4 * 512:(q4 + 1) * 512],
                                                                in_=ps[b][0:rows, :]), r=[("ps", b)], w=[("ystage", s, q4)])
            else:
                dve(lambda e, b=b, s=s, rows=rows, q4=q4: e.tensor_copy(out=ystage[s][0:rows, q4 * 512:(q4 + 1) * 512],
                                                                       in_=ps[b][0:rows, :]), r=[("ps", b)],
                    w=[("ystage", s, q4)])
        dma("sp", "y%d" % s, dst, ystage[s][0:rows, :], r=[("ystage", s, q4) for q4 in range(4)], w=[("y", i)])

    assert stop != 99 or wstate["used"] == len(wq), (wstate, len(wq))

    sems = {e: es.enter_context(nc.semaphore("s_" + e)) for e in Prog.ENGS}
    dma_sems = {k: es.enter_context(nc.semaphore("d_" + k)) for k in sorted(dma_sem_names)}
    block = es.enter_context(nc.Block())
    finals = sorted(dma_sem_names)
    P.emit(nc, block, sems, dma_sems, finals)
    es.close()
    print("ops", len(P.ops), "counts", P.final_counts[0])
    return nc


_CACHE = {}


def kernel(**inputs):
    inp = {k: np.asarray(v) for k, v in inputs.items()}
    if "nc" not in _CACHE:
        _CACHE["nc"] = build_program()
    nc = _CACHE["nc"]
    xp = inp["x_prompt"]; xs = inp["x_sample"]
    in_maps = []
    zeros_halo = np.zeros((NHALO, D), np.float32)
    shared = dict(
        an_g=np.ascontiguousarray(inp["attn_norm_g"].reshape(1, D)),
        w_in=np.ascontiguousarray(inp["w_in"][0]), w_out=np.ascontiguousarray(inp["w_out"][0]),
        w_gate=np.ascontiguousarray(inp["w_gate"][0]), w_up=np.ascontiguousarray(inp["w_up"][0]),
        w_down=np.ascontiguousarray(inp["w_down"][0]), w_ple=np.ascontiguousarray(inp["w_ple"][0]),
        w_pg=np.ascontiguousarray(inp["w_ple_gate"][0]),
    )
    for c in range(8):
        b, m = c // 4, c % 4
        t0 = m * 1024
        cf, cb = host_consts(c, inp)
        d = dict(shared)
        d.update(
            x_main=np.ascontiguousarray(xp[b, t0:t0 + 1024]),
            x_halo=np.ascontiguousarray(xp[b, t0 - 128:t0]) if m > 0 else zeros_halo,
            x_smp=np.ascontiguousarray(xs[4 * c:4 * c + 4, 0]),
            p_main=np.ascontiguousarray(inp["p_prompt"][0, b, t0:t0 + 1024]),
            p_smp=np.ascontiguousarray(inp["p_sample"][0, 4 * c:4 * c + 4, 0]),
            ck=np.ascontiguousarray(inp["cache_k_win"][0, 4 * c:4 * c + 4]),
            cv=np.ascontiguousarray(inp["cache_v_win"][0, 4 * c:4 * c + 4]),
            st=np.ascontiguousarray(inp["state_ret"][0, 4 * c:4 * c + 4]),
            cf=cf, cb=cb,
        )
        in_maps.append(d)
    res = run_bass_kernel_spmd(nc, in_maps, core_ids=list(range(8)))
    R = res.results
    _CACHE["last"] = R
    y_p = np.stack([np.concatenate([R[4 * b + m]["y_main"] for m in range(4)], 0) for b in range(2)], 0)
    y_s = np.concatenate([R[c]["y_smp"] for c in range(8)], 0)[:, None, :]
    kwp = np.stack([R[4 * b + 3]["kwin"] for b in range(2)], 0)[None]
    vwp = np.stack([R[4 * b + 3]["vwin"] for b in range(2)], 0)[None]
    rsp = np.stack([R[4 * b + 3]["rstate"] for b in range(2)], 0)[None]
    kws = np.concatenate([R[c]["ks_out"] for c in range(8)], 0)[None]
    vws = np.concatenate([R[c]["vs_out"] for c in range(8)], 0)[None]
    rss = np.concatenate([R[c]["ss_out"] for c in range(8)], 0)[None]
    return (y_p.astype(np.float32), y_s.astype(np.float32), kwp.astype(np.float32), vwp.astype(np.float32),
            rsp.astype(np.float32), kws.astype(np.float32), vws.astype(np.float32), rss.astype(np.float32))
```

```python
import math
from contextlib import ExitStack

import numpy as np
import ml_dtypes

import concourse.bass as bass
import concourse.mybir as mybir
from concourse.bass_utils import run_bass_kernel_spmd

F32 = mybir.dt.float32
BF16 = mybir.dt.bfloat16
U8 = mybir.dt.uint8
AF = mybir.ActivationFunctionType
ALU = mybir.AluOpType
AX = mybir.AxisListType

D = 2048
NP_ = 1024
NS = 4
NT = NP_ + NS
NHALO = 128
NA = NT + NHALO
KC = 16
DFF = 5632
EPS = 1e-6
ATTN_SCALE = 128 ** -0.5
RET_K_SCALE = 128 ** -0.5
PAST_LEN = 16384
TT = [(0, 344), (344, 344), (688, 340)]
TT_H = TT + [(NT, NHALO)]
NSLOT = 4
SLOT_BYTES = 8192
GAMMA = [1.0 - 2.0 ** (-5 - h) for h in range(8)]


class Op:
    __slots__ = ("eng", "fn", "dma", "idx", "waits", "sig", "val", "fence")

    def __init__(self, eng, fn, dma, idx):
        self.eng = eng
        self.fn = fn
        self.dma = dma
        self.idx = idx
        self.waits = []
        self.sig = dma is not None
        self.val = None
        self.fence = None


class Prog:
    ENGS = ("sp", "act", "dve", "pool", "pe")

    def __init__(self):
        self.ops = []
        self.last_w = {}
        self.readers = {}
        self.last_on = {}
        self.dma_ops = {}
        self.pending_fence = {}

    def add(self, eng, fn, r=(), w=(), dma=None, nofence=False):
        op = Op(eng, fn, dma, len(self.ops))
        psr = [x for x in r if isinstance(x, tuple) and x[0] == "ps"]
        if psr:
            r = [x for x in r if not (isinstance(x, tuple) and x[0] == "ps")]
            w = list(w) + psr
        deps = {}
        for res in r:
            lw = self.last_w.get(res)
            if lw is not None:
                deps.setdefault(lw, set()).add("raw")
        for res in w:
            lw = self.last_w.get(res)
            if lw is not None:
                deps.setdefault(lw, set()).add("waw")
            for rd in self.readers.get(res, ()):
                deps.setdefault(rd, set()).add("war")
        best = {}
        for d, kinds in deps.items():
            if d is op:
                continue
            if d.dma is not None:
                op.waits.append(d)
                continue
            if op.dma is None and d.eng == eng:
                if eng == "pe":
                    continue
            b = best.get(d.eng)
            if b is None or d.idx > b.idx:
                best[d.eng] = d
        for d in best.values():
            d.sig = True
            op.waits.append(d)
        for res in r:
            self.readers.setdefault(res, []).append(op)
        for res in w:
            self.last_w[res] = op
            self.readers[res] = []
        if eng in self.pending_fence and not nofence:
            op.fence = self.pending_fence.pop(eng)
        self.ops.append(op)
        if dma is None:
            self.last_on[eng] = op
        else:
            self.dma_ops.setdefault(dma, []).append(op)
        return op

    def fence(self):
        st = {"comp": dict(self.last_on), "dma": {k: v[-1] for k, v in self.dma_ops.items()}}
        for o in st["comp"].values():
            o.sig = True
        for e in self.ENGS:
            self.pending_fence[e] = st

    def emit(self, nc, block, sems, dma_sems, final_waits):
        cnt = {e: 0 for e in self.ENGS}
        dcnt = {}
        for op in self.ops:
            if op.dma is not None:
                dcnt[op.dma] = dcnt.get(op.dma, 0) + (1 if op.dma == "cc" else 16)
                op.val = dcnt[op.dma]
            elif op.sig:
                cnt[op.eng] += 1
                op.val = cnt[op.eng]
        self.final_counts = (cnt, dcnt)

        def semof(op):
            return dma_sems[op.dma] if op.dma is not None else sems[op.eng]

        def run(engname):
            def body(eng):
                waited = {}

                def wait(sem_key, sem, val):
                    if waited.get(sem_key, 0) >= val:
                        return
                    waited[sem_key] = val
                    eng.wait_ge(sem, val)

                for op in self.ops:
                    if op.eng != engname:
                        continue
                    if op.fence is not None:
                        for o in op.fence["comp"].values():
                            if o.eng != engname or engname != "pe":
                                wait(("c", o.eng), sems[o.eng], o.val)
                        for k, o in op.fence["dma"].items():
                            wait(("d", k), dma_sems[k], o.val)
                    for d in op.waits:
                        key = ("d", d.dma) if d.dma is not None else ("c", d.eng)
                        wait(key, semof(d), d.val)
                    ins = op.fn(eng)
                    if op.dma is not None:
                        ins.then_inc(dma_sems[op.dma], 1 if op.dma == "cc" else 16)
                    elif op.sig:
                        ins.then_inc(sems[op.eng], 1)
                if engname == "sp":
                    for k in final_waits:
                        if k in dcnt:
                            eng.wait_ge(dma_sems[k], dcnt[k])
            return body

        block.sync(run("sp"))
        block.scalar(run("act"))
        block.vector(run("dve"))
        block.gpsimd(run("pool"))
        block.tensor(run("pe"))


CF = {}
_off = 0
for _n, _w in [("ident", 128), ("ones_row", 128), ("fn_g", 16), ("pn_g", 16), ("rg", 8), ("qg", 1), ("kg", 1),
               ("sinks", 8), ("intraT", 1024), ("qdec", 1024), ("kdec", 8), ("kdlong", 64), ("coef", 32),
               ("onehot", 4), ("cos", NT), ("sin", NT)]:
    CF[_n] = (_off, _w)
    _off += _w
NCF = _off
CB = {}
_off = 0
for _n, _w in [("ident", 128), ("ones", 128), ("onesD", 128), ("ones128", 128), ("rrot", 128), ("mask", 384)]:
    CB[_n] = (_off, _w)
    _off += _w
NCB = _off


def host_consts(core, inp):
    m = core % 4
    cf = np.zeros((128, NCF), np.float32)

    def put(name, arr):
        o, w = CF[name]
        cf[:, o:o + w] = np.asarray(arr, np.float32).reshape(128, w) if np.ndim(arr) == 2 else np.broadcast_to(
            np.asarray(arr, np.float32).reshape(1, w), (128, w))

    put("ident", np.eye(128, dtype=np.float32))
    put("ones_row", np.ones((128, 128), np.float32))
    put("fn_g", inp["ffn_norm_g"][0].reshape(16, 128).T)
    put("pn_g", inp["ple_norm_g"][0].reshape(16, 128).T)
    put("rg", inp["ret_out_g"][0].reshape(8, 128).T)
    put("qg", inp["q_norm_g"][0].reshape(128, 1))
    put("kg", inp["k_norm_g"][0].reshape(128, 1))
    put("sinks", inp["attn_sinks"][0].reshape(8))
    g = np.array(GAMMA, np.float64)
    j = np.arange(128)
    diff = j[None, :] - j[:, None]
    intraT = np.where(diff[:, None, :] >= 0, g[None, :, None] ** np.maximum(diff, 0)[:, None, :], 0.0) * RET_K_SCALE
    put("intraT", intraT.reshape(128, 1024))
    qdec = g[:, None] ** (j[None, :] + 1.0)
    put("qdec", qdec.reshape(1024))
    put("kdec", (g[None, :] ** (127.0 - j[:, None])) * RET_K_SCALE)
    c = np.arange(8)
    kdl = g[None, :, None] ** (1023.0 - (128.0 * c[None, None, :] + j[:, None, None])) * RET_K_SCALE
    put("kdlong", kdl.reshape(128, 64))
    coef = np.zeros((4, 8))
    for r in range(4):
        if r < m:
            coef[r] = g ** (1024.0 * (m - r - 1))
    put("coef", coef.reshape(32))
    oh = np.zeros((128, 4), np.float32)
    oh[:4] = np.eye(4)
    put("onehot", oh)
    pos = np.concatenate([m * 1024 + np.arange(1024), np.full(4, PAST_LEN)]).astype(np.float32)
    inv = (np.float32(10000.0) ** (-np.arange(64, dtype=np.float32) / np.float32(64))).astype(np.float32)
    ang = (pos[None, :] * inv[:, None]).astype(np.float32).astype(np.float64)
    cos = np.cos(ang)
    sin = np.sin(ang)
    put("cos", np.concatenate([cos, cos], 0))
    put("sin", np.concatenate([-sin, sin], 0))

    cb = np.zeros((128, NCB), np.float32)

    def putb(name, arr):
        o, w = CB[name]
        cb[:, o:o + w] = arr

    putb("ident", np.eye(128))
    putb("ones", np.ones((128, 128)))
    putb("onesD", np.full((128, 128), 1.0 / D))
    putb("ones128", np.full((128, 128), 1.0 / 128))
    rr = np.zeros((128, 128))
    for p in range(128):
        rr[(p + 64) % 128, p] = 1.0
    putb("rrot", rr)
    NEG = -30000.0
    own = np.where(j[:, None] <= j[None, :], 0.0, NEG).astype(np.float32)
    prev = np.where(j[:, None] >= j[None, :], 0.0, NEG).astype(np.float32)
    putb("mask", np.concatenate([own, prev, prev if m != 0 else np.full((128, 128), NEG, np.float32)], 1))
    return cf, cb.astype(ml_dtypes.bfloat16)


class _Stop(Exception):
    pass


def build_program(dbg=None, stop=99, groups=None):
    nc = bass.Bass("TRN2", target_bir_lowering=False)
    P = Prog()
    dbg = dbg or {}

    stopped = [False]

    def chk(k):
        if stop == k:
            stopped[0] = True

    def din(name, shape, dt=F32):
        return nc.dram_tensor(name, list(shape), dt, kind="ExternalInput").ap()

    def dout(name, shape, dt=F32):
        return nc.dram_tensor(name, list(shape), dt, kind="ExternalOutput").ap()

    x_main = din("x_main", [NP_, D]); x_halo = din("x_halo", [NHALO, D]); x_smp = din("x_smp", [NS, D])
    p_main = din("p_main", [NP_, 256]); p_smp = din("p_smp", [NS, 256])
    ck = din("ck", [NS, 128, 2, 128]); cv = din("cv", [NS, 128, 2, 128]); st = din("st", [NS, 8, 128, 128])
    an_g = din("an_g", [1, D])
    w_in = din("w_in", [D, DFF]); w_out = din("w_out", [D, D]); w_gate = din("w_gate", [D, DFF])
    w_up = din("w_up", [D, DFF]); w_down = din("w_down", [DFF, D]); w_ple = din("w_ple", [256, D])
    w_pg = din("w_pg", [D, D])
    cf_d = din("cf", [128, NCF]); cb_d = din("cb", [128, NCB], BF16)

    y_main = dout("y_main", [NP_, D]); y_smp = dout("y_smp", [NS, D])
    kwin = dout("kwin", [128, 2, 128]); vwin = dout("vwin", [128, 2, 128]); rstate = dout("rstate", [8, 128, 128])
    ks_out = dout("ks_out", [NS, 128, 2, 128]); vs_out = dout("vs_out", [NS, 128, 2, 128])
    ss_out = dout("ss_out", [NS, 8, 128, 128])
    krs = nc.dram_tensor("krs", [8, 128, NT], BF16)
    vts = nc.dram_tensor("vts", [8, 128, NT], BF16)
    ag_in = nc.dram_tensor("ag_in", [8 * 128, 128], F32)
    ag_out = nc.dram_tensor("ag_out", [4 * 8 * 128, 128], F32)
    dbg_out = {}
    for name, shape in dbg.items():
        dbg_out[name] = dout("dbg_" + name, shape)

    def dump(name, ap, r=()):
        if name in dbg_out:
            P.add("pool", lambda e: e.dma_start(out=dbg_out[name], in_=ap), r, [("dbg", name)], dma="o_dbg")

    es = ExitStack()
    total = (nc.sbuf_bytes_remaining - 64) // 64 * 64
    arena = es.enter_context(nc.sbuf_tensor("arena", [128, total], U8))
    ps = [es.enter_context(nc.psum_tensor("ps%d" % i, [128, 512], F32)) for i in range(8)]

    class Alloc:
        def __init__(self, base, limit):
            self.p = base
            self.limit = limit

        def __call__(self, shape, dt):
            esz = 4 if dt == F32 else 2
            n = int(np.prod(shape)) * esz
            off = (self.p + 31) // 32 * 32
            self.p = off + n
            assert self.p <= self.limit, (self.p, self.limit)
            v = arena[:, off:off + n].bitcast(dt)
            if len(shape) == 2:
                return v.rearrange("p (a b) -> p a b", a=shape[0])
            if len(shape) == 3:
                return v.rearrange("p (a b c) -> p a b c", a=shape[0], b=shape[1])
            return v

    A = Alloc(0, total)
    cf = A([NCF], F32)
    cb = A([NCB], BF16)
    negc = A([1], F32); esink = A([8], F32); S0 = A([8, 128], F32)
    misc = A([64], F32)
    wslots = [A([SLOT_BYTES // 2], BF16) for _ in range(NSLOT)]
    aT = A([KC, NA], BF16)
    mixT = A([KC, NT], BF16)
    T2base = A.p
    T2 = Alloc(T2base, T2base + 12 * 1024)
    A.p = T2base + 12 * 1024
    HB = (A.p + 31) // 32 * 32
    hT = A([KC, NT], F32)
    HEND = A.p
    print("sbuf used", A.p, "of", total)

    def cfv(name, lo=0, hi=None):
        o, w = CF[name]
        return cf[:, o + lo:o + (w if hi is None else hi)]

    def cbv(name, lo=0, hi=None):
        o, w = CB[name]
        return cb[:, o + lo:o + (w if hi is None else hi)]

    ident_f = cfv("ident"); ident_b = cbv("ident")

    def psb(i, n=1024):
        return ps[i][:, :].bitcast(BF16)[:, 0:n]

    def pe(fn, r=(), w=()): return P.add("pe", fn, r, w)
    def act(fn, r=(), w=()): return P.add("act", fn, r, w)
    def dve(fn, r=(), w=()): return P.add("dve", fn, r, w)
    def pool(fn, r=(), w=()): return P.add("pool", fn, r, w)
    def dma(q, sem, out, in_, r=(), w=(), nofence=False, slow=False):
        if slow:
            return P.add(q, lambda e: e.dma_start(out=out, in_=in_, allow_slow_non_contiguous=True), r, w, dma=sem)
        return P.add(q, lambda e: e.dma_start(out=out, in_=in_), r, w, dma=sem, nofence=nofence)

    dma_sem_names = set()
    _orig_add = P.add

    def add_track(eng, fn, r=(), w=(), dma=None, nofence=False):
        if stopped[0]:
            return None
        if dma is not None:
            dma_sem_names.add(dma)
        return _orig_add(eng, fn, r, w, dma, nofence)
    P.add = add_track

    rot = {"d": 0, "m": 0}

    att_dense3 = [False]

    def bank_d():
        b = rot["d"] % (3 if att_dense3[0] else 4)
        rot["d"] += 1
        return b

    def bank_m():
        b = 4 + rot["m"] % 4
        rot["m"] += 1
        return b

    wq = []
    wstate = {"issued": 0, "used": 0, "done": 0}
    w_extra = []

    def w_issue_upto(n):
        while wstate["issued"] < min(n, len(wq)):
            i = wstate["issued"]
            src, kc, ncols = wq[i]
            s = i % NSLOT
            dst = wslots[s][:, 0:kc * ncols].rearrange("p (k n) -> p k n", k=kc)
            dma("pool", "w%d" % s, dst, src.rearrange("(k p) n -> p k n", p=128), r=list(w_extra), w=[("w", s)], nofence=True)
            wstate["issued"] += 1

    def w_done(n=1):
        wstate["done"] += n
        w_issue_upto(wstate["done"] + NSLOT)

    def w_next():
        i = wstate["used"]
        assert i < wstate["done"] + NSLOT
        w_issue_upto(i + 1)
        src, kc, ncols = wq[i]
        s = i % NSLOT
        wstate["used"] += 1
        return wslots[s][:, 0:kc * ncols].rearrange("p (k n) -> p k n", k=kc), ("w", s)

    def wblk(wap, k0, k1, c0, ncols):
        return (wap[k0 * 128:k1 * 128, c0:c0 + ncols], k1 - k0, ncols)

    C_AQ, C_AK, C_AV, C_RQ, C_RK, C_RV, C_RG = 0, 1024, 1280, 1536, 2560, 3584, 4608
    for j in range(4):
        wq.append(wblk(w_in, 0, 16, C_RK + 256 * j, 256))
        wq.append(wblk(w_in, 0, 16, C_RV + 256 * j, 256))
    wq.append(wblk(w_in, 0, 16, C_AK, 256))
    wq.append(wblk(w_in, 0, 16, C_AV, 256))
    for j in range(4):
        wq.append(wblk(w_in, 0, 16, C_AQ + 256 * j, 256))
    for j in range(4):
        for cbase in (C_RQ, C_RG):
            wq.append(wblk(w_in, 0, 16, cbase + 256 * j, 256))
    for j in range(8):
        wq.append(wblk(w_out, 0, 16, 256 * j, 256))
    QUART = [(0, 6), (6, 6), (12, 5), (17, 5)]
    for (b0, nb) in QUART:
        for b in range(b0, b0 + nb):
            wq.append(wblk(w_gate, 0, 16, 256 * b, 256))
            wq.append(wblk(w_up, 0, 16, 256 * b, 256))
        for j in range(8):
            wq.append(wblk(w_down, 2 * b0, 2 * (b0 + nb), 256 * j, 256))
    for j in range(8):
        wq.append(wblk(w_pg, 0, 16, 256 * j, 256))

    def dense_block(kc, rhs_fn, rhs_res, evac, tiles=TT, nm=2):
        wv, wres = w_next()
        for ml in range(nm):
            for ti, (c0, n) in enumerate(tiles):
                b = bank_d()
                for k in range(kc):
                    pe(lambda e, b=b, k=k, ml=ml, c0=c0, n=n, wv=wv: e.matmul(
                        ps[b][:, 0:n], lhsT=wv[:, k, ml * 128:(ml + 1) * 128], rhs=rhs_fn(k, c0, n),
                        start=(k == 0), stop=(k == kc - 1)),
                       r=[wres] + list(rhs_res), w=[("ps", b)])
                evac(ml, ti, c0, n, b)
        w_done()

    dma("sp", "c_cb", cb, cb_d, w=["cb"])

    Z = Alloc(HB, total)
    NXT = 4
    xt = [Z([D], F32) for _ in range(NXT)]
    junk = Z([D], BF16)
    xn = [Z([D], BF16) for _ in range(2)]
    gbc = Z([D], F32)
    ss = misc[:, 0:10]; rstd1 = misc[:, 10:20]; tmpa = misc[:, 20:30]
    gpa = misc[:, 30:31]; mx = misc[:, 31:32]; negc1 = misc[:, 32:33]; epsv = misc[:, 34:35]
    dve(lambda e: e.memset(epsv, EPS), w=["epsv"])


    tiles1 = [(x_main[i * 128:(i + 1) * 128, :], 128, i * 128) for i in range(8)]
    tiles1.append((x_halo, 128, NT))
    tiles1.append((x_smp, NS, NP_))
    def p1_load(i, src, rows, c0):
        s4 = i % NXT
        dma("sp", "x%d" % s4, xt[s4][0:rows, :], src, w=[("xt", s4)])

    def p1_stage1(i, src, rows, c0):
        s = i % 2
        s4 = i % NXT
        act(lambda e, s4=s4, rows=rows, i=i: e.activation(out=junk[0:rows, :], in_=xt[s4][0:rows, :], func=AF.Square,
                                                         accum_out=ss[0:rows, i:i + 1]),
            r=[("xt", s4)], w=["junk", ("ss", i)])
        act(lambda e, rows=rows, i=i: e.activation(out=tmpa[0:rows, i:i + 1], in_=ss[0:rows, i:i + 1], func=AF.Sqrt,
                                                   bias=epsv[0:rows, :], scale=1.0 / D),
            r=[("ss", i), "epsv"], w=[("tmpa", i)])
        dve(lambda e, rows=rows, i=i: e.reciprocal(out=rstd1[0:rows, i:i + 1], in_=tmpa[0:rows, i:i + 1]),
            r=[("tmpa", i)], w=[("rstd1", i)])
        dve(lambda e, s=s, s4=s4, rows=rows, i=i: e.scalar_tensor_tensor(
            out=xn[s][0:rows, :], in0=xt[s4][0:rows, :], scalar=rstd1[0:rows, i:i + 1], in1=gbc[0:rows, :],
            op0=ALU.mult, op1=ALU.mult), r=[("xt", s4), ("rstd1", i), "gbc"], w=[("xn", s)])

    def p1_stage2(i, src, rows, c0):
        s = i % 2
        for half in range(2):
            b = bank_m()
            for kk in range(8):
                k = half * 8 + kk
                pe(lambda e, b=b, kk=kk, k=k, s=s, rows=rows: e.transpose(
                    psb(b)[:, kk * 128:kk * 128 + rows], xn[s][0:rows, k * 128:(k + 1) * 128],
                    ident_b[0:rows, 0:rows]), r=[("xn", s), "cb"], w=[("ps", b)])
            src_ps = lambda b=b, rows=rows: psb(b).rearrange("p (a c) -> p a c", a=8)[:, :, 0:rows]
            dst = aT[:, half * 8:(half + 1) * 8, c0:c0 + rows]
            if half == 0:
                act(lambda e, dst=dst, src_ps=src_ps: e.copy(out=dst, in_=src_ps()), r=[("ps", b)], w=[("aT", c0, half)])
            else:
                dve(lambda e, dst=dst, src_ps=src_ps: e.tensor_copy(out=dst, in_=src_ps()), r=[("ps", b)], w=[("aT", c0, half)])

    p1_load(0, *tiles1[0])
    dma("sp", "c_gbc", gbc, an_g.partition_broadcast(128), w=["gbc"])
    for i in range(1, NXT):
        p1_load(i, *tiles1[i])
    dma("sp", "c_cf", cf, cf_d, w=["cf"])
    p1_stage1(0, *tiles1[0])
    for i in range(len(tiles1)):
        if i + 1 < len(tiles1):
            p1_stage1(i + 1, *tiles1[i + 1])
        if i + NXT < len(tiles1):
            p1_load(i + NXT, *tiles1[i + NXT])
        if i in (3, 5, 7, 8):
            w_extra[:] = [("xt", (i + 1) % NXT)]
            w_issue_upto(wstate["issued"] + 1)
            w_extra[:] = []
        p1_stage2(i, *tiles1[i])
    w_issue_upto(NSLOT)
    dve(lambda e: e.tensor_tensor(out=gpa, in0=cfv("qg"), in1=cfv("kg"), op=ALU.mult), r=["cf"], w=["gpa"])
    gpa2 = misc[:, 33:34]
    dve(lambda e: e.tensor_tensor(out=gpa2, in0=gpa, in1=gpa, op=ALU.mult), r=["gpa"], w=["gpa2"])
    b = bank_m()
    pe(lambda e, b=b: e.transpose(ps[b][0:1, 0:128], gpa2, ident_f), r=["gpa2", "cf"], w=[("ps", b)])
    dve(lambda e, b=b: e.tensor_reduce(out=mx[0:1, :], in_=ps[b][0:1, 0:128], axis=AX.X, op=ALU.max),
        r=[("ps", b)], w=["mx"])
    act(lambda e: e.activation(out=mx[0:1, :], in_=mx[0:1, :], func=AF.Sqrt), r=["mx"], w=["mx"])
    dve(lambda e: e.tensor_scalar(out=negc1[0:1, :], in0=mx[0:1, :], scalar1=-(ATTN_SCALE * 128.0), scalar2=None,
                                  op0=ALU.mult), r=["mx"], w=["negc1"])
    b = bank_m()
    pe(lambda e, b=b: e.matmul(ps[b][:, 0:1], lhsT=cfv("ones_row")[0:1, :], rhs=negc1[0:1, :], start=True, stop=True),
       r=["negc1", "cf"], w=[("ps", b)])
    dve(lambda e, b=b: e.tensor_copy(out=negc, in_=ps[b][:, 0:1]), r=[("ps", b)], w=["negc"])
    act(lambda e: e.activation(out=esink, in_=cfv("sinks"), func=AF.Exp, bias=negc, scale=1.0),
        r=["negc", "cf"], w=["esink"])

    P.fence()
    dump("aT", aT, ["aT"])
    chk(1)

    aT_rhs = lambda k, c0, n: aT[:, k, c0:c0 + n]
    cosv = cfv("cos"); sinv = cfv("sin")

    rot_pending = [None]

    def rot_flush():
        if rot_pending[0] is not None:
            f = rot_pending[0]
            rot_pending[0] = None
            f()

    def rotary_ops(tag, b, c0, n, xb, t1, t2, outb):
        act(lambda e: e.copy(out=xb[:, c0:c0 + n], in_=ps[b][:, 0:n]), r=[("ps", b)], w=[(tag, "xb", c0)])
        dve(lambda e: e.tensor_tensor(out=t1[:, c0:c0 + n], in0=ps[b][:, 0:n], in1=cosv[:, c0:c0 + n], op=ALU.mult),
            r=[("ps", b), "cf"], w=[(tag, "t1", c0)])
        rot_flush()

        def part_b():
            b2 = bank_m()
            pe(lambda e: e.matmul(ps[b2][:, 0:n], lhsT=cbv("rrot"), rhs=xb[:, c0:c0 + n], start=True, stop=True),
               r=[(tag, "xb", c0), "cb"], w=[("ps", b2)])
            dve(lambda e: e.tensor_tensor(out=t2[:, c0:c0 + n], in0=ps[b2][:, 0:n], in1=sinv[:, c0:c0 + n], op=ALU.mult),
                r=[("ps", b2), "cf"], w=[(tag, "t2", c0)])
            dve(lambda e: e.tensor_tensor(out=outb[:, c0:c0 + n], in0=t1[:, c0:c0 + n], in1=t2[:, c0:c0 + n], op=ALU.add),
                r=[(tag, "t1", c0), (tag, "t2", c0)], w=[(tag, "rot", c0)])
        rot_pending[0] = part_b

    Z = Alloc(HB, total)
    p1 = []
    for hp in range(2):
        p1.append(dict(xb=Z([NT], BF16), t1=Z([NT], F32), t2=Z([NT], F32), kr=Z([NT], BF16),
                       kD=Z([8, 128], BF16), vT=Z([NT], BF16), vtok=Z([8, 128], BF16), sloc=Z([128], F32)))
    kdl = cfv("kdlong").rearrange("p (h c) -> p h c", h=8)

    for j in range(4):
        hs = (2 * j, 2 * j + 1)

        def evac_k(ml, ti, c0, n, b, hs=hs):
            bf = p1[ml]
            rotary_ops(("p1", ml), b, c0, n, bf["xb"], bf["t1"], bf["t2"], bf["kr"])

        def evac_v(ml, ti, c0, n, b, hs=hs):
            bf = p1[ml]
            act(lambda e: e.copy(out=bf["vT"][:, c0:c0 + n], in_=ps[b][:, 0:n]), r=[("ps", b)],
                w=[("p1", ml, "vT", c0)])
        dense_block(16, aT_rhs, ["aT"], evac_k)
        chk(20)
        dense_block(16, aT_rhs, ["aT"], evac_v)
        rot_flush()
        chk(21)
        for ml in range(2):
            h = hs[ml]
            bf = p1[ml]
            rk_res = [(("p1", ml), "rot", c0) for (c0, n) in TT]
            rv_res = [("p1", ml, "vT", c0) for (c0, n) in TT]
            dma("sp", "spk%d" % ml, krs.ap()[h], bf["kr"], r=rk_res, w=[("krs", h)])
            dma("sp", "spv%d" % ml, vts.ap()[h], bf["vT"], r=rv_res, w=[("vts", h)])
            for half in range(2):
                bk = bank_m()
                for cc in range(4):
                    c = half * 4 + cc
                    pe(lambda e, bk=bk, cc=cc, c=c, bf=bf: e.transpose(
                        psb(bk)[:, cc * 128:(cc + 1) * 128], bf["kr"][:, c * 128:(c + 1) * 128], ident_b),
                       r=rk_res + ["cb"], w=[("ps", bk)])
                for cc in range(4):
                    c = half * 4 + cc
                    dve(lambda e, bk=bk, cc=cc, c=c, bf=bf, h=h: e.tensor_scalar(
                        out=bf["kD"][:, c, :], in0=psb(bk)[:, cc * 128:(cc + 1) * 128], scalar1=kdl[:, h, c:c + 1],
                        scalar2=None, op0=ALU.mult), r=[("ps", bk), "cf"], w=[("p1", ml, "kD", c)])
                bv = bank_m()
                for cc in range(4):
                    c = half * 4 + cc
                    pe(lambda e, bv=bv, cc=cc, c=c, bf=bf: e.transpose(
                        psb(bv)[:, cc * 128:(cc + 1) * 128], bf["vT"][:, c * 128:(c + 1) * 128], ident_b),
                       r=rv_res + ["cb"], w=[("ps", bv)])
                act(lambda e, bv=bv, half=half, bf=bf: e.copy(
                    out=bf["vtok"][:, half * 4:(half + 1) * 4, :],
                    in_=psb(bv)[:, 0:512].rearrange("p (a c) -> p a c", a=4)), r=[("ps", bv)], w=[("p1", ml, "vtok", half)])
            chk(22)
            bs = bank_m()
            for c in range(8):
                pe(lambda e, bs=bs, c=c, bf=bf: e.matmul(ps[bs][:, 0:128], lhsT=bf["kD"][:, c, :], rhs=bf["vtok"][:, c, :],
                                                        start=(c == 0), stop=(c == 7)),
                   r=[("p1", ml, "kD", c), ("p1", ml, "vtok", c // 4)], w=[("ps", bs)])
            dve(lambda e, bs=bs, bf=bf: e.tensor_copy(out=bf["sloc"], in_=ps[bs][:, 0:128]), r=[("ps", bs)],
                w=[("p1", ml, "sloc")])
            chk(23)
            dma("sp", "agi", ag_in.ap()[h * 128:(h + 1) * 128, :], bf["sloc"], r=[("p1", ml, "sloc")], w=["ag_in"])
            chk(24)

    dump("ag_in", ag_in.ap(), ["ag_in"])
    chk(2)
    P.fence()
    P.add("pool", lambda e: e.collective_compute("AllGather", ALU.bypass, replica_groups=groups or [[0, 1, 2, 3], [4, 5, 6, 7]],
                                                 ins=[ag_in.ap().opt()], outs=[ag_out.ap().opt()]),
          r=["ag_in"], w=["ag_out"], dma="cc")
    chk(3)

    Z = Alloc(HB, total)
    zf = [Z([NA], F32) for _ in range(2)]
    sq = [Z([NA], BF16) for _ in range(2)]
    rstdb = [Z([NA], F32)] * 2
    knT = Z([2, NA], BF16)
    kn32 = Z([2, 132], F32)
    v32 = Z([2, 132], F32)
    vTb = Z([2, NA], BF16)
    vtokA = Z([2, 9, 128], BF16)
    qnT = Z([4, NT], BF16)
    qnT2 = Z([4, NT], BF16)
    PTm = [Z([2, 512], BF16) for _ in range(3)]
    rec = [Z([512], F32) for _ in range(2)]
    win_t = Z([2, 128], F32)
    vstok = Z([2, 128], BF16)
    kc_b = [Z([128], BF16) for _ in range(NS)]
    vc_b = [Z([128], BF16) for _ in range(NS)]
    kcT = Z([4, 128], BF16)
    PTc = Z([16], BF16)
    Pn = Z([16], BF16)
    recs = Z([16], F32)
    cnt = {"qk": 0}

    qk_pending = [None]

    def qk_flush():
        if qk_pending[0] is not None:
            f = qk_pending[0]
            qk_pending[0] = None
            f()

    def qknorm(b, c0, n, gname, outbf, tagres, out32=None):
        s = cnt["qk"] % 2
        cnt["qk"] += 1
        act(lambda e: e.activation(out=sq[s][:, 0:n], in_=ps[b][:, 0:n], func=AF.Square), r=[("ps", b)], w=[("sq", s)])
        dve(lambda e: e.tensor_copy(out=zf[s][:, 0:n], in_=ps[b][:, 0:n]), r=[("ps", b)], w=[("zf", s)])
        qk_flush()

        def part_b():
            b2 = 3
            pe(lambda e: e.matmul(ps[b2][:, 0:n], lhsT=cbv("ones128"), rhs=sq[s][:, 0:n], start=True, stop=True),
               r=[("sq", s), "cb"], w=[("ps", b2)])
            act(lambda e: e.activation(out=rstdb[0][:, 0:n], in_=ps[b2][:, 0:n], func=AF.Ln, bias=epsv, scale=1.0),
                r=[("ps", b2)], w=["rstdb"])
            act(lambda e: e.activation(out=rstdb[0][:, 0:n], in_=rstdb[0][:, 0:n], func=AF.Exp, scale=-0.5),
                r=["rstdb"], w=["rstdb"])
            dve(lambda e: e.scalar_tensor_tensor(out=outbf, in0=zf[s][:, 0:n], scalar=cfv(gname), in1=rstdb[0][:, 0:n],
                                                 op0=ALU.mult, op1=ALU.mult),
                r=[("zf", s), "rstdb", "cf"], w=[tagres])
            if out32 is not None:
                lo, hi, dst = out32
                dve(lambda e: e.scalar_tensor_tensor(out=dst, in0=zf[s][:, lo:hi], scalar=cfv(gname),
                                                     in1=rstdb[0][:, lo:hi], op0=ALU.mult, op1=ALU.mult),
                    r=[("zf", s), "rstdb", "cf"], w=[("kn32", tagres)])
        qk_pending[0] = part_b

    def evac_ak(ml, ti, c0, n, b):
        o32 = None
        if ti == 2:
            o32 = (208, 340, kn32[:, ml, :])
        qknorm(b, c0, n, "kg", knT[:, ml, c0:c0 + n], ("knT", ml, c0), o32)

    def evac_av(ml, ti, c0, n, b):
        act(lambda e: e.copy(out=vTb[:, ml, c0:c0 + n], in_=ps[b][:, 0:n]), r=[("ps", b)], w=[("vTb", ml, c0)])
        if ti == 2:
            dve(lambda e: e.tensor_copy(out=v32[:, ml, :], in_=ps[b][:, 208:340]), r=[("ps", b)], w=[("v32", ml)])

    att_dense3 = [True]
    dense_block(16, aT_rhs, ["aT"], evac_ak, tiles=TT_H)
    qk_flush()
    dense_block(16, aT_rhs, ["aT"], evac_av, tiles=TT_H)
    vres = lambda g: [("vTb", g, c0) for (c0, n) in TT_H]
    kres = lambda g: [("knT", g, c0) for (c0, n) in TT_H]
    for g in range(2):
        for grp in range(3):
            blks = [0, 1, 2, 3] if grp == 0 else ([4, 5, 6, 7] if grp == 1 else [8])
            bv = bank_m()
            for ii, blk in enumerate(blks):
                col = NT if blk == 0 else (blk - 1) * 128
                pe(lambda e, bv=bv, ii=ii, col=col, g=g: e.transpose(
                    psb(bv)[:, ii * 128:(ii + 1) * 128], vTb[:, g, col:col + 128], ident_b),
                   r=vres(g) + ["cb"], w=[("ps", bv)])
            nb = len(blks)
            act(lambda e, bv=bv, g=g, blks=blks, nb=nb: e.copy(
                out=vtokA[:, g, blks[0]:blks[0] + nb, :],
                in_=psb(bv)[:, 0:nb * 128].rearrange("p (a c) -> p a c", a=nb)), r=[("ps", bv)], w=[("vtokA", g)])
        bv = bank_m()
        pe(lambda e, bv=bv, g=g: e.transpose(psb(bv)[0:NS, 0:128], vTb[:, g, NP_:NP_ + NS], ident_b),
           r=vres(g) + ["cb"], w=[("ps", bv)])
        act(lambda e, bv=bv, g=g: e.copy(out=vstok[0:NS, g, :], in_=psb(bv)[0:NS, 0:128]), r=[("ps", bv)],
            w=[("vstok", g)])
    for (src32, dst, nm) in ((kn32, kwin, "kw"), (v32, vwin, "vw")):
        for g in range(2):
            bw = bank_m()
            pe(lambda e, bw=bw, g=g, src32=src32: e.transpose(ps[bw][:, 0:128], src32[:, g, 0:128], ident_f),
               r=[("kn32", ("knT", g, 688)), ("v32", g), "cf"], w=[("ps", bw)])
            dve(lambda e, bw=bw, g=g: e.tensor_copy(out=win_t[:, g, :], in_=ps[bw][:, 0:128]), r=[("ps", bw)],
                w=[("win_t", g)])
        dma("sp", "o_" + nm, dst, win_t, r=[("win_t", 0), ("win_t", 1)], w=["out_" + nm])
    dma("sp", "o_ks", ks_out[:, 0:127, :, :], ck[:, 1:128, :, :], w=["ks_out_a"])
    dma("sp", "o_vs", vs_out[:, 0:127, :, :], cv[:, 1:128, :, :], w=["vs_out_a"])
    for g in range(2):
        for s in range(NS):
            dma("sp", "o_ks", ks_out[s, 127, g, :].rearrange("(d o) -> d o", o=1), kn32[:, g, 128 + s:129 + s],
                r=[("kn32", ("knT", g, 688))], w=[("ks_out_b", g, s)], slow=True)
            dma("sp", "o_vs", vs_out[s, 127, g, :].rearrange("(d o) -> d o", o=1), v32[:, g, 128 + s:129 + s],
                r=[("v32", g)], w=[("vs_out_b", g, s)], slow=True)

    mask = cbv("mask").rearrange("p (a c) -> p a c", a=3)
    v3 = lambda ap: ap[:, 0:1024].rearrange("p (a c) -> p a c", a=8)
    esr_f = v3(zf[0]); esr_d = v3(zf[1]); esr_hi = v3(sq[0]); esr_lo = v3(sq[1])
    esr2 = Z([8, 128], BF16)
    oh = cfv("onehot")
    dve(lambda e: e.tensor_copy(out=esr_f[0:2], in_=esink[0:2, :].unsqueeze(2).broadcast_to([2, 8, 128])),
        r=["esink"], w=[("zf", 0)])
    dve(lambda e: e.tensor_copy(out=esr_hi[0:2], in_=esr_f[0:2]), r=[("zf", 0)], w=[("sq", 0)])
    dve(lambda e: e.tensor_tensor(out=esr_d[0:2], in0=esr_f[0:2], in1=esr_hi[0:2], op=ALU.subtract),
        r=[("zf", 0), ("sq", 0)], w=[("zf", 1)])
    dve(lambda e: e.tensor_copy(out=esr_lo[0:2], in_=esr_d[0:2]), r=[("zf", 1)], w=[("sq", 1)])
    dve(lambda e: e.tensor_scalar(out=esr2[0:2], in0=esr_hi[0:2], scalar1=oh[0:2, 0:1], scalar2=None, op0=ALU.mult),
        r=[("sq", 0), "cf"], w=["esr2a"])
    dve(lambda e: e.scalar_tensor_tensor(out=esr2[0:2], in0=esr_lo[0:2], scalar=oh[0:2, 1:2], in1=esr2[0:2],
                                         op0=ALU.mult, op1=ALU.add), r=[("sq", 1), "esr2a", "cf"], w=["esr2"])
    def make_att(g, qnT, qres, skew=1):
        npt = skew + 1
        s_pairs = [(4, 5), (6, 7), (0, 1)]
        alloc_c = {"s": 0}

        def alloc_S():
            if skew == 1:
                return bank_m(), bank_m()
            p = s_pairs[alloc_c["s"] % 3]
            alloc_c["s"] += 1
            return p

        def alloc_PV():
            if skew == 1:
                return bank_m(), bank_m()
            return 2, 3

        def att_S(blk, g=g):
            s = blk % npt
            q_rhs = qnT[:, :, blk * 128:(blk + 1) * 128]
            k_own = knT[:, g, blk * 128:(blk + 1) * 128]
            k_prev = knT[:, g, NT:NT + 128] if blk == 0 else knT[:, g, (blk - 1) * 128:blk * 128]
            bo, bp = alloc_S()
            mprev = 2 if blk == 0 else 1
            for (bb_, kk_, mi) in ((bo, k_own, 0), (bp, k_prev, mprev)):
                pe(lambda e, bb_=bb_, kk_=kk_, q_rhs=q_rhs: e.matmul(ps[bb_][:, :], lhsT=kk_, rhs=q_rhs, start=True, stop=False),
                   r=qres + kres(g), w=[("ps", bb_)])
                pe(lambda e, bb_=bb_, mi=mi: e.matmul(ps[bb_][:, :], lhsT=ident_b,
                                                     rhs=mask[:, mi:mi + 1, :].broadcast_to([128, 4, 128]),
                                                     start=False, stop=True), r=["cb"], w=[("ps", bb_)])
            act(lambda e, bo=bo, s=s: e.activation(out=PTm[s][:, 0, :], in_=ps[bo][:, :], func=AF.Exp, bias=negc,
                                                   scale=ATTN_SCALE), r=[("ps", bo), "negc"], w=[("PTm", s, 0)])
            act(lambda e, bp=bp, s=s: e.activation(out=PTm[s][:, 1, :], in_=ps[bp][:, :], func=AF.Exp, bias=negc,
                                                   scale=ATTN_SCALE), r=[("ps", bp), "negc"], w=[("PTm", s, 1)])

        def att_PV(blk, g=g):
            s = blk % npt
            sr = blk % 2
            bO, bD = alloc_PV()
            for t, vb in ((0, blk + 1), (1, blk)):
                pe(lambda e, bO=bO, t=t, vb=vb, s=s, g=g: e.matmul(ps[bO][:, :], lhsT=vtokA[:, g, vb, :], rhs=PTm[s][:, t, :],
                                                                  start=(t == 0), stop=(t == 1)),
                   r=[("PTm", s, t), ("vtokA", g)], w=[("ps", bO)])
            for t in range(2):
                pe(lambda e, bD=bD, t=t, s=s: e.matmul(ps[bD][:, :], lhsT=cbv("ones"), rhs=PTm[s][:, t, :],
                                                      start=(t == 0), stop=False),
                   r=[("PTm", s, t), "cb"], w=[("ps", bD)])
            pe(lambda e, bD=bD, g=g: e.matmul(ps[bD][:, :], lhsT=cbv("ones")[0:2, :], rhs=esr2[0:2, 4 * g:4 * g + 4, :],
                                             start=False, stop=True), r=["esr2", "cb"], w=[("ps", bD)])
            if blk % 2 == 0:
                act(lambda e, bD=bD, sr=sr: e.activation(out=rec[sr], in_=ps[bD][:, :], func=AF.Ln),
                    r=[("ps", bD)], w=[("rec", sr)])
                act(lambda e, sr=sr: e.activation(out=rec[sr], in_=rec[sr], func=AF.Exp, scale=-1.0), r=[("rec", sr)],
                    w=[("rec", sr)])
            else:
                dve(lambda e, bD=bD, sr=sr: e.reciprocal(out=rec[sr], in_=ps[bD][:, :]), r=[("ps", bD)], w=[("rec", sr)])
            dve(lambda e, bO=bO, sr=sr, g=g, blk=blk: e.tensor_tensor(
                out=mixT[:, 4 * g:4 * g + 4, blk * 128:(blk + 1) * 128],
                in0=ps[bO][:, :].rearrange("p (a c) -> p a c", a=4),
                in1=rec[sr][:, :].rearrange("p (a c) -> p a c", a=4), op=ALU.mult),
                r=[("ps", bO), ("rec", sr)], w=[("mixT", "a", g, blk)])


        def att_gen():
            for b0 in range(skew):
                att_S(b0)
                yield
            for blk in range(8):
                if blk + skew < 8:
                    att_S(blk + skew)
                    yield
                att_PV(blk)
                yield

        def att_sample():
            for sm in range(NS):
                dma("pool", "kc%d" % sm, kc_b[sm], ck[sm, :, g, :], w=[("kc_b", sm)])
                dma("pool", "vc%d" % sm, vc_b[sm], cv[sm, :, g, :], w=[("vc_b", sm)])
            bt = bank_m()
            for sm in range(NS):
                pe(lambda e, bt=bt, sm=sm: e.transpose(psb(bt)[:, sm * 128:(sm + 1) * 128], kc_b[sm], ident_b),
                   r=[("kc_b", sm), "cb"], w=[("ps", bt)])
            act(lambda e, bt=bt: e.copy(out=kcT, in_=psb(bt)[:, 0:512].rearrange("p (a c) -> p a c", a=4)),
                r=[("ps", bt)], w=["kcT"])
            bS = bank_m()
            for sm in range(NS):
                pe(lambda e, bS=bS, sm=sm: e.matmul(ps[bS][:, sm * 4:(sm + 1) * 4], lhsT=kcT[:, sm, :], rhs=qnT[:, :, NP_ + sm],
                                                   start=True, stop=True, skip_group_check=True),
                   r=["kcT"] + qres, w=[("ps", bS)])
            for sm in range(NS):
                pe(lambda e, bS=bS, sm=sm, g=g: e.matmul(ps[bS][0:NS, 32 + sm * 4:32 + (sm + 1) * 4],
                                                        lhsT=knT[:, g, NP_:NP_ + NS], rhs=qnT[:, :, NP_ + sm],
                                                        start=True, stop=True, skip_group_check=True),
                   r=kres(g) + qres, w=[("ps", bS)])
            act(lambda e, bS=bS: e.activation(out=PTc, in_=ps[bS][:, 0:16], func=AF.Exp, bias=negc, scale=ATTN_SCALE),
                r=[("ps", bS), "negc"], w=["PTc"])
            act(lambda e, bS=bS: e.activation(out=Pn[0:NS, :], in_=ps[bS][0:NS, 32:48], func=AF.Exp,
                                              bias=negc[0:NS, :], scale=ATTN_SCALE), r=[("ps", bS), "negc"], w=["Pn"])
            dve(lambda e: e.tensor_tensor(out=Pn[0:NS, :].rearrange("p (s h) -> p s h", s=4),
                                          in0=Pn[0:NS, :].rearrange("p (s h) -> p s h", s=4),
                                          in1=oh[0:NS, 0:4].unsqueeze(2).broadcast_to([NS, 4, 4]), op=ALU.mult),
                r=["Pn", "cf"], w=["Pn"])
            bO = bank_m(); bD = bank_m()
            for sm in range(NS):
                pe(lambda e, bO=bO, sm=sm: e.matmul(ps[bO][:, sm * 4:(sm + 1) * 4], lhsT=vc_b[sm], rhs=PTc[:, sm * 4:(sm + 1) * 4],
                                                   start=(sm == 0), stop=False, skip_group_check=True),
                   r=[("vc_b", sm), "PTc"], w=[("ps", bO)])
                pe(lambda e, bO=bO, sm=sm, g=g: e.matmul(ps[bO][:, sm * 4:(sm + 1) * 4], lhsT=vstok[0:NS, g, :],
                                                        rhs=Pn[0:NS, sm * 4:(sm + 1) * 4], start=False, stop=True,
                                                        skip_group_check=True),
                   r=[("vstok", g), "Pn"], w=[("ps", bO)])
            for sm in range(NS):
                pe(lambda e, bD=bD, sm=sm: e.matmul(ps[bD][:, sm * 4:(sm + 1) * 4], lhsT=cbv("ones"), rhs=PTc[:, sm * 4:(sm + 1) * 4],
                                                   start=(sm == 0), stop=False, skip_group_check=True),
                   r=["PTc", "cb"], w=[("ps", bD)])
                pe(lambda e, bD=bD, sm=sm: e.matmul(ps[bD][:, sm * 4:(sm + 1) * 4], lhsT=cbv("ones")[0:NS, :],
                                                   rhs=Pn[0:NS, sm * 4:(sm + 1) * 4], start=False, stop=False,
                                                   skip_group_check=True), r=["Pn", "cb"], w=[("ps", bD)])
            pe(lambda e, bD=bD, g=g: e.matmul(ps[bD][:, 0:16], lhsT=cbv("ones")[0:2, :],
                                             rhs=esr2[0:2, 4 * g:4 * g + 4, 0].unsqueeze(1).broadcast_to([2, 4, 4]),
                                             start=False, stop=True, skip_group_check=True),
               r=["esr2", "cb"], w=[("ps", bD)])
            act(lambda e, bD=bD: e.activation(out=recs, in_=ps[bD][:, 0:16], func=AF.Ln), r=[("ps", bD)], w=["recs"])
            act(lambda e: e.activation(out=recs, in_=recs, func=AF.Exp, scale=-1.0), r=["recs"], w=["recs"])
            dve(lambda e, bO=bO, g=g: e.tensor_tensor(
                out=mixT[:, 4 * g:4 * g + 4, NP_:NP_ + NS], in0=ps[bO][:, 0:16].rearrange("p (s h) -> p h s", s=4),
                in1=recs[:, :].rearrange("p (s h) -> p h s", s=4), op=ALU.mult),
                r=[("ps", bO), "recs"], w=[("mixT", "as", g)])

        return att_gen, att_sample

    qbufs = [qnT, qnT2]
    qresf = lambda gi: [("qnT", gi, hh, c0) for hh in range(4) for (c0, n) in TT]

    def qproj_gen(gi):
        qb = qbufs[gi]
        for jj in range(2):
            wv, wres = w_next()
            for ml in range(2):
                hh = jj * 2 + ml
                for ti, (c0, n) in enumerate(TT):
                    b = bank_d()
                    for k in range(16):
                        pe(lambda e, b=b, k=k, ml=ml, c0=c0, n=n, wv=wv: e.matmul(
                            ps[b][:, 0:n], lhsT=wv[:, k, ml * 128:(ml + 1) * 128], rhs=aT[:, k, c0:c0 + n],
                            start=(k == 0), stop=(k == 15)), r=[wres, "aT"], w=[("ps", b)])
                    qknorm(b, c0, n, "qg", qb[:, hh, c0:c0 + n], ("qnT", gi, hh, c0))
                    yield
            w_done()
        qk_flush()

    def drive_mix(ga, gb, pat):
        a_alive, b_alive = True, gb is not None
        while a_alive or b_alive:
            if a_alive:
                try:
                    next(ga)
                except StopIteration:
                    a_alive = False
            for _ in range(pat if a_alive else 99):
                if not b_alive:
                    break
                try:
                    next(gb)
                except StopIteration:
                    b_alive = False

    for _ in qproj_gen(0):
        pass
    attg0, atts0 = make_att(0, qbufs[0], qresf(0))
    attg1, atts1 = make_att(1, qbufs[1], qresf(1), skew=2)
    drive_mix(attg0(), qproj_gen(1), 1)
    atts0()
    for _ in attg1():
        pass
    atts1()
    att_dense3[0] = False
    P.fence()
    dump("mixA", mixT[:, 0:8, :], ["mixT"])
    chk(4)

    Z = Alloc(HB, total)
    agl = [Z([8, 128], F32) for _ in range(2)]
    coef = cfv("coef")
    ago = ag_out.ap()
    for r_ in range(4):
        s = r_ % 2
        dma("sp", "agl%d" % s, agl[s], ago[r_ * 1024:(r_ + 1) * 1024, :].rearrange("(h p) n -> p h n", p=128),
            r=["ag_out"], w=[("agl", s)])
        for h in range(8):
            if r_ == 0:
                dve(lambda e, s=s, h=h, r_=r_: e.tensor_scalar(out=S0[:, h, :], in0=agl[s][:, h, :],
                                                              scalar1=coef[:, r_ * 8 + h:r_ * 8 + h + 1], scalar2=None,
                                                              op0=ALU.mult), r=[("agl", s), "cf"], w=[("S0", h)])
            else:
                dve(lambda e, s=s, h=h, r_=r_: e.scalar_tensor_tensor(
                    out=S0[:, h, :], in0=agl[s][:, h, :], scalar=coef[:, r_ * 8 + h:r_ * 8 + h + 1], in1=S0[:, h, :],
                    op0=ALU.mult, op1=ALU.add), r=[("agl", s), "cf", ("S0", h)], w=[("S0", h)])

    dump("S0", S0, [("S0", h) for h in range(8)])
    chk(5)
    G1 = [dict(qr=Z([NT], BF16), kr=Z([NT], BF16), vT=Z([NT], BF16), sg=Z([NT], F32)) for _ in range(2)]
    RT = dict(xb=[Z([344], BF16) for _ in range(3)], t1=[Z([344], F32) for _ in range(3)],
              t2=[Z([344], F32) for _ in range(3)])
    R = dict(qd=Z([8, 128], BF16), kd=Z([8, 128], BF16), vtok=Z([8, 128], BF16), scm=Z([8, 128], BF16),
             Sb=Z([8, 128], BF16), Srun=Z([2, 128], F32), o_sb=Z([NT], F32), sq=Z([NT], BF16), rstd=Z([NT], F32),
             tt=Z([NT], F32), ktok_s=Z([128], BF16), vtok_s=Z([128], BF16))
    vm = [Z([128], BF16) for _ in range(NS)]
    Sst = [Z([128], F32) for _ in range(NS)]
    Snew = [Z([128], F32) for _ in range(NS)]
    Sbs = [Z([128], BF16) for _ in range(NS)]
    intraT = cfv("intraT").rearrange("p (h c) -> p h c", h=8)
    qdecv = cfv("qdec").rearrange("p (h c) -> p h c", h=8)
    kdecv = cfv("kdec")
    rgv = cfv("rg")
    tg = "p2"
    p2c = {"d": 0, "r": 0, "t": 0}

    def bank_d3():
        b = p2c["d"] % 3
        p2c["d"] += 1
        return b

    def bank_r():
        b = 3 + p2c["r"] % 2
        p2c["r"] += 1
        return b
    M0, M1, M2 = 5, 6, 7
    p2blocks = {}

    def rot_a(b, c0, n):
        sl = p2c["t"] % 3
        p2c["t"] += 1
        xb, t1 = RT["xb"][sl], RT["t1"][sl]
        act(lambda e: e.copy(out=xb[:, 0:n], in_=ps[b][:, 0:n]), r=[("ps", b)], w=[("rt_xb", sl)])
        dve(lambda e: e.tensor_tensor(out=t1[:, 0:n], in0=ps[b][:, 0:n], in1=cosv[:, c0:c0 + n], op=ALU.mult),
            r=[("ps", b), "cf"], w=[("rt_t1", sl)])
        return sl

    def rot_b(sl, c0, n, outb, outres):
        xb, t1, t2 = RT["xb"][sl], RT["t1"][sl], RT["t2"][sl]
        b2 = bank_r()
        pe(lambda e: e.matmul(ps[b2][:, 0:n], lhsT=cbv("rrot"), rhs=xb[:, 0:n], start=True, stop=True),
           r=[("rt_xb", sl), "cb"], w=[("ps", b2)])
        dve(lambda e: e.tensor_tensor(out=t2[:, 0:n], in0=ps[b2][:, 0:n], in1=sinv[:, c0:c0 + n], op=ALU.mult),
            r=[("ps", b2), "cf"], w=[("rt_t2", sl)])
        dve(lambda e: e.tensor_tensor(out=outb[:, c0:c0 + n], in0=t1[:, 0:n], in1=t2[:, 0:n], op=ALU.add),
            r=[("rt_t1", sl), ("rt_t2", sl)], w=[outres])

    def stageA(h):
        j, ml = h // 2, h % 2
        if ml == 0:
            p2blocks[j] = [w_next() for _ in range(2)]
        blocks = p2blocks[j]
        g1 = G1[h % 2]
        gp = h % 2
        dma("sp", "ldk%d" % gp, g1["kr"], krs.ap()[h], r=[("krs", h)], w=[(tg, "kr", gp, c0) for (c0, n) in TT])
        dma("sp", "ldv%d" % gp, g1["vT"], vts.ap()[h], r=[("vts", h)], w=[(tg, "vT", gp, c0) for (c0, n) in TT])
        for bi in range(2):
            wv, wres = blocks[bi]
            pending = None
            for ti, (c0, n) in enumerate(TT):
                b = bank_d3()
                for k in range(16):
                    pe(lambda e, b=b, k=k, c0=c0, n=n, wv=wv, ml=ml: e.matmul(
                        ps[b][:, 0:n], lhsT=wv[:, k, ml * 128:(ml + 1) * 128], rhs=aT[:, k, c0:c0 + n],
                        start=(k == 0), stop=(k == 15)), r=[wres, "aT"], w=[("ps", b)])
                if bi == 0:
                    sl = rot_a(b, c0, n)
                    if pending is not None:
                        rot_b(*pending)
                    pending = (sl, c0, n, g1["qr"], (tg, "qr", gp, c0))
                else:
                    act(lambda e, b=b, c0=c0, n=n, g1=g1: e.activation(out=g1["sg"][:, c0:c0 + n], in_=ps[b][:, 0:n],
                                                                       func=AF.Silu), r=[("ps", b)], w=[(tg, "sg", gp, c0)])
                if ti == 2:
                    if pending is not None:
                        rot_b(*pending)
                    if ml == 1:
                        w_done()
                yield

    def stageB(h):
        g1 = G1[h % 2]
        gp = h % 2
        q_res = [(tg, "qr", gp, c0) for (c0, n) in TT]
        k_res = [(tg, "kr", gp, c0) for (c0, n) in TT]
        v_res = [(tg, "vT", gp, c0) for (c0, n) in TT]
        g_res = [(tg, "sg", gp, c0) for (c0, n) in TT]
        dve(lambda e: e.tensor_tensor(out=R["qd"], in0=g1["qr"][:, 0:NP_].rearrange("p (a c) -> p a c", a=8),
                                      in1=qdecv[:, h:h + 1, :].broadcast_to([128, 8, 128]), op=ALU.mult),
            r=q_res + ["cf"], w=[(tg, "qd")])
        for half in range(2):
            for cc in range(4):
                c = half * 4 + cc
                pe(lambda e, cc=cc, c=c: e.transpose(psb(M0)[:, cc * 128:(cc + 1) * 128],
                                                    g1["kr"][:, c * 128:(c + 1) * 128], ident_b),
                   r=k_res + ["cb"], w=[("ps", M0)])
            dve(lambda e, half=half: e.tensor_scalar(
                out=R["kd"][:, half * 4:(half + 1) * 4, :], in0=psb(M0)[:, 0:512].rearrange("p (a c) -> p a c", a=4),
                scalar1=kdecv[:, h:h + 1], scalar2=None, op0=ALU.mult), r=[("ps", M0), "cf"], w=[(tg, "kd", half)])
            for cc in range(4):
                c = half * 4 + cc
                pe(lambda e, cc=cc, c=c: e.transpose(psb(M1)[:, cc * 128:(cc + 1) * 128],
                                                    g1["vT"][:, c * 128:(c + 1) * 128], ident_b),
                   r=v_res + ["cb"], w=[("ps", M1)])
            act(lambda e, half=half: e.copy(out=R["vtok"][:, half * 4:(half + 1) * 4, :],
                                            in_=psb(M1)[:, 0:512].rearrange("p (a c) -> p a c", a=4)),
                r=[("ps", M1)], w=[(tg, "vtok", half)])
        act(lambda e: e.copy(out=R["Sb"][:, 0, :], in_=S0[:, h, :]), r=[("S0", h)], w=[(tg, "Sb", 0)])
        yield
        g128 = float(GAMMA[h] ** 128)
        for half in range(2):
            for cc in range(4):
                c = half * 4 + cc
                pe(lambda e, cc=cc, c=c: e.matmul(ps[M2][:, cc * 128:(cc + 1) * 128], lhsT=R["kd"][:, c, :],
                                                 rhs=R["vtok"][:, c, :], start=True, stop=True),
                   r=[(tg, "kd", half), (tg, "vtok", half)], w=[("ps", M2)])
            for cc in range(4):
                c = half * 4 + cc
                prev = S0[:, h, :] if c == 0 else R["Srun"][:, (c - 1) % 2, :]
                prev_res = ("S0", h) if c == 0 else (tg, "Srun", (c - 1) % 2)
                dve(lambda e, cc=cc, c=c, prev=prev: e.scalar_tensor_tensor(
                    out=R["Srun"][:, c % 2, :], in0=prev, scalar=g128, in1=ps[M2][:, cc * 128:(cc + 1) * 128],
                    op0=ALU.mult, op1=ALU.add), r=[prev_res, ("ps", M2)], w=[(tg, "Srun", c % 2)])
                if c < 7:
                    act(lambda e, c=c: e.copy(out=R["Sb"][:, c + 1, :], in_=R["Srun"][:, c % 2, :]),
                        r=[(tg, "Srun", c % 2)], w=[(tg, "Sb", c + 1)])
                else:
                    dma("sp", "o_rs", rstate[h], R["Srun"][:, c % 2, :], r=[(tg, "Srun", c % 2)], w=[("rstate", h)])
        yield
        for half, bsc in ((0, M0), (1, M1)):
            for cc in range(4):
                c = half * 4 + cc
                pe(lambda e, bsc=bsc, cc=cc, c=c: e.matmul(ps[bsc][:, cc * 128:(cc + 1) * 128],
                                                          lhsT=g1["kr"][:, c * 128:(c + 1) * 128],
                                                          rhs=g1["qr"][:, c * 128:(c + 1) * 128], start=True, stop=True),
                   r=k_res + q_res, w=[("ps", bsc)])
            dve(lambda e, bsc=bsc, half=half: e.tensor_tensor(
                out=R["scm"][:, half * 4:(half + 1) * 4, :], in0=ps[bsc][:, :].rearrange("p (a c) -> p a c", a=4),
                in1=intraT[:, h:h + 1, :].broadcast_to([128, 4, 128]), op=ALU.mult),
                r=[("ps", bsc), "cf"], w=[(tg, "scm", half)])
        yield
        obanks = [M0, M1]
        for half in range(2):
            bo = obanks[half]
            for cc in range(4):
                c = half * 4 + cc
                pe(lambda e, bo=bo, cc=cc, c=c: e.matmul(ps[bo][:, cc * 128:(cc + 1) * 128], lhsT=R["vtok"][:, c, :],
                                                        rhs=R["scm"][:, c, :], start=(cc == 0), stop=False,
                                                        skip_group_check=True),
                   r=[(tg, "vtok", half), (tg, "scm", half)], w=[("ps", bo)])
        pe(lambda e: e.transpose(psb(M2)[0:NS, 0:128], g1["kr"][:, NP_:NT], ident_b), r=k_res + ["cb"], w=[("ps", M2)])
        pe(lambda e: e.transpose(psb(M2)[0:NS, 128:256], g1["vT"][:, NP_:NT], ident_b), r=v_res + ["cb"], w=[("ps", M2)])
        act(lambda e: e.mul(out=R["ktok_s"][0:NS, :], in_=psb(M2)[0:NS, 0:128], mul=RET_K_SCALE),
            r=[("ps", M2)], w=[(tg, "ktok_s")])
        act(lambda e: e.copy(out=R["vtok_s"][0:NS, :], in_=psb(M2)[0:NS, 128:256]), r=[("ps", M2)], w=[(tg, "vtok_s")])
        for sm in range(NS):
            dma("sp", "st%d" % sm, Sst[sm], st[sm, h], w=[("Sst", sm)])
            dve(lambda e, sm=sm: e.tensor_scalar(out=vm[sm][0:NS, :], in0=R["vtok_s"][0:NS, :],
                                                 scalar1=cfv("onehot")[0:NS, sm:sm + 1], scalar2=None, op0=ALU.mult),
                r=[(tg, "vtok_s"), "cf"], w=[("vm", sm)])
        yield
        for half in range(2):
            bo = obanks[half]
            for cc in range(4):
                c = half * 4 + cc
                pe(lambda e, bo=bo, cc=cc, c=c: e.matmul(ps[bo][:, cc * 128:(cc + 1) * 128], lhsT=R["Sb"][:, c, :],
                                                        rhs=R["qd"][:, c, :], start=False, stop=True,
                                                        skip_group_check=True),
                   r=[(tg, "Sb", c), (tg, "qd")], w=[("ps", bo)])

        def o_read(bo, c0, n):
            act(lambda e: e.activation(out=R["sq"][:, c0:c0 + n], in_=ps[bo][:, 0:n], func=AF.Square),
                r=[("ps", bo)], w=[(tg, "sq", c0)])
            dve(lambda e: e.tensor_copy(out=R["o_sb"][:, c0:c0 + n], in_=ps[bo][:, 0:n]),
                r=[("ps", bo)], w=[(tg, "o_sb", c0)])
        o_read(M0, 0, 512)
        o_read(M1, 512, 512)
        yield
        for sm in range(NS):
            bu = M0 if sm % 2 == 0 else M1
            pe(lambda e, bu=bu, sm=sm: e.matmul(ps[bu][:, 0:128], lhsT=R["ktok_s"][0:NS, :], rhs=vm[sm][0:NS, :],
                                               start=True, stop=True), r=[(tg, "ktok_s"), ("vm", sm)], w=[("ps", bu)])
            dve(lambda e, bu=bu, sm=sm: e.scalar_tensor_tensor(out=Snew[sm], in0=Sst[sm], scalar=float(GAMMA[h]),
                                                               in1=ps[bu][:, 0:128], op0=ALU.mult, op1=ALU.add),
                r=[("Sst", sm), ("ps", bu)], w=[("Snew", sm)])
            dma("sp", "o_ss%d" % sm, ss_out[sm, h], Snew[sm], r=[("Snew", sm)], w=[("ss_out", sm, h)])
            act(lambda e, sm=sm: e.copy(out=Sbs[sm], in_=Snew[sm]), r=[("Snew", sm)], w=[("Sbs", sm)])
        yield
        for sm in range(NS):
            pe(lambda e, sm=sm: e.matmul(ps[M2][:, sm:sm + 1], lhsT=Sbs[sm], rhs=g1["qr"][:, NP_ + sm:NP_ + sm + 1],
                                        start=True, stop=True), r=[("Sbs", sm)] + q_res, w=[("ps", M2)])
        o_read(M2, NP_, NS)
        yield
        for pi, (c0, n) in enumerate(((0, 512), (512, 512), (NP_, NS))):
            bm_ = M0 if pi % 2 == 0 else M1
            pe(lambda e, bm_=bm_, c0=c0, n=n: e.matmul(ps[bm_][:, 0:n], lhsT=cbv("ones128"), rhs=R["sq"][:, c0:c0 + n],
                                                      start=True, stop=True), r=[(tg, "sq", c0), "cb"], w=[("ps", bm_)])
            act(lambda e, bm_=bm_, c0=c0, n=n: e.activation(out=R["rstd"][:, c0:c0 + n], in_=ps[bm_][:, 0:n], func=AF.Ln,
                                                            bias=epsv, scale=1.0), r=[("ps", bm_)], w=[(tg, "rstd", c0)])
            act(lambda e, c0=c0, n=n: e.activation(out=R["rstd"][:, c0:c0 + n], in_=R["rstd"][:, c0:c0 + n], func=AF.Exp,
                                                   scale=-0.5), r=[(tg, "rstd", c0)], w=[(tg, "rstd", c0)])
            dve(lambda e, c0=c0, n=n: e.scalar_tensor_tensor(
                out=R["tt"][:, c0:c0 + n], in0=R["o_sb"][:, c0:c0 + n], scalar=rgv[:, h:h + 1],
                in1=R["rstd"][:, c0:c0 + n], op0=ALU.mult, op1=ALU.mult),
                r=[(tg, "o_sb", c0), (tg, "rstd", c0), "cf"], w=[(tg, "tt", c0)])
            pool(lambda e, c0=c0, n=n: e.tensor_tensor(out=mixT[:, 8 + h, c0:c0 + n], in0=R["tt"][:, c0:c0 + n],
                                                       in1=g1["sg"][:, c0:c0 + n], op=ALU.mult),
                 r=[(tg, "tt", c0)] + g_res, w=[("mixT", "r", h, c0)])
        yield

    def drive(gens):
        gens = [g for g in gens if g is not None]
        while gens:
            for g in list(gens):
                try:
                    next(g)
                except StopIteration:
                    gens.remove(g)

    def drive2(gb, ga):
        pat = [1, 1, 1, 0, 1, 0, 1, 1, 9, 9]
        bi = 0
        b_alive, a_alive = True, ga is not None
        while b_alive or a_alive:
            if b_alive:
                try:
                    next(gb)
                except StopIteration:
                    b_alive = False
            na = pat[min(bi, len(pat) - 1)] if b_alive else 99
            bi += 1
            for _ in range(na):
                if not a_alive:
                    break
                try:
                    next(ga)
                except StopIteration:
                    a_alive = False

    drive([stageA(0)])
    for h in range(8):
        drive2(stageB(h), stageA(h + 1) if h + 1 < 8 else None)
    P.fence()
    dump("mixT", mixT, ["mixT"])
    chk(6)

    Zt = Alloc(T2base, T2base + 12 * 1024)
    Zx = Alloc(0, 0)
    xs_off = None
    aT_f32 = aT.rearrange("p a c -> p (a c)").bitcast(F32)
    xstage = [aT_f32[:, j * D:(j + 1) * D] for j in range(4)]
    tiles_x = [(x_main[i * 128:(i + 1) * 128, :], 128, i * 128) for i in range(8)] + [(x_smp, NS, NP_)]
    for i in range(3):
        dma("sp", "x%d" % (i % 4), xstage[i % 4][0:tiles_x[i][1], :], tiles_x[i][0], w=[("xstage", i % 4)])
    for i, (src, rows, c0) in enumerate(tiles_x):
        s = i % 4
        if i + 3 < len(tiles_x):
            j3 = i + 3
            dma("sp", "x%d" % (j3 % 4), xstage[j3 % 4][0:tiles_x[j3][1], :], tiles_x[j3][0], w=[("xstage", j3 % 4)])
        for q4 in range(4):
            b = bank_m()
            for kk in range(4):
                k = q4 * 4 + kk
                pe(lambda e, b=b, kk=kk, k=k, s=s, rows=rows: e.transpose(
                    ps[b][:, kk * 128:kk * 128 + rows], xstage[s][0:rows, k * 128:(k + 1) * 128],
                    ident_f[0:rows, 0:rows]), r=[("xstage", s), "cf"], w=[("ps", b)])
            src_ps = lambda b=b, rows=rows: ps[b][:, :].rearrange("p (a c) -> p a c", a=4)[:, :, 0:rows]
            dst = hT[:, q4 * 4:(q4 + 1) * 4, c0:c0 + rows]
            if q4 % 2 == 0:
                act(lambda e, dst=dst, src_ps=src_ps: e.copy(out=dst, in_=src_ps()), r=[("ps", b)], w=[("hT", q4, i)])
            else:
                dve(lambda e, dst=dst, src_ps=src_ps: e.tensor_copy(out=dst, in_=src_ps()), r=[("ps", b)], w=[("hT", q4, i)])
    P.fence()

    sqs = [Zt([344], BF16) for _ in range(4)]
    rstdn = Zt([NT], F32)
    fT = aT[:, :, 0:NT]

    class NormState:
        pass

    def norm_begin(tag):
        st_ = NormState()
        st_.tag = tag
        st_.banks = [bank_m() for _ in TT]
        st_.pending = None
        st_.cnt = 0
        return st_

    def norm_flush(st_):
        if st_.pending is not None:
            slot, m, ti, n = st_.pending
            pe(lambda e: e.matmul(ps[st_.banks[ti]][:, 0:n], lhsT=cbv("onesD"), rhs=sqs[slot][:, 0:n],
                                  start=(m == 0), stop=(m == 15)),
               r=[("sqs", slot), "cb"], w=[("ps", st_.banks[ti])])
            st_.pending = None

    def norm_tile(st_, m, ti, c0, n):
        norm_flush(st_)
        slot = st_.cnt % 4
        st_.cnt += 1
        act(lambda e: e.activation(out=sqs[slot][:, 0:n], in_=hT[:, m, c0:c0 + n], func=AF.Square),
            r=[("hTm", m, c0)], w=[("sqs", slot)])
        st_.pending = (slot, m, ti, n)

    def norm_finish(st_, gname):
        tag = st_.tag
        norm_flush(st_)
        for ti, (c0, n) in enumerate(TT):
            act(lambda e, ti=ti, c0=c0, n=n: e.activation(out=rstdn[:, c0:c0 + n], in_=ps[st_.banks[ti]][:, 0:n], func=AF.Ln,
                                                          bias=epsv, scale=1.0),
                r=[("ps", st_.banks[ti])], w=[(tag, "rstdn0", ti)])
            act(lambda e, ti=ti, c0=c0, n=n: e.activation(out=rstdn[:, c0:c0 + n], in_=rstdn[:, c0:c0 + n], func=AF.Exp,
                                                          scale=-0.5),
                r=[(tag, "rstdn0", ti)], w=[(tag, "rstdn")])
        gv = cfv(gname)
        for k in range(16):
            dve(lambda e, k=k: e.scalar_tensor_tensor(out=fT[:, k, :], in0=hT[:, k, :], scalar=gv[:, k:k + 1], in1=rstdn,
                                                      op0=ALU.mult, op1=ALU.mult),
                r=[(tag, "rstdn"), "cf"] + [("hTm", k, c0) for (c0, n) in TT], w=[("fT", k)])

    mix_rhs = lambda k, c0, n: mixT[:, k, c0:c0 + n]
    n2 = norm_begin("n2")
    for j in range(8):
        def evac_out(ml, ti, c0, n, b, j=j):
            m = 2 * j + ml
            dve(lambda e: e.tensor_tensor(out=hT[:, m, c0:c0 + n], in0=ps[b][:, 0:n], in1=hT[:, m, c0:c0 + n], op=ALU.add),
                r=[("ps", b), ("hTm", m, c0)], w=[("hTm", m, c0)])
            norm_tile(n2, m, ti, c0, n)
        dense_block(16, mix_rhs, ["mixT"], evac_out)
    P.fence()
    dump("h1", hT, [])
    chk(7)

    norm_finish(n2, "fn_g")
    P.fence()
    dump("fT", fT, [])
    chk(8)

    uT = mixT
    sgt = [Zt([344], F32) for _ in range(2)]
    f_rhs = lambda k, c0, n: fT[:, k, c0:c0 + n]
    cnt_s = {"i": 0}
    for qi, (b0, nb) in enumerate(QUART):
        for bb in range(nb):
            wg, wgres = w_next()
            wu, wures = w_next()
            for ml in range(2):
                cl = 2 * bb + ml
                for ti, (c0, n) in enumerate(TT):
                    bg = bank_d(); bu = bank_d()
                    for (bk_, wv, wres) in ((bg, wg, wgres), (bu, wu, wures)):
                        for k in range(16):
                            pe(lambda e, bk_=bk_, wv=wv, k=k, ml=ml, c0=c0, n=n: e.matmul(
                                ps[bk_][:, 0:n], lhsT=wv[:, k, ml * 128:(ml + 1) * 128], rhs=fT[:, k, c0:c0 + n],
                                start=(k == 0), stop=(k == 15)), r=[wres], w=[("ps", bk_)])
                    s = cnt_s["i"] % 2
                    cnt_s["i"] += 1
                    act(lambda e, bg=bg, s=s, n=n: e.activation(out=sgt[s][:, 0:n], in_=ps[bg][:, 0:n], func=AF.Silu),
                        r=[("ps", bg)], w=[("sgt", s)])
                    dve(lambda e, bu=bu, s=s, n=n, cl=cl, c0=c0: e.tensor_tensor(out=uT[:, cl, c0:c0 + n], in0=ps[bu][:, 0:n],
                                                                               in1=sgt[s][:, 0:n], op=ALU.mult),
                        r=[("ps", bu), ("sgt", s)], w=[("uT", qi)])
            w_done(2)
        kq = 2 * nb
        u_rhs = lambda k, c0, n: uT[:, k, c0:c0 + n]
        if qi == 3:
            n3 = norm_begin("n3")
        for j in range(8):
            def evac_dn(ml, ti, c0, n, b, j=j):
                m = 2 * j + ml
                dve(lambda e: e.tensor_tensor(out=hT[:, m, c0:c0 + n], in0=ps[b][:, 0:n], in1=hT[:, m, c0:c0 + n],
                                              op=ALU.add), r=[("ps", b), ("hTm", m, c0)], w=[("hTm", m, c0)])
                if qi == 3:
                    norm_tile(n3, m, ti, c0, n)
            dense_block(kq, u_rhs, [("uT", qi)], evac_dn)
    P.fence()
    dump("h2", hT, [])
    chk(9)

    norm_finish(n3, "pn_g")
    pst = [uT[:, 0:1, :].rearrange("p a c -> p (a c)").bitcast(F32)[:, 0:256],
           uT[:, 1:2, :].rearrange("p a c -> p (a c)").bitcast(F32)[:, 0:256]]
    peT = uT[:, 4:6, :]
    tiles_p = [(p_main[i * 128:(i + 1) * 128, :], 128, i * 128) for i in range(8)] + [(p_smp, NS, NP_)]
    for i, (src, rows, c0) in enumerate(tiles_p):
        s = i % 2
        dma("sp", "x%d" % s, pst[s][0:rows, :], src, w=[("pst", s)])
        b = bank_m()
        for kk in range(2):
            pe(lambda e, b=b, kk=kk, s=s, rows=rows: e.transpose(ps[b][:, kk * 128:kk * 128 + rows],
                                                                pst[s][0:rows, kk * 128:(kk + 1) * 128],
                                                                ident_f[0:rows, 0:rows]), r=[("pst", s), "cf"], w=[("ps", b)])
        act(lambda e, b=b, rows=rows, c0=c0: e.copy(out=peT[:, :, c0:c0 + rows],
                                                   in_=ps[b][:, 0:256].rearrange("p (a c) -> p a c", a=2)[:, :, 0:rows]),
            r=[("ps", b)], w=["peT"])
    P.fence()
    wple = uT[:, 8:12, :].rearrange("p a c -> p (a c)")[:, 0:4096].rearrange("p (k n) -> p k n", k=2)
    wpleres = "wple"
    dma("pool", "wple", wple, w_ple.rearrange("(k p) n -> p k n", p=128), w=["wple"])
    sB = [uT[:, 12 + i, :].bitcast(F32)[:, 0:344] for i in range(2)]
    tA = [uT[:, 14 + i, :].bitcast(F32)[:, 0:344] for i in range(2)]
    for j in range(8):
        wv, wres = w_next()
        for ml in range(2):
            m = 2 * j + ml
            for ti, (c0, n) in enumerate(TT):
                bA = 2 * (cnt_s["i"] % 4); bB = bA + 1
                for k in range(2):
                    pe(lambda e, bA=bA, k=k, m=m, c0=c0, n=n: e.matmul(ps[bA][:, 0:n], lhsT=wple[:, k, m * 128:(m + 1) * 128],
                                                                      rhs=peT[:, k, c0:c0 + n], start=(k == 0), stop=(k == 1)),
                       r=[wpleres, "peT"], w=[("ps", bA)])
                for k in range(16):
                    pe(lambda e, bB=bB, k=k, ml=ml, c0=c0, n=n, wv=wv: e.matmul(
                        ps[bB][:, 0:n], lhsT=wv[:, k, ml * 128:(ml + 1) * 128], rhs=fT[:, k, c0:c0 + n],
                        start=(k == 0), stop=(k == 15)), r=[wres], w=[("ps", bB)])
                s = cnt_s["i"] % 2
                cnt_s["i"] += 1
                act(lambda e, bB=bB, s=s, n=n: e.activation(out=sB[s][:, 0:n], in_=ps[bB][:, 0:n], func=AF.Sigmoid),
                    r=[("ps", bB)], w=[("sB", s)])
                dve(lambda e, bA=bA, s=s, n=n: e.tensor_tensor(out=tA[s][:, 0:n], in0=ps[bA][:, 0:n], in1=sB[s][:, 0:n],
                                                              op=ALU.mult), r=[("ps", bA), ("sB", s)], w=[("tA", s)])
                dve(lambda e, s=s, n=n, m=m, c0=c0: e.tensor_tensor(out=hT[:, m, c0:c0 + n], in0=tA[s][:, 0:n],
                                                                   in1=hT[:, m, c0:c0 + n], op=ALU.add),
                    r=[("tA", s), ("hTm", m, c0)], w=[("hTm", m, c0)])
        w_done()
    P.fence()
    dump("h3", hT, [])
    chk(10)

    ystage = [aT_f32[:, j * D:(j + 1) * D] for j in range(4)]
    tiles_y = [(y_main[i * 128:(i + 1) * 128, :], 128, i * 128) for i in range(8)] + [(y_smp, NS, NP_)]
    for i, (dst, rows, c0) in enumerate(tiles_y):
        s = i % 4
        for q4 in range(4):
            b = bank_m()
            for kk in range(4):
                k = q4 * 4 + kk
                pe(lambda e, b=b, kk=kk, k=k, rows=rows, c0=c0: e.transpose(ps[b][0:rows, kk * 128:(kk + 1) * 128],
                                                                           hT[:, k, c0:c0 + rows], ident_f),
                   r=["cf"], w=[("ps", b)])
            if q4 % 2 == 0:
                act(lambda e, b=b, s=s, rows=rows, q4=q4: e.copy(out=ystage[s][0:rows, q4 * 512:(q4 + 1) * 512],
                                                                in_=ps[b][0:rows, :]), r=[("ps", b)], w=[("ystage", s, q4)])
            else:
                dve(lambda e, b=b, s=s, rows=rows, q4=q4: e.tensor_copy(out=ystage[s][0:rows, q4 * 512:(q4 + 1) * 512],
                                                                       in_=ps[b][0:rows, :]), r=[("ps", b)],
                    w=[("ystage", s, q4)])
        dma("sp", "y%d" % s, dst, ystage[s][0:rows, :], r=[("ystage", s, q4) for q4 in range(4)], w=[("y", i)])

    assert stop != 99 or wstate["used"] == len(wq), (wstate, len(wq))

    sems = {e: es.enter_context(nc.semaphore("s_" + e)) for e in Prog.ENGS}
    dma_sems = {k: es.enter_context(nc.semaphore("d_" + k)) for k in sorted(dma_sem_names)}
    block = es.enter_context(nc.Block())
    finals = sorted(dma_sem_names)
    P.emit(nc, block, sems, dma_sems, finals)
    es.close()
    print("ops", len(P.ops), "counts", P.final_counts[0])
    return nc


_CACHE = {}


def kernel(**inputs):
    inp = {k: np.asarray(v) for k, v in inputs.items()}
    if "nc" not in _CACHE:
        _CACHE["nc"] = build_program()
    nc = _CACHE["nc"]
    xp = inp["x_prompt"]; xs = inp["x_sample"]
    in_maps = []
    zeros_halo = np.zeros((NHALO, D), np.float32)
    shared = dict(
        an_g=np.ascontiguousarray(inp["attn_norm_g"].reshape(1, D)),
        w_in=np.ascontiguousarray(inp["w_in"][0]), w_out=np.ascontiguousarray(inp["w_out"][0]),
        w_gate=np.ascontiguousarray(inp["w_gate"][0]), w_up=np.ascontiguousarray(inp["w_up"][0]),
        w_down=np.ascontiguousarray(inp["w_down"][0]), w_ple=np.ascontiguousarray(inp["w_ple"][0]),
        w_pg=np.ascontiguousarray(inp["w_ple_gate"][0]),
    )
    for c in range(8):
        b, m = c // 4, c % 4
        t0 = m * 1024
        cf, cb = host_consts(c, inp)
        d = dict(shared)
        d.update(
            x_main=np.ascontiguousarray(xp[b, t0:t0 + 1024]),
            x_halo=np.ascontiguousarray(xp[b, t0 - 128:t0]) if m > 0 else zeros_halo,
            x_smp=np.ascontiguousarray(xs[4 * c:4 * c + 4, 0]),
            p_main=np.ascontiguousarray(inp["p_prompt"][0, b, t0:t0 + 1024]),
            p_smp=np.ascontiguousarray(inp["p_sample"][0, 4 * c:4 * c + 4, 0]),
            ck=np.ascontiguousarray(inp["cache_k_win"][0, 4 * c:4 * c + 4]),
            cv=np.ascontiguousarray(inp["cache_v_win"][0, 4 * c:4 * c + 4]),
            st=np.ascontiguousarray(inp["state_ret"][0, 4 * c:4 * c + 4]),
            cf=cf, cb=cb,
        )
        in_maps.append(d)
    res = run_bass_kernel_spmd(nc, in_maps, core_ids=list(range(8)))
    R = res.results
    _CACHE["last"] = R
    y_p = np.stack([np.concatenate([R[4 * b + m]["y_main"] for m in range(4)], 0) for b in range(2)], 0)
    y_s = np.concatenate([R[c]["y_smp"] for c in range(8)], 0)[:, None, :]
    kwp = np.stack([R[4 * b + 3]["kwin"] for b in range(2)], 0)[None]
    vwp = np.stack([R[4 * b + 3]["vwin"] for b in range(2)], 0)[None]
    rsp = np.stack([R[4 * b + 3]["rstate"] for b in range(2)], 0)[None]
    kws = np.concatenate([R[c]["ks_out"] for c in range(8)], 0)[None]
    vws = np.concatenate([R[c]["vs_out"] for c in range(8)], 0)[None]
    rss = np.concatenate([R[c]["ss_out"] for c in range(8)], 0)[None]
    return (y_p.astype(np.float32), y_s.astype(np.float32), kwp.astype(np.float32), vwp.astype(np.float32),
            rsp.astype(np.float32), kws.astype(np.float32), vws.astype(np.float32), rss.astype(np.float32))
```

```python
import math
from contextlib import ExitStack

import numpy as np
import ml_dtypes

import concourse.bass as bass
import concourse.mybir as mybir
from concourse.bass_utils import run_bass_kernel_spmd

F32 = mybir.dt.float32
BF16 = mybir.dt.bfloat16
U8 = mybir.dt.uint8
AF = mybir.ActivationFunctionType
ALU = mybir.AluOpType
AX = mybir.AxisListType

D = 2048
NP_ = 1024
NS = 4
NT = NP_ + NS
NHALO = 128
NA = NT + NHALO
KC = 16
DFF = 5632
EPS = 1e-6
ATTN_SCALE = 128 ** -0.5
RET_K_SCALE = 128 ** -0.5
PAST_LEN = 16384
TT = [(0, 344), (344, 344), (688, 340)]
TT_H = TT + [(NT, NHALO)]
NSLOT = 4
SLOT_BYTES = 8192
GAMMA = [1.0 - 2.0 ** (-5 - h) for h in range(8)]


class Op:
    __slots__ = ("eng", "fn", "dma", "idx", "waits", "sig", "val", "fence")

    def __init__(self, eng, fn, dma, idx):
        self.eng = eng
        self.fn = fn
        self.dma = dma
        self.idx = idx
        self.waits = []
        self.sig = dma is not None
        self.val = None
        self.fence = None


class Prog:
    ENGS = ("sp", "act", "dve", "pool", "pe")

    def __init__(self):
        self.ops = []
        self.last_w = {}
        self.readers = {}
        self.last_on = {}
        self.dma_ops = {}
        self.pending_fence = {}

    def add(self, eng, fn, r=(), w=(), dma=None, nofence=False):
        op = Op(eng, fn, dma, len(self.ops))
        psr = [x for x in r if isinstance(x, tuple) and x[0] == "ps"]
        if psr:
            r = [x for x in r if not (isinstance(x, tuple) and x[0] == "ps")]
            w = list(w) + psr
        deps = {}
        for res in r:
            lw = self.last_w.get(res)
            if lw is not None:
                deps.setdefault(lw, set()).add("raw")
        for res in w:
            lw = self.last_w.get(res)
            if lw is not None:
                deps.setdefault(lw, set()).add("waw")
            for rd in self.readers.get(res, ()):
                deps.setdefault(rd, set()).add("war")
        best = {}
        for d, kinds in deps.items():
            if d is op:
                continue
            if d.dma is not None:
                op.waits.append(d)
                continue
            if op.dma is None and d.eng == eng:
                if eng == "pe":
                    continue
            b = best.get(d.eng)
            if b is None or d.idx > b.idx:
                best[d.eng] = d
        for d in best.values():
            d.sig = True
            op.waits.append(d)
        for res in r:
            self.readers.setdefault(res, []).append(op)
        for res in w:
            self.last_w[res] = op
            self.readers[res] = []
        if eng in self.pending_fence and not nofence:
            op.fence = self.pending_fence.pop(eng)
        self.ops.append(op)
        if dma is None:
            self.last_on[eng] = op
        else:
            self.dma_ops.setdefault(dma, []).append(op)
        return op

    def fence(self):
        st = {"comp": dict(self.last_on), "dma": {k: v[-1] for k, v in self.dma_ops.items()}}
        for o in st["comp"].values():
            o.sig = True
        for e in self.ENGS:
            self.pending_fence[e] = st

    def emit(self, nc, block, sems, dma_sems, final_waits):
        cnt = {e: 0 for e in self.ENGS}
        dcnt = {}
        for op in self.ops:
            if op.dma is not None:
                dcnt[op.dma] = dcnt.get(op.dma, 0) + (1 if op.dma == "cc" else 16)
                op.val = dcnt[op.dma]
            elif op.sig:
                cnt[op.eng] += 1
                op.val = cnt[op.eng]
        self.final_counts = (cnt, dcnt)

        def semof(op):
            return dma_sems[op.dma] if op.dma is not None else sems[op.eng]

        def run(engname):
            def body(eng):
                waited = {}

                def wait(sem_key, sem, val):
                    if waited.get(sem_key, 0) >= val:
                        return
                    waited[sem_key] = val
                    eng.wait_ge(sem, val)

                for op in self.ops:
                    if op.eng != engname:
                        continue
                    if op.fence is not None:
                        for o in op.fence["comp"].values():
                            if o.eng != engname or engname != "pe":
                                wait(("c", o.eng), sems[o.eng], o.val)
                        for k, o in op.fence["dma"].items():
                            wait(("d", k), dma_sems[k], o.val)
                    for d in op.waits:
                        key = ("d", d.dma) if d.dma is not None else ("c", d.eng)
                        wait(key, semof(d), d.val)
                    ins = op.fn(eng)
                    if op.dma is not None:
                        ins.then_inc(dma_sems[op.dma], 1 if op.dma == "cc" else 16)
                    elif op.sig:
                        ins.then_inc(sems[op.eng], 1)
                if engname == "sp":
                    for k in final_waits:
                        if k in dcnt:
                            eng.wait_ge(dma_sems[k], dcnt[k])
            return body

        block.sync(run("sp"))
        block.scalar(run("act"))
        block.vector(run("dve"))
        block.gpsimd(run("pool"))
        block.tensor(run("pe"))


CF = {}
_off = 0
for _n, _w in [("ident", 128), ("ones_row", 128), ("fn_g", 16), ("pn_g", 16), ("rg", 8), ("qg", 1), ("kg", 1),
               ("sinks", 8), ("intraT", 1024), ("qdec", 1024), ("kdec", 8), ("kdlong", 64), ("coef", 32),
               ("onehot", 4), ("cos", NT), ("sin", NT)]:
    CF[_n] = (_off, _w)
    _off += _w
NCF = _off
CB = {}
_off = 0
for _n, _w in [("ident", 128), ("ones", 128), ("onesD", 128), ("ones128", 128), ("rrot", 128), ("mask", 384)]:
    CB[_n] = (_off, _w)
    _off += _w
NCB = _off


def host_consts(core, inp):
    m = core % 4
    cf = np.zeros((128, NCF), np.float32)

    def put(name, arr):
        o, w = CF[name]
        cf[:, o:o + w] = np.asarray(arr, np.float32).reshape(128, w) if np.ndim(arr) == 2 else np.broadcast_to(
            np.asarray(arr, np.float32).reshape(1, w), (128, w))

    put("ident", np.eye(128, dtype=np.float32))
    put("ones_row", np.ones((128, 128), np.float32))
    put("fn_g", inp["ffn_norm_g"][0].reshape(16, 128).T)
    put("pn_g", inp["ple_norm_g"][0].reshape(16, 128).T)
    put("rg", inp["ret_out_g"][0].reshape(8, 128).T)
    put("qg", inp["q_norm_g"][0].reshape(128, 1))
    put("kg", inp["k_norm_g"][0].reshape(128, 1))
    put("sinks", inp["attn_sinks"][0].reshape(8))
    g = np.array(GAMMA, np.float64)
    j = np.arange(128)
    diff = j[None, :] - j[:, None]
    intraT = np.where(diff[:, None, :] >= 0, g[None, :, None] ** np.maximum(diff, 0)[:, None, :], 0.0) * RET_K_SCALE
    put("intraT", intraT.reshape(128, 1024))
    qdec = g[:, None] ** (j[None, :] + 1.0)
    put("qdec", qdec.reshape(1024))
    put("kdec", (g[None, :] ** (127.0 - j[:, None])) * RET_K_SCALE)
    c = np.arange(8)
    kdl = g[None, :, None] ** (1023.0 - (128.0 * c[None, None, :] + j[:, None, None])) * RET_K_SCALE
    put("kdlong", kdl.reshape(128, 64))
    coef = np.zeros((4, 8))
    for r in range(4):
        if r < m:
            coef[r] = g ** (1024.0 * (m - r - 1))
    put("coef", coef.reshape(32))
    oh = np.zeros((128, 4), np.float32)
    oh[:4] = np.eye(4)
    put("onehot", oh)
    pos = np.concatenate([m * 1024 + np.arange(1024), np.full(4, PAST_LEN)]).astype(np.float32)
    inv = (np.float32(10000.0) ** (-np.arange(64, dtype=np.float32) / np.float32(64))).astype(np.float32)
    ang = (pos[None, :] * inv[:, None]).astype(np.float32).astype(np.float64)
    cos = np.cos(ang)
    sin = np.sin(ang)
    put("cos", np.concatenate([cos, cos], 0))
    put("sin", np.concatenate([-sin, sin], 0))

    cb = np.zeros((128, NCB), np.float32)

    def putb(name, arr):
        o, w = CB[name]
        cb[:, o:o + w] = arr

    putb("ident", np.eye(128))
    putb("ones", np.ones((128, 128)))
    putb("onesD", np.full((128, 128), 1.0 / D))
    putb("ones128", np.full((128, 128), 1.0 / 128))
    rr = np.zeros((128, 128))
    for p in range(128):
        rr[(p + 64) % 128, p] = 1.0
    putb("rrot", rr)
    NEG = -30000.0
    own = np.where(j[:, None] <= j[None, :], 0.0, NEG).astype(np.float32)
    prev = np.where(j[:, None] >= j[None, :], 0.0, NEG).astype(np.float32)
    putb("mask", np.concatenate([own, prev, prev if m != 0 else np.full((128, 128), NEG, np.float32)], 1))
    return cf, cb.astype(ml_dtypes.bfloat16)


class _Stop(Exception):
    pass


def build_program(dbg=None, stop=99, groups=None):
    nc = bass.Bass("TRN2", target_bir_lowering=False)
    P = Prog()
    dbg = dbg or {}

    stopped = [False]

    def chk(k):
        if stop == k:
            stopped[0] = True

    def din(name, shape, dt=F32):
        return nc.dram_tensor(name, list(shape), dt, kind="ExternalInput").ap()

    def dout(name, shape, dt=F32):
        return nc.dram_tensor(name, list(shape), dt, kind="ExternalOutput").ap()

    x_main = din("x_main", [NP_, D]); x_halo = din("x_halo", [NHALO, D]); x_smp = din("x_smp", [NS, D])
    p_main = din("p_main", [NP_, 256]); p_smp = din("p_smp", [NS, 256])
    ck = din("ck", [NS, 128, 2, 128]); cv = din("cv", [NS, 128, 2, 128]); st = din("st", [NS, 8, 128, 128])
    an_g = din("an_g", [1, D])
    w_in = din("w_in", [D, DFF]); w_out = din("w_out", [D, D]); w_gate = din("w_gate", [D, DFF])
    w_up = din("w_up", [D, DFF]); w_down = din("w_down", [DFF, D]); w_ple = din("w_ple", [256, D])
    w_pg = din("w_pg", [D, D])
    cf_d = din("cf", [128, NCF]); cb_d = din("cb", [128, NCB], BF16)

    y_main = dout("y_main", [NP_, D]); y_smp = dout("y_smp", [NS, D])
    kwin = dout("kwin", [128, 2, 128]); vwin = dout("vwin", [128, 2, 128]); rstate = dout("rstate", [8, 128, 128])
    ks_out = dout("ks_out", [NS, 128, 2, 128]); vs_out = dout("vs_out", [NS, 128, 2, 128])
    ss_out = dout("ss_out", [NS, 8, 128, 128])
    krs = nc.dram_tensor("krs", [8, 128, NT], BF16)
    vts = nc.dram_tensor("vts", [8, 128, NT], BF16)
    ag_in = nc.dram_tensor("ag_in", [8 * 128, 128], F32)
    ag_out = nc.dram_tensor("ag_out", [4 * 8 * 128, 128], F32)
    dbg_out = {}
    for name, shape in dbg.items():
        dbg_out[name] = dout("dbg_" + name, shape)

    def dump(name, ap, r=()):
        if name in dbg_out:
            P.add("pool", lambda e: e.dma_start(out=dbg_out[name], in_=ap), r, [("dbg", name)], dma="o_dbg")

    es = ExitStack()
    total = (nc.sbuf_bytes_remaining - 64) // 64 * 64
    arena = es.enter_context(nc.sbuf_tensor("arena", [128, total], U8))
    ps = [es.enter_context(nc.psum_tensor("ps%d" % i, [128, 512], F32)) for i in range(8)]

    class Alloc:
        def __init__(self, base, limit):
            self.p = base
            self.limit = limit

        def __call__(self, shape, dt):
            esz = 4 if dt == F32 else 2
            n = int(np.prod(shape)) * esz
            off = (self.p + 31) // 32 * 32
            self.p = off + n
            assert self.p <= self.limit, (self.p, self.limit)
            v = arena[:, off:off + n].bitcast(dt)
            if len(shape) == 2:
                return v.rearrange("p (a b) -> p a b", a=shape[0])
            if len(shape) == 3:
                return v.rearrange("p (a b c) -> p a b c", a=shape[0], b=shape[1])
            return v

    A = Alloc(0, total)
    cf = A([NCF], F32)
    cb = A([NCB], BF16)
    negc = A([1], F32); esink = A([8], F32); S0 = A([8, 128], F32)
    misc = A([64], F32)
    wslots = [A([SLOT_BYTES // 2], BF16) for _ in range(NSLOT)]
    aT = A([KC, NA], BF16)
    mixT = A([KC, NT], BF16)
    T2base = A.p
    T2 = Alloc(T2base, T2base + 12 * 1024)
    A.p = T2base + 12 * 1024
    HB = (A.p + 31) // 32 * 32
    hT = A([KC, NT], F32)
    HEND = A.p
    print("sbuf used", A.p, "of", total)

    def cfv(name, lo=0, hi=None):
        o, w = CF[name]
        return cf[:, o + lo:o + (w if hi is None else hi)]

    def cbv(name, lo=0, hi=None):
        o, w = CB[name]
        return cb[:, o + lo:o + (w if hi is None else hi)]

    ident_f = cfv("ident"); ident_b = cbv("ident")

    def psb(i, n=1024):
        return ps[i][:, :].bitcast(BF16)[:, 0:n]

    def pe(fn, r=(), w=()): return P.add("pe", fn, r, w)
    def act(fn, r=(), w=()): return P.add("act", fn, r, w)
    def dve(fn, r=(), w=()): return P.add("dve", fn, r, w)
    def pool(fn, r=(), w=()): return P.add("pool", fn, r, w)
    def dma(q, sem, out, in_, r=(), w=(), nofence=False, slow=False):
        if slow:
            return P.add(q, lambda e: e.dma_start(out=out, in_=in_, allow_slow_non_contiguous=True), r, w, dma=sem)
        return P.add(q, lambda e: e.dma_start(out=out, in_=in_), r, w, dma=sem, nofence=nofence)

    dma_sem_names = set()
    _orig_add = P.add

    def add_track(eng, fn, r=(), w=(), dma=None, nofence=False):
        if stopped[0]:
            return None
        if dma is not None:
            dma_sem_names.add(dma)
        return _orig_add(eng, fn, r, w, dma, nofence)
    P.add = add_track

    rot = {"d": 0, "m": 0}

    att_dense3 = [False]

    def bank_d():
        b = rot["d"] % (3 if att_dense3[0] else 4)
        rot["d"] += 1
        return b

    def bank_m():
        b = 4 + rot["m"] % 4
        rot["m"] += 1
        return b

    wq = []
    wstate = {"issued": 0, "used": 0, "done": 0}
    w_extra = []

    def w_issue_upto(n):
        while wstate["issued"] < min(n, len(wq)):
            i = wstate["issued"]
            src, kc, ncols = wq[i]
            s = i % NSLOT
            dst = wslots[s][:, 0:kc * ncols].rearrange("p (k n) -> p k n", k=kc)
            dma("pool", "w%d" % s, dst, src.rearrange("(k p) n -> p k n", p=128), r=list(w_extra), w=[("w", s)], nofence=True)
            wstate["issued"] += 1

    def w_done(n=1):
        wstate["done"] += n
        w_issue_upto(wstate["done"] + NSLOT)

    def w_next():
        i = wstate["used"]
        assert i < wstate["done"] + NSLOT
        w_issue_upto(i + 1)
        src, kc, ncols = wq[i]
        s = i % NSLOT
        wstate["used"] += 1
        return wslots[s][:, 0:kc * ncols].rearrange("p (k n) -> p k n", k=kc), ("w", s)

    def wblk(wap, k0, k1, c0, ncols):
        return (wap[k0 * 128:k1 * 128, c0:c0 + ncols], k1 - k0, ncols)

    C_AQ, C_AK, C_AV, C_RQ, C_RK, C_RV, C_RG = 0, 1024, 1280, 1536, 2560, 3584, 4608
    for j in range(4):
        wq.append(wblk(w_in, 0, 16, C_RK + 256 * j, 256))
        wq.append(wblk(w_in, 0, 16, C_RV + 256 * j, 256))
    wq.append(wblk(w_in, 0, 16, C_AK, 256))
    wq.append(wblk(w_in, 0, 16, C_AV, 256))
    for j in range(4):
        wq.append(wblk(w_in, 0, 16, C_AQ + 256 * j, 256))
    for j in range(4):
        for cbase in (C_RQ, C_RG):
            wq.append(wblk(w_in, 0, 16, cbase + 256 * j, 256))
    for j in range(8):
        wq.append(wblk(w_out, 0, 16, 256 * j, 256))
    QUART = [(0, 6), (6, 6), (12, 5), (17, 5)]
    for (b0, nb) in QUART:
        for b in range(b0, b0 + nb):
            wq.append(wblk(w_gate, 0, 16, 256 * b, 256))
            wq.append(wblk(w_up, 0, 16, 256 * b, 256))
        for j in range(8):
            wq.append(wblk(w_down, 2 * b0, 2 * (b0 + nb), 256 * j, 256))
    for j in range(8):
        wq.append(wblk(w_pg, 0, 16, 256 * j, 256))

    def dense_block(kc, rhs_fn, rhs_res, evac, tiles=TT, nm=2):
        wv, wres = w_next()
        for ml in range(nm):
            for ti, (c0, n) in enumerate(tiles):
                b = bank_d()
                for k in range(kc):
                    pe(lambda e, b=b, k=k, ml=ml, c0=c0, n=n, wv=wv: e.matmul(
                        ps[b][:, 0:n], lhsT=wv[:, k, ml * 128:(ml + 1) * 128], rhs=rhs_fn(k, c0, n),
                        start=(k == 0), stop=(k == kc - 1)),
                       r=[wres] + list(rhs_res), w=[("ps", b)])
                evac(ml, ti, c0, n, b)
        w_done()

    dma("sp", "c_cb", cb, cb_d, w=["cb"])

    Z = Alloc(HB, total)
    NXT = 4
    xt = [Z([D], F32) for _ in range(NXT)]
    junk = Z([D], BF16)
    xn = [Z([D], BF16) for _ in range(2)]
    gbc = Z([D], F32)
    ss = misc[:, 0:10]; rstd1 = misc[:, 10:20]; tmpa = misc[:, 20:30]
    gpa = misc[:, 30:31]; mx = misc[:, 31:32]; negc1 = misc[:, 32:33]; epsv = misc[:, 34:35]
    dve(lambda e: e.memset(epsv, EPS), w=["epsv"])


    tiles1 = [(x_main[i * 128:(i + 1) * 128, :], 128, i * 128) for i in range(8)]
    tiles1.append((x_halo, 128, NT))
    tiles1.append((x_smp, NS, NP_))
    def p1_load(i, src, rows, c0):
        s4 = i % NXT
        dma("sp", "x%d" % s4, xt[s4][0:rows, :], src, w=[("xt", s4)])

    def p1_stage1(i, src, rows, c0):
        s = i % 2
        s4 = i % NXT
        act(lambda e, s4=s4, rows=rows, i=i: e.activation(out=junk[0:rows, :], in_=xt[s4][0:rows, :], func=AF.Square,
                                                         accum_out=ss[0:rows, i:i + 1]),
            r=[("xt", s4)], w=["junk", ("ss", i)])
        act(lambda e, rows=rows, i=i: e.activation(out=tmpa[0:rows, i:i + 1], in_=ss[0:rows, i:i + 1], func=AF.Sqrt,
                                                   bias=epsv[0:rows, :], scale=1.0 / D),
            r=[("ss", i), "epsv"], w=[("tmpa", i)])
        dve(lambda e, rows=rows, i=i: e.reciprocal(out=rstd1[0:rows, i:i + 1], in_=tmpa[0:rows, i:i + 1]),
            r=[("tmpa", i)], w=[("rstd1", i)])
        dve(lambda e, s=s, s4=s4, rows=rows, i=i: e.scalar_tensor_tensor(
            out=xn[s][0:rows, :], in0=xt[s4][0:rows, :], scalar=rstd1[0:rows, i:i + 1], in1=gbc[0:rows, :],
            op0=ALU.mult, op1=ALU.mult), r=[("xt", s4), ("rstd1", i), "gbc"], w=[("xn", s)])

    def p1_stage2(i, src, rows, c0):
        s = i % 2
        for half in range(2):
            b = bank_m()
            for kk in range(8):
                k = half * 8 + kk
                pe(lambda e, b=b, kk=kk, k=k, s=s, rows=rows: e.transpose(
                    psb(b)[:, kk * 128:kk * 128 + rows], xn[s][0:rows, k * 128:(k + 1) * 128],
                    ident_b[0:rows, 0:rows]), r=[("xn", s), "cb"], w=[("ps", b)])
            src_ps = lambda b=b, rows=rows: psb(b).rearrange("p (a c) -> p a c", a=8)[:, :, 0:rows]
            dst = aT[:, half * 8:(half + 1) * 8, c0:c0 + rows]
            if half == 0:
                act(lambda e, dst=dst, src_ps=src_ps: e.copy(out=dst, in_=src_ps()), r=[("ps", b)], w=[("aT", c0, half)])
            else:
                dve(lambda e, dst=dst, src_ps=src_ps: e.tensor_copy(out=dst, in_=src_ps()), r=[("ps", b)], w=[("aT", c0, half)])

    p1_load(0, *tiles1[0])
    dma("sp", "c_gbc", gbc, an_g.partition_broadcast(128), w=["gbc"])
    for i in range(1, NXT):
        p1_load(i, *tiles1[i])
    dma("sp", "c_cf", cf, cf_d, w=["cf"])
    p1_stage1(0, *tiles1[0])
    for i in range(len(tiles1)):
        if i + 1 < len(tiles1):
            p1_stage1(i + 1, *tiles1[i + 1])
        if i + NXT < len(tiles1):
            p1_load(i + NXT, *tiles1[i + NXT])
        if i in (3, 5, 7, 8):
            w_extra[:] = [("xt", (i + 1) % NXT)]
            w_issue_upto(wstate["issued"] + 1)
            w_extra[:] = []
        p1_stage2(i, *tiles1[i])
    w_issue_upto(NSLOT)
    dve(lambda e: e.tensor_tensor(out=gpa, in0=cfv("qg"), in1=cfv("kg"), op=ALU.mult), r=["cf"], w=["gpa"])
    gpa2 = misc[:, 33:34]
    dve(lambda e: e.tensor_tensor(out=gpa2, in0=gpa, in1=gpa, op=ALU.mult), r=["gpa"], w=["gpa2"])
    b = bank_m()
    pe(lambda e, b=b: e.transpose(ps[b][0:1, 0:128], gpa2, ident_f), r=["gpa2", "cf"], w=[("ps", b)])
    dve(lambda e, b=b: e.tensor_reduce(out=mx[0:1, :], in_=ps[b][0:1, 0:128], axis=AX.X, op=ALU.max),
        r=[("ps", b)], w=["mx"])
    act(lambda e: e.activation(out=mx[0:1, :], in_=mx[0:1, :], func=AF.Sqrt), r=["mx"], w=["mx"])
    dve(lambda e: e.tensor_scalar(out=negc1[0:1, :], in0=mx[0:1, :], scalar1=-(ATTN_SCALE * 128.0), scalar2=None,
                                  op0=ALU.mult), r=["mx"], w=["negc1"])
    b = bank_m()
    pe(lambda e, b=b: e.matmul(ps[b][:, 0:1], lhsT=cfv("ones_row")[0:1, :], rhs=negc1[0:1, :], start=True, stop=True),
       r=["negc1", "cf"], w=[("ps", b)])
    dve(lambda e, b=b: e.tensor_copy(out=negc, in_=ps[b][:, 0:1]), r=[("ps", b)], w=["negc"])
    act(lambda e: e.activation(out=esink, in_=cfv("sinks"), func=AF.Exp, bias=negc, scale=1.0),
        r=["negc", "cf"], w=["esink"])

    P.fence()
    dump("aT", aT, ["aT"])
    chk(1)

    aT_rhs = lambda k, c0, n: aT[:, k, c0:c0 + n]
    cosv = cfv("cos"); sinv = cfv("sin")

    rot_pending = [None]

    def rot_flush():
        if rot_pending[0] is not None:
            f = rot_pending[0]
            rot_pending[0] = None
            f()

    def rotary_ops(tag, b, c0, n, xb, t1, t2, outb):
        act(lambda e: e.copy(out=xb[:, c0:c0 + n], in_=ps[b][:, 0:n]), r=[("ps", b)], w=[(tag, "xb", c0)])
        dve(lambda e: e.tensor_tensor(out=t1[:, c0:c0 + n], in0=ps[b][:, 0:n], in1=cosv[:, c0:c0 + n], op=ALU.mult),
            r=[("ps", b), "cf"], w=[(tag, "t1", c0)])
        rot_flush()

        def part_b():
            b2 = bank_m()
            pe(lambda e: e.matmul(ps[b2][:, 0:n], lhsT=cbv("rrot"), rhs=xb[:, c0:c0 + n], start=True, stop=True),
               r=[(tag, "xb", c0), "cb"], w=[("ps", b2)])
            dve(lambda e: e.tensor_tensor(out=t2[:, c0:c0 + n], in0=ps[b2][:, 0:n], in1=sinv[:, c0:c0 + n], op=ALU.mult),
                r=[("ps", b2), "cf"], w=[(tag, "t2", c0)])
            dve(lambda e: e.tensor_tensor(out=outb[:, c0:c0 + n], in0=t1[:, c0:c0 + n], in1=t2[:, c0:c0 + n], op=ALU.add),
                r=[(tag, "t1", c0), (tag, "t2", c0)], w=[(tag, "rot", c0)])
        rot_pending[0] = part_b

    Z = Alloc(HB, total)
    p1 = []
    for hp in range(2):
        p1.append(dict(xb=Z([NT], BF16), t1=Z([NT], F32), t2=Z([NT], F32), kr=Z([NT], BF16),
                       kD=Z([8, 128], BF16), vT=Z([NT], BF16), vtok=Z([8, 128], BF16), sloc=Z([128], F32)))
    kdl = cfv("kdlong").rearrange("p (h c) -> p h c", h=8)

    for j in range(4):
        hs = (2 * j, 2 * j + 1)

        def evac_k(ml, ti, c0, n, b, hs=hs):
            bf = p1[ml]
            rotary_ops(("p1", ml), b, c0, n, bf["xb"], bf["t1"], bf["t2"], bf["kr"])

        def evac_v(ml, ti, c0, n, b, hs=hs):
            bf = p1[ml]
            act(lambda e: e.copy(out=bf["vT"][:, c0:c0 + n], in_=ps[b][:, 0:n]), r=[("ps", b)],
                w=[("p1", ml, "vT", c0)])
        dense_block(16, aT_rhs, ["aT"], evac_k)
        chk(20)
        dense_block(16, aT_rhs, ["aT"], evac_v)
        rot_flush()
        chk(21)
        for ml in range(2):
            h = hs[ml]
            bf = p1[ml]
            rk_res = [(("p1", ml), "rot", c0) for (c0, n) in TT]
            rv_res = [("p1", ml, "vT", c0) for (c0, n) in TT]
            dma("sp", "spk%d" % ml, krs.ap()[h], bf["kr"], r=rk_res, w=[("krs", h)])
            dma("sp", "spv%d" % ml, vts.ap()[h], bf["vT"], r=rv_res, w=[("vts", h)])
            for half in range(2):
                bk = bank_m()
                for cc in range(4):
                    c = half * 4 + cc
                    pe(lambda e, bk=bk, cc=cc, c=c, bf=bf: e.transpose(
                        psb(bk)[:, cc * 128:(cc + 1) * 128], bf["kr"][:, c * 128:(c + 1) * 128], ident_b),
                       r=rk_res + ["cb"], w=[("ps", bk)])
                for cc in range(4):
                    c = half * 4 + cc
                    dve(lambda e, bk=bk, cc=cc, c=c, bf=bf, h=h: e.tensor_scalar(
                        out=bf["kD"][:, c, :], in0=psb(bk)[:, cc * 128:(cc + 1) * 128], scalar1=kdl[:, h, c:c + 1],
                        scalar2=None, op0=ALU.mult), r=[("ps", bk), "cf"], w=[("p1", ml, "kD", c)])
                bv = bank_m()
                for cc in range(4):
                    c = half * 4 + cc
                    pe(lambda e, bv=bv, cc=cc, c=c, bf=bf: e.transpose(
                        psb(bv)[:, cc * 128:(cc + 1) * 128], bf["vT"][:, c * 128:(c + 1) * 128], ident_b),
                       r=rv_res + ["cb"], w=[("ps", bv)])
                act(lambda e, bv=bv, half=half, bf=bf: e.copy(
                    out=bf["vtok"][:, half * 4:(half + 1) * 4, :],
                    in_=psb(bv)[:, 0:512].rearrange("p (a c) -> p a c", a=4)), r=[("ps", bv)], w=[("p1", ml, "vtok", half)])
            chk(22)
            bs = bank_m()
            for c in range(8):
                pe(lambda e, bs=bs, c=c, bf=bf: e.matmul(ps[bs][:, 0:128], lhsT=bf["kD"][:, c, :], rhs=bf["vtok"][:, c, :],
                                                        start=(c == 0), stop=(c == 7)),
                   r=[("p1", ml, "kD", c), ("p1", ml, "vtok", c // 4)], w=[("ps", bs)])
            dve(lambda e, bs=bs, bf=bf: e.tensor_copy(out=bf["sloc"], in_=ps[bs][:, 0:128]), r=[("ps", bs)],
                w=[("p1", ml, "sloc")])
            chk(23)
            dma("sp", "agi", ag_in.ap()[h * 128:(h + 1) * 128, :], bf["sloc"], r=[("p1", ml, "sloc")], w=["ag_in"])
            chk(24)

    dump("ag_in", ag_in.ap(), ["ag_in"])
    chk(2)
    P.fence()
    P.add("pool", lambda e: e.collective_compute("AllGather", ALU.bypass, replica_groups=groups or [[0, 1, 2, 3], [4, 5, 6, 7]],
                                                 ins=[ag_in.ap().opt()], outs=[ag_out.ap().opt()]),
          r=["ag_in"], w=["ag_out"], dma="cc")
    chk(3)

    Z = Alloc(HB, total)
    zf = [Z([NA], F32) for _ in range(2)]
    sq = [Z([NA], BF16) for _ in range(2)]
    rstdb = [Z([NA], F32)] * 2
    knT = Z([2, NA], BF16)
    kn32 = Z([2, 132], F32)
    v32 = Z([2, 132], F32)
    vTb = Z([2, NA], BF16)
    vtokA = Z([2, 9, 128], BF16)
    qnT = Z([4, NT], BF16)
    qnT2 = Z([4, NT], BF16)
    PTm = [Z([2, 512], BF16) for _ in range(3)]
    rec = [Z([512], F32) for _ in range(2)]
    win_t = Z([2, 128], F32)
    vstok = Z([2, 128], BF16)
    kc_b = [Z([128], BF16) for _ in range(NS)]
    vc_b = [Z([128], BF16) for _ in range(NS)]
    kcT = Z([4, 128], BF16)
    PTc = Z([16], BF16)
    Pn = Z([16], BF16)
    recs = Z([16], F32)
    cnt = {"qk": 0}

    qk_pending = [None]

    def qk_flush():
        if qk_pending[0] is not None:
            f = qk_pending[0]
            qk_pending[0] = None
            f()

    def qknorm(b, c0, n, gname, outbf, tagres, out32=None):
        s = cnt["qk"] % 2
        cnt["qk"] += 1
        act(lambda e: e.activation(out=sq[s][:, 0:n], in_=ps[b][:, 0:n], func=AF.Square), r=[("ps", b)], w=[("sq", s)])
        dve(lambda e: e.tensor_copy(out=zf[s][:, 0:n], in_=ps[b][:, 0:n]), r=[("ps", b)], w=[("zf", s)])
        qk_flush()

        def part_b():
            b2 = 3
            pe(lambda e: e.matmul(ps[b2][:, 0:n], lhsT=cbv("ones128"), rhs=sq[s][:, 0:n], start=True, stop=True),
               r=[("sq", s), "cb"], w=[("ps", b2)])
            act(lambda e: e.activation(out=rstdb[0][:, 0:n], in_=ps[b2][:, 0:n], func=AF.Ln, bias=epsv, scale=1.0),
                r=[("ps", b2)], w=["rstdb"])
            act(lambda e: e.activation(out=rstdb[0][:, 0:n], in_=rstdb[0][:, 0:n], func=AF.Exp, scale=-0.5),
                r=["rstdb"], w=["rstdb"])
            dve(lambda e: e.scalar_tensor_tensor(out=outbf, in0=zf[s][:, 0:n], scalar=cfv(gname), in1=rstdb[0][:, 0:n],
                                                 op0=ALU.mult, op1=ALU.mult),
                r=[("zf", s), "rstdb", "cf"], w=[tagres])
            if out32 is not None:
                lo, hi, dst = out32
                dve(lambda e: e.scalar_tensor_tensor(out=dst, in0=zf[s][:, lo:hi], scalar=cfv(gname),
                                                     in1=rstdb[0][:, lo:hi], op0=ALU.mult, op1=ALU.mult),
                    r=[("zf", s), "rstdb", "cf"], w=[("kn32", tagres)])
        qk_pending[0] = part_b

    def evac_ak(ml, ti, c0, n, b):
        o32 = None
        if ti == 2:
            o32 = (208, 340, kn32[:, ml, :])
        qknorm(b, c0, n, "kg", knT[:, ml, c0:c0 + n], ("knT", ml, c0), o32)

    def evac_av(ml, ti, c0, n, b):
        act(lambda e: e.copy(out=vTb[:, ml, c0:c0 + n], in_=ps[b][:, 0:n]), r=[("ps", b)], w=[("vTb", ml, c0)])
        if ti == 2:
            dve(lambda e: e.tensor_copy(out=v32[:, ml, :], in_=ps[b][:, 208:340]), r=[("ps", b)], w=[("v32", ml)])

    att_dense3 = [True]
    dense_block(16, aT_rhs, ["aT"], evac_ak, tiles=TT_H)
    qk_flush()
    dense_block(16, aT_rhs, ["aT"], evac_av, tiles=TT_H)
    vres = lambda g: [("vTb", g, c0) for (c0, n) in TT_H]
    kres = lambda g: [("knT", g, c0) for (c0, n) in TT_H]
    for g in range(2):
        for grp in range(3):
            blks = [0, 1, 2, 3] if grp == 0 else ([4, 5, 6, 7] if grp == 1 else [8])
            bv = bank_m()
            for ii, blk in enumerate(blks):
                col = NT if blk == 0 else (blk - 1) * 128
                pe(lambda e, bv=bv, ii=ii, col=col, g=g: e.transpose(
                    psb(bv)[:, ii * 128:(ii + 1) * 128], vTb[:, g, col:col + 128], ident_b),
                   r=vres(g) + ["cb"], w=[("ps", bv)])
            nb = len(blks)
            act(lambda e, bv=bv, g=g, blks=blks, nb=nb: e.copy(
                out=vtokA[:, g, blks[0]:blks[0] + nb, :],
                in_=psb(bv)[:, 0:nb * 128].rearrange("p (a c) -> p a c", a=nb)), r=[("ps", bv)], w=[("vtokA", g)])
        bv = bank_m()
        pe(lambda e, bv=bv, g=g: e.transpose(psb(bv)[0:NS, 0:128], vTb[:, g, NP_:NP_ + NS], ident_b),
           r=vres(g) + ["cb"], w=[("ps", bv)])
        act(lambda e, bv=bv, g=g: e.copy(out=vstok[0:NS, g, :], in_=psb(bv)[0:NS, 0:128]), r=[("ps", bv)],
            w=[("vstok", g)])
    for (src32, dst, nm) in ((kn32, kwin, "kw"), (v32, vwin, "vw")):
        for g in range(2):
            bw = bank_m()
            pe(lambda e, bw=bw, g=g, src32=src32: e.transpose(ps[bw][:, 0:128], src32[:, g, 0:128], ident_f),
               r=[("kn32", ("knT", g, 688)), ("v32", g), "cf"], w=[("ps", bw)])
            dve(lambda e, bw=bw, g=g: e.tensor_copy(out=win_t[:, g, :], in_=ps[bw][:, 0:128]), r=[("ps", bw)],
                w=[("win_t", g)])
        dma("sp", "o_" + nm, dst, win_t, r=[("win_t", 0), ("win_t", 1)], w=["out_" + nm])
    dma("sp", "o_ks", ks_out[:, 0:127, :, :], ck[:, 1:128, :, :], w=["ks_out_a"])
    dma("sp", "o_vs", vs_out[:, 0:127, :, :], cv[:, 1:128, :, :], w=["vs_out_a"])
    for g in range(2):
        for s in range(NS):
            dma("sp", "o_ks", ks_out[s, 127, g, :].rearrange("(d o) -> d o", o=1), kn32[:, g, 128 + s:129 + s],
                r=[("kn32", ("knT", g, 688))], w=[("ks_out_b", g, s)], slow=True)
            dma("sp", "o_vs", vs_out[s, 127, g, :].rearrange("(d o) -> d o", o=1), v32[:, g, 128 + s:129 + s],
                r=[("v32", g)], w=[("vs_out_b", g, s)], slow=True)

    mask = cbv("mask").rearrange("p (a c) -> p a c", a=3)
    v3 = lambda ap: ap[:, 0:1024].rearrange("p (a c) -> p a c", a=8)
    esr_f = v3(zf[0]); esr_d = v3(zf[1]); esr_hi = v3(sq[0]); esr_lo = v3(sq[1])
    esr2 = Z([8, 128], BF16)
    oh = cfv("onehot")
    dve(lambda e: e.tensor_copy(out=esr_f[0:2], in_=esink[0:2, :].unsqueeze(2).broadcast_to([2, 8, 128])),
        r=["esink"], w=[("zf", 0)])
    dve(lambda e: e.tensor_copy(out=esr_hi[0:2], in_=esr_f[0:2]), r=[("zf", 0)], w=[("sq", 0)])
    dve(lambda e: e.tensor_tensor(out=esr_d[0:2], in0=esr_f[0:2], in1=esr_hi[0:2], op=ALU.subtract),
        r=[("zf", 0), ("sq", 0)], w=[("zf", 1)])
    dve(lambda e: e.tensor_copy(out=esr_lo[0:2], in_=esr_d[0:2]), r=[("zf", 1)], w=[("sq", 1)])
    dve(lambda e: e.tensor_scalar(out=esr2[0:2], in0=esr_hi[0:2], scalar1=oh[0:2, 0:1], scalar2=None, op0=ALU.mult),
        r=[("sq", 0), "cf"], w=["esr2a"])
    dve(lambda e: e.scalar_tensor_tensor(out=esr2[0:2], in0=esr_lo[0:2], scalar=oh[0:2, 1:2], in1=esr2[0:2],
                                         op0=ALU.mult, op1=ALU.add), r=[("sq", 1), "esr2a", "cf"], w=["esr2"])
    def make_att(g, qnT, qres, skew=1, wide=False):
        npt = skew + 1
        alloc_c = {"s": 0, "p": 0}

        def alloc_S():
            if not wide:
                return bank_m(), bank_m()
            p = [(4, 5), (6, 7)][alloc_c["s"] % 2]
            alloc_c["s"] += 1
            return p

        def alloc_PV():
            if not wide:
                return bank_m(), bank_m()
            p = [(0, 1), (2, 3)][alloc_c["p"] % 2]
            alloc_c["p"] += 1
            return p

        def att_S(blk, g=g):
            s = blk % npt
            q_rhs = qnT[:, :, blk * 128:(blk + 1) * 128]
            k_own = knT[:, g, blk * 128:(blk + 1) * 128]
            k_prev = knT[:, g, NT:NT + 128] if blk == 0 else knT[:, g, (blk - 1) * 128:blk * 128]
            bo, bp = alloc_S()
            mprev = 2 if blk == 0 else 1
            for (bb_, kk_, mi) in ((bo, k_own, 0), (bp, k_prev, mprev)):
                pe(lambda e, bb_=bb_, kk_=kk_, q_rhs=q_rhs: e.matmul(ps[bb_][:, :], lhsT=kk_, rhs=q_rhs, start=True, stop=False),
                   r=qres + kres(g), w=[("ps", bb_)])
                pe(lambda e, bb_=bb_, mi=mi: e.matmul(ps[bb_][:, :], lhsT=ident_b,
                                                     rhs=mask[:, mi:mi + 1, :].broadcast_to([128, 4, 128]),
                                                     start=False, stop=True), r=["cb"], w=[("ps", bb_)])
            act(lambda e, bo=bo, s=s: e.activation(out=PTm[s][:, 0, :], in_=ps[bo][:, :], func=AF.Exp, bias=negc,
                                                   scale=ATTN_SCALE), r=[("ps", bo), "negc"], w=[("PTm", s, 0)])
            act(lambda e, bp=bp, s=s: e.activation(out=PTm[s][:, 1, :], in_=ps[bp][:, :], func=AF.Exp, bias=negc,
                                                   scale=ATTN_SCALE), r=[("ps", bp), "negc"], w=[("PTm", s, 1)])

        def att_PV(blk, g=g):
            s = blk % npt
            sr = blk % 2
            bO, bD = alloc_PV()
            for t, vb in ((0, blk + 1), (1, blk)):
                pe(lambda e, bO=bO, t=t, vb=vb, s=s, g=g: e.matmul(ps[bO][:, :], lhsT=vtokA[:, g, vb, :], rhs=PTm[s][:, t, :],
                                                                  start=(t == 0), stop=(t == 1)),
                   r=[("PTm", s, t), ("vtokA", g)], w=[("ps", bO)])
            for t in range(2):
                pe(lambda e, bD=bD, t=t, s=s: e.matmul(ps[bD][:, :], lhsT=cbv("ones"), rhs=PTm[s][:, t, :],
                                                      start=(t == 0), stop=False),
                   r=[("PTm", s, t), "cb"], w=[("ps", bD)])
            pe(lambda e, bD=bD, g=g: e.matmul(ps[bD][:, :], lhsT=cbv("ones")[0:2, :], rhs=esr2[0:2, 4 * g:4 * g + 4, :],
                                             start=False, stop=True), r=["esr2", "cb"], w=[("ps", bD)])
            if blk % 2 == 0:
                act(lambda e, bD=bD, sr=sr: e.activation(out=rec[sr], in_=ps[bD][:, :], func=AF.Ln),
                    r=[("ps", bD)], w=[("rec", sr)])
                act(lambda e, sr=sr: e.activation(out=rec[sr], in_=rec[sr], func=AF.Exp, scale=-1.0), r=[("rec", sr)],
                    w=[("rec", sr)])
            else:
                dve(lambda e, bD=bD, sr=sr: e.reciprocal(out=rec[sr], in_=ps[bD][:, :]), r=[("ps", bD)], w=[("rec", sr)])
            dve(lambda e, bO=bO, sr=sr, g=g, blk=blk: e.tensor_tensor(
                out=mixT[:, 4 * g:4 * g + 4, blk * 128:(blk + 1) * 128],
                in0=ps[bO][:, :].rearrange("p (a c) -> p a c", a=4),
                in1=rec[sr][:, :].rearrange("p (a c) -> p a c", a=4), op=ALU.mult),
                r=[("ps", bO), ("rec", sr)], w=[("mixT", "a", g, blk)])


        def att_gen():
            for b0 in range(skew):
                att_S(b0)
                yield
            for blk in range(8):
                if blk + skew < 8:
                    att_S(blk + skew)
                    yield
                att_PV(blk)
                yield

        def att_sample():
            for sm in range(NS):
                dma("pool", "kc%d" % sm, kc_b[sm], ck[sm, :, g, :], w=[("kc_b", sm)])
                dma("pool", "vc%d" % sm, vc_b[sm], cv[sm, :, g, :], w=[("vc_b", sm)])
            bt = bank_m()
            for sm in range(NS):
                pe(lambda e, bt=bt, sm=sm: e.transpose(psb(bt)[:, sm * 128:(sm + 1) * 128], kc_b[sm], ident_b),
                   r=[("kc_b", sm), "cb"], w=[("ps", bt)])
            act(lambda e, bt=bt: e.copy(out=kcT, in_=psb(bt)[:, 0:512].rearrange("p (a c) -> p a c", a=4)),
                r=[("ps", bt)], w=["kcT"])
            bS = bank_m()
            for sm in range(NS):
                pe(lambda e, bS=bS, sm=sm: e.matmul(ps[bS][:, sm * 4:(sm + 1) * 4], lhsT=kcT[:, sm, :], rhs=qnT[:, :, NP_ + sm],
                                                   start=True, stop=True, skip_group_check=True),
                   r=["kcT"] + qres, w=[("ps", bS)])
            for sm in range(NS):
                pe(lambda e, bS=bS, sm=sm, g=g: e.matmul(ps[bS][0:NS, 32 + sm * 4:32 + (sm + 1) * 4],
                                                        lhsT=knT[:, g, NP_:NP_ + NS], rhs=qnT[:, :, NP_ + sm],
                                                        start=True, stop=True, skip_group_check=True),
                   r=kres(g) + qres, w=[("ps", bS)])
            act(lambda e, bS=bS: e.activation(out=PTc, in_=ps[bS][:, 0:16], func=AF.Exp, bias=negc, scale=ATTN_SCALE),
                r=[("ps", bS), "negc"], w=["PTc"])
            act(lambda e, bS=bS: e.activation(out=Pn[0:NS, :], in_=ps[bS][0:NS, 32:48], func=AF.Exp,
                                              bias=negc[0:NS, :], scale=ATTN_SCALE), r=[("ps", bS), "negc"], w=["Pn"])
            dve(lambda e: e.tensor_tensor(out=Pn[0:NS, :].rearrange("p (s h) -> p s h", s=4),
                                          in0=Pn[0:NS, :].rearrange("p (s h) -> p s h", s=4),
                                          in1=oh[0:NS, 0:4].unsqueeze(2).broadcast_to([NS, 4, 4]), op=ALU.mult),
                r=["Pn", "cf"], w=["Pn"])
            bO = bank_m(); bD = bank_m()
            for sm in range(NS):
                pe(lambda e, bO=bO, sm=sm: e.matmul(ps[bO][:, sm * 4:(sm + 1) * 4], lhsT=vc_b[sm], rhs=PTc[:, sm * 4:(sm + 1) * 4],
                                                   start=(sm == 0), stop=False, skip_group_check=True),
                   r=[("vc_b", sm), "PTc"], w=[("ps", bO)])
                pe(lambda e, bO=bO, sm=sm, g=g: e.matmul(ps[bO][:, sm * 4:(sm + 1) * 4], lhsT=vstok[0:NS, g, :],
                                                        rhs=Pn[0:NS, sm * 4:(sm + 1) * 4], start=False, stop=True,
                                                        skip_group_check=True),
                   r=[("vstok", g), "Pn"], w=[("ps", bO)])
            for sm in range(NS):
                pe(lambda e, bD=bD, sm=sm: e.matmul(ps[bD][:, sm * 4:(sm + 1) * 4], lhsT=cbv("ones"), rhs=PTc[:, sm * 4:(sm + 1) * 4],
                                                   start=(sm == 0), stop=False, skip_group_check=True),
                   r=["PTc", "cb"], w=[("ps", bD)])
                pe(lambda e, bD=bD, sm=sm: e.matmul(ps[bD][:, sm * 4:(sm + 1) * 4], lhsT=cbv("ones")[0:NS, :],
                                                   rhs=Pn[0:NS, sm * 4:(sm + 1) * 4], start=False, stop=False,
                                                   skip_group_check=True), r=["Pn", "cb"], w=[("ps", bD)])
            pe(lambda e, bD=bD, g=g: e.matmul(ps[bD][:, 0:16], lhsT=cbv("ones")[0:2, :],
                                             rhs=esr2[0:2, 4 * g:4 * g + 4, 0].unsqueeze(1).broadcast_to([2, 4, 4]),
                                             start=False, stop=True, skip_group_check=True),
               r=["esr2", "cb"], w=[("ps", bD)])
            act(lambda e, bD=bD: e.activation(out=recs, in_=ps[bD][:, 0:16], func=AF.Ln), r=[("ps", bD)], w=["recs"])
            act(lambda e: e.activation(out=recs, in_=recs, func=AF.Exp, scale=-1.0), r=["recs"], w=["recs"])
            dve(lambda e, bO=bO, g=g: e.tensor_tensor(
                out=mixT[:, 4 * g:4 * g + 4, NP_:NP_ + NS], in0=ps[bO][:, 0:16].rearrange("p (s h) -> p h s", s=4),
                in1=recs[:, :].rearrange("p (s h) -> p h s", s=4), op=ALU.mult),
                r=[("ps", bO), "recs"], w=[("mixT", "as", g)])

        return att_gen, att_sample

    qbufs = [qnT, qnT2]
    qresf = lambda gi: [("qnT", gi, hh, c0) for hh in range(4) for (c0, n) in TT]

    def qproj_gen(gi):
        qb = qbufs[gi]
        for jj in range(2):
            wv, wres = w_next()
            for ml in range(2):
                hh = jj * 2 + ml
                for ti, (c0, n) in enumerate(TT):
                    b = bank_d()
                    for k in range(16):
                        pe(lambda e, b=b, k=k, ml=ml, c0=c0, n=n, wv=wv: e.matmul(
                            ps[b][:, 0:n], lhsT=wv[:, k, ml * 128:(ml + 1) * 128], rhs=aT[:, k, c0:c0 + n],
                            start=(k == 0), stop=(k == 15)), r=[wres, "aT"], w=[("ps", b)])
                    qknorm(b, c0, n, "qg", qb[:, hh, c0:c0 + n], ("qnT", gi, hh, c0))
                    yield
            w_done()
        qk_flush()

    def drive_mix(ga, gb, pat):
        a_alive, b_alive = True, gb is not None
        while a_alive or b_alive:
            if a_alive:
                try:
                    next(ga)
                except StopIteration:
                    a_alive = False
            for _ in range(pat if a_alive else 99):
                if not b_alive:
                    break
                try:
                    next(gb)
                except StopIteration:
                    b_alive = False

    for _ in qproj_gen(0):
        pass
    attg0, atts0 = make_att(0, qbufs[0], qresf(0))
    attg1, atts1 = make_att(1, qbufs[1], qresf(1), skew=1, wide=True)
    drive_mix(attg0(), qproj_gen(1), 1)
    atts0()
    for _ in attg1():
        pass
    atts1()
    att_dense3[0] = False
    P.fence()
    dump("mixA", mixT[:, 0:8, :], ["mixT"])
    chk(4)

    Z = Alloc(HB, total)
    agl = [Z([8, 128], F32) for _ in range(2)]
    coef = cfv("coef")
    ago = ag_out.ap()
    for r_ in range(4):
        s = r_ % 2
        dma("sp", "agl%d" % s, agl[s], ago[r_ * 1024:(r_ + 1) * 1024, :].rearrange("(h p) n -> p h n", p=128),
            r=["ag_out"], w=[("agl", s)])
        for h in range(8):
            if r_ == 0:
                dve(lambda e, s=s, h=h, r_=r_: e.tensor_scalar(out=S0[:, h, :], in0=agl[s][:, h, :],
                                                              scalar1=coef[:, r_ * 8 + h:r_ * 8 + h + 1], scalar2=None,
                                                              op0=ALU.mult), r=[("agl", s), "cf"], w=[("S0", h)])
            else:
                dve(lambda e, s=s, h=h, r_=r_: e.scalar_tensor_tensor(
                    out=S0[:, h, :], in0=agl[s][:, h, :], scalar=coef[:, r_ * 8 + h:r_ * 8 + h + 1], in1=S0[:, h, :],
                    op0=ALU.mult, op1=ALU.add), r=[("agl", s), "cf", ("S0", h)], w=[("S0", h)])

    dump("S0", S0, [("S0", h) for h in range(8)])
    chk(5)
    G1 = [dict(qr=Z([NT], BF16), kr=Z([NT], BF16), vT=Z([NT], BF16), sg=Z([NT], F32)) for _ in range(2)]
    RT = dict(xb=[Z([344], BF16) for _ in range(3)], t1=[Z([344], F32) for _ in range(3)],
              t2=[Z([344], F32) for _ in range(3)])
    R = dict(qd=Z([8, 128], BF16), kd=Z([8, 128], BF16), vtok=Z([8, 128], BF16), scm=Z([8, 128], BF16),
             Sb=Z([8, 128], BF16), Srun=Z([2, 128], F32), o_sb=Z([NT], F32), sq=Z([NT], BF16), rstd=Z([NT], F32),
             tt=Z([NT], F32), ktok_s=Z([128], BF16), vtok_s=Z([128], BF16))
    vm = [Z([128], BF16) for _ in range(NS)]
    Sst = [Z([128], F32) for _ in range(NS)]
    Snew = [Z([128], F32) for _ in range(NS)]
    Sbs = [Z([128], BF16) for _ in range(NS)]
    intraT = cfv("intraT").rearrange("p (h c) -> p h c", h=8)
    qdecv = cfv("qdec").rearrange("p (h c) -> p h c", h=8)
    kdecv = cfv("kdec")
    rgv = cfv("rg")
    tg = "p2"
    p2c = {"d": 0, "r": 0, "t": 0}

    def bank_d3():
        b = p2c["d"] % 3
        p2c["d"] += 1
        return b

    def bank_r():
        b = 3 + p2c["r"] % 2
        p2c["r"] += 1
        return b
    M0, M1, M2 = 5, 6, 7
    p2blocks = {}

    def rot_a(b, c0, n):
        sl = p2c["t"] % 3
        p2c["t"] += 1
        xb, t1 = RT["xb"][sl], RT["t1"][sl]
        act(lambda e: e.copy(out=xb[:, 0:n], in_=ps[b][:, 0:n]), r=[("ps", b)], w=[("rt_xb", sl)])
        dve(lambda e: e.tensor_tensor(out=t1[:, 0:n], in0=ps[b][:, 0:n], in1=cosv[:, c0:c0 + n], op=ALU.mult),
            r=[("ps", b), "cf"], w=[("rt_t1", sl)])
        return sl

    def rot_b(sl, c0, n, outb, outres):
        xb, t1, t2 = RT["xb"][sl], RT["t1"][sl], RT["t2"][sl]
        b2 = bank_r()
        pe(lambda e: e.matmul(ps[b2][:, 0:n], lhsT=cbv("rrot"), rhs=xb[:, 0:n], start=True, stop=True),
           r=[("rt_xb", sl), "cb"], w=[("ps", b2)])
        dve(lambda e: e.tensor_tensor(out=t2[:, 0:n], in0=ps[b2][:, 0:n], in1=sinv[:, c0:c0 + n], op=ALU.mult),
            r=[("ps", b2), "cf"], w=[("rt_t2", sl)])
        dve(lambda e: e.tensor_tensor(out=outb[:, c0:c0 + n], in0=t1[:, 0:n], in1=t2[:, 0:n], op=ALU.add),
            r=[("rt_t1", sl), ("rt_t2", sl)], w=[outres])

    def stageA(h):
        j, ml = h // 2, h % 2
        if ml == 0:
            p2blocks[j] = [w_next() for _ in range(2)]
        blocks = p2blocks[j]
        g1 = G1[h % 2]
        gp = h % 2
        dma("sp", "ldk%d" % gp, g1["kr"], krs.ap()[h], r=[("krs", h)], w=[(tg, "kr", gp, c0) for (c0, n) in TT])
        dma("sp", "ldv%d" % gp, g1["vT"], vts.ap()[h], r=[("vts", h)], w=[(tg, "vT", gp, c0) for (c0, n) in TT])
        for bi in range(2):
            wv, wres = blocks[bi]
            pending = None
            for ti, (c0, n) in enumerate(TT):
                b = bank_d3()
                for k in range(16):
                    pe(lambda e, b=b, k=k, c0=c0, n=n, wv=wv, ml=ml: e.matmul(
                        ps[b][:, 0:n], lhsT=wv[:, k, ml * 128:(ml + 1) * 128], rhs=aT[:, k, c0:c0 + n],
                        start=(k == 0), stop=(k == 15)), r=[wres, "aT"], w=[("ps", b)])
                if bi == 0:
                    sl = rot_a(b, c0, n)
                    if pending is not None:
                        rot_b(*pending)
                    pending = (sl, c0, n, g1["qr"], (tg, "qr", gp, c0))
                else:
                    act(lambda e, b=b, c0=c0, n=n, g1=g1: e.activation(out=g1["sg"][:, c0:c0 + n], in_=ps[b][:, 0:n],
                                                                       func=AF.Silu), r=[("ps", b)], w=[(tg, "sg", gp, c0)])
                if ti == 2:
                    if pending is not None:
                        rot_b(*pending)
                    if ml == 1:
                        w_done()
                yield

    def stageB(h):
        g1 = G1[h % 2]
        gp = h % 2
        q_res = [(tg, "qr", gp, c0) for (c0, n) in TT]
        k_res = [(tg, "kr", gp, c0) for (c0, n) in TT]
        v_res = [(tg, "vT", gp, c0) for (c0, n) in TT]
        g_res = [(tg, "sg", gp, c0) for (c0, n) in TT]
        dve(lambda e: e.tensor_tensor(out=R["qd"], in0=g1["qr"][:, 0:NP_].rearrange("p (a c) -> p a c", a=8),
                                      in1=qdecv[:, h:h + 1, :].broadcast_to([128, 8, 128]), op=ALU.mult),
            r=q_res + ["cf"], w=[(tg, "qd")])
        for half in range(2):
            for cc in range(4):
                c = half * 4 + cc
                pe(lambda e, cc=cc, c=c: e.transpose(psb(M0)[:, cc * 128:(cc + 1) * 128],
                                                    g1["kr"][:, c * 128:(c + 1) * 128], ident_b),
                   r=k_res + ["cb"], w=[("ps", M0)])
            dve(lambda e, half=half: e.tensor_scalar(
                out=R["kd"][:, half * 4:(half + 1) * 4, :], in0=psb(M0)[:, 0:512].rearrange("p (a c) -> p a c", a=4),
                scalar1=kdecv[:, h:h + 1], scalar2=None, op0=ALU.mult), r=[("ps", M0), "cf"], w=[(tg, "kd", half)])
            for cc in range(4):
                c = half * 4 + cc
                pe(lambda e, cc=cc, c=c: e.transpose(psb(M1)[:, cc * 128:(cc + 1) * 128],
                                                    g1["vT"][:, c * 128:(c + 1) * 128], ident_b),
                   r=v_res + ["cb"], w=[("ps", M1)])
            act(lambda e, half=half: e.copy(out=R["vtok"][:, half * 4:(half + 1) * 4, :],
                                            in_=psb(M1)[:, 0:512].rearrange("p (a c) -> p a c", a=4)),
                r=[("ps", M1)], w=[(tg, "vtok", half)])
        act(lambda e: e.copy(out=R["Sb"][:, 0, :], in_=S0[:, h, :]), r=[("S0", h)], w=[(tg, "Sb", 0)])
        yield
        g128 = float(GAMMA[h] ** 128)
        for half in range(2):
            for cc in range(4):
                c = half * 4 + cc
                pe(lambda e, cc=cc, c=c: e.matmul(ps[M2][:, cc * 128:(cc + 1) * 128], lhsT=R["kd"][:, c, :],
                                                 rhs=R["vtok"][:, c, :], start=True, stop=True),
                   r=[(tg, "kd", half), (tg, "vtok", half)], w=[("ps", M2)])
            for cc in range(4):
                c = half * 4 + cc
                prev = S0[:, h, :] if c == 0 else R["Srun"][:, (c - 1) % 2, :]
                prev_res = ("S0", h) if c == 0 else (tg, "Srun", (c - 1) % 2)
                dve(lambda e, cc=cc, c=c, prev=prev: e.scalar_tensor_tensor(
                    out=R["Srun"][:, c % 2, :], in0=prev, scalar=g128, in1=ps[M2][:, cc * 128:(cc + 1) * 128],
                    op0=ALU.mult, op1=ALU.add), r=[prev_res, ("ps", M2)], w=[(tg, "Srun", c % 2)])
                if c < 7:
                    act(lambda e, c=c: e.copy(out=R["Sb"][:, c + 1, :], in_=R["Srun"][:, c % 2, :]),
                        r=[(tg, "Srun", c % 2)], w=[(tg, "Sb", c + 1)])
                else:
                    dma("sp", "o_rs", rstate[h], R["Srun"][:, c % 2, :], r=[(tg, "Srun", c % 2)], w=[("rstate", h)])
        yield
        for half, bsc in ((0, M0), (1, M1)):
            for cc in range(4):
                c = half * 4 + cc
                pe(lambda e, bsc=bsc, cc=cc, c=c: e.matmul(ps[bsc][:, cc * 128:(cc + 1) * 128],
                                                          lhsT=g1["kr"][:, c * 128:(c + 1) * 128],
                                                          rhs=g1["qr"][:, c * 128:(c + 1) * 128], start=True, stop=True),
                   r=k_res + q_res, w=[("ps", bsc)])
            dve(lambda e, bsc=bsc, half=half: e.tensor_tensor(
                out=R["scm"][:, half * 4:(half + 1) * 4, :], in0=ps[bsc][:, :].rearrange("p (a c) -> p a c", a=4),
                in1=intraT[:, h:h + 1, :].broadcast_to([128, 4, 128]), op=ALU.mult),
                r=[("ps", bsc), "cf"], w=[(tg, "scm", half)])
        yield
        obanks = [M0, M1]
        for half in range(2):
            bo = obanks[half]
            for cc in range(4):
                c = half * 4 + cc
                pe(lambda e, bo=bo, cc=cc, c=c: e.matmul(ps[bo][:, cc * 128:(cc + 1) * 128], lhsT=R["vtok"][:, c, :],
                                                        rhs=R["scm"][:, c, :], start=(cc == 0), stop=False,
                                                        skip_group_check=True),
                   r=[(tg, "vtok", half), (tg, "scm", half)], w=[("ps", bo)])
        pe(lambda e: e.transpose(psb(M2)[0:NS, 0:128], g1["kr"][:, NP_:NT], ident_b), r=k_res + ["cb"], w=[("ps", M2)])
        pe(lambda e: e.transpose(psb(M2)[0:NS, 128:256], g1["vT"][:, NP_:NT], ident_b), r=v_res + ["cb"], w=[("ps", M2)])
        act(lambda e: e.mul(out=R["ktok_s"][0:NS, :], in_=psb(M2)[0:NS, 0:128], mul=RET_K_SCALE),
            r=[("ps", M2)], w=[(tg, "ktok_s")])
        act(lambda e: e.copy(out=R["vtok_s"][0:NS, :], in_=psb(M2)[0:NS, 128:256]), r=[("ps", M2)], w=[(tg, "vtok_s")])
        for sm in range(NS):
            dma("sp", "st%d" % sm, Sst[sm], st[sm, h], w=[("Sst", sm)])
            dve(lambda e, sm=sm: e.tensor_scalar(out=vm[sm][0:NS, :], in0=R["vtok_s"][0:NS, :],
                                                 scalar1=cfv("onehot")[0:NS, sm:sm + 1], scalar2=None, op0=ALU.mult),
                r=[(tg, "vtok_s"), "cf"], w=[("vm", sm)])
        yield
        for half in range(2):
            bo = obanks[half]
            for cc in range(4):
                c = half * 4 + cc
                pe(lambda e, bo=bo, cc=cc, c=c: e.matmul(ps[bo][:, cc * 128:(cc + 1) * 128], lhsT=R["Sb"][:, c, :],
                                                        rhs=R["qd"][:, c, :], start=False, stop=True,
                                                        skip_group_check=True),
                   r=[(tg, "Sb", c), (tg, "qd")], w=[("ps", bo)])

        def o_read(bo, c0, n):
            act(lambda e: e.activation(out=R["sq"][:, c0:c0 + n], in_=ps[bo][:, 0:n], func=AF.Square),
                r=[("ps", bo)], w=[(tg, "sq", c0)])
            dve(lambda e: e.tensor_copy(out=R["o_sb"][:, c0:c0 + n], in_=ps[bo][:, 0:n]),
                r=[("ps", bo)], w=[(tg, "o_sb", c0)])
        o_read(M0, 0, 512)
        o_read(M1, 512, 512)
        yield
        for sm in range(NS):
            bu = M0 if sm % 2 == 0 else M1
            pe(lambda e, bu=bu, sm=sm: e.matmul(ps[bu][:, 0:128], lhsT=R["ktok_s"][0:NS, :], rhs=vm[sm][0:NS, :],
                                               start=True, stop=True), r=[(tg, "ktok_s"), ("vm", sm)], w=[("ps", bu)])
            dve(lambda e, bu=bu, sm=sm: e.scalar_tensor_tensor(out=Snew[sm], in0=Sst[sm], scalar=float(GAMMA[h]),
                                                               in1=ps[bu][:, 0:128], op0=ALU.mult, op1=ALU.add),
                r=[("Sst", sm), ("ps", bu)], w=[("Snew", sm)])
            dma("sp", "o_ss%d" % sm, ss_out[sm, h], Snew[sm], r=[("Snew", sm)], w=[("ss_out", sm, h)])
            act(lambda e, sm=sm: e.copy(out=Sbs[sm], in_=Snew[sm]), r=[("Snew", sm)], w=[("Sbs", sm)])
        yield
        for sm in range(NS):
            pe(lambda e, sm=sm: e.matmul(ps[M2][:, sm:sm + 1], lhsT=Sbs[sm], rhs=g1["qr"][:, NP_ + sm:NP_ + sm + 1],
                                        start=True, stop=True), r=[("Sbs", sm)] + q_res, w=[("ps", M2)])
        o_read(M2, NP_, NS)
        yield
        for pi, (c0, n) in enumerate(((0, 512), (512, 512), (NP_, NS))):
            bm_ = M0 if pi % 2 == 0 else M1
            pe(lambda e, bm_=bm_, c0=c0, n=n: e.matmul(ps[bm_][:, 0:n], lhsT=cbv("ones128"), rhs=R["sq"][:, c0:c0 + n],
                                                      start=True, stop=True), r=[(tg, "sq", c0), "cb"], w=[("ps", bm_)])
            act(lambda e, bm_=bm_, c0=c0, n=n: e.activation(out=R["rstd"][:, c0:c0 + n], in_=ps[bm_][:, 0:n], func=AF.Ln,
                                                            bias=epsv, scale=1.0), r=[("ps", bm_)], w=[(tg, "rstd", c0)])
            act(lambda e, c0=c0, n=n: e.activation(out=R["rstd"][:, c0:c0 + n], in_=R["rstd"][:, c0:c0 + n], func=AF.Exp,
                                                   scale=-0.5), r=[(tg, "rstd", c0)], w=[(tg, "rstd", c0)])
            dve(lambda e, c0=c0, n=n: e.scalar_tensor_tensor(
                out=R["tt"][:, c0:c0 + n], in0=R["o_sb"][:, c0:c0 + n], scalar=rgv[:, h:h + 1],
                in1=R["rstd"][:, c0:c0 + n], op0=ALU.mult, op1=ALU.mult),
                r=[(tg, "o_sb", c0), (tg, "rstd", c0), "cf"], w=[(tg, "tt", c0)])
            pool(lambda e, c0=c0, n=n: e.tensor_tensor(out=mixT[:, 8 + h, c0:c0 + n], in0=R["tt"][:, c0:c0 + n],
                                                       in1=g1["sg"][:, c0:c0 + n], op=ALU.mult),
                 r=[(tg, "tt", c0)] + g_res, w=[("mixT", "r", h, c0)])
        yield

    def drive(gens):
        gens = [g for g in gens if g is not None]
        while gens:
            for g in list(gens):
                try:
                    next(g)
                except StopIteration:
                    gens.remove(g)

    def drive2(gb, ga):
        pat = [1, 1, 1, 0, 1, 0, 1, 1, 9, 9]
        bi = 0
        b_alive, a_alive = True, ga is not None
        while b_alive or a_alive:
            if b_alive:
                try:
                    next(gb)
                except StopIteration:
                    b_alive = False
            na = pat[min(bi, len(pat) - 1)] if b_alive else 99
            bi += 1
            for _ in range(na):
                if not a_alive:
                    break
                try:
                    next(ga)
                except StopIteration:
                    a_alive = False

    drive([stageA(0)])
    for h in range(8):
        drive2(stageB(h), stageA(h + 1) if h + 1 < 8 else None)
    P.fence()
    dump("mixT", mixT, ["mixT"])
    chk(6)

    Zt = Alloc(T2base, T2base + 12 * 1024)
    Zx = Alloc(0, 0)
    xs_off = None
    aT_f32 = aT.rearrange("p a c -> p (a c)").bitcast(F32)
    xstage = [aT_f32[:, j * D:(j + 1) * D] for j in range(4)]
    tiles_x = [(x_main[i * 128:(i + 1) * 128, :], 128, i * 128) for i in range(8)] + [(x_smp, NS, NP_)]
    for i in range(3):
        dma("sp", "x%d" % (i % 4), xstage[i % 4][0:tiles_x[i][1], :], tiles_x[i][0], w=[("xstage", i % 4)])
    for i, (src, rows, c0) in enumerate(tiles_x):
        s = i % 4
        if i + 3 < len(tiles_x):
            j3 = i + 3
            dma("sp", "x%d" % (j3 % 4), xstage[j3 % 4][0:tiles_x[j3][1], :], tiles_x[j3][0], w=[("xstage", j3 % 4)])
        for q4 in range(4):
            b = bank_m()
            for kk in range(4):
                k = q4 * 4 + kk
                pe(lambda e, b=b, kk=kk, k=k, s=s, rows=rows: e.transpose(
                    ps[b][:, kk * 128:kk * 128 + rows], xstage[s][0:rows, k * 128:(k + 1) * 128],
                    ident_f[0:rows, 0:rows]), r=[("xstage", s), "cf"], w=[("ps", b)])
            src_ps = lambda b=b, rows=rows: ps[b][:, :].rearrange("p (a c) -> p a c", a=4)[:, :, 0:rows]
            dst = hT[:, q4 * 4:(q4 + 1) * 4, c0:c0 + rows]
            if q4 % 2 == 0:
                act(lambda e, dst=dst, src_ps=src_ps: e.copy(out=dst, in_=src_ps()), r=[("ps", b)], w=[("hT", q4, i)])
            else:
                dve(lambda e, dst=dst, src_ps=src_ps: e.tensor_copy(out=dst, in_=src_ps()), r=[("ps", b)], w=[("hT", q4, i)])
    P.fence()

    sqs = [Zt([344], BF16) for _ in range(4)]
    rstdn = Zt([NT], F32)
    fT = aT[:, :, 0:NT]

    class NormState:
        pass

    def norm_begin(tag):
        st_ = NormState()
        st_.tag = tag
        st_.banks = [bank_m() for _ in TT]
        st_.pending = None
        st_.cnt = 0
        return st_

    def norm_flush(st_):
        if st_.pending is not None:
            slot, m, ti, n = st_.pending
            pe(lambda e: e.matmul(ps[st_.banks[ti]][:, 0:n], lhsT=cbv("onesD"), rhs=sqs[slot][:, 0:n],
                                  start=(m == 0), stop=(m == 15)),
               r=[("sqs", slot), "cb"], w=[("ps", st_.banks[ti])])
            st_.pending = None

    def norm_tile(st_, m, ti, c0, n):
        norm_flush(st_)
        slot = st_.cnt % 4
        st_.cnt += 1
        act(lambda e: e.activation(out=sqs[slot][:, 0:n], in_=hT[:, m, c0:c0 + n], func=AF.Square),
            r=[("hTm", m, c0)], w=[("sqs", slot)])
        st_.pending = (slot, m, ti, n)

    def norm_finish(st_, gname):
        tag = st_.tag
        norm_flush(st_)
        for ti, (c0, n) in enumerate(TT):
            act(lambda e, ti=ti, c0=c0, n=n: e.activation(out=rstdn[:, c0:c0 + n], in_=ps[st_.banks[ti]][:, 0:n], func=AF.Ln,
                                                          bias=epsv, scale=1.0),
                r=[("ps", st_.banks[ti])], w=[(tag, "rstdn0", ti)])
            act(lambda e, ti=ti, c0=c0, n=n: e.activation(out=rstdn[:, c0:c0 + n], in_=rstdn[:, c0:c0 + n], func=AF.Exp,
                                                          scale=-0.5),
                r=[(tag, "rstdn0", ti)], w=[(tag, "rstdn")])
        gv = cfv(gname)
        for k in range(16):
            dve(lambda e, k=k: e.scalar_tensor_tensor(out=fT[:, k, :], in0=hT[:, k, :], scalar=gv[:, k:k + 1], in1=rstdn,
                                                      op0=ALU.mult, op1=ALU.mult),
                r=[(tag, "rstdn"), "cf"] + [("hTm", k, c0) for (c0, n) in TT], w=[("fT", k)])

    mix_rhs = lambda k, c0, n: mixT[:, k, c0:c0 + n]
    n2 = norm_begin("n2")
    for j in range(8):
        def evac_out(ml, ti, c0, n, b, j=j):
            m = 2 * j + ml
            dve(lambda e: e.tensor_tensor(out=hT[:, m, c0:c0 + n], in0=ps[b][:, 0:n], in1=hT[:, m, c0:c0 + n], op=ALU.add),
                r=[("ps", b), ("hTm", m, c0)], w=[("hTm", m, c0)])
            norm_tile(n2, m, ti, c0, n)
        dense_block(16, mix_rhs, ["mixT"], evac_out)
    P.fence()
    dump("h1", hT, [])
    chk(7)

    norm_finish(n2, "fn_g")
    P.fence()
    dump("fT", fT, [])
    chk(8)

    uT = mixT
    sgt = [Zt([344], F32) for _ in range(2)]
    f_rhs = lambda k, c0, n: fT[:, k, c0:c0 + n]
    cnt_s = {"i": 0}
    for qi, (b0, nb) in enumerate(QUART):
        for bb in range(nb):
            wg, wgres = w_next()
            wu, wures = w_next()
            for ml in range(2):
                cl = 2 * bb + ml
                for ti, (c0, n) in enumerate(TT):
                    bg = bank_d(); bu = bank_d()
                    for (bk_, wv, wres) in ((bg, wg, wgres), (bu, wu, wures)):
                        for k in range(16):
                            pe(lambda e, bk_=bk_, wv=wv, k=k, ml=ml, c0=c0, n=n: e.matmul(
                                ps[bk_][:, 0:n], lhsT=wv[:, k, ml * 128:(ml + 1) * 128], rhs=fT[:, k, c0:c0 + n],
                                start=(k == 0), stop=(k == 15)), r=[wres], w=[("ps", bk_)])
                    s = cnt_s["i"] % 2
                    cnt_s["i"] += 1
                    act(lambda e, bg=bg, s=s, n=n: e.activation(out=sgt[s][:, 0:n], in_=ps[bg][:, 0:n], func=AF.Silu),
                        r=[("ps", bg)], w=[("sgt", s)])
                    dve(lambda e, bu=bu, s=s, n=n, cl=cl, c0=c0: e.tensor_tensor(out=uT[:, cl, c0:c0 + n], in0=ps[bu][:, 0:n],
                                                                               in1=sgt[s][:, 0:n], op=ALU.mult),
                        r=[("ps", bu), ("sgt", s)], w=[("uT", qi)])
            w_done(2)
        kq = 2 * nb
        u_rhs = lambda k, c0, n: uT[:, k, c0:c0 + n]
        if qi == 3:
            n3 = norm_begin("n3")
        for j in range(8):
            def evac_dn(ml, ti, c0, n, b, j=j):
                m = 2 * j + ml
                dve(lambda e: e.tensor_tensor(out=hT[:, m, c0:c0 + n], in0=ps[b][:, 0:n], in1=hT[:, m, c0:c0 + n],
                                              op=ALU.add), r=[("ps", b), ("hTm", m, c0)], w=[("hTm", m, c0)])
                if qi == 3:
                    norm_tile(n3, m, ti, c0, n)
            dense_block(kq, u_rhs, [("uT", qi)], evac_dn)
    P.fence()
    dump("h2", hT, [])
    chk(9)

    norm_finish(n3, "pn_g")
    pst = [uT[:, 0:1, :].rearrange("p a c -> p (a c)").bitcast(F32)[:, 0:256],
           uT[:, 1:2, :].rearrange("p a c -> p (a c)").bitcast(F32)[:, 0:256]]
    peT = uT[:, 4:6, :]
    tiles_p = [(p_main[i * 128:(i + 1) * 128, :], 128, i * 128) for i in range(8)] + [(p_smp, NS, NP_)]
    for i, (src, rows, c0) in enumerate(tiles_p):
        s = i % 2
        dma("sp", "x%d" % s, pst[s][0:rows, :], src, w=[("pst", s)])
        b = bank_m()
        for kk in range(2):
            pe(lambda e, b=b, kk=kk, s=s, rows=rows: e.transpose(ps[b][:, kk * 128:kk * 128 + rows],
                                                                pst[s][0:rows, kk * 128:(kk + 1) * 128],
                                                                ident_f[0:rows, 0:rows]), r=[("pst", s), "cf"], w=[("ps", b)])
        act(lambda e, b=b, rows=rows, c0=c0: e.copy(out=peT[:, :, c0:c0 + rows],
                                                   in_=ps[b][:, 0:256].rearrange("p (a c) -> p a c", a=2)[:, :, 0:rows]),
            r=[("ps", b)], w=["peT"])
    P.fence()
    wple = uT[:, 8:12, :].rearrange("p a c -> p (a c)")[:, 0:4096].rearrange("p (k n) -> p k n", k=2)
    wpleres = "wple"
    dma("pool", "wple", wple, w_ple.rearrange("(k p) n -> p k n", p=128), w=["wple"])
    sB = [uT[:, 12 + i, :].bitcast(F32)[:, 0:344] for i in range(2)]
    tA = [uT[:, 14 + i, :].bitcast(F32)[:, 0:344] for i in range(2)]
    for j in range(8):
        wv, wres = w_next()
        for ml in range(2):
            m = 2 * j + ml
            for ti, (c0, n) in enumerate(TT):
                bA = 2 * (cnt_s["i"] % 4); bB = bA + 1
                for k in range(2):
                    pe(lambda e, bA=bA, k=k, m=m, c0=c0, n=n: e.matmul(ps[bA][:, 0:n], lhsT=wple[:, k, m * 128:(m + 1) * 128],
                                                                      rhs=peT[:, k, c0:c0 + n], start=(k == 0), stop=(k == 1)),
                       r=[wpleres, "peT"], w=[("ps", bA)])
                for k in range(16):
                    pe(lambda e, bB=bB, k=k, ml=ml, c0=c0, n=n, wv=wv: e.matmul(
                        ps[bB][:, 0:n], lhsT=wv[:, k, ml * 128:(ml + 1) * 128], rhs=fT[:, k, c0:c0 + n],
                        start=(k == 0), stop=(k == 15)), r=[wres], w=[("ps", bB)])
                s = cnt_s["i"] % 2
                cnt_s["i"] += 1
                act(lambda e, bB=bB, s=s, n=n: e.activation(out=sB[s][:, 0:n], in_=ps[bB][:, 0:n], func=AF.Sigmoid),
                    r=[("ps", bB)], w=[("sB", s)])
                dve(lambda e, bA=bA, s=s, n=n: e.tensor_tensor(out=tA[s][:, 0:n], in0=ps[bA][:, 0:n], in1=sB[s][:, 0:n],
                                                              op=ALU.mult), r=[("ps", bA), ("sB", s)], w=[("tA", s)])
                dve(lambda e, s=s, n=n, m=m, c0=c0: e.tensor_tensor(out=hT[:, m, c0:c0 + n], in0=tA[s][:, 0:n],
                                                                   in1=hT[:, m, c0:c0 + n], op=ALU.add),
                    r=[("tA", s), ("hTm", m, c0)], w=[("hTm", m, c0)])
        w_done()
    P.fence()
    dump("h3", hT, [])
    chk(10)

    ystage = [aT_f32[:, j * D:(j + 1) * D] for j in range(4)]
    tiles_y = [(y_main[i * 128:(i + 1) * 128, :], 128, i * 128) for i in range(8)] + [(y_smp, NS, NP_)]
    for i, (dst, rows, c0) in enumerate(tiles_y):
        s = i % 4
        for q4 in range(4):
            b = bank_m()
            for kk in range(4):
                k = q4 * 4 + kk
                pe(lambda e, b=b, kk=kk, k=k, rows=rows, c0=c0: e.transpose(ps[b][0:rows, kk * 128:(kk + 1) * 128],
                                                                           hT[:, k, c0:c0 + rows], ident_f),
                   r=["cf"], w=[("ps", b)])
            if q4 % 2 == 0:
                act(lambda e, b=b, s=s, rows=rows, q4=q4: e.copy(out=ystage[s][0:rows, q4 * 512:(q4 + 1) * 512],
                                                                in_=ps[b][0:rows, :]), r=[("ps", b)], w=[("ystage", s, q4)])
            else:
                dve(lambda e, b=b, s=s, rows=rows, q4=q4: e.tensor_copy(out=ystage[s][0:rows, q4 * 512:(q4 + 1) * 512],
                                                                       in_=ps[b][0:rows, :]), r=[("ps", b)],
                    w=[("ystage", s, q4)])
        dma("sp", "y%d" % s, dst, ystage[s][0:rows, :], r=[("ystage", s, q4) for q4 in range(4)], w=[("y", i)])

    assert stop != 99 or wstate["used"] == len(wq), (wstate, len(wq))

    sems = {e: es.enter_context(nc.semaphore("s_" + e)) for e in Prog.ENGS}
    dma_sems = {k: es.enter_context(nc.semaphore("d_" + k)) for k in sorted(dma_sem_names)}
    block = es.enter_context(nc.Block())
    finals = sorted(dma_sem_names)
    P.emit(nc, block, sems, dma_sems, finals)
    es.close()
    print("ops", len(P.ops), "counts", P.final_counts[0])
    return nc


_CACHE = {}


def kernel(**inputs):
    inp = {k: np.asarray(v) for k, v in inputs.items()}
    if "nc" not in _CACHE:
        _CACHE["nc"] = build_program()
    nc = _CACHE["nc"]
    xp = inp["x_prompt"]; xs = inp["x_sample"]
    in_maps = []
    zeros_halo = np.zeros((NHALO, D), np.float32)
    shared = dict(
        an_g=np.ascontiguousarray(inp["attn_norm_g"].reshape(1, D)),
        w_in=np.ascontiguousarray(inp["w_in"][0]), w_out=np.ascontiguousarray(inp["w_out"][0]),
        w_gate=np.ascontiguousarray(inp["w_gate"][0]), w_up=np.ascontiguousarray(inp["w_up"][0]),
        w_down=np.ascontiguousarray(inp["w_down"][0]), w_ple=np.ascontiguousarray(inp["w_ple"][0]),
        w_pg=np.ascontiguousarray(inp["w_ple_gate"][0]),
    )
    for c in range(8):
        b, m = c // 4, c % 4
        t0 = m * 1024
        cf, cb = host_consts(c, inp)
        d = dict(shared)
        d.update(
            x_main=np.ascontiguousarray(xp[b, t0:t0 + 1024]),
            x_halo=np.ascontiguousarray(xp[b, t0 - 128:t0]) if m > 0 else zeros_halo,
            x_smp=np.ascontiguousarray(xs[4 * c:4 * c + 4, 0]),
            p_main=np.ascontiguousarray(inp["p_prompt"][0, b, t0:t0 + 1024]),
            p_smp=np.ascontiguousarray(inp["p_sample"][0, 4 * c:4 * c + 4, 0]),
            ck=np.ascontiguousarray(inp["cache_k_win"][0, 4 * c:4 * c + 4]),
            cv=np.ascontiguousarray(inp["cache_v_win"][0, 4 * c:4 * c + 4]),
            st=np.ascontiguousarray(inp["state_ret"][0, 4 * c:4 * c + 4]),
            cf=cf, cb=cb,
        )
        in_maps.append(d)
    res = run_bass_kernel_spmd(nc, in_maps, core_ids=list(range(8)))
    R = res.results
    _CACHE["last"] = R
    y_p = np.stack([np.concatenate([R[4 * b + m]["y_main"] for m in range(4)], 0) for b in range(2)], 0)
    y_s = np.concatenate([R[c]["y_smp"] for c in range(8)], 0)[:, None, :]
    kwp = np.stack([R[4 * b + 3]["kwin"] for b in range(2)], 0)[None]
    vwp = np.stack([R[4 * b + 3]["vwin"] for b in range(2)], 0)[None]
    rsp = np.stack([R[4 * b + 3]["rstate"] for b in range(2)], 0)[None]
    kws = np.concatenate([R[c]["ks_out"] for c in range(8)], 0)[None]
    vws = np.concatenate([R[c]["vs_out"] for c in range(8)], 0)[None]
    rss = np.concatenate([R[c]["ss_out"] for c in range(8)], 0)[None]
    return (y_p.astype(np.float32), y_s.astype(np.float32), kwp.astype(np.float32), vwp.astype(np.float32),
            rsp.astype(np.float32), kws.astype(np.float32), vws.astype(np.float32), rss.astype(np.float32))
```

```python
import math
from contextlib import ExitStack

import numpy as np
import ml_dtypes

import concourse.bass as bass
import concourse.mybir as mybir
from concourse.bass_utils import run_bass_kernel_spmd

F32 = mybir.dt.float32
BF16 = mybir.dt.bfloat16
U8 = mybir.dt.uint8
AF = mybir.ActivationFunctionType
ALU = mybir.AluOpType
AX = mybir.AxisListType

D = 2048
NP_ = 1024
NS = 4
NT = NP_ + NS
NHALO = 128
NA = NT + NHALO
KC = 16
DFF = 5632
EPS = 1e-6
ATTN_SCALE = 128 ** -0.5
RET_K_SCALE = 128 ** -0.5
PAST_LEN = 16384
TT = [(0, 344), (344, 344), (688, 340)]
TT_H = TT + [(NT, NHALO)]
NSLOT = 4
SLOT_BYTES = 8192
GAMMA = [1.0 - 2.0 ** (-5 - h) for h in range(8)]


class Op:
    __slots__ = ("eng", "fn", "dma", "idx", "waits", "sig", "val", "fence")

    def __init__(self, eng, fn, dma, idx):
        self.eng = eng
        self.fn = fn
        self.dma = dma
        self.idx = idx
        self.waits = []
        self.sig = dma is not None
        self.val = None
        self.fence = None


class Prog:
    ENGS = ("sp", "act", "dve", "pool", "pe")

    def __init__(self):
        self.ops = []
        self.last_w = {}
        self.readers = {}
        self.last_on = {}
        self.dma_ops = {}
        self.pending_fence = {}

    def add(self, eng, fn, r=(), w=(), dma=None, nofence=False):
        op = Op(eng, fn, dma, len(self.ops))
        psr = [x for x in r if isinstance(x, tuple) and x[0] == "ps"]
        if psr:
            r = [x for x in r if not (isinstance(x, tuple) and x[0] == "ps")]
            w = list(w) + psr
        deps = {}
        for res in r:
            lw = self.last_w.get(res)
            if lw is not None:
                deps.setdefault(lw, set()).add("raw")
        for res in w:
            lw = self.last_w.get(res)
            if lw is not None:
                deps.setdefault(lw, set()).add("waw")
            for rd in self.readers.get(res, ()):
                deps.setdefault(rd, set()).add("war")
        best = {}
        for d, kinds in deps.items():
            if d is op:
                continue
            if d.dma is not None:
                op.waits.append(d)
                continue
            if op.dma is None and d.eng == eng:
                if eng == "pe":
                    continue
            b = best.get(d.eng)
            if b is None or d.idx > b.idx:
                best[d.eng] = d
        for d in best.values():
            d.sig = True
            op.waits.append(d)
        for res in r:
            self.readers.setdefault(res, []).append(op)
        for res in w:
            self.last_w[res] = op
            self.readers[res] = []
        if eng in self.pending_fence and not nofence:
            op.fence = self.pending_fence.pop(eng)
        self.ops.append(op)
        if dma is None:
            self.last_on[eng] = op
        else:
            self.dma_ops.setdefault(dma, []).append(op)
        return op

    def fence(self):
        st = {"comp": dict(self.last_on),
              "dma": {k: v[-1] for k, v in self.dma_ops.items() if not (k[0] == "w" and k[1:].isdigit())}}
        for o in st["comp"].values():
            o.sig = True
        for e in self.ENGS:
            self.pending_fence[e] = st

    def emit(self, nc, block, sems, dma_sems, final_waits):
        cnt = {e: 0 for e in self.ENGS}
        dcnt = {}
        for op in self.ops:
            if op.dma is not None:
                dcnt[op.dma] = dcnt.get(op.dma, 0) + (1 if op.dma == "cc" else 16)
                op.val = dcnt[op.dma]
            elif op.sig:
                cnt[op.eng] += 1
                op.val = cnt[op.eng]
        self.final_counts = (cnt, dcnt)

        def semof(op):
            return dma_sems[op.dma] if op.dma is not None else sems[op.eng]

        def run(engname):
            def body(eng):
                waited = {}

                def wait(sem_key, sem, val):
                    if waited.get(sem_key, 0) >= val:
                        return
                    waited[sem_key] = val
                    eng.wait_ge(sem, val)

                for op in self.ops:
                    if op.eng != engname:
                        continue
                    if op.fence is not None:
                        for o in op.fence["comp"].values():
                            if o.eng != engname or engname != "pe":
                                wait(("c", o.eng), sems[o.eng], o.val)
                        for k, o in op.fence["dma"].items():
                            wait(("d", k), dma_sems[k], o.val)
                    for d in op.waits:
                        key = ("d", d.dma) if d.dma is not None else ("c", d.eng)
                        wait(key, semof(d), d.val)
                    ins = op.fn(eng)
                    if op.dma is not None:
                        ins.then_inc(dma_sems[op.dma], 1 if op.dma == "cc" else 16)
                    elif op.sig:
                        ins.then_inc(sems[op.eng], 1)
                if engname == "sp":
                    for k in final_waits:
                        if k in dcnt:
                            eng.wait_ge(dma_sems[k], dcnt[k])
            return body

        block.sync(run("sp"))
        block.scalar(run("act"))
        block.vector(run("dve"))
        block.gpsimd(run("pool"))
        block.tensor(run("pe"))


CF = {}
_off = 0
for _n, _w in [("ident", 128), ("ones_row", 128), ("fn_g", 16), ("pn_g", 16), ("rg", 8), ("qg", 1), ("kg", 1),
               ("sinks", 8), ("intraT", 1024), ("qdec", 1024), ("kdec", 8), ("kdlong", 64), ("coef", 32),
               ("onehot", 4), ("cos", NT), ("sin", NT)]:
    CF[_n] = (_off, _w)
    _off += _w
NCF = _off
CB = {}
_off = 0
for _n, _w in [("ident", 128), ("ones", 128), ("onesD", 128), ("ones128", 128), ("rrot", 128), ("mask", 384)]:
    CB[_n] = (_off, _w)
    _off += _w
NCB = _off


def host_consts(core, inp):
    m = core % 4
    cf = np.zeros((128, NCF), np.float32)

    def put(name, arr):
        o, w = CF[name]
        cf[:, o:o + w] = np.asarray(arr, np.float32).reshape(128, w) if np.ndim(arr) == 2 else np.broadcast_to(
            np.asarray(arr, np.float32).reshape(1, w), (128, w))

    put("ident", np.eye(128, dtype=np.float32))
    put("ones_row", np.ones((128, 128), np.float32))
    put("fn_g", inp["ffn_norm_g"][0].reshape(16, 128).T)
    put("pn_g", inp["ple_norm_g"][0].reshape(16, 128).T)
    put("rg", inp["ret_out_g"][0].reshape(8, 128).T)
    put("qg", inp["q_norm_g"][0].reshape(128, 1))
    put("kg", inp["k_norm_g"][0].reshape(128, 1))
    put("sinks", inp["attn_sinks"][0].reshape(8))
    g = np.array(GAMMA, np.float64)
    j = np.arange(128)
    diff = j[None, :] - j[:, None]
    intraT = np.where(diff[:, None, :] >= 0, g[None, :, None] ** np.maximum(diff, 0)[:, None, :], 0.0) * RET_K_SCALE
    put("intraT", intraT.reshape(128, 1024))
    qdec = g[:, None] ** (j[None, :] + 1.0)
    put("qdec", qdec.reshape(1024))
    put("kdec", (g[None, :] ** (127.0 - j[:, None])) * RET_K_SCALE)
    c = np.arange(8)
    kdl = g[None, :, None] ** (1023.0 - (128.0 * c[None, None, :] + j[:, None, None])) * RET_K_SCALE
    put("kdlong", kdl.reshape(128, 64))
    coef = np.zeros((4, 8))
    for r in range(4):
        if r < m:
            coef[r] = g ** (1024.0 * (m - r - 1))
    put("coef", coef.reshape(32))
    oh = np.zeros((128, 4), np.float32)
    oh[:4] = np.eye(4)
    put("onehot", oh)
    pos = np.concatenate([m * 1024 + np.arange(1024), np.full(4, PAST_LEN)]).astype(np.float32)
    inv = (np.float32(10000.0) ** (-np.arange(64, dtype=np.float32) / np.float32(64))).astype(np.float32)
    ang = (pos[None, :] * inv[:, None]).astype(np.float32).astype(np.float64)
    cos = np.cos(ang)
    sin = np.sin(ang)
    put("cos", np.concatenate([cos, cos], 0))
    put("sin", np.concatenate([-sin, sin], 0))

    cb = np.zeros((128, NCB), np.float32)

    def putb(name, arr):
        o, w = CB[name]
        cb[:, o:o + w] = arr

    putb("ident", np.eye(128))
    putb("ones", np.ones((128, 128)))
    putb("onesD", np.full((128, 128), 1.0 / D))
    putb("ones128", np.full((128, 128), 1.0 / 128))
    rr = np.zeros((128, 128))
    for p in range(128):
        rr[(p + 64) % 128, p] = 1.0
    putb("rrot", rr)
    NEG = -30000.0
    own = np.where(j[:, None] <= j[None, :], 0.0, NEG).astype(np.float32)
    prev = np.where(j[:, None] >= j[None, :], 0.0, NEG).astype(np.float32)
    putb("mask", np.concatenate([own, prev, prev if m != 0 else np.full((128, 128), NEG, np.float32)], 1))
    return cf, cb.astype(ml_dtypes.bfloat16)


class _Stop(Exception):
    pass


def build_program(dbg=None, stop=99, groups=None):
    nc = bass.Bass("TRN2", target_bir_lowering=False)
    P = Prog()
    dbg = dbg or {}

    stopped = [False]

    def chk(k):
        if stop == k:
            stopped[0] = True

    def din(name, shape, dt=F32):
        return nc.dram_tensor(name, list(shape), dt, kind="ExternalInput").ap()

    def dout(name, shape, dt=F32):
        return nc.dram_tensor(name, list(shape), dt, kind="ExternalOutput").ap()

    x_main = din("x_main", [NP_, D]); x_halo = din("x_halo", [NHALO, D]); x_smp = din("x_smp", [NS, D])
    p_main = din("p_main", [NP_, 256]); p_smp = din("p_smp", [NS, 256])
    ck = din("ck", [NS, 128, 2, 128]); cv = din("cv", [NS, 128, 2, 128]); st = din("st", [NS, 8, 128, 128])
    an_g = din("an_g", [1, D])
    w_in = din("w_in", [D, DFF]); w_out = din("w_out", [D, D]); w_gate = din("w_gate", [D, DFF])
    w_up = din("w_up", [D, DFF]); w_down = din("w_down", [DFF, D]); w_ple = din("w_ple", [256, D])
    w_pg = din("w_pg", [D, D])
    cf_d = din("cf", [128, NCF]); cb_d = din("cb", [128, NCB], BF16)

    y_main = dout("y_main", [NP_, D]); y_smp = dout("y_smp", [NS, D])
    kwin = dout("kwin", [128, 2, 128]); vwin = dout("vwin", [128, 2, 128]); rstate = dout("rstate", [8, 128, 128])
    ks_out = dout("ks_out", [NS, 128, 2, 128]); vs_out = dout("vs_out", [NS, 128, 2, 128])
    ss_out = dout("ss_out", [NS, 8, 128, 128])
    krs = nc.dram_tensor("krs", [8, 128, NT], BF16)
    vts = nc.dram_tensor("vts", [8, 128, NT], BF16)
    ag_in = nc.dram_tensor("ag_in", [8 * 128, 128], F32)
    ag_out = nc.dram_tensor("ag_out", [4 * 8 * 128, 128], F32)
    dbg_out = {}
    for name, shape in dbg.items():
        dbg_out[name] = dout("dbg_" + name, shape)

    def dump(name, ap, r=()):
        if name in dbg_out:
            P.add("pool", lambda e: e.dma_start(out=dbg_out[name], in_=ap), r, [("dbg", name)], dma="o_dbg")

    es = ExitStack()
    total = (nc.sbuf_bytes_remaining - 64) // 64 * 64
    arena = es.enter_context(nc.sbuf_tensor("arena", [128, total], U8))
    ps = [es.enter_context(nc.psum_tensor("ps%d" % i, [128, 512], F32)) for i in range(8)]

    class Alloc:
        def __init__(self, base, limit):
            self.p = base
            self.limit = limit

        def __call__(self, shape, dt):
            esz = 4 if dt == F32 else 2
            n = int(np.prod(shape)) * esz
            off = (self.p + 31) // 32 * 32
            self.p = off + n
            assert self.p <= self.limit, (self.p, self.limit)
            v = arena[:, off:off + n].bitcast(dt)
            if len(shape) == 2:
                return v.rearrange("p (a b) -> p a b", a=shape[0])
            if len(shape) == 3:
                return v.rearrange("p (a b c) -> p a b c", a=shape[0], b=shape[1])
            return v

    A = Alloc(0, total)
    cf = A([NCF], F32)
    cb = A([NCB], BF16)
    negc = A([1], F32); esink = A([8], F32); S0 = A([8, 128], F32)
    misc = A([64], F32)
    wslots = [A([SLOT_BYTES // 2], BF16) for _ in range(NSLOT)]
    aT = A([KC, NA], BF16)
    mixT = A([KC, NT], BF16)
    T2base = A.p
    T2 = Alloc(T2base, T2base + 12 * 1024)
    A.p = T2base + 12 * 1024
    HB = (A.p + 31) // 32 * 32
    hT = A([KC, NT], F32)
    HEND = A.p
    print("sbuf used", A.p, "of", total)

    def cfv(name, lo=0, hi=None):
        o, w = CF[name]
        return cf[:, o + lo:o + (w if hi is None else hi)]

    def cbv(name, lo=0, hi=None):
        o, w = CB[name]
        return cb[:, o + lo:o + (w if hi is None else hi)]

    ident_f = cfv("ident"); ident_b = cbv("ident")

    def psb(i, n=1024):
        return ps[i][:, :].bitcast(BF16)[:, 0:n]

    def pe(fn, r=(), w=()): return P.add("pe", fn, r, w)
    def act(fn, r=(), w=()): return P.add("act", fn, r, w)
    def dve(fn, r=(), w=()): return P.add("dve", fn, r, w)
    def pool(fn, r=(), w=()): return P.add("pool", fn, r, w)
    def dma(q, sem, out, in_, r=(), w=(), nofence=False, slow=False):
        if slow:
            return P.add(q, lambda e: e.dma_start(out=out, in_=in_, allow_slow_non_contiguous=True), r, w, dma=sem)
        return P.add(q, lambda e: e.dma_start(out=out, in_=in_), r, w, dma=sem, nofence=nofence)

    dma_sem_names = set()
    _orig_add = P.add

    def add_track(eng, fn, r=(), w=(), dma=None, nofence=False):
        if stopped[0]:
            return None
        if dma is not None:
            dma_sem_names.add(dma)
        return _orig_add(eng, fn, r, w, dma, nofence)
    P.add = add_track

    rot = {"d": 0, "m": 0}

    att_dense3 = [False]

    def bank_d():
        b = rot["d"] % (3 if att_dense3[0] else 4)
        rot["d"] += 1
        return b

    def bank_m():
        b = 4 + rot["m"] % 4
        rot["m"] += 1
        return b

    wq = []
    wstate = {"issued": 0, "used": 0, "done": 0}
    w_extra = []

    def w_issue_upto(n):
        while wstate["issued"] < min(n, len(wq)):
            i = wstate["issued"]
            src, kc, ncols = wq[i]
            s = i % NSLOT
            dst = wslots[s][:, 0:kc * ncols].rearrange("p (k n) -> p k n", k=kc)
            dma("pool", "w%d" % s, dst, src.rearrange("(k p) n -> p k n", p=128), r=list(w_extra), w=[("w", s)], nofence=True)
            wstate["issued"] += 1

    def w_done(n=1):
        wstate["done"] += n
        w_issue_upto(wstate["done"] + NSLOT)

    def w_next():
        i = wstate["used"]
        assert i < wstate["done"] + NSLOT
        w_issue_upto(i + 1)
        src, kc, ncols = wq[i]
        s = i % NSLOT
        wstate["used"] += 1
        return wslots[s][:, 0:kc * ncols].rearrange("p (k n) -> p k n", k=kc), ("w", s)

    def wblk(wap, k0, k1, c0, ncols):
        return (wap[k0 * 128:k1 * 128, c0:c0 + ncols], k1 - k0, ncols)

    C_AQ, C_AK, C_AV, C_RQ, C_RK, C_RV, C_RG = 0, 1024, 1280, 1536, 2560, 3584, 4608
    for j in range(4):
        wq.append(wblk(w_in, 0, 16, C_RK + 256 * j, 256))
        wq.append(wblk(w_in, 0, 16, C_RV + 256 * j, 256))
    wq.append(wblk(w_in, 0, 16, C_AK, 256))
    wq.append(wblk(w_in, 0, 16, C_AV, 256))
    for j in range(4):
        wq.append(wblk(w_in, 0, 16, C_AQ + 256 * j, 256))
    for j in range(4):
        for cbase in (C_RQ, C_RG):
            wq.append(wblk(w_in, 0, 16, cbase + 256 * j, 256))
    for j in range(8):
        wq.append(wblk(w_out, 0, 16, 256 * j, 256))
    QUART = [(0, 6), (6, 6), (12, 5), (17, 5)]
    for (b0, nb) in QUART:
        for b in range(b0, b0 + nb):
            wq.append(wblk(w_gate, 0, 16, 256 * b, 256))
            wq.append(wblk(w_up, 0, 16, 256 * b, 256))
        for j in range(8):
            wq.append(wblk(w_down, 2 * b0, 2 * (b0 + nb), 256 * j, 256))
    for j in range(8):
        wq.append(wblk(w_pg, 0, 16, 256 * j, 256))

    def dense_block(kc, rhs_fn, rhs_res, evac, tiles=TT, nm=2):
        wv, wres = w_next()
        for ml in range(nm):
            for ti, (c0, n) in enumerate(tiles):
                b = bank_d()
                for k in range(kc):
                    pe(lambda e, b=b, k=k, ml=ml, c0=c0, n=n, wv=wv: e.matmul(
                        ps[b][:, 0:n], lhsT=wv[:, k, ml * 128:(ml + 1) * 128], rhs=rhs_fn(k, c0, n),
                        start=(k == 0), stop=(k == kc - 1)),
                       r=[wres] + list(rhs_res), w=[("ps", b)])
                evac(ml, ti, c0, n, b)
        w_done()

    dma("sp", "c_cb", cb, cb_d, w=["cb"])

    Z = Alloc(HB, total)
    NXT = 4
    xt = [Z([D], F32) for _ in range(NXT)]
    junk = Z([D], BF16)
    xn = [Z([D], BF16) for _ in range(2)]
    gbc = Z([D], F32)
    ss = misc[:, 0:10]; rstd1 = misc[:, 10:20]; tmpa = misc[:, 20:30]
    gpa = misc[:, 30:31]; mx = misc[:, 31:32]; negc1 = misc[:, 32:33]; epsv = misc[:, 34:35]
    dve(lambda e: e.memset(epsv, EPS), w=["epsv"])


    tiles1 = [(x_main[i * 128:(i + 1) * 128, :], 128, i * 128) for i in range(8)]
    tiles1.append((x_halo, 128, NT))
    tiles1.append((x_smp, NS, NP_))
    def p1_load(i, src, rows, c0):
        s4 = i % NXT
        dma("sp", "x%d" % s4, xt[s4][0:rows, :], src, w=[("xt", s4)])

    def p1_stage1(i, src, rows, c0):
        s = i % 2
        s4 = i % NXT
        act(lambda e, s4=s4, rows=rows, i=i: e.activation(out=junk[0:rows, :], in_=xt[s4][0:rows, :], func=AF.Square,
                                                         accum_out=ss[0:rows, i:i + 1]),
            r=[("xt", s4)], w=["junk", ("ss", i)])
        act(lambda e, rows=rows, i=i: e.activation(out=tmpa[0:rows, i:i + 1], in_=ss[0:rows, i:i + 1], func=AF.Sqrt,
                                                   bias=epsv[0:rows, :], scale=1.0 / D),
            r=[("ss", i), "epsv"], w=[("tmpa", i)])
        dve(lambda e, rows=rows, i=i: e.reciprocal(out=rstd1[0:rows, i:i + 1], in_=tmpa[0:rows, i:i + 1]),
            r=[("tmpa", i)], w=[("rstd1", i)])
        dve(lambda e, s=s, s4=s4, rows=rows, i=i: e.scalar_tensor_tensor(
            out=xn[s][0:rows, :], in0=xt[s4][0:rows, :], scalar=rstd1[0:rows, i:i + 1], in1=gbc[0:rows, :],
            op0=ALU.mult, op1=ALU.mult), r=[("xt", s4), ("rstd1", i), "gbc"], w=[("xn", s)])

    def p1_stage2(i, src, rows, c0):
        s = i % 2
        for half in range(2):
            b = bank_m()
            for kk in range(8):
                k = half * 8 + kk
                pe(lambda e, b=b, kk=kk, k=k, s=s, rows=rows: e.transpose(
                    psb(b)[:, kk * 128:kk * 128 + rows], xn[s][0:rows, k * 128:(k + 1) * 128],
                    ident_b[0:rows, 0:rows]), r=[("xn", s), "cb"], w=[("ps", b)])
            src_ps = lambda b=b, rows=rows: psb(b).rearrange("p (a c) -> p a c", a=8)[:, :, 0:rows]
            dst = aT[:, half * 8:(half + 1) * 8, c0:c0 + rows]
            if half == 0:
                act(lambda e, dst=dst, src_ps=src_ps: e.copy(out=dst, in_=src_ps()), r=[("ps", b)], w=[("aT", c0, half)])
            else:
                dve(lambda e, dst=dst, src_ps=src_ps: e.tensor_copy(out=dst, in_=src_ps()), r=[("ps", b)], w=[("aT", c0, half)])

    p1_load(0, *tiles1[0])
    dma("sp", "c_gbc", gbc, an_g.partition_broadcast(128), w=["gbc"])
    for i in range(1, NXT):
        p1_load(i, *tiles1[i])
    dma("sp", "c_cf", cf, cf_d, w=["cf"])
    p1_stage1(0, *tiles1[0])
    for i in range(len(tiles1)):
        if i + 1 < len(tiles1):
            p1_stage1(i + 1, *tiles1[i + 1])
        if i + NXT < len(tiles1):
            p1_load(i + NXT, *tiles1[i + NXT])
        if i in (3, 5, 7, 8):
            w_extra[:] = [("xt", (i + 1) % NXT)]
            w_issue_upto(wstate["issued"] + 1)
            w_extra[:] = []
        p1_stage2(i, *tiles1[i])
    w_issue_upto(NSLOT)
    dve(lambda e: e.tensor_tensor(out=gpa, in0=cfv("qg"), in1=cfv("kg"), op=ALU.mult), r=["cf"], w=["gpa"])
    gpa2 = misc[:, 33:34]
    dve(lambda e: e.tensor_tensor(out=gpa2, in0=gpa, in1=gpa, op=ALU.mult), r=["gpa"], w=["gpa2"])
    b = bank_m()
    pe(lambda e, b=b: e.transpose(ps[b][0:1, 0:128], gpa2, ident_f), r=["gpa2", "cf"], w=[("ps", b)])
    dve(lambda e, b=b: e.tensor_reduce(out=mx[0:1, :], in_=ps[b][0:1, 0:128], axis=AX.X, op=ALU.max),
        r=[("ps", b)], w=["mx"])
    act(lambda e: e.activation(out=mx[0:1, :], in_=mx[0:1, :], func=AF.Sqrt), r=["mx"], w=["mx"])
    dve(lambda e: e.tensor_scalar(out=negc1[0:1, :], in0=mx[0:1, :], scalar1=-(ATTN_SCALE * 128.0), scalar2=None,
                                  op0=ALU.mult), r=["mx"], w=["negc1"])
    b = bank_m()
    pe(lambda e, b=b: e.matmul(ps[b][:, 0:1], lhsT=cfv("ones_row")[0:1, :], rhs=negc1[0:1, :], start=True, stop=True),
       r=["negc1", "cf"], w=[("ps", b)])
    dve(lambda e, b=b: e.tensor_copy(out=negc, in_=ps[b][:, 0:1]), r=[("ps", b)], w=["negc"])
    act(lambda e: e.activation(out=esink, in_=cfv("sinks"), func=AF.Exp, bias=negc, scale=1.0),
        r=["negc", "cf"], w=["esink"])

    P.fence()
    dump("aT", aT, ["aT"])
    chk(1)

    aT_rhs = lambda k, c0, n: aT[:, k, c0:c0 + n]
    cosv = cfv("cos"); sinv = cfv("sin")

    rot_pending = [None]

    def rot_flush():
        if rot_pending[0] is not None:
            f = rot_pending[0]
            rot_pending[0] = None
            f()

    def rotary_ops(tag, b, c0, n, xb, t1, t2, outb):
        act(lambda e: e.copy(out=xb[:, c0:c0 + n], in_=ps[b][:, 0:n]), r=[("ps", b)], w=[(tag, "xb", c0)])
        dve(lambda e: e.tensor_tensor(out=t1[:, c0:c0 + n], in0=ps[b][:, 0:n], in1=cosv[:, c0:c0 + n], op=ALU.mult),
            r=[("ps", b), "cf"], w=[(tag, "t1", c0)])
        rot_flush()

        def part_b():
            b2 = bank_m()
            pe(lambda e: e.matmul(ps[b2][:, 0:n], lhsT=cbv("rrot"), rhs=xb[:, c0:c0 + n], start=True, stop=True),
               r=[(tag, "xb", c0), "cb"], w=[("ps", b2)])
            dve(lambda e: e.tensor_tensor(out=t2[:, c0:c0 + n], in0=ps[b2][:, 0:n], in1=sinv[:, c0:c0 + n], op=ALU.mult),
                r=[("ps", b2), "cf"], w=[(tag, "t2", c0)])
            dve(lambda e: e.tensor_tensor(out=outb[:, c0:c0 + n], in0=t1[:, c0:c0 + n], in1=t2[:, c0:c0 + n], op=ALU.add),
                r=[(tag, "t1", c0), (tag, "t2", c0)], w=[(tag, "rot", c0)])
        rot_pending[0] = part_b

    Z = Alloc(HB, total)
    p1 = []
    for hp in range(2):
        p1.append(dict(xb=Z([NT], BF16), t1=Z([NT], F32), t2=Z([NT], F32), kr=Z([NT], BF16),
                       kD=Z([8, 128], BF16), vT=Z([NT], BF16), vtok=Z([8, 128], BF16), sloc=Z([128], F32)))
    kdl = cfv("kdlong").rearrange("p (h c) -> p h c", h=8)

    for j in range(4):
        hs = (2 * j, 2 * j + 1)

        def evac_k(ml, ti, c0, n, b, hs=hs):
            bf = p1[ml]
            rotary_ops(("p1", ml), b, c0, n, bf["xb"], bf["t1"], bf["t2"], bf["kr"])

        def evac_v(ml, ti, c0, n, b, hs=hs):
            bf = p1[ml]
            act(lambda e: e.copy(out=bf["vT"][:, c0:c0 + n], in_=ps[b][:, 0:n]), r=[("ps", b)],
                w=[("p1", ml, "vT", c0)])
        dense_block(16, aT_rhs, ["aT"], evac_k)
        chk(20)
        dense_block(16, aT_rhs, ["aT"], evac_v)
        rot_flush()
        chk(21)
        for ml in range(2):
            h = hs[ml]
            bf = p1[ml]
            rk_res = [(("p1", ml), "rot", c0) for (c0, n) in TT]
            rv_res = [("p1", ml, "vT", c0) for (c0, n) in TT]
            dma("sp", "spk%d" % ml, krs.ap()[h], bf["kr"], r=rk_res, w=[("krs", h)])
            dma("sp", "spv%d" % ml, vts.ap()[h], bf["vT"], r=rv_res, w=[("vts", h)])
            for half in range(2):
                bk = bank_m()
                for cc in range(4):
                    c = half * 4 + cc
                    pe(lambda e, bk=bk, cc=cc, c=c, bf=bf: e.transpose(
                        psb(bk)[:, cc * 128:(cc + 1) * 128], bf["kr"][:, c * 128:(c + 1) * 128], ident_b),
                       r=rk_res + ["cb"], w=[("ps", bk)])
                for cc in range(4):
                    c = half * 4 + cc
                    dve(lambda e, bk=bk, cc=cc, c=c, bf=bf, h=h: e.tensor_scalar(
                        out=bf["kD"][:, c, :], in0=psb(bk)[:, cc * 128:(cc + 1) * 128], scalar1=kdl[:, h, c:c + 1],
                        scalar2=None, op0=ALU.mult), r=[("ps", bk), "cf"], w=[("p1", ml, "kD", c)])
                bv = bank_m()
                for cc in range(4):
                    c = half * 4 + cc
                    pe(lambda e, bv=bv, cc=cc, c=c, bf=bf: e.transpose(
                        psb(bv)[:, cc * 128:(cc + 1) * 128], bf["vT"][:, c * 128:(c + 1) * 128], ident_b),
                       r=rv_res + ["cb"], w=[("ps", bv)])
                act(lambda e, bv=bv, half=half, bf=bf: e.copy(
                    out=bf["vtok"][:, half * 4:(half + 1) * 4, :],
                    in_=psb(bv)[:, 0:512].rearrange("p (a c) -> p a c", a=4)), r=[("ps", bv)], w=[("p1", ml, "vtok", half)])
            chk(22)
            bs = bank_m()
            for c in range(8):
                pe(lambda e, bs=bs, c=c, bf=bf: e.matmul(ps[bs][:, 0:128], lhsT=bf["kD"][:, c, :], rhs=bf["vtok"][:, c, :],
                                                        start=(c == 0), stop=(c == 7)),
                   r=[("p1", ml, "kD", c), ("p1", ml, "vtok", c // 4)], w=[("ps", bs)])
            dve(lambda e, bs=bs, bf=bf: e.tensor_copy(out=bf["sloc"], in_=ps[bs][:, 0:128]), r=[("ps", bs)],
                w=[("p1", ml, "sloc")])
            chk(23)
            dma("sp", "agi", ag_in.ap()[h * 128:(h + 1) * 128, :], bf["sloc"], r=[("p1", ml, "sloc")], w=["ag_in"])
            chk(24)

    dump("ag_in", ag_in.ap(), ["ag_in"])
    chk(2)
    P.fence()
    P.add("pool", lambda e: e.collective_compute("AllGather", ALU.bypass, replica_groups=groups or [[0, 1, 2, 3], [4, 5, 6, 7]],
                                                 ins=[ag_in.ap().opt()], outs=[ag_out.ap().opt()]),
          r=["ag_in"], w=["ag_out"], dma="cc")
    chk(3)

    Z = Alloc(HB, total)
    zf = [Z([NA], F32) for _ in range(2)]
    sq = [Z([NA], BF16) for _ in range(2)]
    rstdb = [Z([NA], F32)] * 2
    knT = Z([2, NA], BF16)
    kn32 = Z([2, 132], F32)
    v32 = Z([2, 132], F32)
    vTb = Z([2, NA], BF16)
    vtokA = Z([2, 9, 128], BF16)
    qnT = Z([4, NT], BF16)
    qnT2 = Z([4, NT], BF16)
    PTm = [Z([2, 512], BF16) for _ in range(3)]
    rec = [Z([512], F32) for _ in range(2)]
    win_t = Z([2, 128], F32)
    vstok = Z([2, 128], BF16)
    kc_b = [Z([128], BF16) for _ in range(NS)]
    vc_b = [Z([128], BF16) for _ in range(NS)]
    kcT = Z([4, 128], BF16)
    PTc = Z([16], BF16)
    Pn = Z([16], BF16)
    recs = Z([16], F32)
    cnt = {"qk": 0}

    qk_pending = [None]

    def qk_flush():
        if qk_pending[0] is not None:
            f = qk_pending[0]
            qk_pending[0] = None
            f()

    def qknorm(b, c0, n, gname, outbf, tagres, out32=None):
        s = cnt["qk"] % 2
        cnt["qk"] += 1
        act(lambda e: e.activation(out=sq[s][:, 0:n], in_=ps[b][:, 0:n], func=AF.Square), r=[("ps", b)], w=[("sq", s)])
        dve(lambda e: e.tensor_copy(out=zf[s][:, 0:n], in_=ps[b][:, 0:n]), r=[("ps", b)], w=[("zf", s)])
        qk_flush()

        def part_b():
            b2 = 3
            pe(lambda e: e.matmul(ps[b2][:, 0:n], lhsT=cbv("ones128"), rhs=sq[s][:, 0:n], start=True, stop=True),
               r=[("sq", s), "cb"], w=[("ps", b2)])
            act(lambda e: e.activation(out=rstdb[0][:, 0:n], in_=ps[b2][:, 0:n], func=AF.Ln, bias=epsv, scale=1.0),
                r=[("ps", b2)], w=["rstdb"])
            act(lambda e: e.activation(out=rstdb[0][:, 0:n], in_=rstdb[0][:, 0:n], func=AF.Exp, scale=-0.5),
                r=["rstdb"], w=["rstdb"])
            dve(lambda e: e.scalar_tensor_tensor(out=outbf, in0=zf[s][:, 0:n], scalar=cfv(gname), in1=rstdb[0][:, 0:n],
                                                 op0=ALU.mult, op1=ALU.mult),
                r=[("zf", s), "rstdb", "cf"], w=[tagres])
            if out32 is not None:
                lo, hi, dst = out32
                dve(lambda e: e.scalar_tensor_tensor(out=dst, in0=zf[s][:, lo:hi], scalar=cfv(gname),
                                                     in1=rstdb[0][:, lo:hi], op0=ALU.mult, op1=ALU.mult),
                    r=[("zf", s), "rstdb", "cf"], w=[("kn32", tagres)])
        qk_pending[0] = part_b

    def evac_ak(ml, ti, c0, n, b):
        o32 = None
        if ti == 2:
            o32 = (208, 340, kn32[:, ml, :])
        qknorm(b, c0, n, "kg", knT[:, ml, c0:c0 + n], ("knT", ml, c0), o32)

    def evac_av(ml, ti, c0, n, b):
        act(lambda e: e.copy(out=vTb[:, ml, c0:c0 + n], in_=ps[b][:, 0:n]), r=[("ps", b)], w=[("vTb", ml, c0)])
        if ti == 2:
            dve(lambda e: e.tensor_copy(out=v32[:, ml, :], in_=ps[b][:, 208:340]), r=[("ps", b)], w=[("v32", ml)])

    att_dense3 = [True]
    dense_block(16, aT_rhs, ["aT"], evac_ak, tiles=TT_H)
    qk_flush()
    dense_block(16, aT_rhs, ["aT"], evac_av, tiles=TT_H)
    vres = lambda g: [("vTb", g, c0) for (c0, n) in TT_H]
    kres = lambda g: [("knT", g, c0) for (c0, n) in TT_H]
    for g in range(2):
        for grp in range(3):
            blks = [0, 1, 2, 3] if grp == 0 else ([4, 5, 6, 7] if grp == 1 else [8])
            bv = bank_m()
            for ii, blk in enumerate(blks):
                col = NT if blk == 0 else (blk - 1) * 128
                pe(lambda e, bv=bv, ii=ii, col=col, g=g: e.transpose(
                    psb(bv)[:, ii * 128:(ii + 1) * 128], vTb[:, g, col:col + 128], ident_b),
                   r=vres(g) + ["cb"], w=[("ps", bv)])
            nb = len(blks)
            act(lambda e, bv=bv, g=g, blks=blks, nb=nb: e.copy(
                out=vtokA[:, g, blks[0]:blks[0] + nb, :],
                in_=psb(bv)[:, 0:nb * 128].rearrange("p (a c) -> p a c", a=nb)), r=[("ps", bv)], w=[("vtokA", g)])
        bv = bank_m()
        pe(lambda e, bv=bv, g=g: e.transpose(psb(bv)[0:NS, 0:128], vTb[:, g, NP_:NP_ + NS], ident_b),
           r=vres(g) + ["cb"], w=[("ps", bv)])
        act(lambda e, bv=bv, g=g: e.copy(out=vstok[0:NS, g, :], in_=psb(bv)[0:NS, 0:128]), r=[("ps", bv)],
            w=[("vstok", g)])
    for (src32, dst, nm) in ((kn32, kwin, "kw"), (v32, vwin, "vw")):
        for g in range(2):
            bw = bank_m()
            pe(lambda e, bw=bw, g=g, src32=src32: e.transpose(ps[bw][:, 0:128], src32[:, g, 0:128], ident_f),
               r=[("kn32", ("knT", g, 688)), ("v32", g), "cf"], w=[("ps", bw)])
            dve(lambda e, bw=bw, g=g: e.tensor_copy(out=win_t[:, g, :], in_=ps[bw][:, 0:128]), r=[("ps", bw)],
                w=[("win_t", g)])
        dma("sp", "o_" + nm, dst, win_t, r=[("win_t", 0), ("win_t", 1)], w=["out_" + nm])
    dma("sp", "o_ks", ks_out[:, 0:127, :, :], ck[:, 1:128, :, :], w=["ks_out_a"])
    dma("sp", "o_vs", vs_out[:, 0:127, :, :], cv[:, 1:128, :, :], w=["vs_out_a"])
    for g in range(2):
        for s in range(NS):
            dma("sp", "o_ks", ks_out[s, 127, g, :].rearrange("(d o) -> d o", o=1), kn32[:, g, 128 + s:129 + s],
                r=[("kn32", ("knT", g, 688))], w=[("ks_out_b", g, s)], slow=True)
            dma("sp", "o_vs", vs_out[s, 127, g, :].rearrange("(d o) -> d o", o=1), v32[:, g, 128 + s:129 + s],
                r=[("v32", g)], w=[("vs_out_b", g, s)], slow=True)

    mask = cbv("mask").rearrange("p (a c) -> p a c", a=3)
    v3 = lambda ap: ap[:, 0:1024].rearrange("p (a c) -> p a c", a=8)
    esr_f = v3(zf[0]); esr_d = v3(zf[1]); esr_hi = v3(sq[0]); esr_lo = v3(sq[1])
    esr2 = Z([8, 128], BF16)
    oh = cfv("onehot")
    dve(lambda e: e.tensor_copy(out=esr_f[0:2], in_=esink[0:2, :].unsqueeze(2).broadcast_to([2, 8, 128])),
        r=["esink"], w=[("zf", 0)])
    dve(lambda e: e.tensor_copy(out=esr_hi[0:2], in_=esr_f[0:2]), r=[("zf", 0)], w=[("sq", 0)])
    dve(lambda e: e.tensor_tensor(out=esr_d[0:2], in0=esr_f[0:2], in1=esr_hi[0:2], op=ALU.subtract),
        r=[("zf", 0), ("sq", 0)], w=[("zf", 1)])
    dve(lambda e: e.tensor_copy(out=esr_lo[0:2], in_=esr_d[0:2]), r=[("zf", 1)], w=[("sq", 1)])
    dve(lambda e: e.tensor_scalar(out=esr2[0:2], in0=esr_hi[0:2], scalar1=oh[0:2, 0:1], scalar2=None, op0=ALU.mult),
        r=[("sq", 0), "cf"], w=["esr2a"])
    dve(lambda e: e.scalar_tensor_tensor(out=esr2[0:2], in0=esr_lo[0:2], scalar=oh[0:2, 1:2], in1=esr2[0:2],
                                         op0=ALU.mult, op1=ALU.add), r=[("sq", 1), "esr2a", "cf"], w=["esr2"])
    def make_att(g, qnT, qres, skew=1, wide=False):
        npt = skew + 1
        alloc_c = {"s": 0, "p": 0}

        def alloc_S():
            if not wide:
                return bank_m(), bank_m()
            p = [(4, 5), (6, 7)][alloc_c["s"] % 2]
            alloc_c["s"] += 1
            return p

        def alloc_PV():
            if not wide:
                return bank_m(), bank_m()
            p = [(0, 1), (2, 3)][alloc_c["p"] % 2]
            alloc_c["p"] += 1
            return p

        def att_S(blk, g=g):
            s = blk % npt
            q_rhs = qnT[:, :, blk * 128:(blk + 1) * 128]
            k_own = knT[:, g, blk * 128:(blk + 1) * 128]
            k_prev = knT[:, g, NT:NT + 128] if blk == 0 else knT[:, g, (blk - 1) * 128:blk * 128]
            bo, bp = alloc_S()
            mprev = 2 if blk == 0 else 1
            for (bb_, kk_, mi) in ((bo, k_own, 0), (bp, k_prev, mprev)):
                pe(lambda e, bb_=bb_, kk_=kk_, q_rhs=q_rhs: e.matmul(ps[bb_][:, :], lhsT=kk_, rhs=q_rhs, start=True, stop=False),
                   r=qres + kres(g), w=[("ps", bb_)])
                pe(lambda e, bb_=bb_, mi=mi: e.matmul(ps[bb_][:, :], lhsT=ident_b,
                                                     rhs=mask[:, mi:mi + 1, :].broadcast_to([128, 4, 128]),
                                                     start=False, stop=True), r=["cb"], w=[("ps", bb_)])
            act(lambda e, bo=bo, s=s: e.activation(out=PTm[s][:, 0, :], in_=ps[bo][:, :], func=AF.Exp, bias=negc,
                                                   scale=ATTN_SCALE), r=[("ps", bo), "negc"], w=[("PTm", s, 0)])
            act(lambda e, bp=bp, s=s: e.activation(out=PTm[s][:, 1, :], in_=ps[bp][:, :], func=AF.Exp, bias=negc,
                                                   scale=ATTN_SCALE), r=[("ps", bp), "negc"], w=[("PTm", s, 1)])

        def att_PV(blk, g=g):
            s = blk % npt
            sr = blk % 2
            bO, bD = alloc_PV()
            for t, vb in ((0, blk + 1), (1, blk)):
                pe(lambda e, bO=bO, t=t, vb=vb, s=s, g=g: e.matmul(ps[bO][:, :], lhsT=vtokA[:, g, vb, :], rhs=PTm[s][:, t, :],
                                                                  start=(t == 0), stop=(t == 1)),
                   r=[("PTm", s, t), ("vtokA", g)], w=[("ps", bO)])
            for t in range(2):
                pe(lambda e, bD=bD, t=t, s=s: e.matmul(ps[bD][:, :], lhsT=cbv("ones"), rhs=PTm[s][:, t, :],
                                                      start=(t == 0), stop=False),
                   r=[("PTm", s, t), "cb"], w=[("ps", bD)])
            pe(lambda e, bD=bD, g=g: e.matmul(ps[bD][:, :], lhsT=cbv("ones")[0:2, :], rhs=esr2[0:2, 4 * g:4 * g + 4, :],
                                             start=False, stop=True), r=["esr2", "cb"], w=[("ps", bD)])
            if blk % 2 == 0:
                act(lambda e, bD=bD, sr=sr: e.activation(out=rec[sr], in_=ps[bD][:, :], func=AF.Ln),
                    r=[("ps", bD)], w=[("rec", sr)])
                act(lambda e, sr=sr: e.activation(out=rec[sr], in_=rec[sr], func=AF.Exp, scale=-1.0), r=[("rec", sr)],
                    w=[("rec", sr)])
            else:
                dve(lambda e, bD=bD, sr=sr: e.reciprocal(out=rec[sr], in_=ps[bD][:, :]), r=[("ps", bD)], w=[("rec", sr)])
            dve(lambda e, bO=bO, sr=sr, g=g, blk=blk: e.tensor_tensor(
                out=mixT[:, 4 * g:4 * g + 4, blk * 128:(blk + 1) * 128],
                in0=ps[bO][:, :].rearrange("p (a c) -> p a c", a=4),
                in1=rec[sr][:, :].rearrange("p (a c) -> p a c", a=4), op=ALU.mult),
                r=[("ps", bO), ("rec", sr)], w=[("mixT", "a", g, blk)])


        def att_gen():
            for b0 in range(skew):
                att_S(b0)
                yield
            for blk in range(8):
                if blk + skew < 8:
                    att_S(blk + skew)
                    yield
                att_PV(blk)
                yield

        def att_sample():
            for sm in range(NS):
                dma("pool", "kc%d" % sm, kc_b[sm], ck[sm, :, g, :], w=[("kc_b", sm)])
                dma("pool", "vc%d" % sm, vc_b[sm], cv[sm, :, g, :], w=[("vc_b", sm)])
            bt = bank_m()
            for sm in range(NS):
                pe(lambda e, bt=bt, sm=sm: e.transpose(psb(bt)[:, sm * 128:(sm + 1) * 128], kc_b[sm], ident_b),
                   r=[("kc_b", sm), "cb"], w=[("ps", bt)])
            act(lambda e, bt=bt: e.copy(out=kcT, in_=psb(bt)[:, 0:512].rearrange("p (a c) -> p a c", a=4)),
                r=[("ps", bt)], w=["kcT"])
            bS = bank_m()
            for sm in range(NS):
                pe(lambda e, bS=bS, sm=sm: e.matmul(ps[bS][:, sm * 4:(sm + 1) * 4], lhsT=kcT[:, sm, :], rhs=qnT[:, :, NP_ + sm],
                                                   start=True, stop=True, skip_group_check=True),
                   r=["kcT"] + qres, w=[("ps", bS)])
            for sm in range(NS):
                pe(lambda e, bS=bS, sm=sm, g=g: e.matmul(ps[bS][0:NS, 32 + sm * 4:32 + (sm + 1) * 4],
                                                        lhsT=knT[:, g, NP_:NP_ + NS], rhs=qnT[:, :, NP_ + sm],
                                                        start=True, stop=True, skip_group_check=True),
                   r=kres(g) + qres, w=[("ps", bS)])
            act(lambda e, bS=bS: e.activation(out=PTc, in_=ps[bS][:, 0:16], func=AF.Exp, bias=negc, scale=ATTN_SCALE),
                r=[("ps", bS), "negc"], w=["PTc"])
            act(lambda e, bS=bS: e.activation(out=Pn[0:NS, :], in_=ps[bS][0:NS, 32:48], func=AF.Exp,
                                              bias=negc[0:NS, :], scale=ATTN_SCALE), r=[("ps", bS), "negc"], w=["Pn"])
            dve(lambda e: e.tensor_tensor(out=Pn[0:NS, :].rearrange("p (s h) -> p s h", s=4),
                                          in0=Pn[0:NS, :].rearrange("p (s h) -> p s h", s=4),
                                          in1=oh[0:NS, 0:4].unsqueeze(2).broadcast_to([NS, 4, 4]), op=ALU.mult),
                r=["Pn", "cf"], w=["Pn"])
            bO = bank_m(); bD = bank_m()
            for sm in range(NS):
                pe(lambda e, bO=bO, sm=sm: e.matmul(ps[bO][:, sm * 4:(sm + 1) * 4], lhsT=vc_b[sm], rhs=PTc[:, sm * 4:(sm + 1) * 4],
                                                   start=(sm == 0), stop=False, skip_group_check=True),
                   r=[("vc_b", sm), "PTc"], w=[("ps", bO)])
                pe(lambda e, bO=bO, sm=sm, g=g: e.matmul(ps[bO][:, sm * 4:(sm + 1) * 4], lhsT=vstok[0:NS, g, :],
                                                        rhs=Pn[0:NS, sm * 4:(sm + 1) * 4], start=False, stop=True,
                                                        skip_group_check=True),
                   r=[("vstok", g), "Pn"], w=[("ps", bO)])
            for sm in range(NS):
                pe(lambda e, bD=bD, sm=sm: e.matmul(ps[bD][:, sm * 4:(sm + 1) * 4], lhsT=cbv("ones"), rhs=PTc[:, sm * 4:(sm + 1) * 4],
                                                   start=(sm == 0), stop=False, skip_group_check=True),
                   r=["PTc", "cb"], w=[("ps", bD)])
                pe(lambda e, bD=bD, sm=sm: e.matmul(ps[bD][:, sm * 4:(sm + 1) * 4], lhsT=cbv("ones")[0:NS, :],
                                                   rhs=Pn[0:NS, sm * 4:(sm + 1) * 4], start=False, stop=False,
                                                   skip_group_check=True), r=["Pn", "cb"], w=[("ps", bD)])
            pe(lambda e, bD=bD, g=g: e.matmul(ps[bD][:, 0:16], lhsT=cbv("ones")[0:2, :],
                                             rhs=esr2[0:2, 4 * g:4 * g + 4, 0].unsqueeze(1).broadcast_to([2, 4, 4]),
                                             start=False, stop=True, skip_group_check=True),
               r=["esr2", "cb"], w=[("ps", bD)])
            act(lambda e, bD=bD: e.activation(out=recs, in_=ps[bD][:, 0:16], func=AF.Ln), r=[("ps", bD)], w=["recs"])
            act(lambda e: e.activation(out=recs, in_=recs, func=AF.Exp, scale=-1.0), r=["recs"], w=["recs"])
            dve(lambda e, bO=bO, g=g: e.tensor_tensor(
                out=mixT[:, 4 * g:4 * g + 4, NP_:NP_ + NS], in0=ps[bO][:, 0:16].rearrange("p (s h) -> p h s", s=4),
                in1=recs[:, :].rearrange("p (s h) -> p h s", s=4), op=ALU.mult),
                r=[("ps", bO), "recs"], w=[("mixT", "as", g)])

        return att_gen, att_sample

    qbufs = [qnT, qnT2]
    qresf = lambda gi: [("qnT", gi, hh, c0) for hh in range(4) for (c0, n) in TT]

    def qproj_gen(gi):
        qb = qbufs[gi]
        for jj in range(2):
            wv, wres = w_next()
            for ml in range(2):
                hh = jj * 2 + ml
                for ti, (c0, n) in enumerate(TT):
                    b = bank_d()
                    for k in range(16):
                        pe(lambda e, b=b, k=k, ml=ml, c0=c0, n=n, wv=wv: e.matmul(
                            ps[b][:, 0:n], lhsT=wv[:, k, ml * 128:(ml + 1) * 128], rhs=aT[:, k, c0:c0 + n],
                            start=(k == 0), stop=(k == 15)), r=[wres, "aT"], w=[("ps", b)])
                    qknorm(b, c0, n, "qg", qb[:, hh, c0:c0 + n], ("qnT", gi, hh, c0))
                    yield
            w_done()
        qk_flush()

    def drive_mix(ga, gb, pat):
        a_alive, b_alive = True, gb is not None
        while a_alive or b_alive:
            if a_alive:
                try:
                    next(ga)
                except StopIteration:
                    a_alive = False
            for _ in range(pat if a_alive else 99):
                if not b_alive:
                    break
                try:
                    next(gb)
                except StopIteration:
                    b_alive = False

    for _ in qproj_gen(0):
        pass
    attg0, atts0 = make_att(0, qbufs[0], qresf(0))
    attg1, atts1 = make_att(1, qbufs[1], qresf(1), skew=1, wide=True)
    drive_mix(attg0(), qproj_gen(1), 1)
    atts0()
    for _ in attg1():
        pass
    atts1()
    att_dense3[0] = False
    P.fence()
    dump("mixA", mixT[:, 0:8, :], ["mixT"])
    chk(4)

    Z = Alloc(HB, total)
    agl = [Z([8, 128], F32) for _ in range(2)]
    coef = cfv("coef")
    ago = ag_out.ap()
    for r_ in range(4):
        s = r_ % 2
        dma("sp", "agl%d" % s, agl[s], ago[r_ * 1024:(r_ + 1) * 1024, :].rearrange("(h p) n -> p h n", p=128),
            r=["ag_out"], w=[("agl", s)])
        for h in range(8):
            if r_ == 0:
                dve(lambda e, s=s, h=h, r_=r_: e.tensor_scalar(out=S0[:, h, :], in0=agl[s][:, h, :],
                                                              scalar1=coef[:, r_ * 8 + h:r_ * 8 + h + 1], scalar2=None,
                                                              op0=ALU.mult), r=[("agl", s), "cf"], w=[("S0", h)])
            else:
                dve(lambda e, s=s, h=h, r_=r_: e.scalar_tensor_tensor(
                    out=S0[:, h, :], in0=agl[s][:, h, :], scalar=coef[:, r_ * 8 + h:r_ * 8 + h + 1], in1=S0[:, h, :],
                    op0=ALU.mult, op1=ALU.add), r=[("agl", s), "cf", ("S0", h)], w=[("S0", h)])

    dump("S0", S0, [("S0", h) for h in range(8)])
    chk(5)
    G1 = [dict(qr=Z([NT], BF16), kr=Z([NT], BF16), vT=Z([NT], BF16), sg=Z([NT], F32)) for _ in range(2)]
    RT = dict(xb=[Z([344], BF16) for _ in range(3)], t1=[Z([344], F32) for _ in range(3)],
              t2=[Z([344], F32) for _ in range(3)])
    R = dict(qd=Z([8, 128], BF16), kd=Z([8, 128], BF16), vtok=Z([8, 128], BF16), scm=Z([8, 128], BF16),
             Sb=Z([8, 128], BF16), Srun=Z([2, 128], F32), o_sb=Z([NT], F32), sq=Z([NT], BF16), rstd=Z([NT], F32),
             tt=Z([NT], F32), ktok_s=Z([128], BF16), vtok_s=Z([128], BF16))
    vm = [Z([128], BF16) for _ in range(NS)]
    Sst = [Z([128], F32) for _ in range(NS)]
    Snew = [Z([128], F32) for _ in range(NS)]
    Sbs = [Z([128], BF16) for _ in range(NS)]
    intraT = cfv("intraT").rearrange("p (h c) -> p h c", h=8)
    qdecv = cfv("qdec").rearrange("p (h c) -> p h c", h=8)
    kdecv = cfv("kdec")
    rgv = cfv("rg")
    tg = "p2"
    p2c = {"d": 0, "r": 0, "t": 0}

    def bank_d3():
        b = p2c["d"] % 3
        p2c["d"] += 1
        return b

    def bank_r():
        b = 3 + p2c["r"] % 2
        p2c["r"] += 1
        return b
    M0, M1, M2 = 5, 6, 7
    p2blocks = {}

    def rot_a(b, c0, n):
        sl = p2c["t"] % 3
        p2c["t"] += 1
        xb, t1 = RT["xb"][sl], RT["t1"][sl]
        act(lambda e: e.copy(out=xb[:, 0:n], in_=ps[b][:, 0:n]), r=[("ps", b)], w=[("rt_xb", sl)])
        dve(lambda e: e.tensor_tensor(out=t1[:, 0:n], in0=ps[b][:, 0:n], in1=cosv[:, c0:c0 + n], op=ALU.mult),
            r=[("ps", b), "cf"], w=[("rt_t1", sl)])
        return sl

    def rot_b(sl, c0, n, outb, outres):
        xb, t1, t2 = RT["xb"][sl], RT["t1"][sl], RT["t2"][sl]
        b2 = bank_r()
        pe(lambda e: e.matmul(ps[b2][:, 0:n], lhsT=cbv("rrot"), rhs=xb[:, 0:n], start=True, stop=True),
           r=[("rt_xb", sl), "cb"], w=[("ps", b2)])
        dve(lambda e: e.tensor_tensor(out=t2[:, 0:n], in0=ps[b2][:, 0:n], in1=sinv[:, c0:c0 + n], op=ALU.mult),
            r=[("ps", b2), "cf"], w=[("rt_t2", sl)])
        dve(lambda e: e.tensor_tensor(out=outb[:, c0:c0 + n], in0=t1[:, 0:n], in1=t2[:, 0:n], op=ALU.add),
            r=[("rt_t1", sl), ("rt_t2", sl)], w=[outres])

    def stageA(h):
        j, ml = h // 2, h % 2
        if ml == 0:
            p2blocks[j] = [w_next() for _ in range(2)]
        blocks = p2blocks[j]
        g1 = G1[h % 2]
        gp = h % 2
        dma("sp", "ldk%d" % gp, g1["kr"], krs.ap()[h], r=[("krs", h)], w=[(tg, "kr", gp, c0) for (c0, n) in TT])
        dma("sp", "ldv%d" % gp, g1["vT"], vts.ap()[h], r=[("vts", h)], w=[(tg, "vT", gp, c0) for (c0, n) in TT])
        for bi in range(2):
            wv, wres = blocks[bi]
            pending = None
            for ti, (c0, n) in enumerate(TT):
                b = bank_d3()
                for k in range(16):
                    pe(lambda e, b=b, k=k, c0=c0, n=n, wv=wv, ml=ml: e.matmul(
                        ps[b][:, 0:n], lhsT=wv[:, k, ml * 128:(ml + 1) * 128], rhs=aT[:, k, c0:c0 + n],
                        start=(k == 0), stop=(k == 15)), r=[wres, "aT"], w=[("ps", b)])
                if bi == 0:
                    sl = rot_a(b, c0, n)
                    if pending is not None:
                        rot_b(*pending)
                    pending = (sl, c0, n, g1["qr"], (tg, "qr", gp, c0))
                else:
                    act(lambda e, b=b, c0=c0, n=n, g1=g1: e.activation(out=g1["sg"][:, c0:c0 + n], in_=ps[b][:, 0:n],
                                                                       func=AF.Silu), r=[("ps", b)], w=[(tg, "sg", gp, c0)])
                if ti == 2:
                    if pending is not None:
                        rot_b(*pending)
                    if ml == 1:
                        w_done()
                yield

    def stageB(h):
        g1 = G1[h % 2]
        gp = h % 2
        q_res = [(tg, "qr", gp, c0) for (c0, n) in TT]
        k_res = [(tg, "kr", gp, c0) for (c0, n) in TT]
        v_res = [(tg, "vT", gp, c0) for (c0, n) in TT]
        g_res = [(tg, "sg", gp, c0) for (c0, n) in TT]
        dve(lambda e: e.tensor_tensor(out=R["qd"], in0=g1["qr"][:, 0:NP_].rearrange("p (a c) -> p a c", a=8),
                                      in1=qdecv[:, h:h + 1, :].broadcast_to([128, 8, 128]), op=ALU.mult),
            r=q_res + ["cf"], w=[(tg, "qd")])
        for half in range(2):
            for cc in range(4):
                c = half * 4 + cc
                pe(lambda e, cc=cc, c=c: e.transpose(psb(M0)[:, cc * 128:(cc + 1) * 128],
                                                    g1["kr"][:, c * 128:(c + 1) * 128], ident_b),
                   r=k_res + ["cb"], w=[("ps", M0)])
            dve(lambda e, half=half: e.tensor_scalar(
                out=R["kd"][:, half * 4:(half + 1) * 4, :], in0=psb(M0)[:, 0:512].rearrange("p (a c) -> p a c", a=4),
                scalar1=kdecv[:, h:h + 1], scalar2=None, op0=ALU.mult), r=[("ps", M0), "cf"], w=[(tg, "kd", half)])
            for cc in range(4):
                c = half * 4 + cc
                pe(lambda e, cc=cc, c=c: e.transpose(psb(M1)[:, cc * 128:(cc + 1) * 128],
                                                    g1["vT"][:, c * 128:(c + 1) * 128], ident_b),
                   r=v_res + ["cb"], w=[("ps", M1)])
            act(lambda e, half=half: e.copy(out=R["vtok"][:, half * 4:(half + 1) * 4, :],
                                            in_=psb(M1)[:, 0:512].rearrange("p (a c) -> p a c", a=4)),
                r=[("ps", M1)], w=[(tg, "vtok", half)])
        act(lambda e: e.copy(out=R["Sb"][:, 0, :], in_=S0[:, h, :]), r=[("S0", h)], w=[(tg, "Sb", 0)])
        yield
        g128 = float(GAMMA[h] ** 128)
        for half in range(2):
            for cc in range(4):
                c = half * 4 + cc
                pe(lambda e, cc=cc, c=c: e.matmul(ps[M2][:, cc * 128:(cc + 1) * 128], lhsT=R["kd"][:, c, :],
                                                 rhs=R["vtok"][:, c, :], start=True, stop=True),
                   r=[(tg, "kd", half), (tg, "vtok", half)], w=[("ps", M2)])
            for cc in range(4):
                c = half * 4 + cc
                prev = S0[:, h, :] if c == 0 else R["Srun"][:, (c - 1) % 2, :]
                prev_res = ("S0", h) if c == 0 else (tg, "Srun", (c - 1) % 2)
                dve(lambda e, cc=cc, c=c, prev=prev: e.scalar_tensor_tensor(
                    out=R["Srun"][:, c % 2, :], in0=prev, scalar=g128, in1=ps[M2][:, cc * 128:(cc + 1) * 128],
                    op0=ALU.mult, op1=ALU.add), r=[prev_res, ("ps", M2)], w=[(tg, "Srun", c % 2)])
                if c < 7:
                    act(lambda e, c=c: e.copy(out=R["Sb"][:, c + 1, :], in_=R["Srun"][:, c % 2, :]),
                        r=[(tg, "Srun", c % 2)], w=[(tg, "Sb", c + 1)])
                else:
                    dma("sp", "o_rs", rstate[h], R["Srun"][:, c % 2, :], r=[(tg, "Srun", c % 2)], w=[("rstate", h)])
        yield
        for half, bsc in ((0, M0), (1, M1)):
            for cc in range(4):
                c = half * 4 + cc
                pe(lambda e, bsc=bsc, cc=cc, c=c: e.matmul(ps[bsc][:, cc * 128:(cc + 1) * 128],
                                                          lhsT=g1["kr"][:, c * 128:(c + 1) * 128],
                                                          rhs=g1["qr"][:, c * 128:(c + 1) * 128], start=True, stop=True),
                   r=k_res + q_res, w=[("ps", bsc)])
            dve(lambda e, bsc=bsc, half=half: e.tensor_tensor(
                out=R["scm"][:, half * 4:(half + 1) * 4, :], in0=ps[bsc][:, :].rearrange("p (a c) -> p a c", a=4),
                in1=intraT[:, h:h + 1, :].broadcast_to([128, 4, 128]), op=ALU.mult),
                r=[("ps", bsc), "cf"], w=[(tg, "scm", half)])
        yield
        obanks = [M0, M1]
        for half in range(2):
            bo = obanks[half]
            for cc in range(4):
                c = half * 4 + cc
                pe(lambda e, bo=bo, cc=cc, c=c: e.matmul(ps[bo][:, cc * 128:(cc + 1) * 128], lhsT=R["vtok"][:, c, :],
                                                        rhs=R["scm"][:, c, :], start=(cc == 0), stop=False,
                                                        skip_group_check=True),
                   r=[(tg, "vtok", half), (tg, "scm", half)], w=[("ps", bo)])
        pe(lambda e: e.transpose(psb(M2)[0:NS, 0:128], g1["kr"][:, NP_:NT], ident_b), r=k_res + ["cb"], w=[("ps", M2)])
        pe(lambda e: e.transpose(psb(M2)[0:NS, 128:256], g1["vT"][:, NP_:NT], ident_b), r=v_res + ["cb"], w=[("ps", M2)])
        act(lambda e: e.mul(out=R["ktok_s"][0:NS, :], in_=psb(M2)[0:NS, 0:128], mul=RET_K_SCALE),
            r=[("ps", M2)], w=[(tg, "ktok_s")])
        act(lambda e: e.copy(out=R["vtok_s"][0:NS, :], in_=psb(M2)[0:NS, 128:256]), r=[("ps", M2)], w=[(tg, "vtok_s")])
        for sm in range(NS):
            dma("sp", "st%d" % sm, Sst[sm], st[sm, h], w=[("Sst", sm)])
            dve(lambda e, sm=sm: e.tensor_scalar(out=vm[sm][0:NS, :], in0=R["vtok_s"][0:NS, :],
                                                 scalar1=cfv("onehot")[0:NS, sm:sm + 1], scalar2=None, op0=ALU.mult),
                r=[(tg, "vtok_s"), "cf"], w=[("vm", sm)])
        yield
        for half in range(2):
            bo = obanks[half]
            for cc in range(4):
                c = half * 4 + cc
                pe(lambda e, bo=bo, cc=cc, c=c: e.matmul(ps[bo][:, cc * 128:(cc + 1) * 128], lhsT=R["Sb"][:, c, :],
                                                        rhs=R["qd"][:, c, :], start=False, stop=True,
                                                        skip_group_check=True),
                   r=[(tg, "Sb", c), (tg, "qd")], w=[("ps", bo)])

        def o_read(bo, c0, n):
            act(lambda e: e.activation(out=R["sq"][:, c0:c0 + n], in_=ps[bo][:, 0:n], func=AF.Square),
                r=[("ps", bo)], w=[(tg, "sq", c0)])
            dve(lambda e: e.tensor_copy(out=R["o_sb"][:, c0:c0 + n], in_=ps[bo][:, 0:n]),
                r=[("ps", bo)], w=[(tg, "o_sb", c0)])
        o_read(M0, 0, 512)
        o_read(M1, 512, 512)
        yield
        for sm in range(NS):
            bu = M0 if sm % 2 == 0 else M1
            pe(lambda e, bu=bu, sm=sm: e.matmul(ps[bu][:, 0:128], lhsT=R["ktok_s"][0:NS, :], rhs=vm[sm][0:NS, :],
                                               start=True, stop=True), r=[(tg, "ktok_s"), ("vm", sm)], w=[("ps", bu)])
            dve(lambda e, bu=bu, sm=sm: e.scalar_tensor_tensor(out=Snew[sm], in0=Sst[sm], scalar=float(GAMMA[h]),
                                                               in1=ps[bu][:, 0:128], op0=ALU.mult, op1=ALU.add),
                r=[("Sst", sm), ("ps", bu)], w=[("Snew", sm)])
            dma("sp", "o_ss%d" % sm, ss_out[sm, h], Snew[sm], r=[("Snew", sm)], w=[("ss_out", sm, h)])
            act(lambda e, sm=sm: e.copy(out=Sbs[sm], in_=Snew[sm]), r=[("Snew", sm)], w=[("Sbs", sm)])
        yield
        for sm in range(NS):
            pe(lambda e, sm=sm: e.matmul(ps[M2][:, sm:sm + 1], lhsT=Sbs[sm], rhs=g1["qr"][:, NP_ + sm:NP_ + sm + 1],
                                        start=True, stop=True), r=[("Sbs", sm)] + q_res, w=[("ps", M2)])
        o_read(M2, NP_, NS)
        yield
        for pi, (c0, n) in enumerate(((0, 512), (512, 512), (NP_, NS))):
            bm_ = M0 if pi % 2 == 0 else M1
            pe(lambda e, bm_=bm_, c0=c0, n=n: e.matmul(ps[bm_][:, 0:n], lhsT=cbv("ones128"), rhs=R["sq"][:, c0:c0 + n],
                                                      start=True, stop=True), r=[(tg, "sq", c0), "cb"], w=[("ps", bm_)])
            act(lambda e, bm_=bm_, c0=c0, n=n: e.activation(out=R["rstd"][:, c0:c0 + n], in_=ps[bm_][:, 0:n], func=AF.Ln,
                                                            bias=epsv, scale=1.0), r=[("ps", bm_)], w=[(tg, "rstd", c0)])
            act(lambda e, c0=c0, n=n: e.activation(out=R["rstd"][:, c0:c0 + n], in_=R["rstd"][:, c0:c0 + n], func=AF.Exp,
                                                   scale=-0.5), r=[(tg, "rstd", c0)], w=[(tg, "rstd", c0)])
            dve(lambda e, c0=c0, n=n: e.scalar_tensor_tensor(
                out=R["tt"][:, c0:c0 + n], in0=R["o_sb"][:, c0:c0 + n], scalar=rgv[:, h:h + 1],
                in1=R["rstd"][:, c0:c0 + n], op0=ALU.mult, op1=ALU.mult),
                r=[(tg, "o_sb", c0), (tg, "rstd", c0), "cf"], w=[(tg, "tt", c0)])
            pool(lambda e, c0=c0, n=n: e.tensor_tensor(out=mixT[:, 8 + h, c0:c0 + n], in0=R["tt"][:, c0:c0 + n],
                                                       in1=g1["sg"][:, c0:c0 + n], op=ALU.mult),
                 r=[(tg, "tt", c0)] + g_res, w=[("mixT", "r", h, c0)])
        yield

    def drive(gens):
        gens = [g for g in gens if g is not None]
        while gens:
            for g in list(gens):
                try:
                    next(g)
                except StopIteration:
                    gens.remove(g)

    def drive2(gb, ga):
        pat = [1, 1, 1, 0, 1, 0, 1, 1, 9, 9]
        bi = 0
        b_alive, a_alive = True, ga is not None
        while b_alive or a_alive:
            if b_alive:
                try:
                    next(gb)
                except StopIteration:
                    b_alive = False
            na = pat[min(bi, len(pat) - 1)] if b_alive else 99
            bi += 1
            for _ in range(na):
                if not a_alive:
                    break
                try:
                    next(ga)
                except StopIteration:
                    a_alive = False

    drive([stageA(0)])
    for h in range(8):
        drive2(stageB(h), stageA(h + 1) if h + 1 < 8 else None)
    P.fence()
    dump("mixT", mixT, ["mixT"])
    chk(6)

    Zt = Alloc(T2base, T2base + 12 * 1024)
    Zx = Alloc(0, 0)
    xs_off = None
    aT_f32 = aT.rearrange("p a c -> p (a c)").bitcast(F32)
    xstage = [aT_f32[:, j * D:(j + 1) * D] for j in range(4)]
    tiles_x = [(x_main[i * 128:(i + 1) * 128, :], 128, i * 128) for i in range(8)] + [(x_smp, NS, NP_)]
    for i in range(3):
        dma("sp", "x%d" % (i % 4), xstage[i % 4][0:tiles_x[i][1], :], tiles_x[i][0], w=[("xstage", i % 4)])
    for i, (src, rows, c0) in enumerate(tiles_x):
        s = i % 4
        if i + 3 < len(tiles_x):
            j3 = i + 3
            dma("sp", "x%d" % (j3 % 4), xstage[j3 % 4][0:tiles_x[j3][1], :], tiles_x[j3][0], w=[("xstage", j3 % 4)])
        for q4 in range(4):
            b = bank_m()
            for kk in range(4):
                k = q4 * 4 + kk
                pe(lambda e, b=b, kk=kk, k=k, s=s, rows=rows: e.transpose(
                    ps[b][:, kk * 128:kk * 128 + rows], xstage[s][0:rows, k * 128:(k + 1) * 128],
                    ident_f[0:rows, 0:rows]), r=[("xstage", s), "cf"], w=[("ps", b)])
            src_ps = lambda b=b, rows=rows: ps[b][:, :].rearrange("p (a c) -> p a c", a=4)[:, :, 0:rows]
            dst = hT[:, q4 * 4:(q4 + 1) * 4, c0:c0 + rows]
            if q4 % 2 == 0:
                act(lambda e, dst=dst, src_ps=src_ps: e.copy(out=dst, in_=src_ps()), r=[("ps", b)], w=[("hT", q4, i)])
            else:
                dve(lambda e, dst=dst, src_ps=src_ps: e.tensor_copy(out=dst, in_=src_ps()), r=[("ps", b)], w=[("hT", q4, i)])
    P.fence()

    sqs = [Zt([344], BF16) for _ in range(4)]
    rstdn = Zt([NT], F32)
    fT = aT[:, :, 0:NT]

    class NormState:
        pass

    def norm_begin(tag):
        st_ = NormState()
        st_.tag = tag
        st_.banks = [bank_m() for _ in TT]
        st_.pending = None
        st_.cnt = 0
        return st_

    def norm_flush(st_):
        if st_.pending is not None:
            slot, m, ti, n = st_.pending
            pe(lambda e: e.matmul(ps[st_.banks[ti]][:, 0:n], lhsT=cbv("onesD"), rhs=sqs[slot][:, 0:n],
                                  start=(m == 0), stop=(m == 15)),
               r=[("sqs", slot), "cb"], w=[("ps", st_.banks[ti])])
            st_.pending = None

    def norm_tile(st_, m, ti, c0, n):
        norm_flush(st_)
        slot = st_.cnt % 4
        st_.cnt += 1
        act(lambda e: e.activation(out=sqs[slot][:, 0:n], in_=hT[:, m, c0:c0 + n], func=AF.Square),
            r=[("hTm", m, c0)], w=[("sqs", slot)])
        st_.pending = (slot, m, ti, n)

    def norm_finish(st_, gname):
        tag = st_.tag
        norm_flush(st_)
        for ti, (c0, n) in enumerate(TT):
            act(lambda e, ti=ti, c0=c0, n=n: e.activation(out=rstdn[:, c0:c0 + n], in_=ps[st_.banks[ti]][:, 0:n], func=AF.Ln,
                                                          bias=epsv, scale=1.0),
                r=[("ps", st_.banks[ti])], w=[(tag, "rstdn0", ti)])
            act(lambda e, ti=ti, c0=c0, n=n: e.activation(out=rstdn[:, c0:c0 + n], in_=rstdn[:, c0:c0 + n], func=AF.Exp,
                                                          scale=-0.5),
                r=[(tag, "rstdn0", ti)], w=[(tag, "rstdn")])
        gv = cfv(gname)
        for k in range(16):
            dve(lambda e, k=k: e.scalar_tensor_tensor(out=fT[:, k, :], in0=hT[:, k, :], scalar=gv[:, k:k + 1], in1=rstdn,
                                                      op0=ALU.mult, op1=ALU.mult),
                r=[(tag, "rstdn"), "cf"] + [("hTm", k, c0) for (c0, n) in TT], w=[("fT", k)])

    mix_rhs = lambda k, c0, n: mixT[:, k, c0:c0 + n]
    n2 = norm_begin("n2")
    for j in range(8):
        def evac_out(ml, ti, c0, n, b, j=j):
            m = 2 * j + ml
            dve(lambda e: e.tensor_tensor(out=hT[:, m, c0:c0 + n], in0=ps[b][:, 0:n], in1=hT[:, m, c0:c0 + n], op=ALU.add),
                r=[("ps", b), ("hTm", m, c0)], w=[("hTm", m, c0)])
            norm_tile(n2, m, ti, c0, n)
        dense_block(16, mix_rhs, ["mixT"], evac_out)
    P.fence()
    dump("h1", hT, [])
    chk(7)

    norm_finish(n2, "fn_g")
    P.fence()
    dump("fT", fT, [])
    chk(8)

    uT = mixT
    sgt = [Zt([344], F32) for _ in range(2)]
    f_rhs = lambda k, c0, n: fT[:, k, c0:c0 + n]
    cnt_s = {"i": 0}
    for qi, (b0, nb) in enumerate(QUART):
        for bb in range(nb):
            wg, wgres = w_next()
            wu, wures = w_next()
            for ml in range(2):
                cl = 2 * bb + ml
                for ti, (c0, n) in enumerate(TT):
                    bg = bank_d(); bu = bank_d()
                    for (bk_, wv, wres) in ((bg, wg, wgres), (bu, wu, wures)):
                        for k in range(16):
                            pe(lambda e, bk_=bk_, wv=wv, k=k, ml=ml, c0=c0, n=n: e.matmul(
                                ps[bk_][:, 0:n], lhsT=wv[:, k, ml * 128:(ml + 1) * 128], rhs=fT[:, k, c0:c0 + n],
                                start=(k == 0), stop=(k == 15)), r=[wres], w=[("ps", bk_)])
                    s = cnt_s["i"] % 2
                    cnt_s["i"] += 1
                    act(lambda e, bg=bg, s=s, n=n: e.activation(out=sgt[s][:, 0:n], in_=ps[bg][:, 0:n], func=AF.Silu),
                        r=[("ps", bg)], w=[("sgt", s)])
                    dve(lambda e, bu=bu, s=s, n=n, cl=cl, c0=c0: e.tensor_tensor(out=uT[:, cl, c0:c0 + n], in0=ps[bu][:, 0:n],
                                                                               in1=sgt[s][:, 0:n], op=ALU.mult),
                        r=[("ps", bu), ("sgt", s)], w=[("uT", qi)])
            w_done(2)
        kq = 2 * nb
        u_rhs = lambda k, c0, n: uT[:, k, c0:c0 + n]
        if qi == 3:
            n3 = norm_begin("n3")
        for j in range(8):
            def evac_dn(ml, ti, c0, n, b, j=j):
                m = 2 * j + ml
                dve(lambda e: e.tensor_tensor(out=hT[:, m, c0:c0 + n], in0=ps[b][:, 0:n], in1=hT[:, m, c0:c0 + n],
                                              op=ALU.add), r=[("ps", b), ("hTm", m, c0)], w=[("hTm", m, c0)])
                if qi == 3:
                    norm_tile(n3, m, ti, c0, n)
            dense_block(kq, u_rhs, [("uT", qi)], evac_dn)
    P.fence()
    dump("h2", hT, [])
    chk(9)

    norm_finish(n3, "pn_g")
    pst = [uT[:, 0:1, :].rearrange("p a c -> p (a c)").bitcast(F32)[:, 0:256],
           uT[:, 1:2, :].rearrange("p a c -> p (a c)").bitcast(F32)[:, 0:256]]
    peT = uT[:, 4:6, :]
    tiles_p = [(p_main[i * 128:(i + 1) * 128, :], 128, i * 128) for i in range(8)] + [(p_smp, NS, NP_)]
    for i, (src, rows, c0) in enumerate(tiles_p):
        s = i % 2
        dma("sp", "x%d" % s, pst[s][0:rows, :], src, w=[("pst", s)])
        b = bank_m()
        for kk in range(2):
            pe(lambda e, b=b, kk=kk, s=s, rows=rows: e.transpose(ps[b][:, kk * 128:kk * 128 + rows],
                                                                pst[s][0:rows, kk * 128:(kk + 1) * 128],
                                                                ident_f[0:rows, 0:rows]), r=[("pst", s), "cf"], w=[("ps", b)])
        act(lambda e, b=b, rows=rows, c0=c0: e.copy(out=peT[:, :, c0:c0 + rows],
                                                   in_=ps[b][:, 0:256].rearrange("p (a c) -> p a c", a=2)[:, :, 0:rows]),
            r=[("ps", b)], w=["peT"])
    P.fence()
    wple = uT[:, 8:12, :].rearrange("p a c -> p (a c)")[:, 0:4096].rearrange("p (k n) -> p k n", k=2)
    wpleres = "wple"
    dma("pool", "wple", wple, w_ple.rearrange("(k p) n -> p k n", p=128), w=["wple"])
    sB = [uT[:, 12 + i, :].bitcast(F32)[:, 0:344] for i in range(2)]
    tA = [uT[:, 14 + i, :].bitcast(F32)[:, 0:344] for i in range(2)]
    for j in range(8):
        wv, wres = w_next()
        for ml in range(2):
            m = 2 * j + ml
            for ti, (c0, n) in enumerate(TT):
                bA = 2 * (cnt_s["i"] % 4); bB = bA + 1
                for k in range(2):
                    pe(lambda e, bA=bA, k=k, m=m, c0=c0, n=n: e.matmul(ps[bA][:, 0:n], lhsT=wple[:, k, m * 128:(m + 1) * 128],
                                                                      rhs=peT[:, k, c0:c0 + n], start=(k == 0), stop=(k == 1)),
                       r=[wpleres, "peT"], w=[("ps", bA)])
                for k in range(16):
                    pe(lambda e, bB=bB, k=k, ml=ml, c0=c0, n=n, wv=wv: e.matmul(
                        ps[bB][:, 0:n], lhsT=wv[:, k, ml * 128:(ml + 1) * 128], rhs=fT[:, k, c0:c0 + n],
                        start=(k == 0), stop=(k == 15)), r=[wres], w=[("ps", bB)])
                s = cnt_s["i"] % 2
                cnt_s["i"] += 1
                act(lambda e, bB=bB, s=s, n=n: e.activation(out=sB[s][:, 0:n], in_=ps[bB][:, 0:n], func=AF.Sigmoid),
                    r=[("ps", bB)], w=[("sB", s)])
                dve(lambda e, bA=bA, s=s, n=n: e.tensor_tensor(out=tA[s][:, 0:n], in0=ps[bA][:, 0:n], in1=sB[s][:, 0:n],
                                                              op=ALU.mult), r=[("ps", bA), ("sB", s)], w=[("tA", s)])
                dve(lambda e, s=s, n=n, m=m, c0=c0: e.tensor_tensor(out=hT[:, m, c0:c0 + n], in0=tA[s][:, 0:n],
                                                                   in1=hT[:, m, c0:c0 + n], op=ALU.add),
                    r=[("tA", s), ("hTm", m, c0)], w=[("hTm", m, c0)])
        w_done()
    P.fence()
    dump("h3", hT, [])
    chk(10)

    ystage = [aT_f32[:, j * D:(j + 1) * D] for j in range(4)]
    tiles_y = [(y_main[i * 128:(i + 1) * 128, :], 128, i * 128) for i in range(8)] + [(y_smp, NS, NP_)]
    for i, (dst, rows, c0) in enumerate(tiles_y):
        s = i % 4
        for q4 in range(4):
            b = bank_m()
            for kk in range(4):
                k = q4 * 4 + kk
                pe(lambda e, b=b, kk=kk, k=k, rows=rows, c0=c0: e.transpose(ps[b][0:rows, kk * 128:(kk + 1) * 128],
                                                                           hT[:, k, c0:c0 + rows], ident_f),
                   r=["cf"], w=[("ps", b)])
            if q4 % 2 == 0:
                act(lambda e, b=b, s=s, rows=rows, q4=q4: e.copy(out=ystage[s][0:rows, q4 * 512:(q4 + 1) * 512],
                                                                in_=ps[b][0:rows, :]), r=[("ps", b)], w=[("ystage", s, q4)])
            else:
                dve(lambda e, b=b, s=s, rows=rows, q4=q4: e.tensor_copy(out=ystage[s][0:rows, q4 * 512:(q4 + 1) * 512],
                                                                       in_=ps[b][0:rows, :]), r=[("ps", b)],
                    w=[("ystage", s, q4)])
        dma("sp", "y%d" % s, dst, ystage[s][0:rows, :], r=[("ystage", s, q4) for q4 in range(4)], w=[("y", i)])

    assert stop != 99 or wstate["used"] == len(wq), (wstate, len(wq))

    sems = {e: es.enter_context(nc.semaphore("s_" + e)) for e in Prog.ENGS}
    dma_sems = {k: es.enter_context(nc.semaphore("d_" + k)) for k in sorted(dma_sem_names)}
    block = es.enter_context(nc.Block())
    finals = sorted(dma_sem_names)
    P.emit(nc, block, sems, dma_sems, finals)
    es.close()
    print("ops", len(P.ops), "counts", P.final_counts[0])
    return nc


_CACHE = {}


def kernel(**inputs):
    inp = {k: np.asarray(v) for k, v in inputs.items()}
    if "nc" not in _CACHE:
        _CACHE["nc"] = build_program()
    nc = _CACHE["nc"]
    xp = inp["x_prompt"]; xs = inp["x_sample"]
    in_maps = []
    zeros_halo = np.zeros((NHALO, D), np.float32)
    shared = dict(
        an_g=np.ascontiguousarray(inp["attn_norm_g"].reshape(1, D)),
        w_in=np.ascontiguousarray(inp["w_in"][0]), w_out=np.ascontiguousarray(inp["w_out"][0]),
        w_gate=np.ascontiguousarray(inp["w_gate"][0]), w_up=np.ascontiguousarray(inp["w_up"][0]),
        w_down=np.ascontiguousarray(inp["w_down"][0]), w_ple=np.ascontiguousarray(inp["w_ple"][0]),
        w_pg=np.ascontiguousarray(inp["w_ple_gate"][0]),
    )
    for c in range(8):
        b, m = c // 4, c % 4
        t0 = m * 1024
        cf, cb = host_consts(c, inp)
        d = dict(shared)
        d.update(
            x_main=np.ascontiguousarray(xp[b, t0:t0 + 1024]),
            x_halo=np.ascontiguousarray(xp[b, t0 - 128:t0]) if m > 0 else zeros_halo,
            x_smp=np.ascontiguousarray(xs[4 * c:4 * c + 4, 0]),
            p_main=np.ascontiguousarray(inp["p_prompt"][0, b, t0:t0 + 1024]),
            p_smp=np.ascontiguousarray(inp["p_sample"][0, 4 * c:4 * c + 4, 0]),
            ck=np.ascontiguousarray(inp["cache_k_win"][0, 4 * c:4 * c + 4]),
            cv=np.ascontiguousarray(inp["cache_v_win"][0, 4 * c:4 * c + 4]),
            st=np.ascontiguousarray(inp["state_ret"][0, 4 * c:4 * c + 4]),
            cf=cf, cb=cb,
        )
        in_maps.append(d)
    res = run_bass_kernel_spmd(nc, in_maps, core_ids=list(range(8)))
    R = res.results
    _CACHE["last"] = R
    y_p = np.stack([np.concatenate([R[4 * b + m]["y_main"] for m in range(4)], 0) for b in range(2)], 0)
    y_s = np.concatenate([R[c]["y_smp"] for c in range(8)], 0)[:, None, :]
    kwp = np.stack([R[4 * b + 3]["kwin"] for b in range(2)], 0)[None]
    vwp = np.stack([R[4 * b + 3]["vwin"] for b in range(2)], 0)[None]
    rsp = np.stack([R[4 * b + 3]["rstate"] for b in range(2)], 0)[None]
    kws = np.concatenate([R[c]["ks_out"] for c in range(8)], 0)[None]
    vws = np.concatenate([R[c]["vs_out"] for c in range(8)], 0)[None]
    rss = np.concatenate([R[c]["ss_out"] for c in range(8)], 0)[None]
    return (y_p.astype(np.float32), y_s.astype(np.float32), kwp.astype(np.float32), vwp.astype(np.float32),
            rsp.astype(np.float32), kws.astype(np.float32), vws.astype(np.float32), rss.astype(np.float32))
```
